# Optimizing a Trainium2 kernel written in Bass

```python
import math
import jax, jax.numpy as jnp
from jax import lax
import numpy as np

D_MODEL = 1024
BATCH = 16
SEQ = 2048
DEPTH = 1
DEC_BATCH = 16
DEC_SEQ = 4096
PAST_LEN = 128

SSD_EXPAND = 2
D_INNER = SSD_EXPAND * D_MODEL
SSD_HEAD_DIM = 64
SSD_HEADS = D_INNER // SSD_HEAD_DIM
SSD_GROUPS = 8
SSD_HPG = SSD_HEADS // SSD_GROUPS
SSD_STATE = 128
CONV_WIDTH = 5
SSD_CHUNK = 128
CONV_CH = D_INNER + 2 * SSD_GROUPS * SSD_STATE

MLA_HEADS = 16
Q_LORA = 384
KV_LORA = 256
QK_NOPE = 64
QK_ROPE = 32
V_HEAD = 64
ROPE_THETA = 10000.0
Q_BLOCK = 128

D_FF = 4 * D_MODEL
N_ADA = 6
EPS = 1e-6

IN_SIZES = (D_INNER,
            CONV_CH,
            2 * SSD_HEADS,
            Q_LORA,
            KV_LORA + QK_ROPE,
            2 * D_MODEL)
D_IN_PROJ = sum(IN_SIZES)

kernel_name = "hybrid_ssd_mla_gated_encoder"

F32 = jnp.float32


def _split(t, sizes):
    idx = [int(i) for i in np.cumsum(sizes)[:-1]]
    return jnp.split(t, idx, axis=-1)


def _rms_norm(x, g):
    xf = x.astype(F32)
    y = xf * lax.rsqrt(jnp.mean(jnp.square(xf), axis=-1, keepdims=True) + EPS)
    return (y * g.astype(F32)).astype(x.dtype)


def _dwconv(u, w, b):
    out = lax.conv_general_dilated(
        u, w[:, None, :].astype(u.dtype), window_strides=(1,),
        padding=[(CONV_WIDTH // 2, CONV_WIDTH // 2)],
        dimension_numbers=('NWC', 'WIO', 'NWC'),
        feature_group_count=u.shape[-1])
    return out + b.astype(u.dtype)


def _ssd_scan(x, dt, a, b_in, c_in):
    bsz, s = x.shape[:2]
    nc = s // SSD_CHUNK

    def chunk(t):
        return t.astype(F32).reshape((bsz, nc, SSD_CHUNK) + t.shape[2:])

    xc, dtc, bc, cc = chunk(x), chunk(dt), chunk(b_in), chunk(c_in)
    cum = jnp.cumsum(dtc * a, axis=2)
    seg = cum[:, :, :, None] - cum[:, :, None, :]
    lower = jnp.tril(jnp.ones((SSD_CHUNK, SSD_CHUNK), bool))[None, None, :, :, None, None]
    decay = jnp.exp(jnp.where(lower, seg, -jnp.inf))
    cb = jnp.einsum('bclgn,bcsgn->bclsg', cc, bc)
    m = cb[..., None] * decay * dtc[:, :, None]
    y_diag = jnp.einsum('bclsge,bcsgep->bclgep', m, xc)
    decay_end = jnp.exp(cum[:, :, -1:] - cum)
    xw = xc * (decay_end * dtc)[..., None]
    states = jnp.einsum('bcsgn,bcsgep->bcgepn', bc, xw)
    chunk_decay = jnp.exp(cum[:, :, -1])

    def step(h, inp):
        st, dec = inp
        return h * dec[..., None, None] + st, h

    h0 = jnp.zeros((bsz, SSD_GROUPS, SSD_HPG, SSD_HEAD_DIM, SSD_STATE), F32)
    _, prev = lax.scan(step, h0, (jnp.moveaxis(states, 1, 0), jnp.moveaxis(chunk_decay, 1, 0)))
    y_off = jnp.einsum('bclgn,cbgepn->bclgep', cc, prev) * jnp.exp(cum)[..., None]
    return (y_diag + y_off).reshape(x.shape)


def _ssd_bidir(x, dt_f, dt_b, a_f, a_b, b_in, c_in):
    flip = lambda t: jnp.flip(t, axis=1)
    y_f = _ssd_scan(x, dt_f, a_f, b_in, c_in)
    y_b = flip(_ssd_scan(flip(x), flip(dt_b), a_b, flip(b_in), flip(c_in)))
    return y_f + y_b


def _rope_tables(s):
    inv = 1.0 / (ROPE_THETA ** (jnp.arange(0, QK_ROPE, 2, dtype=F32) / QK_ROPE))
    ang = jnp.arange(s, dtype=F32)[:, None] * inv[None, :]
    return jnp.cos(ang), jnp.sin(ang)


def _apply_rope(t, cos, sin):
    t1, t2 = jnp.split(t.astype(F32), 2, axis=-1)
    c, s_ = cos[:, None, :], sin[:, None, :]
    return jnp.concatenate([t1 * c - t2 * s_, t1 * s_ + t2 * c], axis=-1).astype(t.dtype)


def _mla(q_a, kv_a, g_qn, w_qb, g_kvn, w_kvb):
    bsz, s, _ = q_a.shape
    q = (_rms_norm(q_a, g_qn) @ w_qb).reshape(bsz, s, MLA_HEADS, QK_NOPE + QK_ROPE)
    q_nope, q_rope = q[..., :QK_NOPE], q[..., QK_NOPE:]
    c_kv, k_rope = kv_a[..., :KV_LORA], kv_a[..., KV_LORA:]
    kv = (_rms_norm(c_kv, g_kvn) @ w_kvb).reshape(bsz, s, MLA_HEADS, QK_NOPE + V_HEAD)
    k_nope, v = kv[..., :QK_NOPE], kv[..., QK_NOPE:]
    cos, sin = _rope_tables(s)
    q_rope = _apply_rope(q_rope, cos, sin)
    k_rope = _apply_rope(k_rope[:, :, None, :], cos, sin)[:, :, 0]
    scale = (QK_NOPE + QK_ROPE) ** -0.5
    nb = s // Q_BLOCK

    def blk(t):
        return jnp.moveaxis(t.reshape(bsz, nb, Q_BLOCK, MLA_HEADS, t.shape[-1]), 1, 0)

    def attend(qs):
        qn, qr = qs
        logits = (jnp.einsum('bqhd,bkhd->bhqk', qn, k_nope)
                  + jnp.einsum('bqhr,bkr->bhqk', qr, k_rope)).astype(F32) * scale
        p = jax.nn.softmax(logits, axis=-1).astype(v.dtype)
        return jnp.einsum('bhqk,bkhd->bqhd', p, v)

    o = lax.map(attend, (blk(q_nope), blk(q_rope)))
    return jnp.moveaxis(o, 0, 1).reshape(bsz, s, MLA_HEADS * V_HEAD)


def _layer(x, c, w_ada, b_ada, g_norm1, w_in, conv_w, conv_b, dt_bias_fwd, dt_bias_bwd,
           a_log_fwd, a_log_bwd, d_skip, g_ssd_norm, w_ssd_out, g_q_norm, w_q_b,
           g_kv_norm, w_kv_b, w_mla_out, w_o, g_norm2, w_mlp_in, w_mlp_out):
    bsz, s, _ = x.shape
    ada = jax.nn.silu(c) @ w_ada + b_ada
    shift1, scale1, gate1, shift2, scale2, gate2 = jnp.split(ada[:, None, :], N_ADA, axis=-1)

    h = _rms_norm(x, g_norm1) * (1.0 + scale1) + shift1
    z, xbc, dt_raw, q_a, kv_a, gates = _split(h @ w_in, IN_SIZES)

    xbc = jax.nn.silu(_dwconv(xbc, conv_w, conv_b))
    xs, b_in, c_in = _split(xbc, (D_INNER, SSD_GROUPS * SSD_STATE, SSD_GROUPS * SSD_STATE))
    xs = xs.reshape(bsz, s, SSD_GROUPS, SSD_HPG, SSD_HEAD_DIM)
    b_in = b_in.reshape(bsz, s, SSD_GROUPS, SSD_STATE)
    c_in = c_in.reshape(bsz, s, SSD_GROUPS, SSD_STATE)
    dt_raw = dt_raw.astype(F32)
    dt_f = jax.nn.softplus(dt_raw[..., :SSD_HEADS] + dt_bias_fwd.astype(F32)).reshape(bsz, s, SSD_GROUPS, SSD_HPG)
    dt_b = jax.nn.softplus(dt_raw[..., SSD_HEADS:] + dt_bias_bwd.astype(F32)).reshape(bsz, s, SSD_GROUPS, SSD_HPG)
    a_f = -jnp.exp(a_log_fwd.astype(F32)).reshape(SSD_GROUPS, SSD_HPG)
    a_b = -jnp.exp(a_log_bwd.astype(F32)).reshape(SSD_GROUPS, SSD_HPG)
    y = _ssd_bidir(xs, dt_f, dt_b, a_f, a_b, b_in, c_in)
    y = y + d_skip.astype(F32).reshape(SSD_GROUPS, SSD_HPG)[..., None] * xs.astype(F32)
    y = y.astype(x.dtype).reshape(bsz, s, D_INNER)
    yg = (y * jax.nn.silu(z)).reshape(bsz, s, SSD_GROUPS, D_INNER // SSD_GROUPS)
    yg = _rms_norm(yg, jnp.ones((), F32)).reshape(bsz, s, D_INNER) * g_ssd_norm.astype(x.dtype)
    y_a = yg @ w_ssd_out

    y_b = _mla(q_a, kv_a, g_q_norm, w_q_b, g_kv_norm, w_kv_b) @ w_mla_out

    g_a, g_b = jnp.split(jax.nn.sigmoid(gates), 2, axis=-1)
    mixed = (g_a * y_a + g_b * y_b) @ w_o
    x = x + gate1 * mixed

    h2 = _rms_norm(x, g_norm2) * (1.0 + scale2) + shift2
    x = x + gate2 * (jnp.square(jax.nn.relu(h2 @ w_mlp_in)) @ w_mlp_out)
    return x


def setup_inputs(seed: int = 0) -> dict:
    key = jax.random.key(seed)
    ks = iter(jax.random.split(key, 40))

    def nrm(shape, scale):
        return jax.random.normal(next(ks), shape, F32) * scale

    def gain(shape):
        return 1.0 + nrm(shape, 0.05)

    L = DEPTH
    dt0 = jnp.exp(jax.random.uniform(next(ks), (L, SSD_HEADS), F32,
                                     math.log(1e-3), math.log(1e-1)))
    dt1 = jnp.exp(jax.random.uniform(next(ks), (L, SSD_HEADS), F32,
                                     math.log(1e-3), math.log(1e-1)))
    inv_sp = lambda d: d + jnp.log(-jnp.expm1(-d))
    return {
        "x_prompt": nrm((BATCH, SEQ, D_MODEL), 1.0),
        "x_sample": nrm((DEC_BATCH, DEC_SEQ, D_MODEL), 1.0),
        "c_prompt": nrm((BATCH, D_MODEL), 1.0),
        "c_sample": nrm((DEC_BATCH, D_MODEL), 1.0),
        "w_ada": nrm((L, D_MODEL, N_ADA * D_MODEL), 0.5 * D_MODEL ** -0.5),
        "b_ada": nrm((L, N_ADA * D_MODEL), 0.02),
        "g_norm1": gain((L, D_MODEL)),
        "w_in": nrm((L, D_MODEL, D_IN_PROJ), D_MODEL ** -0.5),
        "conv_w": nrm((L, CONV_WIDTH, CONV_CH), CONV_WIDTH ** -0.5),
        "conv_b": nrm((L, CONV_CH), 0.02),
        "dt_bias_fwd": inv_sp(dt0),
        "dt_bias_bwd": inv_sp(dt1),
        "a_log_fwd": jnp.log(jax.random.uniform(next(ks), (L, SSD_HEADS), F32, 1.0, 16.0)),
        "a_log_bwd": jnp.log(jax.random.uniform(next(ks), (L, SSD_HEADS), F32, 1.0, 16.0)),
        "d_skip": 1.0 + nrm((L, SSD_HEADS), 0.1),
        "g_ssd_norm": gain((L, D_INNER)),
        "w_ssd_out": nrm((L, D_INNER, D_MODEL), D_INNER ** -0.5),
        "g_q_norm": gain((L, Q_LORA)),
        "w_q_b": nrm((L, Q_LORA, MLA_HEADS * (QK_NOPE + QK_ROPE)), Q_LORA ** -0.5),
        "g_kv_norm": gain((L, KV_LORA)),
        "w_kv_b": nrm((L, KV_LORA, MLA_HEADS * (QK_NOPE + V_HEAD)), KV_LORA ** -0.5),
        "w_mla_out": nrm((L, MLA_HEADS * V_HEAD, D_MODEL), (MLA_HEADS * V_HEAD) ** -0.5),
        "w_o": nrm((L, D_MODEL, D_MODEL), D_MODEL ** -0.5),
        "g_norm2": gain((L, D_MODEL)),
        "w_mlp_in": nrm((L, D_MODEL, D_FF), D_MODEL ** -0.5),
        "w_mlp_out": nrm((L, D_FF, D_MODEL), D_FF ** -0.5),
        "g_final": gain((D_MODEL,)),
    }


def reference(x_prompt, x_sample, c_prompt, c_sample, w_ada, b_ada, g_norm1, w_in, conv_w,
              conv_b, dt_bias_fwd, dt_bias_bwd, a_log_fwd, a_log_bwd, d_skip, g_ssd_norm,
              w_ssd_out, g_q_norm, w_q_b, g_kv_norm, w_kv_b, w_mla_out, w_o, g_norm2,
              w_mlp_in, w_mlp_out, g_final):
    def trunk(x, c):
        for l in range(DEPTH):
            x = _layer(x, c, w_ada[l], b_ada[l], g_norm1[l], w_in[l], conv_w[l], conv_b[l],
                       dt_bias_fwd[l], dt_bias_bwd[l], a_log_fwd[l], a_log_bwd[l], d_skip[l],
                       g_ssd_norm[l], w_ssd_out[l], g_q_norm[l], w_q_b[l], g_kv_norm[l],
                       w_kv_b[l], w_mla_out[l], w_o[l], g_norm2[l], w_mlp_in[l], w_mlp_out[l])
        return _rms_norm(x, g_final)

    y_prompt = trunk(x_prompt, c_prompt)
    y_sample = trunk(x_sample, c_sample)
    return (y_prompt, y_sample)
```

```python
import numpy as np
import concourse.bass as bass
import concourse.mybir as mybir
from concourse.bass_utils import run_bass_kernel_spmd

F32 = mybir.dt.float32
BF16 = mybir.dt.bfloat16
AF = mybir.ActivationFunctionType
ALU = mybir.AluOpType
AX = mybir.AxisListType

D = 1024
DI = 2048
NH = 32
DIN = 8928
EPS = 1e-6
C_Z, C_X, C_DT, C_QA, C_KV, C_KR, C_G = 0, 2048, 6144, 6208, 6592, 6848, 6880
ATT_SCALE = 96 ** -0.5


class Buf:
    __slots__ = ("w", "r")

    def __init__(self):
        self.w = None
        self.r = {}


class Eng:
    def __init__(self, key, h, sem):
        self.key, self.h, self.sem = key, h, sem
        self.count = 0
        self.waited = {}
        self.slots = []
        self.dma_i = 0


class Slot:
    def __init__(self, key, sem):
        self.key, self.sem, self.count = key, sem, 0


class Sch:
    def __init__(self, nc, stack, nslots=12):
        self.nc = nc
        self.E = {}
        for key, h in (("pe", nc.tensor), ("act", nc.scalar), ("dve", nc.vector),
                       ("pool", nc.gpsimd), ("sp", nc.sync)):
            sem = stack.enter_context(nc.semaphore("sem_" + key))
            self.E[key] = Eng(key, h, sem)
        for q in ("sp", "pool", "act"):
            for i in range(nslots):
                sem = stack.enter_context(nc.semaphore("dq_%s_%d" % (q, i)))
                self.E[q].slots.append(Slot("dq_%s_%d" % (q, i), sem))

    def _wait(self, e, deps):
        for (key, sem, val) in deps:
            if key == e.key and key == "pe":
                continue
            if e.waited.get(key, 0) >= val:
                continue
            e.h.wait_ge(sem, val)
            e.waited[key] = val

    @staticmethod
    def _deps(reads, writes):
        deps = []
        for b in reads:
            if b.w is not None:
                deps.append(b.w)
        for b in writes:
            if b.w is not None:
                deps.append(b.w)
            for k, (s, v) in b.r.items():
                deps.append((k, s, v))
        return deps

    @staticmethod
    def _mark(tok, reads, writes):
        for b in reads:
            b.r[tok[0]] = (tok[1], tok[2])
        for b in writes:
            b.w = tok
            b.r = {}

    def op(self, eng, fn, reads=(), writes=()):
        e = self.E[eng]
        self._wait(e, self._deps(reads, writes))
        ins = fn(e.h)
        e.count += 1
        ins.then_inc(e.sem, 1)
        self._mark((e.key, e.sem, e.count), reads, writes)

    def dma(self, q, out, in_, reads=(), writes=()):
        e = self.E[q]
        self._wait(e, self._deps(reads, writes))
        sl = e.slots[e.dma_i % len(e.slots)]
        e.dma_i += 1
        if sl.count > 0:
            self._wait(e, [(sl.key, sl.sem, 16 * sl.count)])
        e.h.dma_start(out=out, in_=in_).then_inc(sl.sem, 16)
        sl.count += 1
        self._mark((sl.key, sl.sem, 16 * sl.count), reads, writes)

    def barrier(self):
        toks = []
        for e in self.E.values():
            if e.count:
                toks.append((e.key, e.sem, e.count))
            for sl in e.slots:
                if sl.count:
                    toks.append((sl.key, sl.sem, 16 * sl.count))
        for e in self.E.values():
            self._wait(e, toks)


def make_consts():
    i = np.arange(128)
    c = np.zeros((128, 6, 128), np.float32)
    c[:, 0, :] = np.eye(128)
    c[:, 1, :] = (i[:, None] <= i[None, :])
    c[:, 2, :] = (i[:, None] >= i[None, :])
    c[:, 3, :] = (i[:, None] > i[None, :])
    c[:, 4, :] = (i[:, None] < i[None, :])
    c[:, 5, :] = 1.0
    return c.reshape(128, 768)


def rope_tables(smax):
    inv = (1.0 / (np.float32(10000.0) ** (np.arange(0, 32, 2, dtype=np.float32) / np.float32(32)))).astype(np.float32)
    ang = np.arange(smax, dtype=np.float32)[:, None] * inv[None, :]
    cos = np.cos(ang).astype(np.float32).T
    sin = np.sin(ang).astype(np.float32).T
    return (np.ascontiguousarray(np.concatenate([cos, cos], 0)),
            np.ascontiguousarray(np.concatenate([sin, sin], 0)))


def build(seqs=(2048, 2048, 4096, 4096), phases=("p0", "p1a", "p1b", "p1c", "p1d", "p2", "p3", "p4a", "p4b"), debug=False):
    nc = bass.Bass("TRN2", target_bir_lowering=False)
    NS = len(seqs)
    NT = sum(seqs)
    SMAX = max(seqs)
    tok0 = [sum(seqs[:i]) for i in range(NS)]
    skind = "ExternalOutput" if debug else "Internal"

    def din(name, shape, dt=F32):
        return nc.dram_tensor(name, list(shape), dt, kind="ExternalInput").ap()

    def dscr(name, shape, dt):
        return nc.dram_tensor(name, list(shape), dt, kind=skind).ap()

    x_d = din("x", [NT, D])
    c_d = din("c", [NS, D])
    w_ada = din("w_ada", [D, 6 * D])
    b_ada = din("b_ada", [6 * D])
    g_norm1 = din("g_norm1", [D])
    w_in = din("w_in", [D, DIN])
    conv_w = din("conv_w", [5, 4096])
    conv_b = din("conv_b", [4096])
    dtb = din("dt_bias", [64])
    alog = din("a_log", [64])
    dskip = din("d_skip", [32])
    g_ssd = din("g_ssd_norm", [DI])
    w_ssd_out = din("w_ssd_out", [DI, D])
    g_q = din("g_q_norm", [384])
    w_q_b = din("w_q_b", [384, 1536])
    g_kv = din("g_kv_norm", [256])
    w_kv_b = din("w_kv_b", [256, 2048])
    w_mla_out = din("w_mla_out", [D, D])
    w_o = din("w_o", [D, D])
    g_norm2 = din("g_norm2", [D])
    w_mlp_in = din("w_mlp_in", [D, 4 * D])
    w_mlp_out = din("w_mlp_out", [4 * D, D])
    g_final = din("g_final", [D])
    consts_d = din("consts", [128, 768])
    cos_d = din("cos_t", [32, SMAX])
    sin_d = din("sin_t", [32, SMAX])

    y_d = nc.dram_tensor("y", [NT, D], F32, kind="ExternalOutput").ap()

    adaT_d = dscr("adaT_s", [NS, 4, D], F32)
    gate_d = dscr("gate_s", [NS, 2, D], F32)
    uT_d = dscr("uT_s", [4096, NT], F32)
    zs_d = dscr("zs_s", [NT, DI], BF16)
    dt_d = dscr("dt_s", [NT, 64], F32)
    qnT_d = dscr("qnT_s", [384, NT], BF16)
    cnT_d = dscr("cnT_s", [256, NT], BF16)
    kr_d = dscr("kr_s", [32, NT], BF16)
    gT_d = dscr("gT_s", [2048, NT], BF16)
    xtok_d = dscr("xtok_s", [NT, 3072], BF16)
    bcT_d = dscr("bcT_s", [2048, NT], BF16)
    KT_d = dscr("KT_s", [1024, NT], BF16)
    QT_d = dscr("QT_s", [1536, NT], BF16)
    V_d = dscr("V_s", [16, NT * 64], BF16)
    OT_d = dscr("OT_s", [D, NT], BF16)
    prevb_d = dscr("prevb_s", [SMAX // 128, 128, DI], BF16)
    ygT_d = dscr("ygT_s", [DI, NT], BF16)
    x1_d = dscr("x1_s", [NT, D], F32)
    h2T_d = dscr("h2T_s", [D, NT], BF16)

    from contextlib import ExitStack
    with ExitStack() as top:
        S = Sch(nc, top)
        _uid = [0]

        def sb(st, name, shape, dt):
            _uid[0] += 1
            return st.enter_context(nc.sbuf_tensor("%s_%d" % (name, _uid[0]), list(shape), dt))
        psall = top.enter_context(nc.psum_tensor("psall", [128, 4096], F32))
        banks = [psall[:, i * 512:(i + 1) * 512] for i in range(8)]
        bbuf = [Buf() for _ in range(8)]
        bi = {None: 0, "b": 0}
        pools = {None: list(range(8)), "b": [5, 6, 7]}

        def bank(pool=None):
            lst = pools[pool]
            i = lst[bi[pool] % len(lst)]
            bi[pool] += 1
            return banks[i], bbuf[i]

        cst32 = sb(top, "cst32", [128, 768], F32)
        cstb = sb(top, "cstb", [128, 768], BF16)
        epst = sb(top, "epst", [128, 1], F32)
        onet = sb(top, "onet", [128, 1], F32)
        B_c = Buf()
        S.dma("sp", cst32[:], consts_d, writes=[B_c])
        S.op("dve", lambda v: v.tensor_copy(cstb[:], cst32[:]), reads=[B_c], writes=[B_c])
        S.op("dve", lambda v: v.memset(epst[:], EPS), writes=[B_c])
        S.op("dve", lambda v: v.memset(onet[:], 1.0), writes=[B_c])
        identb = cstb[:, 0:128]
        Ufb, Ubb, Lfb, Lbb, onesb = (cstb[:, 128 * k:128 * (k + 1)] for k in range(1, 6))
        Uf32, Ub32 = cst32[:, 128:256], cst32[:, 256:384]
        ones32 = cst32[:, 640:768]

        def rstd_from_ss(eng_list, out, ss, n, rb, wb):
            S.op("act", lambda a: a.activation(out, ss, AF.Sqrt, bias=epst[0:out.shape[0], :], scale=1.0 / n), reads=rb + [B_c], writes=wb)
            S.op("dve", lambda v: v.reciprocal(out, out), reads=wb, writes=wb)

        if "p0" in phases:
            with ExitStack() as st:
                cT = sb(st, "p0_cT", [128, 8, NS], F32)
                cbc = sb(st, "p0_cbc", [128, 8, NS, 128], F32)
                wa = [sb(st, "p0_wa%d" % i, [128, 8, 1024], F32) for i in range(2)]
                bT = sb(st, "p0_bT", [128, 48], F32)
                brow = sb(st, "p0_brow", [128, 2, 1024], F32)
                gT1 = sb(st, "p0_g", [128, 2, 8], F32)
                res = sb(st, "p0_res", [128, 6, 8, NS], F32)
                vec = sb(st, "p0_vec", [128, NS, 4, 8], F32)
                grow = sb(st, "p0_grow", [128, 1024], F32)
                Bc, Bw, Bb, Br, Bv, Bg = Buf(), [Buf(), Buf()], Buf(), Buf(), Buf(), Buf()
                with nc.allow_non_contiguous_dma(reason="tiny transposed loads"):
                    for b in range(NS):
                        S.dma("sp", cT[:, :, b], c_d[b, :].rearrange("(kc p) -> p kc", p=128), writes=[Bc])
                    S.dma("sp", bT[:], b_ada.rearrange("(j p) -> p j", p=128), writes=[Bb])
                    S.dma("sp", gT1[:, 0, :], g_norm1.rearrange("(j p) -> p j", p=128), writes=[Bb])
                    S.dma("sp", gT1[:, 1, :], g_norm2.rearrange("(j p) -> p j", p=128), writes=[Bb])
                S.dma("sp", brow[:, 0, :], b_ada[2048:3072].partition_broadcast(128), writes=[Bb])
                S.dma("sp", brow[:, 1, :], b_ada[5120:6144].partition_broadcast(128), writes=[Bb])
                S.op("act", lambda a: a.activation(cT[:], cT[:], AF.Silu), reads=[Bc], writes=[Bc])
                S.op("dve", lambda v: v.tensor_copy(cbc[:], cT[:].unsqueeze(3).broadcast_to([128, 8, NS, 128])), reads=[Bc], writes=[Bc])
                for j in range(6):
                    w = wa[j % 2]
                    S.dma("sp", w[:], w_ada[:, j * 1024:(j + 1) * 1024].rearrange("(kc p) f -> p kc f", p=128), writes=[Bw[j % 2]])
                    pt, pb = bank()
                    for oc in range(8):
                        for kc in range(8):
                            S.op("pe", lambda pe, oc=oc, kc=kc: pe.matmul(pt[:, oc * NS:(oc + 1) * NS], w[:, kc, oc * 128:(oc + 1) * 128], cT[:, kc, :], start=(kc == 0), stop=(kc == 7)),
                                 reads=[Bw[j % 2], Bc], writes=[pb])
                    S.op("dve", lambda v, j=j: v.tensor_tensor(res[:, j, :, :], pt[:, 0:8 * NS].rearrange("p (o b) -> p o b", b=NS),
                                                           bT[:, j * 8:(j + 1) * 8].unsqueeze(2).broadcast_to([128, 8, NS]), ALU.add),
                         reads=[pb, Bb], writes=[Br])
                    if j in (2, 5):
                        gi = 0 if j == 2 else 1
                        for b in range(NS):
                            for hf in range(2):
                                pt2, pb2 = bank()
                                for kc in range(8):
                                    S.op("pe", lambda pe, kc=kc, b=b, hf=hf: pe.matmul(pt2[:], cbc[:, kc, b, :], w[:, kc, hf * 512:(hf + 1) * 512], start=(kc == 0), stop=(kc == 7)),
                                         reads=[Bw[j % 2], Bc], writes=[pb2])
                                S.op("dve", lambda v, hf=hf, gi=gi: v.tensor_tensor(grow[:, hf * 512:(hf + 1) * 512], pt2[:], brow[:, gi, hf * 512:(hf + 1) * 512], ALU.add),
                                     reads=[pb2, Bb], writes=[Bg])
                            S.dma("sp", gate_d[b, gi, :], grow[0:1, :], reads=[Bg])
                for b in range(NS):
                    for (k, jsc, jsh, gi) in ((0, 1, 0, 0), (2, 4, 3, 1)):
                        S.op("dve", lambda v, b=b, k=k, jsc=jsc, gi=gi: v.scalar_tensor_tensor(vec[:, b, k, :], res[:, jsc, :, b], 1.0, gT1[:, gi, :], ALU.add, ALU.mult), reads=[Br, Bb], writes=[Bv])
                        S.op("dve", lambda v, b=b, k=k, jsh=jsh: v.tensor_copy(vec[:, b, k + 1, :], res[:, jsh, :, b]), reads=[Br], writes=[Bv])
                with nc.allow_non_contiguous_dma(reason="tiny transposed stores"):
                    for b in range(NS):
                        for k in range(4):
                            S.dma("sp", adaT_d[b, k, :].rearrange("(kc p) -> p kc", p=128), vec[:, b, k, :], reads=[Bv])
                S.barrier()

        for part in ("a", "b"):
            if ("p1" + part) not in phases:
                continue
            WC = 4096 if part == "a" else 4832
            cm = (lambda c: c - 2048) if part == "a" else (lambda c: c if c < 2048 else c - 4096)
            with ExitStack() as st:
                W = sb(st, "p1_W", [128, 8, WC], BF16)
                wkr = sb(st, "p1_wkr", [128, 8, 64], BF16)
                dtb_bc = sb(st, "p1_dtb", [128, 4, 64], F32)
                ab = sb(st, "p1_ab", [128, 2, 8], F32)
                xt = [sb(st, "p1_x%d" % i, [128, 4, 1024], F32) for i in range(1)] * 2
                junk = sb(st, "p1_junk", [128, 1024], BF16)
                ss = sb(st, "p1_ss", [128, 4], F32)
                xn = sb(st, "p1_xn", [128, 4, 1024], BF16)
                hT = [sb(st, "p1_hT%d" % i, [128, 8, 512], BF16) for i in range(1)] * 2
                stg = [sb(st, "p1_stg%d" % i, [128, 512], F32) for i in range(4)]
                stgb = [sb(st, "p1_stgb%d" % i, [128, 512], BF16) for i in range(4)]
                qa = sb(st, "p1_qa", [128, 5, 512], F32) if part == "b" else None
                sq = sb(st, "p1_sq", [128, 5, 512], F32) if part == "b" else None
                rs = sb(st, "p1_rs", [128, 2, 512], F32)
                qn = sb(st, "p1_qn", [128, 5, 512], BF16)
                cs = sb(st, "p1_cs", [32, 2, 512], F32)
                krt = sb(st, "p1_krt", [32, 2, 512], F32)
                krb = sb(st, "p1_krb", [32, 512], BF16)
                zst = [sb(st, "p1_zst%d" % i, [128, 2048], BF16) if part == "b" else None for i in range(2)]
                dts = sb(st, "p1_dts", [128, 5, 256], F32)
                BW, Bab, Bx, Bss, Bxn, BhT = Buf(), Buf(), [Buf()] * 2, Buf(), Buf(), [Buf()] * 2
                Bstg, Bstgb = [Buf() for _ in range(4)], [Buf() for _ in range(4)]
                Bqa, Bsq, Brs, Bqn, Bcs, Bkr, Bkrb, Bz, Bdts = Buf(), Buf(), Buf(), Buf(), Buf(), Buf(), Buf(), [Buf(), Buf()], Buf()
                for kc in range(8):
                    if part == "a":
                        S.dma("pool", W[:, kc, :], w_in[kc * 128:(kc + 1) * 128, 2048:6144], writes=[BW])
                    else:
                        S.dma("pool", W[:, kc, 0:2048], w_in[kc * 128:(kc + 1) * 128, 0:2048], writes=[BW])
                        S.dma("pool", W[:, kc, 2048:4832], w_in[kc * 128:(kc + 1) * 128, 6144:8928], writes=[BW])
                CKR = cm(C_KR) if part == "b" else 0
                S.op("dve", lambda v: v.tensor_copy(wkr[:, :, 0:32], W[:, :, CKR:CKR + 32]), reads=[BW], writes=[BW])
                S.op("dve", lambda v: v.tensor_scalar(wkr[:, :, 32:48], W[:, :, CKR + 16:CKR + 32], -1.0, None, ALU.mult), reads=[BW], writes=[BW])
                S.op("dve", lambda v: v.tensor_copy(wkr[:, :, 48:64], W[:, :, CKR:CKR + 16]), reads=[BW], writes=[BW])
                for s4 in range(4):
                    S.dma("sp", dtb_bc[:, s4, :], dtb.partition_broadcast(128), writes=[BW])
                stg_i = [0]
                for si in range(NS):
                    with nc.allow_non_contiguous_dma(reason="tiny"):
                        S.dma("sp", ab[:, 0, :], adaT_d[si, 0, :].rearrange("(kc p) -> p kc", p=128), writes=[Bab])
                        S.dma("sp", ab[:, 1, :], adaT_d[si, 1, :].rearrange("(kc p) -> p kc", p=128), writes=[Bab])
                    for ti in range(seqs[si] // 512):
                        g0 = tok0[si] + ti * 512
                        p0 = ti * 512
                        par = (g0 // 512) % 2
                        X, H = xt[par], hT[par]
                        S.dma("sp", X[:], x_d[g0:g0 + 512, :].rearrange("(s p) f -> p s f", p=128), writes=[Bx[par]])
                        for s4 in range(4):
                            S.op("act", lambda a, s4=s4: a.activation(junk[:], X[:, s4, :], AF.Square, accum_out=ss[:, s4:s4 + 1]), reads=[Bx[par]], writes=[Bss])
                        rstd_from_ss(None, ss[:], ss[:], 1024.0, [Bss], [Bss])
                        for s4 in range(4):
                            S.op("pool" if s4 % 2 else "dve", lambda v, s4=s4: v.tensor_scalar(xn[:, s4, :], X[:, s4, :], ss[:, s4:s4 + 1], None, ALU.mult), reads=[Bx[par], Bss], writes=[Bxn])
                        for kc in range(8):
                            pt, pb = bank()
                            ptb = pt[:].bitcast(BF16)
                            for s4 in range(4):
                                S.op("pe", lambda pe, s4=s4, kc=kc: pe.transpose(ptb[:, s4 * 128:(s4 + 1) * 128], xn[:, s4, kc * 128:(kc + 1) * 128], identb), reads=[Bxn, B_c], writes=[pb])
                            S.op("dve", lambda v, kc=kc: v.tensor_scalar(H[:, kc, :], ptb[:, 0:512], ab[:, 0, kc:kc + 1], ab[:, 1, kc:kc + 1], ALU.mult, ALU.add), reads=[pb, Bab], writes=[BhT[par]])

                        def fm(cols, m, lw=None):
                            pt, pb = bank()
                            for kc in range(8):
                                l = (lw if lw is not None else W)
                                cc0 = cols if lw is not None else cm(cols)
                                S.op("pe", lambda pe, kc=kc, l=l, cc0=cc0: pe.matmul(pt[0:m, :], l[:, kc, cc0:cc0 + m], H[:, kc, :], start=(kc == 0), stop=(kc == 7)), reads=[BW, BhT[par]], writes=[pb])
                            return pt, pb
                        for j in range(32 if part == "a" else 0):
                            pt, pb = fm(C_X + j * 128, 128)
                            k = stg_i[0] % 4
                            stg_i[0] += 1
                            S.op("act", lambda a, k=k: a.copy(stg[k][:], pt[:]), reads=[pb], writes=[Bstg[k]])
                            S.dma("sp", uT_d[j * 128:(j + 1) * 128, g0:g0 + 512], stg[k][:], reads=[Bstg[k]])
                        if part == "a":
                            continue
                        for j in range(5):
                            pt, pb = fm(C_QA + j * 128, 128)
                            S.op("act", lambda a, j=j: a.copy(qa[:, j, :], pt[:]), reads=[pb], writes=[Bqa])
                            S.op("pool", lambda v, j=j: v.tensor_tensor(sq[:, j, :], qa[:, j, :], qa[:, j, :], ALU.mult), reads=[Bqa], writes=[Bsq])
                        for (r, j0, n) in ((0, 0, 3), (1, 3, 2)):
                            pt, pb = bank()
                            for j in range(n):
                                S.op("pe", lambda pe, j=j: pe.matmul(pt[:], ones32, sq[:, j0 + j, :], start=(j == 0), stop=(j == n - 1)), reads=[Bsq, B_c], writes=[pb])
                            S.op("act", lambda a, r=r, n=n: a.activation(rs[:, r, :], pt[:], AF.Sqrt, bias=epst[:], scale=1.0 / (128 * n)), reads=[pb, B_c], writes=[Brs])
                            S.op("dve", lambda v, r=r: v.reciprocal(rs[:, r, :], rs[:, r, :]), reads=[Brs], writes=[Brs])
                            for j in range(n):
                                S.op("dve", lambda v, j=j, r=r: v.tensor_tensor(qn[:, j0 + j, :], qa[:, j0 + j, :], rs[:, r, :], ALU.mult), reads=[Bqa, Brs], writes=[Bqn])
                        S.dma("sp", qnT_d[:, g0:g0 + 512].rearrange("(j p) t -> p j t", p=128), qn[:, 0:3, :], reads=[Bqn])
                        S.dma("sp", cnT_d[:, g0:g0 + 512].rearrange("(j p) t -> p j t", p=128), qn[:, 3:5, :], reads=[Bqn])
                        S.dma("sp", cs[:, 0, :], cos_d[:, p0:p0 + 512], writes=[Bcs])
                        S.dma("sp", cs[:, 1, :], sin_d[:, p0:p0 + 512], writes=[Bcs])
                        pA, pbA = fm(0, 32, wkr)
                        pB, pbB = fm(32, 32, wkr)
                        S.op("dve", lambda v: v.tensor_tensor(krt[:, 0, :], pA[0:32, :], cs[:, 0, :], ALU.mult), reads=[pbA, Bcs], writes=[Bkr])
                        S.op("dve", lambda v: v.tensor_tensor(krt[:, 1, :], pB[0:32, :], cs[:, 1, :], ALU.mult), reads=[pbB, Bcs], writes=[Bkr])
                        S.op("dve", lambda v: v.tensor_tensor(krb[:], krt[:, 0, :], krt[:, 1, :], ALU.add), reads=[Bkr], writes=[Bkrb])
                        S.dma("sp", kr_d[:, g0:g0 + 512], krb[:], reads=[Bkrb])
                        for j in range(16):
                            pt, pb = fm(C_G + j * 128, 128)
                            k = stg_i[0] % 4
                            stg_i[0] += 1
                            S.op("act", lambda a, k=k: a.activation(stgb[k][:], pt[:], AF.Sigmoid), reads=[pb], writes=[Bstgb[k]])
                            S.dma("sp", gT_d[j * 128:(j + 1) * 128, g0:g0 + 512], stgb[k][:], reads=[Bstgb[k]])
                        for s4 in range(4):
                            Z = zst[s4 % 2]
                            for cg in range(4):
                                pt, pb = bank()
                                for kc in range(8):
                                    S.op("pe", lambda pe, kc=kc, s4=s4, cg=cg: pe.matmul(pt[:], H[:, kc, s4 * 128:(s4 + 1) * 128], W[:, kc, cg * 512:(cg + 1) * 512], start=(kc == 0), stop=(kc == 7)), reads=[BW, BhT[par]], writes=[pb])
                                S.op("act", lambda a, s4=s4, cg=cg: a.activation(Z[:, cg * 512:(cg + 1) * 512], pt[:], AF.Silu), reads=[pb], writes=[Bz[s4 % 2]])
                            S.dma("sp", zs_d[g0 + s4 * 128:g0 + (s4 + 1) * 128, :], Z[:], reads=[Bz[s4 % 2]])
                        pt, pb = bank()
                        for s4 in range(4):
                            for kc in range(8):
                                S.op("pe", lambda pe, kc=kc, s4=s4: pe.matmul(pt[:, s4 * 64:(s4 + 1) * 64], H[:, kc, s4 * 128:(s4 + 1) * 128], W[:, kc, cm(C_DT):cm(C_DT) + 64], start=(kc == 0), stop=(kc == 7)), reads=[BW, BhT[par]], writes=[pb])
                        d0, d1, d2, d3, d4 = (dts[:, k, :] for k in range(5))
                        S.op("dve", lambda v: v.tensor_tensor(d0, pt[:, 0:256], dtb_bc[:].rearrange("p a b -> p (a b)"), ALU.add), reads=[pb, BW], writes=[Bdts])
                        S.op("dve", lambda v: v.tensor_scalar(d1, d0, -1.0, None, ALU.mult), reads=[Bdts], writes=[Bdts])
                        S.op("dve", lambda v: v.tensor_tensor(d1, d0, d1, ALU.min), reads=[Bdts], writes=[Bdts])
                        S.op("act", lambda a: a.activation(d2, d1, AF.Exp), reads=[Bdts], writes=[Bdts])
                        S.op("act", lambda a: a.activation(d3, d2, AF.Ln, bias=onet[:], scale=1.0), reads=[Bdts, B_c], writes=[Bdts])
                        S.op("dve", lambda v: v.scalar_tensor_tensor(d4, d0, 0.0, d3, ALU.max, ALU.add), reads=[Bdts], writes=[Bdts])
                        S.dma("sp", dt_d[g0:g0 + 512, :].rearrange("(s p) f -> p s f", p=128), d4.rearrange("p (s f) -> p s f", f=64), reads=[Bdts])
                S.barrier()

        if "p1c" in phases:
            with ExitStack() as st:
                cw = sb(st, "pc_cw", [128, 32, 6], F32)
                u = [sb(st, "pc_u%d" % i, [128, 516], F32) for i in range(4)]
                acc = [sb(st, "pc_acc%d" % i, [128, 512], F32) for i in range(2)]
                ob = [sb(st, "pc_ob%d" % i, [128, 512], BF16) for i in range(4)]
                tk = [sb(st, "pc_tk%d" % i, [128, 4, 3072], BF16) for i in range(2)]
                Bcw, Bu, Bacc, Bob, Btk = Buf(), [Buf() for _ in range(4)], [Buf(), Buf()], [Buf() for _ in range(4)], [Buf(), Buf()]
                with nc.allow_non_contiguous_dma(reason="tiny"):
                    for k in range(5):
                        S.dma("sp", cw[:, :, k], conv_w[k, :].rearrange("(j p) -> p j", p=128), writes=[Bcw])
                    S.dma("sp", cw[:, :, 5], conv_b.rearrange("(j p) -> p j", p=128), writes=[Bcw])
                it = 0
                for si in range(NS):
                    nt = seqs[si] // 512
                    for ti in range(nt):
                        g0 = tok0[si] + ti * 512
                        par = (g0 // 512) % 2
                        TK = tk[par]
                        for j in range(32):
                            U, BU = u[it % 4], Bu[it % 4]
                            A, BA = acc[it % 2], Bacc[it % 2]
                            O, BO = ob[it % 4], Bob[it % 4]
                            eng = "dve"
                            it += 1
                            lo = 0 if ti > 0 else 2
                            hi = 516 if ti < nt - 1 else 514
                            if lo:
                                S.op(eng, lambda v: v.memset(U[:, 0:2], 0.0), writes=[BU])
                            if hi < 516:
                                S.op(eng, lambda v: v.memset(U[:, 514:516], 0.0), writes=[BU])
                            S.dma("sp", U[:, lo:hi], uT_d[j * 128:(j + 1) * 128, g0 - 2 + lo:g0 - 2 + hi], writes=[BU])
                            S.op(eng, lambda v, j=j: v.tensor_scalar(A[:], U[:, 0:512], cw[:, j, 0:1], None, ALU.mult), reads=[BU, Bcw], writes=[BA])
                            for k in range(1, 5):
                                S.op(eng, lambda v, j=j, k=k: v.scalar_tensor_tensor(A[:], U[:, k:k + 512], cw[:, j, k:k + 1], A[:], ALU.mult, ALU.add), reads=[BU, Bcw, BA], writes=[BA])
                            S.op("act", lambda a, j=j: a.activation(O[:], A[:], AF.Silu, bias=cw[:, j, 5:6], scale=1.0), reads=[BA, Bcw], writes=[BO])
                            if j >= 16:
                                S.dma("sp", bcT_d[(j - 16) * 128:(j - 15) * 128, g0:g0 + 512], O[:], reads=[BO])
                            if j < 24:
                                pt, pb = bank()
                                ptb = pt[:].bitcast(BF16)
                                for s4 in range(4):
                                    S.op("pe", lambda pe, s4=s4: pe.transpose(ptb[:, s4 * 128:(s4 + 1) * 128], O[:, s4 * 128:(s4 + 1) * 128], identb), reads=[BO, B_c], writes=[pb])
                                S.op("act", lambda a, j=j: a.copy(TK[:, :, j * 128:(j + 1) * 128], ptb[:, 0:512].rearrange("p (s f) -> p s f", f=128)), reads=[pb], writes=[Btk[par]])
                        S.dma("sp", xtok_d[g0:g0 + 512, :].rearrange("(s p) f -> p s f", p=128), TK[:], reads=[Btk[par]])
                S.barrier()

        ctx = dict(locals())
        if "p1d" in phases:
            emit_p1d(ctx)
        with ExitStack() as gstack:
            ctx["gstack"] = gstack
            gens = []
            if "p2" in phases:
                gens.append(gen_p2(ctx))
            if "p3" in phases:
                gens.append(gen_p3(ctx))
            run_interleaved(gens)
            S.barrier()
        if "p4a" in phases:
            emit_p4a(ctx)
        if "p4b" in phases:
            emit_p4b(ctx)
        S.barrier()
    return nc


def run_interleaved(gens):
    if not gens:
        return
    if len(gens) == 1:
        for _ in gens[0]:
            pass
        return
    prog = [0.0] * len(gens)
    alive = [True] * len(gens)
    while any(alive):
        k = min((i for i in range(len(gens)) if alive[i]), key=lambda i: prog[i])
        try:
            prog[k] = next(gens[k])
        except StopIteration:
            alive[k] = False


def emit_p1d(c):
    from contextlib import ExitStack
    S, nc, sb, bank, seqs, tok0, NS = c["S"], c["nc"], c["sb"], c["bank"], c["seqs"], c["tok0"], c["NS"]
    qnT_d, cnT_d, KT_d, QT_d, V_d, cos_d, sin_d = c["qnT_d"], c["cnT_d"], c["KT_d"], c["QT_d"], c["V_d"], c["cos_d"], c["sin_d"]
    w_q_b, w_kv_b, g_q, g_kv = c["w_q_b"], c["w_kv_b"], c["g_q"], c["g_kv"]
    with ExitStack() as st:
        wq = sb(st, "pd_wq", [128, 3, 1536], BF16)
        wqB = sb(st, "pd_wqB", [128, 3, 16, 96], BF16)
        wkv = sb(st, "pd_wkv", [128, 2, 2048], BF16)
        wk = sb(st, "pd_wk", [128, 2, 1024], BF16)
        wv = sb(st, "pd_wv", [128, 2, 1024], BF16)
        gq = sb(st, "pd_gq", [128, 5], F32)
        qn = [sb(st, "pd_qn", [128, 3, 512], BF16) for _ in range(2)]
        cn = [sb(st, "pd_cn", [128, 2, 512], BF16) for _ in range(2)]
        cst = [sb(st, "pd_cs", [128, 2, 512], F32) for _ in range(2)]
        kst = [sb(st, "pd_kst", [128, 512], BF16) for _ in range(3)]
        vst = [sb(st, "pd_vst", [128, 512], BF16) for _ in range(3)]
        qst = [sb(st, "pd_qst", [128, 512], BF16) for _ in range(3)]
        rt = [sb(st, "pd_rt", [128, 2, 512], F32) for _ in range(2)]
        Bw2 = Buf()
        Bqn, Bcn, Bcs = [Buf(), Buf()], [Buf(), Buf()], [Buf(), Buf()]
        Bk, Bv, Bq, Brt = [Buf() for _ in range(3)], [Buf() for _ in range(3)], [Buf() for _ in range(3)], [Buf(), Buf()]
        S.dma("pool", wq[:], w_q_b.rearrange("(kc p) f -> p kc f", p=128), writes=[Bw2])
        S.dma("pool", wkv[:], w_kv_b.rearrange("(kc p) f -> p kc f", p=128), writes=[Bw2])
        with nc.allow_non_contiguous_dma(reason="tiny"):
            S.dma("sp", gq[:, 0:3], g_q.rearrange("(j p) -> p j", p=128), writes=[Bw2])
            S.dma("sp", gq[:, 3:5], g_kv.rearrange("(j p) -> p j", p=128), writes=[Bw2])
        for kc in range(3):
            S.op("dve", lambda v: v.tensor_scalar(wq[:, kc, :], wq[:, kc, :], gq[:, kc:kc + 1], None, ALU.mult), reads=[Bw2], writes=[Bw2])
        for kc in range(2):
            S.op("dve", lambda v: v.tensor_scalar(wkv[:, kc, :], wkv[:, kc, :], gq[:, 3 + kc:4 + kc], None, ALU.mult), reads=[Bw2], writes=[Bw2])
            w4 = wkv[:, kc, :].rearrange("p (h t f) -> p h t f", t=2, f=64)
            S.op("dve", lambda v: v.tensor_copy(wk[:, kc, :].rearrange("p (h f) -> p h f", f=64), w4[:, :, 0, :]), reads=[Bw2], writes=[Bw2])
            S.op("dve", lambda v: v.tensor_copy(wv[:, kc, :].rearrange("p (h f) -> p h f", f=64), w4[:, :, 1, :]), reads=[Bw2], writes=[Bw2])
        S.op("dve", lambda v: v.memset(wqB[:], 0.0), writes=[Bw2])
        wq4 = wq[:].rearrange("p k (h f) -> p k h f", f=96)
        for kc in range(3):
            S.op("dve", lambda v: v.tensor_scalar(wqB[:, kc, :, 64:80], wq4[:, kc, :, 80:96], -1.0, None, ALU.mult), reads=[Bw2], writes=[Bw2])
            S.op("dve", lambda v: v.tensor_copy(wqB[:, kc, :, 80:96], wq4[:, kc, :, 64:80]), reads=[Bw2], writes=[Bw2])
        it = 0
        ki = vi = qi = 0
        for si in range(NS):
            Sq, t0 = seqs[si], tok0[si]
            nch = Sq // 128
            for ti in range(Sq // 512):
                g0 = t0 + ti * 512
                p0 = ti * 512
                k = it % 2
                it += 1
                QN, CN, CS = qn[k], cn[k], cst[k]
                S.dma("sp", QN[:], qnT_d[:, g0:g0 + 512].rearrange("(j p) t -> p j t", p=128), writes=[Bqn[k]])
                S.dma("sp", CN[:], cnT_d[:, g0:g0 + 512].rearrange("(j p) t -> p j t", p=128), writes=[Bcn[k]])
                S.dma("sp", CS[64:96, 0, :], cos_d[:, p0:p0 + 512], writes=[Bcs[k]])
                S.dma("sp", CS[64:96, 1, :], sin_d[:, p0:p0 + 512], writes=[Bcs[k]])
                for pr in range(8):
                    pt, pb = bank()
                    for kc in range(2):
                        S.op("pe", lambda pe: pe.matmul(pt, wk[:, kc, pr * 128:(pr + 1) * 128], CN[:, kc, :], start=(kc == 0), stop=(kc == 1)), reads=[Bw2, Bcn[k]], writes=[pb])
                    K_, BK_ = kst[ki % 3], Bk[ki % 3]
                    ki += 1
                    S.op("act", lambda a: a.copy(K_[:], pt), reads=[pb], writes=[BK_])
                    S.dma("sp", KT_d[pr * 128:(pr + 1) * 128, g0:g0 + 512], K_[:], reads=[BK_])
                for s4 in range(4):
                    cidx = ti * 4 + s4
                    for hf in range(2):
                        pt, pb = bank()
                        for kc in range(2):
                            S.op("pe", lambda pe: pe.matmul(pt, CN[:, kc, s4 * 128:(s4 + 1) * 128], wv[:, kc, hf * 512:(hf + 1) * 512], start=(kc == 0), stop=(kc == 1)), reads=[Bw2, Bcn[k]], writes=[pb])
                        V_, BV_ = vst[vi % 3], Bv[vi % 3]
                        vi += 1
                        S.op("dve", lambda v: v.tensor_copy(V_[:], pt), reads=[pb], writes=[BV_])
                        dst = V_d[hf * 8:(hf + 1) * 8, t0 * 64:(t0 + Sq) * 64].rearrange("h (p c f) -> p h c f", p=128, f=64)[:, :, cidx, :]
                        S.dma("sp", dst, V_[:].rearrange("p (h f) -> p h f", f=64), reads=[BV_])
                for h in range(16):
                    pA, pbA = bank()
                    for kc in range(3):
                        S.op("pe", lambda pe: pe.matmul(pA[0:96, :], wq[:, kc, h * 96:(h + 1) * 96], QN[:, kc, :], start=(kc == 0), stop=(kc == 2)), reads=[Bw2, Bqn[k]], writes=[pbA])
                    pB, pbB = bank()
                    for kc in range(3):
                        S.op("pe", lambda pe: pe.matmul(pB[0:96, :], wqB[:, kc, h, :], QN[:, kc, :], start=(kc == 0), stop=(kc == 2)), reads=[Bw2, Bqn[k]], writes=[pbB])
                    Q_, BQ_ = qst[qi % 3], Bq[qi % 3]
                    RT, BRT = rt[qi % 2], Brt[qi % 2]
                    qi += 1
                    S.op("act", lambda a: a.copy(Q_[0:64, :], pA[0:64, :]), reads=[pbA], writes=[BQ_])
                    S.op("dve", lambda v: v.tensor_tensor(RT[64:96, 0, :], pA[64:96, :], CS[64:96, 0, :], ALU.mult), reads=[pbA, Bcs[k]], writes=[BRT])
                    S.op("dve", lambda v: v.tensor_tensor(RT[64:96, 1, :], pB[64:96, :], CS[64:96, 1, :], ALU.mult), reads=[pbB, Bcs[k]], writes=[BRT])
                    S.op("pool", lambda v: v.tensor_tensor(Q_[64:96, :], RT[64:96, 0, :], RT[64:96, 1, :], ALU.add), reads=[BRT], writes=[BQ_])
                    S.dma("sp", QT_d[h * 96:(h + 1) * 96, g0:g0 + 512], Q_[0:96, :], reads=[BQ_])
        S.barrier()


def gen_p2(c):
    from contextlib import ExitStack
    S, nc, sb, seqs, tok0, NS = c["S"], c["nc"], c["sb"], c["seqs"], c["tok0"], c["NS"]
    banks, bbuf, SMAX = c["banks"], c["bbuf"], c["SMAX"]
    psall = c["psall"]
    KT_d, QT_d, V_d, kr_d, OT_d = c["KT_d"], c["QT_d"], c["V_d"], c["kr_d"], c["OT_d"]
    st = c["gstack"]
    if True:
        KT = [sb(st, "p2_KT", [128, SMAX], BF16) for _ in range(2)]
        QT = [sb(st, "p2_QT", [128, SMAX], BF16) for _ in range(2)]
        VA = [sb(st, "p2_VA", [128, SMAX // 128, 128], BF16) for _ in range(2)]
        PT = [sb(st, "p2_PT", [128, 1024], BF16) for _ in range(3)]
        rd = sb(st, "p2_rd", [128, 512], F32)
        og = [sb(st, "p2_og", [128, 512], BF16) for _ in range(2)]
        BK, BQ, BV = [Buf(), Buf()], [Buf(), Buf()], [Buf(), Buf()]
        BPT, Brd, Bog = [Buf() for _ in range(3)], Buf(), [Buf(), Buf()]
        for i in range(2):
            S.op("pool", lambda v: v.memset(VA[i][:, :, 64:128], 1.0), writes=[BV[i]])
        heads = [(si, h) for si in range(NS) for h in range(16)]
        total = float(sum(seqs[si] // 512 * (seqs[si] // 256) * 5 for si, h in heads)) + 1.0
        done = 0

        def load(idx):
            si, h = heads[idx]
            hb = idx % 2
            Sq, t0 = seqs[si], tok0[si]
            S.dma("sp", KT[hb][0:64, 0:Sq], KT_d[h * 64:(h + 1) * 64, t0:t0 + Sq], writes=[BK[hb]])
            S.dma("sp", KT[hb][64:96, 0:Sq], kr_d[:, t0:t0 + Sq], writes=[BK[hb]])
            S.dma("sp", QT[hb][0:96, 0:Sq], QT_d[h * 96:(h + 1) * 96, t0:t0 + Sq], writes=[BQ[hb]])
            S.dma("sp", VA[hb][:, 0:Sq // 128, 0:64], V_d[h, t0 * 64:(t0 + Sq) * 64].rearrange("(p c f) -> p c f", p=128, f=64), writes=[BV[hb]])

        load(0)
        pi = 0
        oi = 0
        for idx, (si, h) in enumerate(heads):
            if idx + 1 < len(heads):
                load(idx + 1)
            hb = idx % 2
            K_, Q_, V_ = KT[hb], QT[hb], VA[hb]
            Sq, t0 = seqs[si], tok0[si]
            npair = Sq // 256
            for qt in range(Sq // 512):
                ql = slice(qt * 512, (qt + 1) * 512)
                pO, pbO = banks[4], bbuf[4]
                pend = None
                for cp in range(npair + 1):
                    if cp < npair:
                        b0 = (cp % 2) * 2
                        for e in range(2):
                            cc = cp * 2 + e
                            S.op("pe", lambda pe: pe.matmul(banks[b0 + e], K_[0:96, cc * 128:(cc + 1) * 128], Q_[0:96, ql], start=True, stop=True), reads=[BK[hb], BQ[hb]], writes=[bbuf[b0 + e]])
                            done += 1
                            yield done / total
                        P_, BP = PT[pi % 3], BPT[pi % 3]
                        pi += 1
                        S.op("act", lambda a: a.activation(P_[:], psall[:, b0 * 512:(b0 + 2) * 512], AF.Exp, scale=ATT_SCALE), reads=[bbuf[b0], bbuf[b0 + 1]], writes=[BP])
                        done += 1
                        yield done / total
                    if pend is not None:
                        pcp, PP, BPP = pend
                        for e in range(2):
                            cc = pcp * 2 + e
                            S.op("pe", lambda pe: pe.matmul(pO, V_[:, cc, :], PP[:, e * 512:(e + 1) * 512], start=(cc == 0), stop=(cc == 2 * npair - 1)), reads=[BV[hb], BPP], writes=[pbO])
                            done += 1
                            yield done / total
                    if cp < npair:
                        pend = (cp, P_, BP)
                O_, BO_ = og[oi % 2], Bog[oi % 2]
                oi += 1
                S.op("dve", lambda v: v.reciprocal(rd[64:128, :], pO[64:128, :]), reads=[pbO], writes=[Brd])
                S.op("dve", lambda v: v.tensor_tensor(O_[0:64, :], pO[0:64, :], rd[64:128, :], ALU.mult), reads=[pbO, Brd], writes=[BO_])
                S.dma("sp", OT_d[h * 64:(h + 1) * 64, t0 + qt * 512:t0 + (qt + 1) * 512], O_[0:64, :], reads=[BO_])
        yield 1.0


def gen_p3(c):
    from contextlib import ExitStack
    S, nc, sb, seqs, tok0, NS = c["S"], c["nc"], c["sb"], c["seqs"], c["tok0"], c["NS"]
    bank0 = c["bank"]
    bank = lambda: bank0("b")
    identb, Ufb, Ubb, Lfb, Lbb, onesb, B_c, epst = c["identb"], c["Ufb"], c["Ubb"], c["Lfb"], c["Lbb"], c["onesb"], c["B_c"], c["epst"]
    xtok_d, bcT_d, dt_d, zs_d, prevb_d, ygT_d, alog, dskip = c["xtok_d"], c["bcT_d"], c["dt_d"], c["zs_d"], c["prevb_d"], c["ygT_d"], c["alog"], c["dskip"]
    Q = "pool"
    st = c["gstack"]
    if True:
        a_bc = sb(st, "p3_a", [128, 64], F32)
        D_bc = sb(st, "p3_D", [128, 32], F32)
        Hs = sb(st, "p3_H", [128, 2048], F32)
        Hb16 = sb(st, "p3_Hb", [128, 2048], BF16)
        xts = [sb(st, "p3_xt", [128, 3072], BF16) for _ in range(2)]
        bcs = [sb(st, "p3_bc", [128, 16, 128], BF16) for _ in range(2)]
        dts_ = [sb(st, "p3_dt", [128, 64], F32) for _ in range(2)]
        zss = [sb(st, "p3_zs", [128, 2048], BF16) for _ in range(2)]
        pvs = [sb(st, "p3_pv", [128, 2048], BF16) for _ in range(2)]
        dA = sb(st, "p3_dA", [128, 64], F32)
        dAh = sb(st, "p3_dAh", [128, 64], BF16)
        dAhf = sb(st, "p3_dAhf", [128, 64], F32)
        dAl = sb(st, "p3_dAl", [128, 64], BF16)
        ct = sb(st, "p3_ct", [128, 128], F32)
        Et = sb(st, "p3_E", [128, 64], F32)
        wgt = sb(st, "p3_w", [128, 64], F32)
        ec = sb(st, "p3_ec", [128, 64], F32)
        dec = sb(st, "p3_dec", [128, 64], F32)
        xw = sb(st, "p3_xw", [128, 2048], BF16)
        xdt = [sb(st, "p3_xdt", [128, 2048], BF16) for _ in range(2)]
        Rs = [sb(st, "p3_R", [128, 2, 8, 128], BF16) for _ in range(2)]
        CBm = sb(st, "p3_CBm", [128, 2, 8, 128], BF16)
        Exs = [sb(st, "p3_Ex", [128, 512], BF16) for _ in range(2)]
        MTs = [sb(st, "p3_MT", [128, 2, 8, 128], BF16) for _ in range(2)]
        y = sb(st, "p3_y", [128, 2048], F32)
        t1s = [sb(st, "p3_t1", [128, 512], F32) for _ in range(2)]
        t2s = [sb(st, "p3_t2", [128, 512], F32) for _ in range(2)]
        sqt = sb(st, "p3_sq", [128, 2048], F32)
        gs = sb(st, "p3_gs", [128, 8], F32)
        ygn = sb(st, "p3_ygn", [128, 2048], BF16)
        ygs = [sb(st, "p3_ygs", [128, 16, 128], BF16) for _ in range(2)]
        Bk, BH, BHb, Bprevd = Buf(), Buf(), Buf(), Buf()
        Bxt, Bbc, Bdt, Bzs, Bpv = [Buf(), Buf()], [Buf(), Buf()], [Buf(), Buf()], [Buf(), Buf()], [Buf(), Buf()]
        BdA, Bct, BE, Bxw, Bxdt, BRs, BCB, BEx, BMTs = Buf(), Buf(), Buf(), Buf(), [Buf(), Buf()], [Buf(), Buf()], Buf(), [Buf(), Buf()], [Buf(), Buf()]
        By, Bt1, Bt2, Bsq, Bgs, Bygn, Bygs = Buf(), [Buf(), Buf()], [Buf(), Buf()], Buf(), Buf(), Buf(), [Buf(), Buf()]
        S.dma(Q, a_bc[:], alog.partition_broadcast(128), writes=[Bk])
        S.dma(Q, D_bc[:], dskip.partition_broadcast(128), writes=[Bk])
        S.op("act", lambda a: a.activation(a_bc[:], a_bc[:], AF.Exp), reads=[Bk], writes=[Bk])
        S.op("dve", lambda v: v.tensor_scalar(a_bc[:], a_bc[:], -1.0, None, ALU.mult), reads=[Bk], writes=[Bk])
        total = float(sum((sq // 128) * 292.4 for sq in seqs))
        done = [0.0]

        def tick():
            done[0] += 1.0
            return min(done[0] / total, 0.999)

        def bc3(ap2, n):
            return ap2.unsqueeze(2).broadcast_to([128, ap2.shape[1], n])

        v3 = lambda ap: ap.rearrange("p (h f) -> p h f", f=64)

        def prep(dtt, Bd):
            S.op("dve", lambda v: v.tensor_tensor(dA[:], dtt[:], a_bc[:], ALU.mult), reads=[Bd, Bk], writes=[BdA])
            yield tick()
            S.op("dve", lambda v: v.tensor_copy(dAh[:], dA[:]), reads=[BdA], writes=[BdA])
            yield tick()
            S.op("dve", lambda v: v.tensor_copy(dAhf[:], dAh[:]), reads=[BdA], writes=[BdA])
            yield tick()
            S.op("dve", lambda v: v.tensor_tensor(dAhf[:], dA[:], dAhf[:], ALU.subtract), reads=[BdA], writes=[BdA])
            yield tick()
            S.op("dve", lambda v: v.tensor_copy(dAl[:], dAhf[:]), reads=[BdA], writes=[BdA])
            yield tick()
            pc, pbc = bank()
            for (o0, o1, L) in ((0, 32, Ufb), (32, 64, Ubb)):
                S.op("pe", lambda pe: pe.matmul(pc[:, o0:o1], L, dAh[:, o0:o1], start=True, stop=False), reads=[BdA, B_c], writes=[pbc])
                yield tick()
                S.op("pe", lambda pe: pe.matmul(pc[:, o0:o1], L, dAl[:, o0:o1], start=False, stop=True), reads=[BdA, B_c], writes=[pbc])
                yield tick()
            S.op("pe", lambda pe: pe.matmul(pc[:, 64:128], onesb, dAh[:], start=True, stop=False), reads=[BdA, B_c], writes=[pbc])
            yield tick()
            S.op("pe", lambda pe: pe.matmul(pc[:, 64:128], onesb, dAl[:], start=False, stop=True), reads=[BdA, B_c], writes=[pbc])
            yield tick()
            S.op("dve", lambda v: v.tensor_copy(ct[:], pc[:, 0:128]), reads=[pbc], writes=[Bct])
            yield tick()
            S.op("dve", lambda v: v.tensor_tensor(Et[:], ct[:, 64:128], ct[:, 0:64], ALU.subtract), reads=[Bct], writes=[BE])
            yield tick()
            S.op("act", lambda a: a.activation(Et[:], Et[:], AF.Exp), reads=[BE], writes=[BE])
            yield tick()
            S.op("dve", lambda v: v.tensor_tensor(wgt[:], dtt[:], Et[:], ALU.mult), reads=[BE, Bd], writes=[BE])
            yield tick()
            S.op("act", lambda a: a.activation(dec[:], ct[:, 64:128], AF.Exp), reads=[Bct], writes=[BE])
            yield tick()
            S.op("act", lambda a: a.activation(ec[:], ct[:, 0:64], AF.Exp), reads=[Bct], writes=[BE])
            yield tick()

        def state_update(X, BX, d):
            S.op("pool", lambda v: v.tensor_tensor(v3(xw[:]), v3(X[:, 0:2048]), bc3(wgt[:, d * 32:(d + 1) * 32], 64), ALU.mult), reads=[BX, BE], writes=[Bxw])
            yield tick()
            S.op("dve", lambda v: v.tensor_tensor(v3(Hs[:]), v3(Hs[:]), bc3(dec[:, d * 32:(d + 1) * 32], 64), ALU.mult), reads=[BE, BHb], writes=[BH])
            yield tick()
            for gp in range(4):
                ps, pbs = bank()
                for gg in range(2):
                    g = gp * 2 + gg
                    S.op("pe", lambda pe: pe.matmul(ps[:, gg * 256:(gg + 1) * 256], X[:, 2048 + g * 128:2048 + (g + 1) * 128], xw[:, g * 256:(g + 1) * 256], start=True, stop=True), reads=[BX, Bxw], writes=[pbs])
                    yield tick()
                S.op("dve", lambda v: v.tensor_tensor(Hs[:, gp * 512:(gp + 1) * 512], Hs[:, gp * 512:(gp + 1) * 512], ps, ALU.add), reads=[pbs], writes=[BH])
                yield tick()

        it = 0
        for si in range(NS):
            Sq, t0 = seqs[si], tok0[si]
            nch = Sq // 128
            S.op("pool", lambda v: v.memset(Hs[:], 0.0), reads=[BHb], writes=[BH])
            yield tick()
            for cidx in range(nch - 1, -1, -1):
                g0 = t0 + cidx * 128
                k = it % 2
                it += 1
                S.dma(Q, xts[k][:], xtok_d[g0:g0 + 128, :], writes=[Bxt[k]])
                yield tick()
                S.dma(Q, dts_[k][:], dt_d[g0:g0 + 128, :], writes=[Bdt[k]])
                yield tick()
                S.op("act", lambda a: a.copy(Hb16[:], Hs[:]), reads=[BH], writes=[BHb])
                yield tick()
                S.dma(Q, prevb_d[cidx], Hb16[:], reads=[BHb], writes=[Bprevd])
                yield tick()
                if cidx > 0:
                    yield from prep(dts_[k], Bdt[k])
                    yield from state_update(xts[k], Bxt[k], 1)
            S.op("pool", lambda v: v.memset(Hs[:], 0.0), reads=[BHb], writes=[BH])
            yield tick()
            for cidx in range(nch):
                g0 = t0 + cidx * 128
                k = it % 2
                it += 1
                X, BX, BCt, BBC, dtt, Bd, Z, BZ, PV, BPV = xts[k], Bxt[k], bcs[k], Bbc[k], dts_[k], Bdt[k], zss[k], Bzs[k], pvs[k], Bpv[k]
                S.dma(Q, X[:], xtok_d[g0:g0 + 128, :], writes=[BX])
                yield tick()
                S.dma(Q, BCt[:], bcT_d[:, g0:g0 + 128].rearrange("(j p) t -> p j t", p=128), writes=[BBC])
                yield tick()
                S.dma(Q, dtt[:], dt_d[g0:g0 + 128, :], writes=[Bd])
                yield tick()
                S.dma(Q, Z[:], zs_d[g0:g0 + 128, :], writes=[BZ])
                yield tick()
                S.dma(Q, PV[:], prevb_d[cidx], reads=[Bprevd], writes=[BPV])
                yield tick()
                S.op("act", lambda a: a.copy(Hb16[:], Hs[:]), reads=[BH], writes=[BHb])
                yield tick()
                yield from prep(dtt, Bd)
                if cidx < nch - 1:
                    yield from state_update(X, BX, 0)
                for g4 in range(2):
                    pcb, pbcb = bank()
                    for gg in range(4):
                        g = g4 * 4 + gg
                        S.op("pe", lambda pe: pe.matmul(pcb[:, gg * 128:(gg + 1) * 128], BCt[:, g, :], BCt[:, 8 + g, :], start=True, stop=True), reads=[BBC], writes=[pbcb])
                        yield tick()
                    for d, Um in ((0, Ufb), (1, Ubb)):
                        S.op("dve", lambda v: v.tensor_tensor(CBm[:, d, g4 * 4:(g4 + 1) * 4, :], pcb.rearrange("p (g l) -> p g l", l=128), Um.unsqueeze(1).broadcast_to([128, 4, 128]), ALU.mult), reads=[pbcb, B_c], writes=[BCB])
                        yield tick()
                for d in range(2):
                    S.op("pool", lambda v: v.tensor_tensor(v3(xdt[d][:]), v3(X[:, 0:2048]), bc3(dtt[:, d * 32:(d + 1) * 32], 64), ALU.mult), reads=[BX, Bd], writes=[Bxdt[d]])
                    yield tick()
                ei = 0
                for gp in range(4):
                    MT, BMT = MTs[gp % 2], BMTs[gp % 2]
                    for d, Lm, Um in ((0, Lfb, Ufb), (1, Lbb, Ubb)):
                        R, BR = Rs[d], BRs[d]
                        h0 = d * 32 + gp * 8
                        for kk, src in ((0, dAh), (1, dAl)):
                            S.op("dve" if kk else "pool", lambda v: v.tensor_tensor(R[:, kk, :, :], Um.unsqueeze(1).broadcast_to([128, 8, 128]), bc3(src[:, h0:h0 + 8], 128), ALU.mult), reads=[BdA, B_c], writes=[BR])
                            yield tick()
                        for gg in range(2):
                            g = gp * 2 + gg
                            pseg, pbseg = bank()
                            S.op("pe", lambda pe: pe.matmul(pseg, Lm, R[:, 0, gg * 4:(gg + 1) * 4, :].rearrange("p h l -> p (h l)"), start=True, stop=False), reads=[BR, B_c], writes=[pbseg])
                            yield tick()
                            S.op("pe", lambda pe: pe.matmul(pseg, Lm, R[:, 1, gg * 4:(gg + 1) * 4, :].rearrange("p h l -> p (h l)"), start=False, stop=True), reads=[BR, B_c], writes=[pbseg])
                            yield tick()
                            Ex, BE_ = Exs[ei % 2], BEx[ei % 2]
                            ei += 1
                            S.op("act", lambda a: a.activation(Ex[:], pseg, AF.Exp), reads=[pbseg], writes=[BE_])
                            yield tick()
                            S.op("dve" if gg else "pool", lambda v: v.tensor_tensor(MT[:, d, gg * 4:(gg + 1) * 4, :], Ex[:].rearrange("p (h l) -> p h l", l=128), CBm[:, d, g:g + 1, :].broadcast_to([128, 4, 128]), ALU.mult), reads=[BE_, BCB], writes=[BMT])
                            yield tick()
                    py, pby = bank()
                    pof, pbof = bank()
                    pob, pbob = bank()
                    for gg in range(2):
                        g = gp * 2 + gg
                        for j in range(4):
                            h = 4 * g + j
                            for d in range(2):
                                S.op("pe", lambda pe: pe.matmul(py[:, gg * 256 + j * 64:gg * 256 + (j + 1) * 64], MT[:, d, gg * 4 + j, :], xdt[d][:, h * 64:(h + 1) * 64], start=(d == 0), stop=(d == 1)), reads=[BMT, Bxdt[d]], writes=[pby])
                                yield tick()
                        S.op("pe", lambda pe: pe.matmul(pof[:, gg * 256:(gg + 1) * 256], BCt[:, 8 + g, :], Hb16[:, g * 256:(g + 1) * 256], start=True, stop=True), reads=[BBC, BHb], writes=[pbof])
                        yield tick()
                        S.op("pe", lambda pe: pe.matmul(pob[:, gg * 256:(gg + 1) * 256], BCt[:, 8 + g, :], PV[:, g * 256:(g + 1) * 256], start=True, stop=True), reads=[BBC, BPV], writes=[pbob])
                        yield tick()
                    T1, BT1, T2, BT2 = t1s[gp % 2], Bt1[gp % 2], t2s[gp % 2], Bt2[gp % 2]
                    S.op("dve", lambda v: v.tensor_tensor(v3(T1[:]), v3(pof), bc3(ec[:, gp * 8:gp * 8 + 8], 64), ALU.mult), reads=[pbof, BE], writes=[BT1])
                    yield tick()
                    S.op("dve", lambda v: v.tensor_tensor(v3(T2[:]), v3(pob), bc3(ec[:, 32 + gp * 8:32 + gp * 8 + 8], 64), ALU.mult), reads=[pbob, BE], writes=[BT2])
                    yield tick()
                    S.op("pool", lambda v: v.tensor_tensor(T1[:], T1[:], T2[:], ALU.add), reads=[BT2], writes=[BT1])
                    yield tick()
                    S.op("pool", lambda v: v.tensor_tensor(v3(T2[:]), v3(X[:, gp * 512:(gp + 1) * 512]), bc3(D_bc[:, gp * 8:gp * 8 + 8], 64), ALU.mult), reads=[BX, Bk], writes=[BT2])
                    yield tick()
                    S.op("pool", lambda v: v.tensor_tensor(T1[:], T1[:], T2[:], ALU.add), reads=[BT2], writes=[BT1])
                    yield tick()
                    S.op("dve", lambda v: v.tensor_tensor(y[:, gp * 512:(gp + 1) * 512], py, T1[:], ALU.add), reads=[pby, BT1], writes=[By])
                    yield tick()
                S.op("dve", lambda v: v.tensor_tensor(y[:], y[:], Z[:], ALU.mult), reads=[BZ], writes=[By])
                yield tick()
                S.op("pool", lambda v: v.tensor_tensor(sqt[:], y[:], y[:], ALU.mult), reads=[By], writes=[Bsq])
                yield tick()
                S.op("dve", lambda v: v.tensor_reduce(gs[:], sqt[:].rearrange("p (g f) -> p g f", f=256), AX.X, ALU.add), reads=[Bsq], writes=[Bgs])
                yield tick()
                S.op("act", lambda a: a.activation(gs[:], gs[:], AF.Sqrt, bias=epst[:], scale=1.0 / 256), reads=[Bgs, B_c], writes=[Bgs])
                yield tick()
                S.op("dve", lambda v: v.reciprocal(gs[:], gs[:]), reads=[Bgs], writes=[Bgs])
                yield tick()
                S.op("dve", lambda v: v.tensor_tensor(ygn[:].rearrange("p (g f) -> p g f", f=256), y[:].rearrange("p (g f) -> p g f", f=256), bc3(gs[:], 256), ALU.mult), reads=[By, Bgs], writes=[Bygn])
                yield tick()
                YS, BYS = ygs[k], Bygs[k]
                for hf in range(2):
                    pt, pbt = bank()
                    ptb = pt.bitcast(BF16)
                    for jj in range(8):
                        j = hf * 8 + jj
                        S.op("pe", lambda pe: pe.transpose(ptb[:, jj * 128:(jj + 1) * 128], ygn[:, j * 128:(j + 1) * 128], identb), reads=[Bygn, B_c], writes=[pbt])
                        yield tick()
                    S.op("act", lambda a: a.copy(YS[:, hf * 8:(hf + 1) * 8, :], ptb[:, 0:1024].rearrange("p (j t) -> p j t", t=128)), reads=[pbt], writes=[BYS])
                    yield tick()
                S.dma(Q, ygT_d[:, g0:g0 + 128].rearrange("(j p) t -> p j t", p=128), YS[:], reads=[BYS])
                yield tick()
        yield 1.0


def emit_p4a(c):
    from contextlib import ExitStack
    S, nc, sb, bank, seqs, tok0, NS = c["S"], c["nc"], c["sb"], c["bank"], c["seqs"], c["tok0"], c["NS"]
    identb, B_c, epst = c["identb"], c["B_c"], c["epst"]
    ygT_d, OT_d, gT_d, x_d, x1_d, h2T_d, gate_d, adaT_d = c["ygT_d"], c["OT_d"], c["gT_d"], c["x_d"], c["x1_d"], c["h2T_d"], c["gate_d"], c["adaT_d"]
    w_ssd_out, w_mla_out, w_o, g_ssd = c["w_ssd_out"], c["w_mla_out"], c["w_o"], c["g_ssd"]
    with ExitStack() as st:
        ws = sb(st, "p4_ws", [128, 16, 1024], BF16)
        wm = sb(st, "p4_wm", [128, 8, 1024], BF16)
        wo = sb(st, "p4_wo", [128, 8, 1024], BF16)
        gsn = sb(st, "p4_gsn", [128, 16], F32)
        yg = sb(st, "p4_yg", [128, 16, 512], BF16)
        ot = sb(st, "p4_ot", [128, 8, 512], BF16)
        gt = sb(st, "p4_gt", [128, 16, 512], BF16)
        xt = sb(st, "p4_x", [128, 4, 1024], F32)
        mixf = sb(st, "p4_mixf", [128, 512], F32)
        mixf2 = sb(st, "p4_mixf2", [128, 512], F32)
        mix = sb(st, "p4_mix", [128, 8, 512], BF16)
        g1 = sb(st, "p4_g1", [128, 1024], F32)
        ab = sb(st, "p4_ab", [128, 2, 8], F32)
        x1 = sb(st, "p4_x1", [128, 4, 1024], F32)
        junk = sb(st, "p4_junk", [128, 1024], BF16)
        ss = sb(st, "p4_ss", [128, 4], F32)
        xn = sb(st, "p4_xn", [128, 4, 1024], BF16)
        h2 = sb(st, "p4_h2", [128, 8, 512], BF16)
        Bw, Byg, Bot, Bgt, Bx, Bmf, Bmf2, Bmix, Bg1, Bab, Bx1, Bss, Bxn, Bh2 = (Buf() for _ in range(14))
        S.dma("pool", ws[:], w_ssd_out.rearrange("(kc p) f -> p kc f", p=128), writes=[Bw])
        S.dma("pool", wm[:], w_mla_out.rearrange("(kc p) f -> p kc f", p=128), writes=[Bw])
        S.dma("pool", wo[:], w_o.rearrange("(kc p) f -> p kc f", p=128), writes=[Bw])
        with nc.allow_non_contiguous_dma(reason="tiny"):
            S.dma("sp", gsn[:], g_ssd.rearrange("(j p) -> p j", p=128), writes=[Bw])
        for kc in range(16):
            S.op("dve", lambda v: v.tensor_scalar(ws[:, kc, :], ws[:, kc, :], gsn[:, kc:kc + 1], None, ALU.mult), reads=[Bw], writes=[Bw])
        for si in range(NS):
            S.dma("sp", g1[:], gate_d[si, 0, :].partition_broadcast(128), writes=[Bg1])
            with nc.allow_non_contiguous_dma(reason="tiny"):
                S.dma("sp", ab[:, 0, :], adaT_d[si, 2, :].rearrange("(kc p) -> p kc", p=128), writes=[Bab])
                S.dma("sp", ab[:, 1, :], adaT_d[si, 3, :].rearrange("(kc p) -> p kc", p=128), writes=[Bab])
            for ti in range(seqs[si] // 512):
                g0 = tok0[si] + ti * 512
                S.dma("sp", yg[:], ygT_d[:, g0:g0 + 512].rearrange("(j p) t -> p j t", p=128), writes=[Byg])
                S.dma("sp", ot[:], OT_d[:, g0:g0 + 512].rearrange("(j p) t -> p j t", p=128), writes=[Bot])
                S.dma("sp", gt[:], gT_d[:, g0:g0 + 512].rearrange("(j p) t -> p j t", p=128), writes=[Bgt])
                S.dma("sp", xt[:], x_d[g0:g0 + 512, :].rearrange("(s p) f -> p s f", p=128), writes=[Bx])
                for oc in range(8):
                    pa, pba = bank()
                    for kc in range(16):
                        S.op("pe", lambda pe: pe.matmul(pa[:], ws[:, kc, oc * 128:(oc + 1) * 128], yg[:, kc, :], start=(kc == 0), stop=(kc == 15)), reads=[Bw, Byg], writes=[pba])
                    pm, pbm = bank()
                    for kc in range(8):
                        S.op("pe", lambda pe: pe.matmul(pm[:], wm[:, kc, oc * 128:(oc + 1) * 128], ot[:, kc, :], start=(kc == 0), stop=(kc == 7)), reads=[Bw, Bot], writes=[pbm])
                    S.op("dve", lambda v: v.tensor_tensor(mixf[:], pa[:], gt[:, oc, :], ALU.mult), reads=[pba, Bgt], writes=[Bmf])
                    S.op("dve", lambda v: v.tensor_tensor(mixf2[:], pm[:], gt[:, 8 + oc, :], ALU.mult), reads=[pbm, Bgt], writes=[Bmf2])
                    S.op("pool", lambda v: v.tensor_tensor(mix[:, oc, :], mixf[:], mixf2[:], ALU.add), reads=[Bmf, Bmf2], writes=[Bmix])
                for s4 in range(4):
                    for hf in range(2):
                        po, pbo = bank()
                        for kc in range(8):
                            S.op("pe", lambda pe: pe.matmul(po[:], mix[:, kc, s4 * 128:(s4 + 1) * 128], wo[:, kc, hf * 512:(hf + 1) * 512], start=(kc == 0), stop=(kc == 7)), reads=[Bw, Bmix], writes=[pbo])
                        S.op("dve", lambda v: v.tensor_tensor(x1[:, s4, hf * 512:(hf + 1) * 512], po[:], g1[:, hf * 512:(hf + 1) * 512], ALU.mult), reads=[pbo, Bg1], writes=[Bx1])
                    S.op("pool", lambda v: v.tensor_tensor(x1[:, s4, :], x1[:, s4, :], xt[:, s4, :], ALU.add), reads=[Bx], writes=[Bx1])
                    S.op("act", lambda a: a.activation(junk[:], x1[:, s4, :], AF.Square, accum_out=ss[:, s4:s4 + 1]), reads=[Bx1], writes=[Bss])
                S.dma("sp", x1_d[g0:g0 + 512, :].rearrange("(s p) f -> p s f", p=128), x1[:], reads=[Bx1])
                S.op("act", lambda a: a.activation(ss[:], ss[:], AF.Sqrt, bias=epst[:], scale=1.0 / 1024), reads=[Bss, B_c], writes=[Bss])
                S.op("dve", lambda v: v.reciprocal(ss[:], ss[:]), reads=[Bss], writes=[Bss])
                for s4 in range(4):
                    S.op("pool" if s4 % 2 else "dve", lambda v: v.tensor_scalar(xn[:, s4, :], x1[:, s4, :], ss[:, s4:s4 + 1], None, ALU.mult), reads=[Bx1, Bss], writes=[Bxn])
                for kc in range(8):
                    pt, pbt = bank()
                    ptb = pt[:].bitcast(BF16)
                    for s4 in range(4):
                        S.op("pe", lambda pe: pe.transpose(ptb[:, s4 * 128:(s4 + 1) * 128], xn[:, s4, kc * 128:(kc + 1) * 128], identb), reads=[Bxn, B_c], writes=[pbt])
                    S.op("dve", lambda v: v.tensor_scalar(h2[:, kc, :], ptb[:, 0:512], ab[:, 0, kc:kc + 1], ab[:, 1, kc:kc + 1], ALU.mult, ALU.add), reads=[pbt, Bab], writes=[Bh2])
                S.dma("sp", h2T_d[:, g0:g0 + 512].rearrange("(j p) t -> p j t", p=128), h2[:], reads=[Bh2])
        S.barrier()


def emit_p4b(c):
    from contextlib import ExitStack
    S, nc, sb, bank, seqs, tok0, NS = c["S"], c["nc"], c["sb"], c["bank"], c["seqs"], c["tok0"], c["NS"]
    B_c, epst = c["B_c"], c["epst"]
    x1_d, h2T_d, gate_d, y_d, w_mlp_in, w_mlp_out, g_final = c["x1_d"], c["h2T_d"], c["gate_d"], c["y_d"], c["w_mlp_in"], c["w_mlp_out"], c["g_final"]
    TT = 256
    with ExitStack() as st:
        w1 = sb(st, "p5_w1", [128, 8, 4096], BF16)
        w2 = sb(st, "p5_w2", [128, 32, 1024], BF16)
        gf = sb(st, "p5_gf", [128, 1024], F32)
        g2 = sb(st, "p5_g2", [128, 1024], F32)
        h2 = [sb(st, "p5_h2", [128, 8, TT], BF16) for _ in range(2)]
        x1 = [sb(st, "p5_x1", [128, 2, 1024], F32) for _ in range(2)]
        rl = [sb(st, "p5_rl", [128, TT], F32) for _ in range(2)]
        rT = sb(st, "p5_rT", [128, 32, TT], BF16)
        x2 = sb(st, "p5_x2", [128, 2, 1024], F32)
        junk = sb(st, "p5_junk", [128, 1024], BF16)
        ss = sb(st, "p5_ss", [128, 2], F32)
        yo = sb(st, "p5_yo", [128, 2, 1024], F32)
        Bw, Bg2, Bh2, Bx1, Brl, BrT, Bx2, Bss, Byo = Buf(), Buf(), [Buf(), Buf()], [Buf(), Buf()], [Buf(), Buf()], Buf(), Buf(), Buf(), Buf()
        S.dma("pool", w1[:], w_mlp_in.rearrange("(kc p) f -> p kc f", p=128), writes=[Bw])
        for q4 in range(4):
            S.dma("pool", w2[:, q4 * 8:(q4 + 1) * 8, :], w_mlp_out[q4 * 1024:(q4 + 1) * 1024, :].rearrange("(kc p) f -> p kc f", p=128), writes=[Bw])
        S.dma("sp", gf[:], g_final.partition_broadcast(128), writes=[Bw])
        it = 0
        for si in range(NS):
            S.dma("sp", g2[:], gate_d[si, 1, :].partition_broadcast(128), writes=[Bg2])
            for ti in range(seqs[si] // TT):
                g0 = tok0[si] + ti * TT
                k = it % 2
                it += 1
                H, X1 = h2[k], x1[k]
                S.dma("sp", H[:], h2T_d[:, g0:g0 + TT].rearrange("(j p) t -> p j t", p=128), writes=[Bh2[k]])
                S.dma("sp", X1[:], x1_d[g0:g0 + TT, :].rearrange("(s p) f -> p s f", p=128), writes=[Bx1[k]])
                for fc in range(32):
                    pf, pbf = bank()
                    for kc in range(8):
                        S.op("pe", lambda pe: pe.matmul(pf[:, 0:TT], w1[:, kc, fc * 128:(fc + 1) * 128], H[:, kc, :], start=(kc == 0), stop=(kc == 7)), reads=[Bw, Bh2[k]], writes=[pbf])
                    RL, BRL = rl[fc % 2], Brl[fc % 2]
                    S.op("act", lambda a: a.activation(RL[:], pf[:, 0:TT], AF.Relu), reads=[pbf], writes=[BRL])
                    S.op("pool" if fc % 2 else "dve", lambda v: v.tensor_tensor(rT[:, fc, :], RL[:], RL[:], ALU.mult), reads=[BRL], writes=[BrT])
                for s2 in range(2):
                    for hf in range(2):
                        po, pbo = bank()
                        for kc in range(32):
                            S.op("pe", lambda pe: pe.matmul(po[:], rT[:, kc, s2 * 128:(s2 + 1) * 128], w2[:, kc, hf * 512:(hf + 1) * 512], start=(kc == 0), stop=(kc == 31)), reads=[Bw, BrT], writes=[pbo])
                        S.op("dve", lambda v: v.tensor_tensor(x2[:, s2, hf * 512:(hf + 1) * 512], po[:], g2[:, hf * 512:(hf + 1) * 512], ALU.mult), reads=[pbo, Bg2], writes=[Bx2])
                    S.op("pool", lambda v: v.tensor_tensor(x2[:, s2, :], x2[:, s2, :], X1[:, s2, :], ALU.add), reads=[Bx1[k]], writes=[Bx2])
                    S.op("act", lambda a: a.activation(junk[:], x2[:, s2, :], AF.Square, accum_out=ss[:, s2:s2 + 1]), reads=[Bx2], writes=[Bss])
                S.op("act", lambda a: a.activation(ss[:], ss[:], AF.Sqrt, bias=epst[:], scale=1.0 / 1024), reads=[Bss, B_c], writes=[Bss])
                S.op("dve", lambda v: v.reciprocal(ss[:], ss[:]), reads=[Bss], writes=[Bss])
                for s2 in range(2):
                    S.op("dve", lambda v: v.scalar_tensor_tensor(yo[:, s2, :], x2[:, s2, :], ss[:, s2:s2 + 1], gf[:], ALU.mult, ALU.mult), reads=[Bx2, Bss, Bw], writes=[Byo])
                S.dma("sp", y_d[g0:g0 + TT, :].rearrange("(s p) f -> p s f", p=128), yo[:], reads=[Byo])
        S.barrier()


def core_inputs(x_all, c_all, W, g_final, cos, sin):
    f = lambda a: np.ascontiguousarray(np.asarray(a, dtype=np.float32))
    return {
        "x": f(x_all), "c": f(c_all),
        "w_ada": f(W["w_ada"]), "b_ada": f(W["b_ada"]), "g_norm1": f(W["g_norm1"]), "w_in": f(W["w_in"]),
        "conv_w": f(W["conv_w"]), "conv_b": f(W["conv_b"]),
        "dt_bias": f(np.concatenate([W["dt_bias_fwd"], W["dt_bias_bwd"]])),
        "a_log": f(np.concatenate([W["a_log_fwd"], W["a_log_bwd"]])),
        "d_skip": f(W["d_skip"]), "g_ssd_norm": f(W["g_ssd_norm"]), "w_ssd_out": f(W["w_ssd_out"]),
        "g_q_norm": f(W["g_q_norm"]), "w_q_b": f(W["w_q_b"]), "g_kv_norm": f(W["g_kv_norm"]), "w_kv_b": f(W["w_kv_b"]),
        "w_mla_out": f(W["w_mla_out"]), "w_o": f(W["w_o"]), "g_norm2": f(W["g_norm2"]),
        "w_mlp_in": f(W["w_mlp_in"]), "w_mlp_out": f(W["w_mlp_out"]), "g_final": f(g_final),
        "consts": make_consts(), "cos_t": f(cos), "sin_t": f(sin),
    }


_WNAMES = ["w_ada", "b_ada", "g_norm1", "w_in", "conv_w", "conv_b", "dt_bias_fwd", "dt_bias_bwd", "a_log_fwd",
           "a_log_bwd", "d_skip", "g_ssd_norm", "w_ssd_out", "g_q_norm", "w_q_b", "g_kv_norm", "w_kv_b",
           "w_mla_out", "w_o", "g_norm2", "w_mlp_in", "w_mlp_out"]


def kernel(x_prompt, x_sample, c_prompt, c_sample, g_final, **kw):
    W = {k: np.asarray(kw[k])[0] for k in _WNAMES}
    x_prompt = np.asarray(x_prompt); x_sample = np.asarray(x_sample)
    c_prompt = np.asarray(c_prompt); c_sample = np.asarray(c_sample)
    n = 8
    seqs = (2048, 2048, 4096, 4096)
    cos, sin = rope_tables(4096)
    nc = build(seqs=seqs)
    in_maps = []
    for i in range(n):
        xa = np.concatenate([x_prompt[2 * i].reshape(-1, D), x_prompt[2 * i + 1].reshape(-1, D),
                             x_sample[2 * i].reshape(-1, D), x_sample[2 * i + 1].reshape(-1, D)], axis=0)
        ca = np.stack([c_prompt[2 * i], c_prompt[2 * i + 1], c_sample[2 * i], c_sample[2 * i + 1]], axis=0)
        in_maps.append(core_inputs(xa, ca, W, g_final, cos, sin))
    res = run_bass_kernel_spmd(nc, in_maps, core_ids=list(range(n)))
    yp = np.zeros((16, 2048, D), np.float32)
    ys = np.zeros((16, 4096, D), np.float32)
    for i in range(n):
        y = np.asarray(res.results[i]["y"])
        yp[2 * i] = y[0:2048]
        yp[2 * i + 1] = y[2048:4096]
        ys[2 * i] = y[4096:8192]
        ys[2 * i + 1] = y[8192:12288]
    return (yp, ys)
```

```python
import numpy as np
import concourse.bass as bass
import concourse.mybir as mybir
from concourse.bass_utils import run_bass_kernel_spmd

F32 = mybir.dt.float32
BF16 = mybir.dt.bfloat16
AF = mybir.ActivationFunctionType
ALU = mybir.AluOpType
AX = mybir.AxisListType

D = 1024
DI = 2048
NH = 32
DIN = 8928
EPS = 1e-6
C_Z, C_X, C_DT, C_QA, C_KV, C_KR, C_G = 0, 2048, 6144, 6208, 6592, 6848, 6880
ATT_SCALE = 96 ** -0.5


class Buf:
    __slots__ = ("w", "r")

    def __init__(self):
        self.w = None
        self.r = {}


class Eng:
    def __init__(self, key, h, sem):
        self.key, self.h, self.sem = key, h, sem
        self.count = 0
        self.waited = {}
        self.slots = []
        self.dma_i = 0


class Slot:
    def __init__(self, key, sem):
        self.key, self.sem, self.count = key, sem, 0


class Sch:
    def __init__(self, nc, stack, nslots=12):
        self.nc = nc
        self.E = {}
        for key, h in (("pe", nc.tensor), ("act", nc.scalar), ("dve", nc.vector),
                       ("pool", nc.gpsimd), ("sp", nc.sync)):
            sem = stack.enter_context(nc.semaphore("sem_" + key))
            self.E[key] = Eng(key, h, sem)
        for q in ("sp", "pool", "act"):
            for i in range(nslots):
                sem = stack.enter_context(nc.semaphore("dq_%s_%d" % (q, i)))
                self.E[q].slots.append(Slot("dq_%s_%d" % (q, i), sem))

    def _wait(self, e, deps):
        for (key, sem, val) in deps:
            if key == e.key and key == "pe":
                continue
            if e.waited.get(key, 0) >= val:
                continue
            e.h.wait_ge(sem, val)
            e.waited[key] = val

    @staticmethod
    def _deps(reads, writes):
        deps = []
        for b in reads:
            if b.w is not None:
                deps.append(b.w)
        for b in writes:
            if b.w is not None:
                deps.append(b.w)
            for k, (s, v) in b.r.items():
                deps.append((k, s, v))
        return deps

    @staticmethod
    def _mark(tok, reads, writes):
        for b in reads:
            b.r[tok[0]] = (tok[1], tok[2])
        for b in writes:
            b.w = tok
            b.r = {}

    def op(self, eng, fn, reads=(), writes=()):
        e = self.E[eng]
        self._wait(e, self._deps(reads, writes))
        ins = fn(e.h)
        e.count += 1
        ins.then_inc(e.sem, 1)
        self._mark((e.key, e.sem, e.count), reads, writes)

    def mm(self, out, pairs, reads=(), writes=()):
        n = len(pairs)

        def fn(pe):
            ins = None
            for i, (l, r) in enumerate(pairs):
                ins = pe.matmul(out, l, r, start=(i == 0), stop=(i == n - 1))
            return ins
        self.op("pe", fn, reads=reads, writes=writes)

    def dma(self, q, out, in_, reads=(), writes=()):
        e = self.E[q]
        self._wait(e, self._deps(reads, writes))
        sl = e.slots[e.dma_i % len(e.slots)]
        e.dma_i += 1
        if sl.count > 0:
            self._wait(e, [(sl.key, sl.sem, 16 * sl.count)])
        e.h.dma_start(out=out, in_=in_).then_inc(sl.sem, 16)
        sl.count += 1
        self._mark((sl.key, sl.sem, 16 * sl.count), reads, writes)

    def barrier(self):
        toks = []
        for e in self.E.values():
            if e.count:
                toks.append((e.key, e.sem, e.count))
            for sl in e.slots:
                if sl.count:
                    toks.append((sl.key, sl.sem, 16 * sl.count))
        for e in self.E.values():
            self._wait(e, toks)


def make_consts():
    i = np.arange(128)
    c = np.zeros((128, 6, 128), np.float32)
    c[:, 0, :] = np.eye(128)
    c[:, 1, :] = (i[:, None] <= i[None, :])
    c[:, 2, :] = (i[:, None] >= i[None, :])
    c[:, 3, :] = (i[:, None] > i[None, :])
    c[:, 4, :] = (i[:, None] < i[None, :])
    c[:, 5, :] = 1.0
    return c.reshape(128, 768)


def rope_tables(smax):
    inv = (1.0 / (np.float32(10000.0) ** (np.arange(0, 32, 2, dtype=np.float32) / np.float32(32)))).astype(np.float32)
    ang = np.arange(smax, dtype=np.float32)[:, None] * inv[None, :]
    cos = np.cos(ang).astype(np.float32).T
    sin = np.sin(ang).astype(np.float32).T
    return (np.ascontiguousarray(np.concatenate([cos, cos], 0)),
            np.ascontiguousarray(np.concatenate([sin, sin], 0)))


def build(seqs=(2048, 2048, 4096, 4096), phases=("p0", "p1a", "p1b", "p1c", "p1d", "p2", "p3", "p4a", "p4b"), debug=False):
    nc = bass.Bass("TRN2", target_bir_lowering=False)
    NS = len(seqs)
    NT = sum(seqs)
    SMAX = max(seqs)
    tok0 = [sum(seqs[:i]) for i in range(NS)]
    skind = "ExternalOutput" if debug else "Internal"

    def din(name, shape, dt=F32):
        return nc.dram_tensor(name, list(shape), dt, kind="ExternalInput").ap()

    def dscr(name, shape, dt):
        return nc.dram_tensor(name, list(shape), dt, kind=skind).ap()

    x_d = din("x", [NT, D])
    c_d = din("c", [NS, D])
    w_ada = din("w_ada", [D, 6 * D])
    b_ada = din("b_ada", [6 * D])
    g_norm1 = din("g_norm1", [D])
    w_in = din("w_in", [D, DIN])
    conv_w = din("conv_w", [5, 4096])
    conv_b = din("conv_b", [4096])
    dtb = din("dt_bias", [64])
    alog = din("a_log", [64])
    dskip = din("d_skip", [32])
    g_ssd = din("g_ssd_norm", [DI])
    w_ssd_out = din("w_ssd_out", [DI, D])
    g_q = din("g_q_norm", [384])
    w_q_b = din("w_q_b", [384, 1536])
    g_kv = din("g_kv_norm", [256])
    w_kv_b = din("w_kv_b", [256, 2048])
    w_mla_out = din("w_mla_out", [D, D])
    w_o = din("w_o", [D, D])
    g_norm2 = din("g_norm2", [D])
    w_mlp_in = din("w_mlp_in", [D, 4 * D])
    w_mlp_out = din("w_mlp_out", [4 * D, D])
    g_final = din("g_final", [D])
    consts_d = din("consts", [128, 768])
    cos_d = din("cos_t", [32, SMAX])
    sin_d = din("sin_t", [32, SMAX])

    y_d = nc.dram_tensor("y", [NT, D], F32, kind="ExternalOutput").ap()

    adaT_d = dscr("adaT_s", [NS, 4, D], F32)
    gate_d = dscr("gate_s", [NS, 2, D], F32)
    uT_d = dscr("uT_s", [4096, NT], BF16)
    zs_d = dscr("zs_s", [NT, DI], BF16)
    dt_d = dscr("dt_s", [NT, 64], F32)
    qnT_d = dscr("qnT_s", [384, NT], BF16)
    cnT_d = dscr("cnT_s", [256, NT], BF16)
    kr_d = dscr("kr_s", [32, NT], BF16)
    gT_d = dscr("gT_s", [2048, NT], BF16)
    xtok_d = dscr("xtok_s", [NT, 3072], BF16)
    bcT_d = dscr("bcT_s", [2048, NT], BF16)
    KT_d = dscr("KT_s", [1024, NT], BF16)
    QT_d = dscr("QT_s", [1536, NT], BF16)
    V_d = dscr("V_s", [16, NT * 64], BF16)
    OT_d = dscr("OT_s", [D, NT], BF16)
    prevb_d = dscr("prevb_s", [SMAX // 128, 128, DI], BF16)
    ygT_d = dscr("ygT_s", [DI, NT], BF16)
    x1_d = dscr("x1_s", [NT, D], F32)
    h2T_d = dscr("h2T_s", [D, NT], BF16)

    from contextlib import ExitStack
    with ExitStack() as top:
        S = Sch(nc, top)
        _uid = [0]

        def sb(st, name, shape, dt):
            _uid[0] += 1
            return st.enter_context(nc.sbuf_tensor("%s_%d" % (name, _uid[0]), list(shape), dt))
        psall = top.enter_context(nc.psum_tensor("psall", [128, 4096], F32))
        banks = [psall[:, i * 512:(i + 1) * 512] for i in range(8)]
        bbuf = [Buf() for _ in range(8)]
        bi = {None: 0, "b": 0}
        pools = {None: list(range(8)), "b": [5, 6, 7]}

        def bank(pool=None):
            lst = pools[pool]
            i = lst[bi[pool] % len(lst)]
            bi[pool] += 1
            return banks[i], bbuf[i]

        cst32 = sb(top, "cst32", [128, 768], F32)
        cstb = sb(top, "cstb", [128, 768], BF16)
        epst = sb(top, "epst", [128, 1], F32)
        onet = sb(top, "onet", [128, 1], F32)
        B_c = Buf()
        S.dma("sp", cst32[:], consts_d, writes=[B_c])
        S.op("dve", lambda v: v.tensor_copy(cstb[:], cst32[:]), reads=[B_c], writes=[B_c])
        S.op("dve", lambda v: v.memset(epst[:], EPS), writes=[B_c])
        S.op("dve", lambda v: v.memset(onet[:], 1.0), writes=[B_c])
        identb = cstb[:, 0:128]
        Ufb, Ubb, Lfb, Lbb, onesb = (cstb[:, 128 * k:128 * (k + 1)] for k in range(1, 6))
        Uf32, Ub32 = cst32[:, 128:256], cst32[:, 256:384]
        ones32 = cst32[:, 640:768]

        def rstd_from_ss(eng_list, out, ss, n, rb, wb):
            S.op("act", lambda a: a.activation(out, ss, AF.Sqrt, bias=epst[0:out.shape[0], :], scale=1.0 / n), reads=rb + [B_c], writes=wb)
            S.op("dve", lambda v: v.reciprocal(out, out), reads=wb, writes=wb)

        if "p0" in phases:
            with ExitStack() as st:
                cT = sb(st, "p0_cT", [128, 8, NS], F32)
                cbc = sb(st, "p0_cbc", [128, 8, NS, 128], F32)
                wa = [sb(st, "p0_wa%d" % i, [128, 8, 1024], F32) for i in range(2)]
                bT = sb(st, "p0_bT", [128, 48], F32)
                brow = sb(st, "p0_brow", [128, 2, 1024], F32)
                gT1 = sb(st, "p0_g", [128, 2, 8], F32)
                res = sb(st, "p0_res", [128, 6, 8, NS], F32)
                vec = sb(st, "p0_vec", [128, NS, 4, 8], F32)
                grow = sb(st, "p0_grow", [128, 1024], F32)
                Bc, Bw, Bb, Br, Bv, Bg = Buf(), [Buf(), Buf()], Buf(), Buf(), Buf(), Buf()
                with nc.allow_non_contiguous_dma(reason="tiny transposed loads"):
                    for b in range(NS):
                        S.dma("sp", cT[:, :, b], c_d[b, :].rearrange("(kc p) -> p kc", p=128), writes=[Bc])
                    S.dma("sp", bT[:], b_ada.rearrange("(j p) -> p j", p=128), writes=[Bb])
                    S.dma("sp", gT1[:, 0, :], g_norm1.rearrange("(j p) -> p j", p=128), writes=[Bb])
                    S.dma("sp", gT1[:, 1, :], g_norm2.rearrange("(j p) -> p j", p=128), writes=[Bb])
                S.dma("sp", brow[:, 0, :], b_ada[2048:3072].partition_broadcast(128), writes=[Bb])
                S.dma("sp", brow[:, 1, :], b_ada[5120:6144].partition_broadcast(128), writes=[Bb])
                S.op("act", lambda a: a.activation(cT[:], cT[:], AF.Silu), reads=[Bc], writes=[Bc])
                S.op("dve", lambda v: v.tensor_copy(cbc[:], cT[:].unsqueeze(3).broadcast_to([128, 8, NS, 128])), reads=[Bc], writes=[Bc])
                for j in range(6):
                    w = wa[j % 2]
                    S.dma("sp", w[:], w_ada[:, j * 1024:(j + 1) * 1024].rearrange("(kc p) f -> p kc f", p=128), writes=[Bw[j % 2]])
                    pt, pb = bank()
                    for oc in range(8):
                        for kc in range(8):
                            S.op("pe", lambda pe, oc=oc, kc=kc: pe.matmul(pt[:, oc * NS:(oc + 1) * NS], w[:, kc, oc * 128:(oc + 1) * 128], cT[:, kc, :], start=(kc == 0), stop=(kc == 7)),
                                 reads=[Bw[j % 2], Bc], writes=[pb])
                    S.op("dve", lambda v, j=j: v.tensor_tensor(res[:, j, :, :], pt[:, 0:8 * NS].rearrange("p (o b) -> p o b", b=NS),
                                                           bT[:, j * 8:(j + 1) * 8].unsqueeze(2).broadcast_to([128, 8, NS]), ALU.add),
                         reads=[pb, Bb], writes=[Br])
                    if j in (2, 5):
                        gi = 0 if j == 2 else 1
                        for b in range(NS):
                            for hf in range(2):
                                pt2, pb2 = bank()
                                for kc in range(8):
                                    S.op("pe", lambda pe, kc=kc, b=b, hf=hf: pe.matmul(pt2[:], cbc[:, kc, b, :], w[:, kc, hf * 512:(hf + 1) * 512], start=(kc == 0), stop=(kc == 7)),
                                         reads=[Bw[j % 2], Bc], writes=[pb2])
                                S.op("dve", lambda v, hf=hf, gi=gi: v.tensor_tensor(grow[:, hf * 512:(hf + 1) * 512], pt2[:], brow[:, gi, hf * 512:(hf + 1) * 512], ALU.add),
                                     reads=[pb2, Bb], writes=[Bg])
                            S.dma("sp", gate_d[b, gi, :], grow[0:1, :], reads=[Bg])
                for b in range(NS):
                    for (k, jsc, jsh, gi) in ((0, 1, 0, 0), (2, 4, 3, 1)):
                        S.op("dve", lambda v, b=b, k=k, jsc=jsc, gi=gi: v.scalar_tensor_tensor(vec[:, b, k, :], res[:, jsc, :, b], 1.0, gT1[:, gi, :], ALU.add, ALU.mult), reads=[Br, Bb], writes=[Bv])
                        S.op("dve", lambda v, b=b, k=k, jsh=jsh: v.tensor_copy(vec[:, b, k + 1, :], res[:, jsh, :, b]), reads=[Br], writes=[Bv])
                with nc.allow_non_contiguous_dma(reason="tiny transposed stores"):
                    for b in range(NS):
                        for k in range(4):
                            S.dma("sp", adaT_d[b, k, :].rearrange("(kc p) -> p kc", p=128), vec[:, b, k, :], reads=[Bv])
                S.barrier()

        for part in ("a", "b"):
            if ("p1" + part) not in phases:
                continue
            WC = 4096 if part == "a" else 4832
            cm = (lambda c: c - 2048) if part == "a" else (lambda c: c if c < 2048 else c - 4096)
            with ExitStack() as st:
                W = sb(st, "p1_W", [128, 8, WC], BF16)
                wkr = sb(st, "p1_wkr", [128, 8, 64], BF16)
                dtb_bc = sb(st, "p1_dtb", [128, 4, 64], F32)
                ab = sb(st, "p1_ab", [128, 2, 8], F32)
                xt = [sb(st, "p1_x%d" % i, [128, 4, 1024], F32) for i in range(1)] * 2
                junk = sb(st, "p1_junk", [128, 1024], BF16)
                ss = sb(st, "p1_ss", [128, 4], F32)
                xn = sb(st, "p1_xn", [128, 4, 1024], BF16)
                hT = [sb(st, "p1_hT%d" % i, [128, 8, 512], BF16) for i in range(1)] * 2
                stg = [sb(st, "p1_stg%d" % i, [128, 512], F32) for i in range(4)]
                stgb = [sb(st, "p1_stgb%d" % i, [128, 512], BF16) for i in range(4)]
                qa = sb(st, "p1_qa", [128, 5, 512], F32) if part == "b" else None
                sq = sb(st, "p1_sq", [128, 5, 512], F32) if part == "b" else None
                rs = sb(st, "p1_rs", [128, 2, 512], F32)
                qn = sb(st, "p1_qn", [128, 5, 512], BF16)
                cs = sb(st, "p1_cs", [32, 2, 512], F32)
                krt = sb(st, "p1_krt", [32, 2, 512], F32)
                krb = sb(st, "p1_krb", [32, 512], BF16)
                zst = [sb(st, "p1_zst%d" % i, [128, 2048], BF16) if part == "b" else None for i in range(2)]
                dts = sb(st, "p1_dts", [128, 5, 256], F32)
                BW, Bab, Bx, Bss, Bxn, BhT = Buf(), Buf(), [Buf()] * 2, Buf(), Buf(), [Buf()] * 2
                Bstg, Bstgb = [Buf() for _ in range(4)], [Buf() for _ in range(4)]
                Bqa, Bsq, Brs, Bqn, Bcs, Bkr, Bkrb, Bz, Bdts = Buf(), Buf(), Buf(), Buf(), Buf(), Buf(), Buf(), [Buf(), Buf()], Buf()
                for kc in range(8):
                    if part == "a":
                        S.dma("pool", W[:, kc, :], w_in[kc * 128:(kc + 1) * 128, 2048:6144], writes=[BW])
                    else:
                        S.dma("pool", W[:, kc, 0:2048], w_in[kc * 128:(kc + 1) * 128, 0:2048], writes=[BW])
                        S.dma("pool", W[:, kc, 2048:4832], w_in[kc * 128:(kc + 1) * 128, 6144:8928], writes=[BW])
                CKR = cm(C_KR) if part == "b" else 0
                S.op("dve", lambda v: v.tensor_copy(wkr[:, :, 0:32], W[:, :, CKR:CKR + 32]), reads=[BW], writes=[BW])
                S.op("dve", lambda v: v.tensor_scalar(wkr[:, :, 32:48], W[:, :, CKR + 16:CKR + 32], -1.0, None, ALU.mult), reads=[BW], writes=[BW])
                S.op("dve", lambda v: v.tensor_copy(wkr[:, :, 48:64], W[:, :, CKR:CKR + 16]), reads=[BW], writes=[BW])
                for s4 in range(4):
                    S.dma("sp", dtb_bc[:, s4, :], dtb.partition_broadcast(128), writes=[BW])
                stg_i = [0]
                for si in range(NS):
                    with nc.allow_non_contiguous_dma(reason="tiny"):
                        S.dma("sp", ab[:, 0, :], adaT_d[si, 0, :].rearrange("(kc p) -> p kc", p=128), writes=[Bab])
                        S.dma("sp", ab[:, 1, :], adaT_d[si, 1, :].rearrange("(kc p) -> p kc", p=128), writes=[Bab])
                    for ti in range(seqs[si] // 512):
                        g0 = tok0[si] + ti * 512
                        p0 = ti * 512
                        par = (g0 // 512) % 2
                        X, H = xt[par], hT[par]
                        S.dma("sp", X[:], x_d[g0:g0 + 512, :].rearrange("(s p) f -> p s f", p=128), writes=[Bx[par]])
                        for s4 in range(4):
                            S.op("act", lambda a, s4=s4: a.activation(junk[:], X[:, s4, :], AF.Square, accum_out=ss[:, s4:s4 + 1]), reads=[Bx[par]], writes=[Bss])
                        rstd_from_ss(None, ss[:], ss[:], 1024.0, [Bss], [Bss])
                        for s4 in range(4):
                            S.op("pool" if s4 % 2 else "dve", lambda v, s4=s4: v.tensor_scalar(xn[:, s4, :], X[:, s4, :], ss[:, s4:s4 + 1], None, ALU.mult), reads=[Bx[par], Bss], writes=[Bxn])
                        for kc in range(8):
                            pt, pb = bank()
                            ptb = pt[:].bitcast(BF16)
                            for s4 in range(4):
                                S.op("pe", lambda pe, s4=s4, kc=kc: pe.transpose(ptb[:, s4 * 128:(s4 + 1) * 128], xn[:, s4, kc * 128:(kc + 1) * 128], identb), reads=[Bxn, B_c], writes=[pb])
                            S.op("dve", lambda v, kc=kc: v.tensor_scalar(H[:, kc, :], ptb[:, 0:512], ab[:, 0, kc:kc + 1], ab[:, 1, kc:kc + 1], ALU.mult, ALU.add), reads=[pb, Bab], writes=[BhT[par]])

                        def fm(cols, m, lw=None):
                            pt, pb = bank()
                            l = (lw if lw is not None else W)
                            cc0 = cols if lw is not None else cm(cols)
                            S.mm(pt[0:m, :], [(l[:, kc, cc0:cc0 + m], H[:, kc, :]) for kc in range(8)], reads=[BW, BhT[par]], writes=[pb])
                            return pt, pb
                        for j in range(32 if part == "a" else 0):
                            pt, pb = fm(C_X + j * 128, 128)
                            k = stg_i[0] % 4
                            stg_i[0] += 1
                            S.op("act", lambda a, k=k: a.copy(stgb[k][:], pt[:]), reads=[pb], writes=[Bstgb[k]])
                            S.dma("sp", uT_d[j * 128:(j + 1) * 128, g0:g0 + 512], stgb[k][:], reads=[Bstgb[k]])
                        if part == "a":
                            continue
                        for j in range(5):
                            pt, pb = fm(C_QA + j * 128, 128)
                            S.op("act", lambda a, j=j: a.copy(qa[:, j, :], pt[:]), reads=[pb], writes=[Bqa])
                            S.op("pool", lambda v, j=j: v.tensor_tensor(sq[:, j, :], qa[:, j, :], qa[:, j, :], ALU.mult), reads=[Bqa], writes=[Bsq])
                        for (r, j0, n) in ((0, 0, 3), (1, 3, 2)):
                            pt, pb = bank()
                            for j in range(n):
                                S.op("pe", lambda pe, j=j: pe.matmul(pt[:], ones32, sq[:, j0 + j, :], start=(j == 0), stop=(j == n - 1)), reads=[Bsq, B_c], writes=[pb])
                            S.op("act", lambda a, r=r, n=n: a.activation(rs[:, r, :], pt[:], AF.Sqrt, bias=epst[:], scale=1.0 / (128 * n)), reads=[pb, B_c], writes=[Brs])
                            S.op("dve", lambda v, r=r: v.reciprocal(rs[:, r, :], rs[:, r, :]), reads=[Brs], writes=[Brs])
                            for j in range(n):
                                S.op("dve", lambda v, j=j, r=r: v.tensor_tensor(qn[:, j0 + j, :], qa[:, j0 + j, :], rs[:, r, :], ALU.mult), reads=[Bqa, Brs], writes=[Bqn])
                        S.dma("sp", qnT_d[:, g0:g0 + 512].rearrange("(j p) t -> p j t", p=128), qn[:, 0:3, :], reads=[Bqn])
                        S.dma("sp", cnT_d[:, g0:g0 + 512].rearrange("(j p) t -> p j t", p=128), qn[:, 3:5, :], reads=[Bqn])
                        S.dma("sp", cs[:, 0, :], cos_d[:, p0:p0 + 512], writes=[Bcs])
                        S.dma("sp", cs[:, 1, :], sin_d[:, p0:p0 + 512], writes=[Bcs])
                        pA, pbA = fm(0, 32, wkr)
                        pB, pbB = fm(32, 32, wkr)
                        S.op("dve", lambda v: v.tensor_tensor(krt[:, 0, :], pA[0:32, :], cs[:, 0, :], ALU.mult), reads=[pbA, Bcs], writes=[Bkr])
                        S.op("dve", lambda v: v.tensor_tensor(krt[:, 1, :], pB[0:32, :], cs[:, 1, :], ALU.mult), reads=[pbB, Bcs], writes=[Bkr])
                        S.op("dve", lambda v: v.tensor_tensor(krb[:], krt[:, 0, :], krt[:, 1, :], ALU.add), reads=[Bkr], writes=[Bkrb])
                        S.dma("sp", kr_d[:, g0:g0 + 512], krb[:], reads=[Bkrb])
                        for j in range(16):
                            pt, pb = fm(C_G + j * 128, 128)
                            k = stg_i[0] % 4
                            stg_i[0] += 1
                            S.op("act", lambda a, k=k: a.activation(stgb[k][:], pt[:], AF.Sigmoid), reads=[pb], writes=[Bstgb[k]])
                            S.dma("sp", gT_d[j * 128:(j + 1) * 128, g0:g0 + 512], stgb[k][:], reads=[Bstgb[k]])
                        for s4 in range(4):
                            Z = zst[s4 % 2]
                            for cg in range(4):
                                pt, pb = bank()
                                S.mm(pt[:], [(H[:, kc, s4 * 128:(s4 + 1) * 128], W[:, kc, cg * 512:(cg + 1) * 512]) for kc in range(8)], reads=[BW, BhT[par]], writes=[pb])
                                S.op("act", lambda a, s4=s4, cg=cg: a.activation(Z[:, cg * 512:(cg + 1) * 512], pt[:], AF.Silu), reads=[pb], writes=[Bz[s4 % 2]])
                            S.dma("sp", zs_d[g0 + s4 * 128:g0 + (s4 + 1) * 128, :], Z[:], reads=[Bz[s4 % 2]])
                        pt, pb = bank()
                        for s4 in range(4):
                            S.mm(pt[:, s4 * 64:(s4 + 1) * 64], [(H[:, kc, s4 * 128:(s4 + 1) * 128], W[:, kc, cm(C_DT):cm(C_DT) + 64]) for kc in range(8)], reads=[BW, BhT[par]], writes=[pb])
                        d0, d1, d2, d3, d4 = (dts[:, k, :] for k in range(5))
                        S.op("dve", lambda v: v.tensor_tensor(d0, pt[:, 0:256], dtb_bc[:].rearrange("p a b -> p (a b)"), ALU.add), reads=[pb, BW], writes=[Bdts])
                        S.op("dve", lambda v: v.tensor_scalar(d1, d0, -1.0, None, ALU.mult), reads=[Bdts], writes=[Bdts])
                        S.op("dve", lambda v: v.tensor_tensor(d1, d0, d1, ALU.min), reads=[Bdts], writes=[Bdts])
                        S.op("act", lambda a: a.activation(d2, d1, AF.Exp), reads=[Bdts], writes=[Bdts])
                        S.op("act", lambda a: a.activation(d3, d2, AF.Ln, bias=onet[:], scale=1.0), reads=[Bdts, B_c], writes=[Bdts])
                        S.op("dve", lambda v: v.scalar_tensor_tensor(d4, d0, 0.0, d3, ALU.max, ALU.add), reads=[Bdts], writes=[Bdts])
                        S.dma("sp", dt_d[g0:g0 + 512, :].rearrange("(s p) f -> p s f", p=128), d4.rearrange("p (s f) -> p s f", f=64), reads=[Bdts])
                S.barrier()

        if "p1c" in phases:
            with ExitStack() as st:
                cw = sb(st, "pc_cw", [128, 32, 6], F32)
                u = [sb(st, "pc_u%d" % i, [128, 516], BF16) for i in range(4)]
                dw = sb(st, "pc_dw", [128, 32, 5, 128], BF16)
                ob = [sb(st, "pc_ob%d" % i, [128, 512], BF16) for i in range(4)]
                tk = [sb(st, "pc_tk%d" % i, [128, 4, 3072], BF16) for i in range(2)]
                Bcw, Bu, Bacc, Bob, Btk = Buf(), [Buf() for _ in range(4)], [Buf(), Buf()], [Buf() for _ in range(4)], [Buf(), Buf()]
                with nc.allow_non_contiguous_dma(reason="tiny"):
                    for k in range(5):
                        S.dma("sp", cw[:, :, k], conv_w[k, :].rearrange("(j p) -> p j", p=128), writes=[Bcw])
                    S.dma("sp", cw[:, :, 5], conv_b.rearrange("(j p) -> p j", p=128), writes=[Bcw])
                for j in range(32):
                    S.op("dve", lambda v, j=j: v.tensor_tensor(dw[:, j, :, :], identb.unsqueeze(1).broadcast_to([128, 5, 128]), cw[:, j, 0:5].unsqueeze(2).broadcast_to([128, 5, 128]), ALU.mult), reads=[Bcw, B_c], writes=[Bcw])
                it = 0
                for si in range(NS):
                    nt = seqs[si] // 512
                    for ti in range(nt):
                        g0 = tok0[si] + ti * 512
                        par = (g0 // 512) % 2
                        TK = tk[par]
                        for j in range(32):
                            U, BU = u[it % 4], Bu[it % 4]
                            O, BO = ob[it % 4], Bob[it % 4]
                            eng = "dve"
                            it += 1
                            lo = 0 if ti > 0 else 2
                            hi = 516 if ti < nt - 1 else 514
                            if lo:
                                S.op(eng, lambda v: v.memset(U[:, 0:2], 0.0), writes=[BU])
                            if hi < 516:
                                S.op(eng, lambda v: v.memset(U[:, 514:516], 0.0), writes=[BU])
                            S.dma("sp", U[:, lo:hi], uT_d[j * 128:(j + 1) * 128, g0 - 2 + lo:g0 - 2 + hi], writes=[BU])
                            pcv, pbcv = bank()
                            S.mm(pcv, [(dw[:, j, k, :], U[:, k:k + 512]) for k in range(5)], reads=[BU, Bcw], writes=[pbcv])
                            S.op("act", lambda a, j=j: a.activation(O[:], pcv, AF.Silu, bias=cw[:, j, 5:6], scale=1.0), reads=[pbcv, Bcw], writes=[BO])
                            if j >= 16:
                                S.dma("sp", bcT_d[(j - 16) * 128:(j - 15) * 128, g0:g0 + 512], O[:], reads=[BO])
                            if j < 24:
                                pt, pb = bank()
                                ptb = pt[:].bitcast(BF16)
                                for s4 in range(4):
                                    S.op("pe", lambda pe, s4=s4: pe.transpose(ptb[:, s4 * 128:(s4 + 1) * 128], O[:, s4 * 128:(s4 + 1) * 128], identb), reads=[BO, B_c], writes=[pb])
                                S.op("act", lambda a, j=j: a.copy(TK[:, :, j * 128:(j + 1) * 128], ptb[:, 0:512].rearrange("p (s f) -> p s f", f=128)), reads=[pb], writes=[Btk[par]])
                        S.dma("sp", xtok_d[g0:g0 + 512, :].rearrange("(s p) f -> p s f", p=128), TK[:], reads=[Btk[par]])
                S.barrier()

        ctx = dict(locals())
        if "p1d" in phases:
            emit_p1d(ctx)
        with ExitStack() as gstack:
            ctx["gstack"] = gstack
            gens = []
            if "p2" in phases:
                gens.append(gen_p2(ctx))
            if "p3" in phases:
                gens.append(gen_p3(ctx))
            run_interleaved(gens)
            S.barrier()
        if "p4a" in phases:
            emit_p4a(ctx)
        if "p4b" in phases:
            emit_p4b(ctx)
        S.barrier()
    return nc


def run_interleaved(gens):
    if not gens:
        return
    if len(gens) == 1:
        for _ in gens[0]:
            pass
        return
    prog = [0.0] * len(gens)
    alive = [True] * len(gens)
    while any(alive):
        k = min((i for i in range(len(gens)) if alive[i]), key=lambda i: prog[i])
        try:
            prog[k] = next(gens[k])
        except StopIteration:
            alive[k] = False


def emit_p1d(c):
    from contextlib import ExitStack
    S, nc, sb, bank, seqs, tok0, NS = c["S"], c["nc"], c["sb"], c["bank"], c["seqs"], c["tok0"], c["NS"]
    qnT_d, cnT_d, KT_d, QT_d, V_d, cos_d, sin_d = c["qnT_d"], c["cnT_d"], c["KT_d"], c["QT_d"], c["V_d"], c["cos_d"], c["sin_d"]
    w_q_b, w_kv_b, g_q, g_kv = c["w_q_b"], c["w_kv_b"], c["g_q"], c["g_kv"]
    with ExitStack() as st:
        wq = sb(st, "pd_wq", [128, 3, 1536], BF16)
        wqB = sb(st, "pd_wqB", [128, 3, 16, 96], BF16)
        wkv = sb(st, "pd_wkv", [128, 2, 2048], BF16)
        wk = sb(st, "pd_wk", [128, 2, 1024], BF16)
        wv = sb(st, "pd_wv", [128, 2, 1024], BF16)
        gq = sb(st, "pd_gq", [128, 5], F32)
        qn = [sb(st, "pd_qn", [128, 3, 512], BF16) for _ in range(2)]
        cn = [sb(st, "pd_cn", [128, 2, 512], BF16) for _ in range(2)]
        cst = [sb(st, "pd_cs", [128, 2, 512], F32) for _ in range(2)]
        kst = [sb(st, "pd_kst", [128, 512], BF16) for _ in range(3)]
        vst = [sb(st, "pd_vst", [128, 512], BF16) for _ in range(3)]
        qst = [sb(st, "pd_qst", [128, 512], BF16) for _ in range(3)]
        rt = [sb(st, "pd_rt", [128, 2, 512], F32) for _ in range(2)]
        Bw2 = Buf()
        Bqn, Bcn, Bcs = [Buf(), Buf()], [Buf(), Buf()], [Buf(), Buf()]
        Bk, Bv, Bq, Brt = [Buf() for _ in range(3)], [Buf() for _ in range(3)], [Buf() for _ in range(3)], [Buf(), Buf()]
        S.dma("pool", wq[:], w_q_b.rearrange("(kc p) f -> p kc f", p=128), writes=[Bw2])
        S.dma("pool", wkv[:], w_kv_b.rearrange("(kc p) f -> p kc f", p=128), writes=[Bw2])
        with nc.allow_non_contiguous_dma(reason="tiny"):
            S.dma("sp", gq[:, 0:3], g_q.rearrange("(j p) -> p j", p=128), writes=[Bw2])
            S.dma("sp", gq[:, 3:5], g_kv.rearrange("(j p) -> p j", p=128), writes=[Bw2])
        for kc in range(3):
            S.op("dve", lambda v: v.tensor_scalar(wq[:, kc, :], wq[:, kc, :], gq[:, kc:kc + 1], None, ALU.mult), reads=[Bw2], writes=[Bw2])
        for kc in range(2):
            S.op("dve", lambda v: v.tensor_scalar(wkv[:, kc, :], wkv[:, kc, :], gq[:, 3 + kc:4 + kc], None, ALU.mult), reads=[Bw2], writes=[Bw2])
            w4 = wkv[:, kc, :].rearrange("p (h t f) -> p h t f", t=2, f=64)
            S.op("dve", lambda v: v.tensor_copy(wk[:, kc, :].rearrange("p (h f) -> p h f", f=64), w4[:, :, 0, :]), reads=[Bw2], writes=[Bw2])
            S.op("dve", lambda v: v.tensor_copy(wv[:, kc, :].rearrange("p (h f) -> p h f", f=64), w4[:, :, 1, :]), reads=[Bw2], writes=[Bw2])
        S.op("dve", lambda v: v.memset(wqB[:], 0.0), writes=[Bw2])
        wq4 = wq[:].rearrange("p k (h f) -> p k h f", f=96)
        for kc in range(3):
            S.op("dve", lambda v: v.tensor_scalar(wqB[:, kc, :, 64:80], wq4[:, kc, :, 80:96], -1.0, None, ALU.mult), reads=[Bw2], writes=[Bw2])
            S.op("dve", lambda v: v.tensor_copy(wqB[:, kc, :, 80:96], wq4[:, kc, :, 64:80]), reads=[Bw2], writes=[Bw2])
        it = 0
        ki = vi = qi = 0
        for si in range(NS):
            Sq, t0 = seqs[si], tok0[si]
            nch = Sq // 128
            for ti in range(Sq // 512):
                g0 = t0 + ti * 512
                p0 = ti * 512
                k = it % 2
                it += 1
                QN, CN, CS = qn[k], cn[k], cst[k]
                S.dma("sp", QN[:], qnT_d[:, g0:g0 + 512].rearrange("(j p) t -> p j t", p=128), writes=[Bqn[k]])
                S.dma("sp", CN[:], cnT_d[:, g0:g0 + 512].rearrange("(j p) t -> p j t", p=128), writes=[Bcn[k]])
                S.dma("sp", CS[64:96, 0, :], cos_d[:, p0:p0 + 512], writes=[Bcs[k]])
                S.dma("sp", CS[64:96, 1, :], sin_d[:, p0:p0 + 512], writes=[Bcs[k]])
                for pr in range(8):
                    pt, pb = bank()
                    S.mm(pt, [(wk[:, kc, pr * 128:(pr + 1) * 128], CN[:, kc, :]) for kc in range(2)], reads=[Bw2, Bcn[k]], writes=[pb])
                    K_, BK_ = kst[ki % 3], Bk[ki % 3]
                    ki += 1
                    S.op("act", lambda a: a.copy(K_[:], pt), reads=[pb], writes=[BK_])
                    S.dma("sp", KT_d[pr * 128:(pr + 1) * 128, g0:g0 + 512], K_[:], reads=[BK_])
                for s4 in range(4):
                    cidx = ti * 4 + s4
                    for hf in range(2):
                        pt, pb = bank()
                        S.mm(pt, [(CN[:, kc, s4 * 128:(s4 + 1) * 128], wv[:, kc, hf * 512:(hf + 1) * 512]) for kc in range(2)], reads=[Bw2, Bcn[k]], writes=[pb])
                        V_, BV_ = vst[vi % 3], Bv[vi % 3]
                        vi += 1
                        S.op("dve", lambda v: v.tensor_copy(V_[:], pt), reads=[pb], writes=[BV_])
                        dst = V_d[hf * 8:(hf + 1) * 8, t0 * 64:(t0 + Sq) * 64].rearrange("h (p c f) -> p h c f", p=128, f=64)[:, :, cidx, :]
                        S.dma("sp", dst, V_[:].rearrange("p (h f) -> p h f", f=64), reads=[BV_])
                for h in range(16):
                    pA, pbA = bank()
                    S.mm(pA[0:96, :], [(wq[:, kc, h * 96:(h + 1) * 96], QN[:, kc, :]) for kc in range(3)], reads=[Bw2, Bqn[k]], writes=[pbA])
                    pB, pbB = bank()
                    S.mm(pB[0:96, :], [(wqB[:, kc, h, :], QN[:, kc, :]) for kc in range(3)], reads=[Bw2, Bqn[k]], writes=[pbB])
                    Q_, BQ_ = qst[qi % 3], Bq[qi % 3]
                    RT, BRT = rt[qi % 2], Brt[qi % 2]
                    qi += 1
                    S.op("act", lambda a: a.copy(Q_[0:64, :], pA[0:64, :]), reads=[pbA], writes=[BQ_])
                    S.op("dve", lambda v: v.tensor_tensor(RT[64:96, 0, :], pA[64:96, :], CS[64:96, 0, :], ALU.mult), reads=[pbA, Bcs[k]], writes=[BRT])
                    S.op("dve", lambda v: v.tensor_tensor(RT[64:96, 1, :], pB[64:96, :], CS[64:96, 1, :], ALU.mult), reads=[pbB, Bcs[k]], writes=[BRT])
                    S.op("pool", lambda v: v.tensor_tensor(Q_[64:96, :], RT[64:96, 0, :], RT[64:96, 1, :], ALU.add), reads=[BRT], writes=[BQ_])
                    S.dma("sp", QT_d[h * 96:(h + 1) * 96, g0:g0 + 512], Q_[0:96, :], reads=[BQ_])
        S.barrier()


def gen_p2(c):
    from contextlib import ExitStack
    S, nc, sb, seqs, tok0, NS = c["S"], c["nc"], c["sb"], c["seqs"], c["tok0"], c["NS"]
    banks, bbuf, SMAX = c["banks"], c["bbuf"], c["SMAX"]
    psall = c["psall"]
    KT_d, QT_d, V_d, kr_d, OT_d = c["KT_d"], c["QT_d"], c["V_d"], c["kr_d"], c["OT_d"]
    st = c["gstack"]
    if True:
        KT = [sb(st, "p2_KT", [128, SMAX], BF16) for _ in range(2)]
        QT = [sb(st, "p2_QT", [128, SMAX], BF16) for _ in range(2)]
        VA = [sb(st, "p2_VA", [128, SMAX // 128, 128], BF16) for _ in range(2)]
        PT = [sb(st, "p2_PT", [128, 1024], BF16) for _ in range(3)]
        rd = sb(st, "p2_rd", [128, 512], F32)
        og = [sb(st, "p2_og", [128, 512], BF16) for _ in range(2)]
        BK, BQ, BV = [Buf(), Buf()], [Buf(), Buf()], [Buf(), Buf()]
        BPT, Brd, Bog = [Buf() for _ in range(3)], Buf(), [Buf(), Buf()]
        for i in range(2):
            S.op("pool", lambda v: v.memset(VA[i][:, :, 64:128], 1.0), writes=[BV[i]])
        heads = [(si, h) for si in range(NS) for h in range(16)]
        total = float(sum(seqs[si] // 512 * (seqs[si] // 256) * 5 for si, h in heads)) + 1.0
        done = 0

        def load(idx):
            si, h = heads[idx]
            hb = idx % 2
            Sq, t0 = seqs[si], tok0[si]
            S.dma("sp", KT[hb][0:64, 0:Sq], KT_d[h * 64:(h + 1) * 64, t0:t0 + Sq], writes=[BK[hb]])
            S.dma("sp", KT[hb][64:96, 0:Sq], kr_d[:, t0:t0 + Sq], writes=[BK[hb]])
            S.dma("sp", QT[hb][0:96, 0:Sq], QT_d[h * 96:(h + 1) * 96, t0:t0 + Sq], writes=[BQ[hb]])
            S.dma("sp", VA[hb][:, 0:Sq // 128, 0:64], V_d[h, t0 * 64:(t0 + Sq) * 64].rearrange("(p c f) -> p c f", p=128, f=64), writes=[BV[hb]])

        load(0)
        pi = 0
        oi = 0
        for idx, (si, h) in enumerate(heads):
            if idx + 1 < len(heads):
                load(idx + 1)
            hb = idx % 2
            K_, Q_, V_ = KT[hb], QT[hb], VA[hb]
            Sq, t0 = seqs[si], tok0[si]
            npair = Sq // 256
            for qt in range(Sq // 512):
                ql = slice(qt * 512, (qt + 1) * 512)
                pO, pbO = banks[4], bbuf[4]
                pend = None
                for cp in range(npair + 1):
                    if cp < npair:
                        b0 = (cp % 2) * 2
                        for e in range(2):
                            cc = cp * 2 + e
                            S.op("pe", lambda pe: pe.matmul(banks[b0 + e], K_[0:96, cc * 128:(cc + 1) * 128], Q_[0:96, ql], start=True, stop=True), reads=[BK[hb], BQ[hb]], writes=[bbuf[b0 + e]])
                            done += 1
                            yield done / total
                        P_, BP = PT[pi % 3], BPT[pi % 3]
                        pi += 1
                        S.op("act", lambda a: a.activation(P_[:], psall[:, b0 * 512:(b0 + 2) * 512], AF.Exp, scale=ATT_SCALE), reads=[bbuf[b0], bbuf[b0 + 1]], writes=[BP])
                        done += 1
                        yield done / total
                    if pend is not None:
                        pcp, PP, BPP = pend
                        for e in range(2):
                            cc = pcp * 2 + e
                            S.op("pe", lambda pe: pe.matmul(pO, V_[:, cc, :], PP[:, e * 512:(e + 1) * 512], start=(cc == 0), stop=(cc == 2 * npair - 1)), reads=[BV[hb], BPP], writes=[pbO])
                            done += 1
                            yield done / total
                    if cp < npair:
                        pend = (cp, P_, BP)
                O_, BO_ = og[oi % 2], Bog[oi % 2]
                oi += 1
                S.op("dve", lambda v: v.reciprocal(rd[64:128, :], pO[64:128, :]), reads=[pbO], writes=[Brd])
                S.op("dve", lambda v: v.tensor_tensor(O_[0:64, :], pO[0:64, :], rd[64:128, :], ALU.mult), reads=[pbO, Brd], writes=[BO_])
                S.dma("sp", OT_d[h * 64:(h + 1) * 64, t0 + qt * 512:t0 + (qt + 1) * 512], O_[0:64, :], reads=[BO_])
        yield 1.0


def gen_p3(c):
    from contextlib import ExitStack
    S, nc, sb, seqs, tok0, NS = c["S"], c["nc"], c["sb"], c["seqs"], c["tok0"], c["NS"]
    bank0 = c["bank"]
    bank = lambda: bank0("b")
    identb, Ufb, Ubb, Lfb, Lbb, onesb, B_c, epst = c["identb"], c["Ufb"], c["Ubb"], c["Lfb"], c["Lbb"], c["onesb"], c["B_c"], c["epst"]
    xtok_d, bcT_d, dt_d, zs_d, prevb_d, ygT_d, alog, dskip = c["xtok_d"], c["bcT_d"], c["dt_d"], c["zs_d"], c["prevb_d"], c["ygT_d"], c["alog"], c["dskip"]
    Q = "pool"
    st = c["gstack"]
    if True:
        a_bc = sb(st, "p3_a", [128, 64], F32)
        D_bc = sb(st, "p3_D", [128, 32], F32)
        Hs = sb(st, "p3_H", [128, 2048], F32)
        Hb16 = sb(st, "p3_Hb", [128, 2048], BF16)
        xts = [sb(st, "p3_xt", [128, 3072], BF16) for _ in range(2)]
        bcs = [sb(st, "p3_bc", [128, 16, 128], BF16) for _ in range(2)]
        dts_ = [sb(st, "p3_dt", [128, 64], F32) for _ in range(2)]
        zss = [sb(st, "p3_zs", [128, 2048], BF16) for _ in range(2)]
        pvs = [sb(st, "p3_pv", [128, 2048], BF16) for _ in range(2)]
        dA = sb(st, "p3_dA", [128, 64], F32)
        dAh = sb(st, "p3_dAh", [128, 64], BF16)
        dAhf = sb(st, "p3_dAhf", [128, 64], F32)
        dAl = sb(st, "p3_dAl", [128, 64], BF16)
        ct = sb(st, "p3_ct", [128, 128], F32)
        Et = sb(st, "p3_E", [128, 64], F32)
        wgt = sb(st, "p3_w", [128, 64], F32)
        ec = sb(st, "p3_ec", [128, 64], F32)
        dec = sb(st, "p3_dec", [128, 64], F32)
        xw = sb(st, "p3_xw", [128, 2048], BF16)
        xdt = [sb(st, "p3_xdt", [128, 2048], BF16) for _ in range(2)]
        Rs = [sb(st, "p3_R", [128, 2, 8, 128], BF16) for _ in range(2)]
        CBm = sb(st, "p3_CBm", [128, 2, 8, 128], BF16)
        Exs = [sb(st, "p3_Ex", [128, 512], BF16) for _ in range(2)]
        MTs = [sb(st, "p3_MT", [128, 2, 8, 128], BF16) for _ in range(2)]
        y = sb(st, "p3_y", [128, 2048], F32)
        t1s = [sb(st, "p3_t1", [128, 512], F32) for _ in range(2)]
        t2s = [sb(st, "p3_t2", [128, 512], F32) for _ in range(2)]
        sqt = sb(st, "p3_sq", [128, 2048], F32)
        gs = sb(st, "p3_gs", [128, 8], F32)
        ygn = sb(st, "p3_ygn", [128, 2048], BF16)
        ygs = [sb(st, "p3_ygs", [128, 16, 128], BF16) for _ in range(2)]
        Bk, BH, BHb, Bprevd = Buf(), Buf(), Buf(), Buf()
        Bxt, Bbc, Bdt, Bzs, Bpv = [Buf(), Buf()], [Buf(), Buf()], [Buf(), Buf()], [Buf(), Buf()], [Buf(), Buf()]
        BdA, Bct, BE, Bxw, Bxdt, BRs, BCB, BEx, BMTs = Buf(), Buf(), Buf(), Buf(), [Buf(), Buf()], [Buf(), Buf()], Buf(), [Buf(), Buf()], [Buf(), Buf()]
        By, Bt1, Bt2, Bsq, Bgs, Bygn, Bygs = Buf(), [Buf(), Buf()], [Buf(), Buf()], Buf(), Buf(), Buf(), [Buf(), Buf()]
        S.dma(Q, a_bc[:], alog.partition_broadcast(128), writes=[Bk])
        S.dma(Q, D_bc[:], dskip.partition_broadcast(128), writes=[Bk])
        S.op("act", lambda a: a.activation(a_bc[:], a_bc[:], AF.Exp), reads=[Bk], writes=[Bk])
        S.op("dve", lambda v: v.tensor_scalar(a_bc[:], a_bc[:], -1.0, None, ALU.mult), reads=[Bk], writes=[Bk])
        total = float(sum((sq // 128) * 292.4 for sq in seqs))
        done = [0.0]

        def tick():
            done[0] += 1.0
            return min(done[0] / total, 0.999)

        def bc3(ap2, n):
            return ap2.unsqueeze(2).broadcast_to([128, ap2.shape[1], n])

        v3 = lambda ap: ap.rearrange("p (h f) -> p h f", f=64)

        def prep(dtt, Bd):
            S.op("dve", lambda v: v.tensor_tensor(dA[:], dtt[:], a_bc[:], ALU.mult), reads=[Bd, Bk], writes=[BdA])
            yield tick()
            S.op("dve", lambda v: v.tensor_copy(dAh[:], dA[:]), reads=[BdA], writes=[BdA])
            yield tick()
            S.op("dve", lambda v: v.tensor_copy(dAhf[:], dAh[:]), reads=[BdA], writes=[BdA])
            yield tick()
            S.op("dve", lambda v: v.tensor_tensor(dAhf[:], dA[:], dAhf[:], ALU.subtract), reads=[BdA], writes=[BdA])
            yield tick()
            S.op("dve", lambda v: v.tensor_copy(dAl[:], dAhf[:]), reads=[BdA], writes=[BdA])
            yield tick()
            pc, pbc = bank()
            for (o0, o1, L) in ((0, 32, Ufb), (32, 64, Ubb)):
                S.op("pe", lambda pe: pe.matmul(pc[:, o0:o1], L, dAh[:, o0:o1], start=True, stop=False), reads=[BdA, B_c], writes=[pbc])
                yield tick()
                S.op("pe", lambda pe: pe.matmul(pc[:, o0:o1], L, dAl[:, o0:o1], start=False, stop=True), reads=[BdA, B_c], writes=[pbc])
                yield tick()
            S.op("pe", lambda pe: pe.matmul(pc[:, 64:128], onesb, dAh[:], start=True, stop=False), reads=[BdA, B_c], writes=[pbc])
            yield tick()
            S.op("pe", lambda pe: pe.matmul(pc[:, 64:128], onesb, dAl[:], start=False, stop=True), reads=[BdA, B_c], writes=[pbc])
            yield tick()
            S.op("dve", lambda v: v.tensor_copy(ct[:], pc[:, 0:128]), reads=[pbc], writes=[Bct])
            yield tick()
            S.op("dve", lambda v: v.tensor_tensor(Et[:], ct[:, 64:128], ct[:, 0:64], ALU.subtract), reads=[Bct], writes=[BE])
            yield tick()
            S.op("act", lambda a: a.activation(Et[:], Et[:], AF.Exp), reads=[BE], writes=[BE])
            yield tick()
            S.op("dve", lambda v: v.tensor_tensor(wgt[:], dtt[:], Et[:], ALU.mult), reads=[BE, Bd], writes=[BE])
            yield tick()
            S.op("act", lambda a: a.activation(dec[:], ct[:, 64:128], AF.Exp), reads=[Bct], writes=[BE])
            yield tick()
            S.op("act", lambda a: a.activation(ec[:], ct[:, 0:64], AF.Exp), reads=[Bct], writes=[BE])
            yield tick()

        def state_update(X, BX, d):
            S.op("pool", lambda v: v.tensor_tensor(v3(xw[:]), v3(X[:, 0:2048]), bc3(wgt[:, d * 32:(d + 1) * 32], 64), ALU.mult), reads=[BX, BE], writes=[Bxw])
            yield tick()
            S.op("dve", lambda v: v.tensor_tensor(v3(Hs[:]), v3(Hs[:]), bc3(dec[:, d * 32:(d + 1) * 32], 64), ALU.mult), reads=[BE, BHb], writes=[BH])
            yield tick()
            for gp in range(4):
                ps, pbs = bank()
                for gg in range(2):
                    g = gp * 2 + gg
                    S.op("pe", lambda pe: pe.matmul(ps[:, gg * 256:(gg + 1) * 256], X[:, 2048 + g * 128:2048 + (g + 1) * 128], xw[:, g * 256:(g + 1) * 256], start=True, stop=True), reads=[BX, Bxw], writes=[pbs])
                    yield tick()
                S.op("dve", lambda v: v.tensor_tensor(Hs[:, gp * 512:(gp + 1) * 512], Hs[:, gp * 512:(gp + 1) * 512], ps, ALU.add), reads=[pbs], writes=[BH])
                yield tick()

        it = 0
        for si in range(NS):
            Sq, t0 = seqs[si], tok0[si]
            nch = Sq // 128
            S.op("pool", lambda v: v.memset(Hs[:], 0.0), reads=[BHb], writes=[BH])
            yield tick()
            for cidx in range(nch - 1, -1, -1):
                g0 = t0 + cidx * 128
                k = it % 2
                it += 1
                S.dma(Q, xts[k][:], xtok_d[g0:g0 + 128, :], writes=[Bxt[k]])
                yield tick()
                S.dma(Q, dts_[k][:], dt_d[g0:g0 + 128, :], writes=[Bdt[k]])
                yield tick()
                S.op("act", lambda a: a.copy(Hb16[:], Hs[:]), reads=[BH], writes=[BHb])
                yield tick()
                S.dma(Q, prevb_d[cidx], Hb16[:], reads=[BHb], writes=[Bprevd])
                yield tick()
                if cidx > 0:
                    yield from prep(dts_[k], Bdt[k])
                    yield from state_update(xts[k], Bxt[k], 1)
            S.op("pool", lambda v: v.memset(Hs[:], 0.0), reads=[BHb], writes=[BH])
            yield tick()
            for cidx in range(nch):
                g0 = t0 + cidx * 128
                k = it % 2
                it += 1
                X, BX, BCt, BBC, dtt, Bd, Z, BZ, PV, BPV = xts[k], Bxt[k], bcs[k], Bbc[k], dts_[k], Bdt[k], zss[k], Bzs[k], pvs[k], Bpv[k]
                S.dma(Q, X[:], xtok_d[g0:g0 + 128, :], writes=[BX])
                yield tick()
                S.dma(Q, BCt[:], bcT_d[:, g0:g0 + 128].rearrange("(j p) t -> p j t", p=128), writes=[BBC])
                yield tick()
                S.dma(Q, dtt[:], dt_d[g0:g0 + 128, :], writes=[Bd])
                yield tick()
                S.dma(Q, Z[:], zs_d[g0:g0 + 128, :], writes=[BZ])
                yield tick()
                S.dma(Q, PV[:], prevb_d[cidx], reads=[Bprevd], writes=[BPV])
                yield tick()
                S.op("act", lambda a: a.copy(Hb16[:], Hs[:]), reads=[BH], writes=[BHb])
                yield tick()
                yield from prep(dtt, Bd)
                if cidx < nch - 1:
                    yield from state_update(X, BX, 0)
                for g4 in range(2):
                    pcb, pbcb = bank()
                    for gg in range(4):
                        g = g4 * 4 + gg
                        S.op("pe", lambda pe: pe.matmul(pcb[:, gg * 128:(gg + 1) * 128], BCt[:, g, :], BCt[:, 8 + g, :], start=True, stop=True), reads=[BBC], writes=[pbcb])
                        yield tick()
                    for d, Um in ((0, Ufb), (1, Ubb)):
                        S.op("dve", lambda v: v.tensor_tensor(CBm[:, d, g4 * 4:(g4 + 1) * 4, :], pcb.rearrange("p (g l) -> p g l", l=128), Um.unsqueeze(1).broadcast_to([128, 4, 128]), ALU.mult), reads=[pbcb, B_c], writes=[BCB])
                        yield tick()
                for d in range(2):
                    S.op("pool", lambda v: v.tensor_tensor(v3(xdt[d][:]), v3(X[:, 0:2048]), bc3(dtt[:, d * 32:(d + 1) * 32], 64), ALU.mult), reads=[BX, Bd], writes=[Bxdt[d]])
                    yield tick()
                ei = 0
                for gp in range(4):
                    MT, BMT = MTs[gp % 2], BMTs[gp % 2]
                    for d, Lm, Um in ((0, Lfb, Ufb), (1, Lbb, Ubb)):
                        R, BR = Rs[d], BRs[d]
                        h0 = d * 32 + gp * 8
                        for kk, src in ((0, dAh), (1, dAl)):
                            S.op("dve" if kk else "pool", lambda v: v.tensor_tensor(R[:, kk, :, :], Um.unsqueeze(1).broadcast_to([128, 8, 128]), bc3(src[:, h0:h0 + 8], 128), ALU.mult), reads=[BdA, B_c], writes=[BR])
                            yield tick()
                        for gg in range(2):
                            g = gp * 2 + gg
                            pseg, pbseg = bank()
                            S.op("pe", lambda pe: pe.matmul(pseg, Lm, R[:, 0, gg * 4:(gg + 1) * 4, :].rearrange("p h l -> p (h l)"), start=True, stop=False), reads=[BR, B_c], writes=[pbseg])
                            yield tick()
                            S.op("pe", lambda pe: pe.matmul(pseg, Lm, R[:, 1, gg * 4:(gg + 1) * 4, :].rearrange("p h l -> p (h l)"), start=False, stop=True), reads=[BR, B_c], writes=[pbseg])
                            yield tick()
                            Ex, BE_ = Exs[ei % 2], BEx[ei % 2]
                            ei += 1
                            S.op("act", lambda a: a.activation(Ex[:], pseg, AF.Exp), reads=[pbseg], writes=[BE_])
                            yield tick()
                            S.op("dve" if gg else "pool", lambda v: v.tensor_tensor(MT[:, d, gg * 4:(gg + 1) * 4, :], Ex[:].rearrange("p (h l) -> p h l", l=128), CBm[:, d, g:g + 1, :].broadcast_to([128, 4, 128]), ALU.mult), reads=[BE_, BCB], writes=[BMT])
                            yield tick()
                    py, pby = bank()
                    pof, pbof = bank()
                    pob, pbob = bank()
                    for gg in range(2):
                        g = gp * 2 + gg
                        for j in range(4):
                            h = 4 * g + j
                            for d in range(2):
                                S.op("pe", lambda pe: pe.matmul(py[:, gg * 256 + j * 64:gg * 256 + (j + 1) * 64], MT[:, d, gg * 4 + j, :], xdt[d][:, h * 64:(h + 1) * 64], start=(d == 0), stop=(d == 1)), reads=[BMT, Bxdt[d]], writes=[pby])
                                yield tick()
                        S.op("pe", lambda pe: pe.matmul(pof[:, gg * 256:(gg + 1) * 256], BCt[:, 8 + g, :], Hb16[:, g * 256:(g + 1) * 256], start=True, stop=True), reads=[BBC, BHb], writes=[pbof])
                        yield tick()
                        S.op("pe", lambda pe: pe.matmul(pob[:, gg * 256:(gg + 1) * 256], BCt[:, 8 + g, :], PV[:, g * 256:(g + 1) * 256], start=True, stop=True), reads=[BBC, BPV], writes=[pbob])
                        yield tick()
                    T1, BT1, T2, BT2 = t1s[gp % 2], Bt1[gp % 2], t2s[gp % 2], Bt2[gp % 2]
                    S.op("dve", lambda v: v.tensor_tensor(v3(T1[:]), v3(pof), bc3(ec[:, gp * 8:gp * 8 + 8], 64), ALU.mult), reads=[pbof, BE], writes=[BT1])
                    yield tick()
                    S.op("dve", lambda v: v.tensor_tensor(v3(T2[:]), v3(pob), bc3(ec[:, 32 + gp * 8:32 + gp * 8 + 8], 64), ALU.mult), reads=[pbob, BE], writes=[BT2])
                    yield tick()
                    S.op("pool", lambda v: v.tensor_tensor(T1[:], T1[:], T2[:], ALU.add), reads=[BT2], writes=[BT1])
                    yield tick()
                    S.op("pool", lambda v: v.tensor_tensor(v3(T2[:]), v3(X[:, gp * 512:(gp + 1) * 512]), bc3(D_bc[:, gp * 8:gp * 8 + 8], 64), ALU.mult), reads=[BX, Bk], writes=[BT2])
                    yield tick()
                    S.op("pool", lambda v: v.tensor_tensor(T1[:], T1[:], T2[:], ALU.add), reads=[BT2], writes=[BT1])
                    yield tick()
                    S.op("dve", lambda v: v.tensor_tensor(y[:, gp * 512:(gp + 1) * 512], py, T1[:], ALU.add), reads=[pby, BT1], writes=[By])
                    yield tick()
                S.op("dve", lambda v: v.tensor_tensor(y[:], y[:], Z[:], ALU.mult), reads=[BZ], writes=[By])
                yield tick()
                S.op("pool", lambda v: v.tensor_tensor(sqt[:], y[:], y[:], ALU.mult), reads=[By], writes=[Bsq])
                yield tick()
                S.op("dve", lambda v: v.tensor_reduce(gs[:], sqt[:].rearrange("p (g f) -> p g f", f=256), AX.X, ALU.add), reads=[Bsq], writes=[Bgs])
                yield tick()
                S.op("act", lambda a: a.activation(gs[:], gs[:], AF.Sqrt, bias=epst[:], scale=1.0 / 256), reads=[Bgs, B_c], writes=[Bgs])
                yield tick()
                S.op("dve", lambda v: v.reciprocal(gs[:], gs[:]), reads=[Bgs], writes=[Bgs])
                yield tick()
                S.op("dve", lambda v: v.tensor_tensor(ygn[:].rearrange("p (g f) -> p g f", f=256), y[:].rearrange("p (g f) -> p g f", f=256), bc3(gs[:], 256), ALU.mult), reads=[By, Bgs], writes=[Bygn])
                yield tick()
                YS, BYS = ygs[k], Bygs[k]
                for hf in range(2):
                    pt, pbt = bank()
                    ptb = pt.bitcast(BF16)
                    for jj in range(8):
                        j = hf * 8 + jj
                        S.op("pe", lambda pe: pe.transpose(ptb[:, jj * 128:(jj + 1) * 128], ygn[:, j * 128:(j + 1) * 128], identb), reads=[Bygn, B_c], writes=[pbt])
                        yield tick()
                    S.op("act", lambda a: a.copy(YS[:, hf * 8:(hf + 1) * 8, :], ptb[:, 0:1024].rearrange("p (j t) -> p j t", t=128)), reads=[pbt], writes=[BYS])
                    yield tick()
                S.dma(Q, ygT_d[:, g0:g0 + 128].rearrange("(j p) t -> p j t", p=128), YS[:], reads=[BYS])
                yield tick()
        yield 1.0


def emit_p4a(c):
    from contextlib import ExitStack
    S, nc, sb, bank, seqs, tok0, NS = c["S"], c["nc"], c["sb"], c["bank"], c["seqs"], c["tok0"], c["NS"]
    identb, B_c, epst = c["identb"], c["B_c"], c["epst"]
    ygT_d, OT_d, gT_d, x_d, x1_d, h2T_d, gate_d, adaT_d = c["ygT_d"], c["OT_d"], c["gT_d"], c["x_d"], c["x1_d"], c["h2T_d"], c["gate_d"], c["adaT_d"]
    w_ssd_out, w_mla_out, w_o, g_ssd = c["w_ssd_out"], c["w_mla_out"], c["w_o"], c["g_ssd"]
    with ExitStack() as st:
        ws = sb(st, "p4_ws", [128, 16, 1024], BF16)
        wm = sb(st, "p4_wm", [128, 8, 1024], BF16)
        wo = sb(st, "p4_wo", [128, 8, 1024], BF16)
        gsn = sb(st, "p4_gsn", [128, 16], F32)
        yg = sb(st, "p4_yg", [128, 16, 512], BF16)
        ot = sb(st, "p4_ot", [128, 8, 512], BF16)
        gt = sb(st, "p4_gt", [128, 16, 512], BF16)
        xt = sb(st, "p4_x", [128, 4, 1024], F32)
        mixf = sb(st, "p4_mixf", [128, 512], F32)
        mixf2 = sb(st, "p4_mixf2", [128, 512], F32)
        mix = sb(st, "p4_mix", [128, 8, 512], BF16)
        g1 = sb(st, "p4_g1", [128, 1024], F32)
        ab = sb(st, "p4_ab", [128, 2, 8], F32)
        x1 = sb(st, "p4_x1", [128, 4, 1024], F32)
        junk = sb(st, "p4_junk", [128, 1024], BF16)
        ss = sb(st, "p4_ss", [128, 4], F32)
        xn = sb(st, "p4_xn", [128, 4, 1024], BF16)
        h2 = sb(st, "p4_h2", [128, 8, 512], BF16)
        Bw, Byg, Bot, Bgt, Bx, Bmf, Bmf2, Bmix, Bg1, Bab, Bx1, Bss, Bxn, Bh2 = (Buf() for _ in range(14))
        S.dma("pool", ws[:], w_ssd_out.rearrange("(kc p) f -> p kc f", p=128), writes=[Bw])
        S.dma("pool", wm[:], w_mla_out.rearrange("(kc p) f -> p kc f", p=128), writes=[Bw])
        S.dma("pool", wo[:], w_o.rearrange("(kc p) f -> p kc f", p=128), writes=[Bw])
        with nc.allow_non_contiguous_dma(reason="tiny"):
            S.dma("sp", gsn[:], g_ssd.rearrange("(j p) -> p j", p=128), writes=[Bw])
        for kc in range(16):
            S.op("dve", lambda v: v.tensor_scalar(ws[:, kc, :], ws[:, kc, :], gsn[:, kc:kc + 1], None, ALU.mult), reads=[Bw], writes=[Bw])
        for si in range(NS):
            S.dma("sp", g1[:], gate_d[si, 0, :].partition_broadcast(128), writes=[Bg1])
            with nc.allow_non_contiguous_dma(reason="tiny"):
                S.dma("sp", ab[:, 0, :], adaT_d[si, 2, :].rearrange("(kc p) -> p kc", p=128), writes=[Bab])
                S.dma("sp", ab[:, 1, :], adaT_d[si, 3, :].rearrange("(kc p) -> p kc", p=128), writes=[Bab])
            for ti in range(seqs[si] // 512):
                g0 = tok0[si] + ti * 512
                S.dma("sp", yg[:], ygT_d[:, g0:g0 + 512].rearrange("(j p) t -> p j t", p=128), writes=[Byg])
                S.dma("sp", ot[:], OT_d[:, g0:g0 + 512].rearrange("(j p) t -> p j t", p=128), writes=[Bot])
                S.dma("sp", gt[:], gT_d[:, g0:g0 + 512].rearrange("(j p) t -> p j t", p=128), writes=[Bgt])
                S.dma("sp", xt[:], x_d[g0:g0 + 512, :].rearrange("(s p) f -> p s f", p=128), writes=[Bx])
                for oc in range(8):
                    pa, pba = bank()
                    S.mm(pa, [(ws[:, kc, oc * 128:(oc + 1) * 128], yg[:, kc, :]) for kc in range(16)], reads=[Bw, Byg], writes=[pba])
                    pm, pbm = bank()
                    S.mm(pm, [(wm[:, kc, oc * 128:(oc + 1) * 128], ot[:, kc, :]) for kc in range(8)], reads=[Bw, Bot], writes=[pbm])
                    S.op("dve", lambda v: v.tensor_tensor(mixf[:], pa[:], gt[:, oc, :], ALU.mult), reads=[pba, Bgt], writes=[Bmf])
                    S.op("dve", lambda v: v.tensor_tensor(mixf2[:], pm[:], gt[:, 8 + oc, :], ALU.mult), reads=[pbm, Bgt], writes=[Bmf2])
                    S.op("pool", lambda v: v.tensor_tensor(mix[:, oc, :], mixf[:], mixf2[:], ALU.add), reads=[Bmf, Bmf2], writes=[Bmix])
                for s4 in range(4):
                    for hf in range(2):
                        po, pbo = bank()
                        S.mm(po, [(mix[:, kc, s4 * 128:(s4 + 1) * 128], wo[:, kc, hf * 512:(hf + 1) * 512]) for kc in range(8)], reads=[Bw, Bmix], writes=[pbo])
                        S.op("dve", lambda v: v.tensor_tensor(x1[:, s4, hf * 512:(hf + 1) * 512], po[:], g1[:, hf * 512:(hf + 1) * 512], ALU.mult), reads=[pbo, Bg1], writes=[Bx1])
                    S.op("pool", lambda v: v.tensor_tensor(x1[:, s4, :], x1[:, s4, :], xt[:, s4, :], ALU.add), reads=[Bx], writes=[Bx1])
                    S.op("act", lambda a: a.activation(junk[:], x1[:, s4, :], AF.Square, accum_out=ss[:, s4:s4 + 1]), reads=[Bx1], writes=[Bss])
                S.dma("sp", x1_d[g0:g0 + 512, :].rearrange("(s p) f -> p s f", p=128), x1[:], reads=[Bx1])
                S.op("act", lambda a: a.activation(ss[:], ss[:], AF.Sqrt, bias=epst[:], scale=1.0 / 1024), reads=[Bss, B_c], writes=[Bss])
                S.op("dve", lambda v: v.reciprocal(ss[:], ss[:]), reads=[Bss], writes=[Bss])
                for s4 in range(4):
                    S.op("pool" if s4 % 2 else "dve", lambda v: v.tensor_scalar(xn[:, s4, :], x1[:, s4, :], ss[:, s4:s4 + 1], None, ALU.mult), reads=[Bx1, Bss], writes=[Bxn])
                for kc in range(8):
                    pt, pbt = bank()
                    ptb = pt[:].bitcast(BF16)
                    for s4 in range(4):
                        S.op("pe", lambda pe: pe.transpose(ptb[:, s4 * 128:(s4 + 1) * 128], xn[:, s4, kc * 128:(kc + 1) * 128], identb), reads=[Bxn, B_c], writes=[pbt])
                    S.op("dve", lambda v: v.tensor_scalar(h2[:, kc, :], ptb[:, 0:512], ab[:, 0, kc:kc + 1], ab[:, 1, kc:kc + 1], ALU.mult, ALU.add), reads=[pbt, Bab], writes=[Bh2])
                S.dma("sp", h2T_d[:, g0:g0 + 512].rearrange("(j p) t -> p j t", p=128), h2[:], reads=[Bh2])
        S.barrier()


def emit_p4b(c):
    from contextlib import ExitStack
    S, nc, sb, bank, seqs, tok0, NS = c["S"], c["nc"], c["sb"], c["bank"], c["seqs"], c["tok0"], c["NS"]
    B_c, epst = c["B_c"], c["epst"]
    x1_d, h2T_d, gate_d, y_d, w_mlp_in, w_mlp_out, g_final = c["x1_d"], c["h2T_d"], c["gate_d"], c["y_d"], c["w_mlp_in"], c["w_mlp_out"], c["g_final"]
    TT = 256
    with ExitStack() as st:
        w1 = sb(st, "p5_w1", [128, 8, 4096], BF16)
        w2 = sb(st, "p5_w2", [128, 32, 1024], BF16)
        gf = sb(st, "p5_gf", [128, 1024], F32)
        g2 = sb(st, "p5_g2", [128, 1024], F32)
        h2 = [sb(st, "p5_h2", [128, 8, TT], BF16) for _ in range(2)]
        x1 = [sb(st, "p5_x1", [128, 2, 1024], F32) for _ in range(2)]
        rl = [sb(st, "p5_rl", [128, TT], F32) for _ in range(2)]
        rT = sb(st, "p5_rT", [128, 32, TT], BF16)
        x2 = sb(st, "p5_x2", [128, 2, 1024], F32)
        junk = sb(st, "p5_junk", [128, 1024], BF16)
        ss = sb(st, "p5_ss", [128, 2], F32)
        yo = sb(st, "p5_yo", [128, 2, 1024], F32)
        Bw, Bg2, Bh2, Bx1, Brl, BrT, Bx2, Bss, Byo = Buf(), Buf(), [Buf(), Buf()], [Buf(), Buf()], [Buf(), Buf()], Buf(), Buf(), Buf(), Buf()
        S.dma("pool", w1[:], w_mlp_in.rearrange("(kc p) f -> p kc f", p=128), writes=[Bw])
        for q4 in range(4):
            S.dma("pool", w2[:, q4 * 8:(q4 + 1) * 8, :], w_mlp_out[q4 * 1024:(q4 + 1) * 1024, :].rearrange("(kc p) f -> p kc f", p=128), writes=[Bw])
        S.dma("sp", gf[:], g_final.partition_broadcast(128), writes=[Bw])
        it = 0
        for si in range(NS):
            S.dma("sp", g2[:], gate_d[si, 1, :].partition_broadcast(128), writes=[Bg2])
            for ti in range(seqs[si] // TT):
                g0 = tok0[si] + ti * TT
                k = it % 2
                it += 1
                H, X1 = h2[k], x1[k]
                S.dma("sp", H[:], h2T_d[:, g0:g0 + TT].rearrange("(j p) t -> p j t", p=128), writes=[Bh2[k]])
                S.dma("sp", X1[:], x1_d[g0:g0 + TT, :].rearrange("(s p) f -> p s f", p=128), writes=[Bx1[k]])
                for fc in range(32):
                    pf, pbf = bank()
                    S.mm(pf[:, 0:TT], [(w1[:, kc, fc * 128:(fc + 1) * 128], H[:, kc, :]) for kc in range(8)], reads=[Bw, Bh2[k]], writes=[pbf])
                    RL, BRL = rl[fc % 2], Brl[fc % 2]
                    S.op("act", lambda a: a.activation(RL[:], pf[:, 0:TT], AF.Relu), reads=[pbf], writes=[BRL])
                    S.op("pool" if fc % 2 else "dve", lambda v: v.tensor_tensor(rT[:, fc, :], RL[:], RL[:], ALU.mult), reads=[BRL], writes=[BrT])
                for s2 in range(2):
                    for hf in range(2):
                        po, pbo = bank()
                        S.mm(po, [(rT[:, kc, s2 * 128:(s2 + 1) * 128], w2[:, kc, hf * 512:(hf + 1) * 512]) for kc in range(32)], reads=[Bw, BrT], writes=[pbo])
                        S.op("dve", lambda v: v.tensor_tensor(x2[:, s2, hf * 512:(hf + 1) * 512], po[:], g2[:, hf * 512:(hf + 1) * 512], ALU.mult), reads=[pbo, Bg2], writes=[Bx2])
                    S.op("pool", lambda v: v.tensor_tensor(x2[:, s2, :], x2[:, s2, :], X1[:, s2, :], ALU.add), reads=[Bx1[k]], writes=[Bx2])
                    S.op("act", lambda a: a.activation(junk[:], x2[:, s2, :], AF.Square, accum_out=ss[:, s2:s2 + 1]), reads=[Bx2], writes=[Bss])
                S.op("act", lambda a: a.activation(ss[:], ss[:], AF.Sqrt, bias=epst[:], scale=1.0 / 1024), reads=[Bss, B_c], writes=[Bss])
                S.op("dve", lambda v: v.reciprocal(ss[:], ss[:]), reads=[Bss], writes=[Bss])
                for s2 in range(2):
                    S.op("dve", lambda v: v.scalar_tensor_tensor(yo[:, s2, :], x2[:, s2, :], ss[:, s2:s2 + 1], gf[:], ALU.mult, ALU.mult), reads=[Bx2, Bss, Bw], writes=[Byo])
                S.dma("sp", y_d[g0:g0 + TT, :].rearrange("(s p) f -> p s f", p=128), yo[:], reads=[Byo])
        S.barrier()


def core_inputs(x_all, c_all, W, g_final, cos, sin):
    f = lambda a: np.ascontiguousarray(np.asarray(a, dtype=np.float32))
    return {
        "x": f(x_all), "c": f(c_all),
        "w_ada": f(W["w_ada"]), "b_ada": f(W["b_ada"]), "g_norm1": f(W["g_norm1"]), "w_in": f(W["w_in"]),
        "conv_w": f(W["conv_w"]), "conv_b": f(W["conv_b"]),
        "dt_bias": f(np.concatenate([W["dt_bias_fwd"], W["dt_bias_bwd"]])),
        "a_log": f(np.concatenate([W["a_log_fwd"], W["a_log_bwd"]])),
        "d_skip": f(W["d_skip"]), "g_ssd_norm": f(W["g_ssd_norm"]), "w_ssd_out": f(W["w_ssd_out"]),
        "g_q_norm": f(W["g_q_norm"]), "w_q_b": f(W["w_q_b"]), "g_kv_norm": f(W["g_kv_norm"]), "w_kv_b": f(W["w_kv_b"]),
        "w_mla_out": f(W["w_mla_out"]), "w_o": f(W["w_o"]), "g_norm2": f(W["g_norm2"]),
        "w_mlp_in": f(W["w_mlp_in"]), "w_mlp_out": f(W["w_mlp_out"]), "g_final": f(g_final),
        "consts": make_consts(), "cos_t": f(cos), "sin_t": f(sin),
    }


_WNAMES = ["w_ada", "b_ada", "g_norm1", "w_in", "conv_w", "conv_b", "dt_bias_fwd", "dt_bias_bwd", "a_log_fwd",
           "a_log_bwd", "d_skip", "g_ssd_norm", "w_ssd_out", "g_q_norm", "w_q_b", "g_kv_norm", "w_kv_b",
           "w_mla_out", "w_o", "g_norm2", "w_mlp_in", "w_mlp_out"]


def kernel(x_prompt, x_sample, c_prompt, c_sample, g_final, **kw):
    W = {k: np.asarray(kw[k])[0] for k in _WNAMES}
    x_prompt = np.asarray(x_prompt); x_sample = np.asarray(x_sample)
    c_prompt = np.asarray(c_prompt); c_sample = np.asarray(c_sample)
    n = 8
    seqs = (2048, 2048, 4096, 4096)
    cos, sin = rope_tables(4096)
    import os
    ph = os.environ.get("KPHASES")
    nc = build(seqs=seqs, phases=tuple(ph.split(","))) if ph else build(seqs=seqs)
    in_maps = []
    for i in range(n):
        xa = np.concatenate([x_prompt[2 * i].reshape(-1, D), x_prompt[2 * i + 1].reshape(-1, D),
                             x_sample[2 * i].reshape(-1, D), x_sample[2 * i + 1].reshape(-1, D)], axis=0)
        ca = np.stack([c_prompt[2 * i], c_prompt[2 * i + 1], c_sample[2 * i], c_sample[2 * i + 1]], axis=0)
        in_maps.append(core_inputs(xa, ca, W, g_final, cos, sin))
    res = run_bass_kernel_spmd(nc, in_maps, core_ids=list(range(n)))
    yp = np.zeros((16, 2048, D), np.float32)
    ys = np.zeros((16, 4096, D), np.float32)
    for i in range(n):
        y = np.asarray(res.results[i]["y"])
        yp[2 * i] = y[0:2048]
        yp[2 * i + 1] = y[2048:4096]
        ys[2 * i] = y[4096:8192]
        ys[2 * i + 1] = y[8192:12288]
    return (yp, ys)
```

```python
import numpy as np
import concourse.bass as bass
import concourse.mybir as mybir
from concourse.bass_utils import run_bass_kernel_spmd

F32 = mybir.dt.float32
BF16 = mybir.dt.bfloat16
AF = mybir.ActivationFunctionType
ALU = mybir.AluOpType
AX = mybir.AxisListType

D = 1024
DI = 2048
NH = 32
DIN = 8928
EPS = 1e-6
C_Z, C_X, C_DT, C_QA, C_KV, C_KR, C_G = 0, 2048, 6144, 6208, 6592, 6848, 6880
ATT_SCALE = 96 ** -0.5


class Buf:
    __slots__ = ("w", "r")

    def __init__(self):
        self.w = None
        self.r = {}


class Eng:
    def __init__(self, key, h, sem):
        self.key, self.h, self.sem = key, h, sem
        self.count = 0
        self.waited = {}
        self.slots = []
        self.dma_i = 0


class Slot:
    def __init__(self, key, sem):
        self.key, self.sem, self.count = key, sem, 0


class Sch:
    def __init__(self, nc, stack, nslots=12):
        self.nc = nc
        self.E = {}
        for key, h in (("pe", nc.tensor), ("act", nc.scalar), ("dve", nc.vector),
                       ("pool", nc.gpsimd), ("sp", nc.sync)):
            sem = stack.enter_context(nc.semaphore("sem_" + key))
            self.E[key] = Eng(key, h, sem)
        for q in ("sp", "pool", "act"):
            for i in range(nslots):
                sem = stack.enter_context(nc.semaphore("dq_%s_%d" % (q, i)))
                self.E[q].slots.append(Slot("dq_%s_%d" % (q, i), sem))

    def _wait(self, e, deps):
        for (key, sem, val) in deps:
            if key == e.key and key == "pe":
                continue
            if e.waited.get(key, 0) >= val:
                continue
            e.h.wait_ge(sem, val)
            e.waited[key] = val

    @staticmethod
    def _deps(reads, writes):
        deps = []
        for b in reads:
            if b.w is not None:
                deps.append(b.w)
        for b in writes:
            if b.w is not None:
                deps.append(b.w)
            for k, (s, v) in b.r.items():
                deps.append((k, s, v))
        return deps

    @staticmethod
    def _mark(tok, reads, writes):
        for b in reads:
            b.r[tok[0]] = (tok[1], tok[2])
        for b in writes:
            b.w = tok
            b.r = {}

    def op(self, eng, fn, reads=(), writes=()):
        e = self.E[eng]
        self._wait(e, self._deps(reads, writes))
        ins = fn(e.h)
        e.count += 1
        ins.then_inc(e.sem, 1)
        self._mark((e.key, e.sem, e.count), reads, writes)

    def mm(self, out, pairs, reads=(), writes=()):
        n = len(pairs)

        def fn(pe):
            ins = None
            for i, (l, r) in enumerate(pairs):
                ins = pe.matmul(out, l, r, start=(i == 0), stop=(i == n - 1))
            return ins
        self.op("pe", fn, reads=reads, writes=writes)

    def dma(self, q, out, in_, reads=(), writes=()):
        if q == "st":
            q = "pool"
        e = self.E[q]
        self._wait(e, self._deps(reads, writes))
        sl = e.slots[e.dma_i % len(e.slots)]
        e.dma_i += 1
        if sl.count > 0:
            self._wait(e, [(sl.key, sl.sem, 16 * sl.count)])
        e.h.dma_start(out=out, in_=in_).then_inc(sl.sem, 16)
        sl.count += 1
        self._mark((sl.key, sl.sem, 16 * sl.count), reads, writes)

    def barrier(self):
        toks = []
        for e in self.E.values():
            if e.count:
                toks.append((e.key, e.sem, e.count))
            for sl in e.slots:
                if sl.count:
                    toks.append((sl.key, sl.sem, 16 * sl.count))
        for e in self.E.values():
            self._wait(e, toks)


def make_consts():
    i = np.arange(128)
    c = np.zeros((128, 6, 128), np.float32)
    c[:, 0, :] = np.eye(128)
    c[:, 1, :] = (i[:, None] <= i[None, :])
    c[:, 2, :] = (i[:, None] >= i[None, :])
    c[:, 3, :] = (i[:, None] > i[None, :])
    c[:, 4, :] = (i[:, None] < i[None, :])
    c[:, 5, :] = 1.0
    return c.reshape(128, 768)


def rope_tables(smax):
    inv = (1.0 / (np.float32(10000.0) ** (np.arange(0, 32, 2, dtype=np.float32) / np.float32(32)))).astype(np.float32)
    ang = np.arange(smax, dtype=np.float32)[:, None] * inv[None, :]
    cos = np.cos(ang).astype(np.float32).T
    sin = np.sin(ang).astype(np.float32).T
    return (np.ascontiguousarray(np.concatenate([cos, cos], 0)),
            np.ascontiguousarray(np.concatenate([sin, sin], 0)))


def build(seqs=(2048, 2048, 4096, 4096), phases=("p0", "p1a", "p1b", "p1c", "p1d", "p2", "p3", "p4a", "p4b"), debug=False):
    nc = bass.Bass("TRN2", target_bir_lowering=False)
    NS = len(seqs)
    NT = sum(seqs)
    SMAX = max(seqs)
    tok0 = [sum(seqs[:i]) for i in range(NS)]
    skind = "ExternalOutput" if debug else "Internal"

    def din(name, shape, dt=F32):
        return nc.dram_tensor(name, list(shape), dt, kind="ExternalInput").ap()

    def dscr(name, shape, dt):
        return nc.dram_tensor(name, list(shape), dt, kind=skind).ap()

    x_d = din("x", [NT, D])
    c_d = din("c", [NS, D])
    w_ada = din("w_ada", [D, 6 * D])
    b_ada = din("b_ada", [6 * D])
    g_norm1 = din("g_norm1", [D])
    w_in = din("w_in", [D, DIN])
    conv_w = din("conv_w", [5, 4096])
    conv_b = din("conv_b", [4096])
    dtb = din("dt_bias", [64])
    alog = din("a_log", [64])
    dskip = din("d_skip", [32])
    g_ssd = din("g_ssd_norm", [DI])
    w_ssd_out = din("w_ssd_out", [DI, D])
    g_q = din("g_q_norm", [384])
    w_q_b = din("w_q_b", [384, 1536])
    g_kv = din("g_kv_norm", [256])
    w_kv_b = din("w_kv_b", [256, 2048])
    w_mla_out = din("w_mla_out", [D, D])
    w_o = din("w_o", [D, D])
    g_norm2 = din("g_norm2", [D])
    w_mlp_in = din("w_mlp_in", [D, 4 * D])
    w_mlp_out = din("w_mlp_out", [4 * D, D])
    g_final = din("g_final", [D])
    consts_d = din("consts", [128, 768])
    cos_d = din("cos_t", [32, SMAX])
    sin_d = din("sin_t", [32, SMAX])

    y_d = nc.dram_tensor("y", [NT, D], F32, kind="ExternalOutput").ap()

    adaT_d = dscr("adaT_s", [NS, 4, D], F32)
    gate_d = dscr("gate_s", [NS, 2, D], F32)
    uT_d = dscr("uT_s", [4096, NT], BF16)
    zs_d = dscr("zs_s", [NT, DI], BF16)
    dt_d = dscr("dt_s", [NT, 64], F32)
    qnT_d = dscr("qnT_s", [384, NT], BF16)
    cnT_d = dscr("cnT_s", [256, NT], BF16)
    kr_d = dscr("kr_s", [32, NT], BF16)
    gT_d = dscr("gT_s", [2048, NT], BF16)
    xtok_d = dscr("xtok_s", [NT, 3072], BF16)
    bcT_d = dscr("bcT_s", [2048, NT], BF16)
    KT_d = dscr("KT_s", [1024, NT], BF16)
    QT_d = dscr("QT_s", [1536, NT], BF16)
    V_d = dscr("V_s", [16, NT * 64], BF16)
    OT_d = dscr("OT_s", [D, NT], BF16)
    prevb_d = dscr("prevb_s", [SMAX // 128, 128, DI], BF16)
    ygT_d = dscr("ygT_s", [DI, NT], BF16)
    x1_d = dscr("x1_s", [NT, D], F32)
    h2T_d = dscr("h2T_s", [D, NT], BF16)

    from contextlib import ExitStack
    with ExitStack() as top:
        S = Sch(nc, top)
        _uid = [0]

        def sb(st, name, shape, dt):
            _uid[0] += 1
            return st.enter_context(nc.sbuf_tensor("%s_%d" % (name, _uid[0]), list(shape), dt))
        psall = top.enter_context(nc.psum_tensor("psall", [128, 4096], F32))
        banks = [psall[:, i * 512:(i + 1) * 512] for i in range(8)]
        bbuf = [Buf() for _ in range(8)]
        bi = {None: 0, "b": 0}
        pools = {None: list(range(8)), "b": [5, 6, 7]}

        def bank(pool=None):
            lst = pools[pool]
            i = lst[bi[pool] % len(lst)]
            bi[pool] += 1
            return banks[i], bbuf[i]

        cst32 = sb(top, "cst32", [128, 768], F32)
        cstb = sb(top, "cstb", [128, 768], BF16)
        epst = sb(top, "epst", [128, 1], F32)
        onet = sb(top, "onet", [128, 1], F32)
        B_c = Buf()
        S.dma("sp", cst32[:], consts_d, writes=[B_c])
        S.op("dve", lambda v: v.tensor_copy(cstb[:], cst32[:]), reads=[B_c], writes=[B_c])
        S.op("dve", lambda v: v.memset(epst[:], EPS), writes=[B_c])
        S.op("dve", lambda v: v.memset(onet[:], 1.0), writes=[B_c])
        identb = cstb[:, 0:128]
        Ufb, Ubb, Lfb, Lbb, onesb = (cstb[:, 128 * k:128 * (k + 1)] for k in range(1, 6))
        Uf32, Ub32 = cst32[:, 128:256], cst32[:, 256:384]
        ones32 = cst32[:, 640:768]

        def rstd_from_ss(eng_list, out, ss, n, rb, wb):
            S.op("act", lambda a: a.activation(out, ss, AF.Sqrt, bias=epst[0:out.shape[0], :], scale=1.0 / n), reads=rb + [B_c], writes=wb)
            S.op("dve", lambda v: v.reciprocal(out, out), reads=wb, writes=wb)

        if "p0" in phases:
            with ExitStack() as st:
                cT = sb(st, "p0_cT", [128, 8, NS], F32)
                cbc = sb(st, "p0_cbc", [128, 8, NS, 128], F32)
                wa = [sb(st, "p0_wa%d" % i, [128, 8, 1024], F32) for i in range(2)]
                bT = sb(st, "p0_bT", [128, 48], F32)
                brow = sb(st, "p0_brow", [128, 2, 1024], F32)
                gT1 = sb(st, "p0_g", [128, 2, 8], F32)
                res = sb(st, "p0_res", [128, 6, 8, NS], F32)
                vec = sb(st, "p0_vec", [128, NS, 4, 8], F32)
                grow = sb(st, "p0_grow", [128, 1024], F32)
                Bc, Bw, Bb, Br, Bv, Bg = Buf(), [Buf(), Buf()], Buf(), Buf(), Buf(), Buf()
                with nc.allow_non_contiguous_dma(reason="tiny transposed loads"):
                    for b in range(NS):
                        S.dma("sp", cT[:, :, b], c_d[b, :].rearrange("(kc p) -> p kc", p=128), writes=[Bc])
                    S.dma("sp", bT[:], b_ada.rearrange("(j p) -> p j", p=128), writes=[Bb])
                    S.dma("sp", gT1[:, 0, :], g_norm1.rearrange("(j p) -> p j", p=128), writes=[Bb])
                    S.dma("sp", gT1[:, 1, :], g_norm2.rearrange("(j p) -> p j", p=128), writes=[Bb])
                S.dma("sp", brow[:, 0, :], b_ada[2048:3072].partition_broadcast(128), writes=[Bb])
                S.dma("sp", brow[:, 1, :], b_ada[5120:6144].partition_broadcast(128), writes=[Bb])
                S.op("act", lambda a: a.activation(cT[:], cT[:], AF.Silu), reads=[Bc], writes=[Bc])
                S.op("dve", lambda v: v.tensor_copy(cbc[:], cT[:].unsqueeze(3).broadcast_to([128, 8, NS, 128])), reads=[Bc], writes=[Bc])
                for j in range(6):
                    w = wa[j % 2]
                    S.dma("sp", w[:], w_ada[:, j * 1024:(j + 1) * 1024].rearrange("(kc p) f -> p kc f", p=128), writes=[Bw[j % 2]])
                    pt, pb = bank()
                    for oc in range(8):
                        for kc in range(8):
                            S.op("pe", lambda pe, oc=oc, kc=kc: pe.matmul(pt[:, oc * NS:(oc + 1) * NS], w[:, kc, oc * 128:(oc + 1) * 128], cT[:, kc, :], start=(kc == 0), stop=(kc == 7)),
                                 reads=[Bw[j % 2], Bc], writes=[pb])
                    S.op("dve", lambda v, j=j: v.tensor_tensor(res[:, j, :, :], pt[:, 0:8 * NS].rearrange("p (o b) -> p o b", b=NS),
                                                           bT[:, j * 8:(j + 1) * 8].unsqueeze(2).broadcast_to([128, 8, NS]), ALU.add),
                         reads=[pb, Bb], writes=[Br])
                    if j in (2, 5):
                        gi = 0 if j == 2 else 1
                        for b in range(NS):
                            for hf in range(2):
                                pt2, pb2 = bank()
                                for kc in range(8):
                                    S.op("pe", lambda pe, kc=kc, b=b, hf=hf: pe.matmul(pt2[:], cbc[:, kc, b, :], w[:, kc, hf * 512:(hf + 1) * 512], start=(kc == 0), stop=(kc == 7)),
                                         reads=[Bw[j % 2], Bc], writes=[pb2])
                                S.op("dve", lambda v, hf=hf, gi=gi: v.tensor_tensor(grow[:, hf * 512:(hf + 1) * 512], pt2[:], brow[:, gi, hf * 512:(hf + 1) * 512], ALU.add),
                                     reads=[pb2, Bb], writes=[Bg])
                            S.dma("st", gate_d[b, gi, :], grow[0:1, :], reads=[Bg])
                for b in range(NS):
                    for (k, jsc, jsh, gi) in ((0, 1, 0, 0), (2, 4, 3, 1)):
                        S.op("dve", lambda v, b=b, k=k, jsc=jsc, gi=gi: v.scalar_tensor_tensor(vec[:, b, k, :], res[:, jsc, :, b], 1.0, gT1[:, gi, :], ALU.add, ALU.mult), reads=[Br, Bb], writes=[Bv])
                        S.op("dve", lambda v, b=b, k=k, jsh=jsh: v.tensor_copy(vec[:, b, k + 1, :], res[:, jsh, :, b]), reads=[Br], writes=[Bv])
                with nc.allow_non_contiguous_dma(reason="tiny transposed stores"):
                    for b in range(NS):
                        for k in range(4):
                            S.dma("st", adaT_d[b, k, :].rearrange("(kc p) -> p kc", p=128), vec[:, b, k, :], reads=[Bv])
                S.barrier()

        for part in ("a", "b"):
            if ("p1" + part) not in phases:
                continue
            WC = 4096 if part == "a" else 4832
            cm = (lambda c: c - 2048) if part == "a" else (lambda c: c if c < 2048 else c - 4096)
            with ExitStack() as st:
                W = sb(st, "p1_W", [128, 8, WC], BF16)
                wkr = sb(st, "p1_wkr", [128, 8, 64], BF16)
                dtb_bc = sb(st, "p1_dtb", [128, 4, 64], F32)
                ab = sb(st, "p1_ab", [128, 2, 8], F32)
                xt = [sb(st, "p1_x%d" % i, [128, 4, 1024], F32) for i in range(1)] * 2
                junk = sb(st, "p1_junk", [128, 1024], BF16)
                ssl = [sb(st, "p1_ss", [128, 4], F32) for _ in range(2)]
                xnl = [sb(st, "p1_xn", [128, 4, 1024], BF16) for _ in range(2)]
                Bssl, Bxnl = [Buf(), Buf()], [Buf(), Buf()]
                hT = [sb(st, "p1_hT%d" % i, [128, 8, 512], BF16) for i in range(1)] * 2
                stg = [sb(st, "p1_stg%d" % i, [128, 512], F32) for i in range(4)]
                stgb = [sb(st, "p1_stgb%d" % i, [128, 512], BF16) for i in range(4)]
                qa = sb(st, "p1_qa", [128, 5, 512], F32) if part == "b" else None
                sq = sb(st, "p1_sq", [128, 5, 512], F32) if part == "b" else None
                rs = sb(st, "p1_rs", [128, 2, 512], F32)
                qn = sb(st, "p1_qn", [128, 5, 512], BF16)
                cs = sb(st, "p1_cs", [32, 2, 512], F32)
                krt = sb(st, "p1_krt", [32, 2, 512], F32)
                krb = sb(st, "p1_krb", [32, 512], BF16)
                zst = [sb(st, "p1_zst%d" % i, [128, 2048], BF16) if part == "b" else None for i in range(2)]
                dts = sb(st, "p1_dts", [128, 5, 256], F32)
                BW, Bab, Bx, Bss, Bxn, BhT = Buf(), Buf(), [Buf()] * 2, Buf(), Buf(), [Buf()] * 2
                Bstg, Bstgb = [Buf() for _ in range(4)], [Buf() for _ in range(4)]
                Bqa, Bsq, Brs, Bqn, Bcs, Bkr, Bkrb, Bz, Bdts = Buf(), Buf(), Buf(), Buf(), Buf(), Buf(), Buf(), [Buf(), Buf()], Buf()
                for kc in range(8):
                    if part == "a":
                        S.dma("pool", W[:, kc, :], w_in[kc * 128:(kc + 1) * 128, 2048:6144], writes=[BW])
                    else:
                        S.dma("pool", W[:, kc, 0:2048], w_in[kc * 128:(kc + 1) * 128, 0:2048], writes=[BW])
                        S.dma("pool", W[:, kc, 2048:4832], w_in[kc * 128:(kc + 1) * 128, 6144:8928], writes=[BW])
                CKR = cm(C_KR) if part == "b" else 0
                S.op("dve", lambda v: v.tensor_copy(wkr[:, :, 0:32], W[:, :, CKR:CKR + 32]), reads=[BW], writes=[BW])
                S.op("dve", lambda v: v.tensor_scalar(wkr[:, :, 32:48], W[:, :, CKR + 16:CKR + 32], -1.0, None, ALU.mult), reads=[BW], writes=[BW])
                S.op("dve", lambda v: v.tensor_copy(wkr[:, :, 48:64], W[:, :, CKR:CKR + 16]), reads=[BW], writes=[BW])
                for s4 in range(4):
                    S.dma("sp", dtb_bc[:, s4, :], dtb.partition_broadcast(128), writes=[BW])
                stg_i = [0]
                tiles = [(si, ti) for si in range(NS) for ti in range(seqs[si] // 512)]

                def stageA(idx):
                    si, ti = tiles[idx]
                    g0 = tok0[si] + ti * 512
                    k2 = idx % 2
                    X = xt[0]
                    S.dma("sp", X[:], x_d[g0:g0 + 512, :].rearrange("(s p) f -> p s f", p=128), writes=[Bx[0]])
                    for s4 in range(4):
                        S.op("act", lambda a: a.activation(junk[:], X[:, s4, :], AF.Square, accum_out=ssl[k2][:, s4:s4 + 1]), reads=[Bx[0]], writes=[Bssl[k2]])
                    rstd_from_ss(None, ssl[k2][:], ssl[k2][:], 1024.0, [Bssl[k2]], [Bssl[k2]])
                    for s4 in range(4):
                        S.op("pool" if s4 % 2 else "dve", lambda v: v.tensor_scalar(xnl[k2][:, s4, :], X[:, s4, :], ssl[k2][:, s4:s4 + 1], None, ALU.mult), reads=[Bx[0], Bssl[k2]], writes=[Bxnl[k2]])

                def stageB(idx):
                    si, ti = tiles[idx]
                    k2 = idx % 2
                    H = hT[0]
                    if ti == 0:
                        with nc.allow_non_contiguous_dma(reason="tiny"):
                            S.dma("sp", ab[:, 0, :], adaT_d[si, 0, :].rearrange("(kc p) -> p kc", p=128), writes=[Bab])
                            S.dma("sp", ab[:, 1, :], adaT_d[si, 1, :].rearrange("(kc p) -> p kc", p=128), writes=[Bab])
                    for kc in range(8):
                        pt, pb = bank()
                        ptb = pt[:].bitcast(BF16)
                        for s4 in range(4):
                            S.op("pe", lambda pe: pe.transpose(ptb[:, s4 * 128:(s4 + 1) * 128], xnl[k2][:, s4, kc * 128:(kc + 1) * 128], identb), reads=[Bxnl[k2], B_c], writes=[pb])
                        S.op("dve", lambda v: v.tensor_scalar(H[:, kc, :], ptb[:, 0:512], ab[:, 0, kc:kc + 1], ab[:, 1, kc:kc + 1], ALU.mult, ALU.add), reads=[pb, Bab], writes=[BhT[0]])

                def body(idx):
                    si, ti = tiles[idx]
                    g0 = tok0[si] + ti * 512
                    p0 = ti * 512
                    par = 0
                    H = hT[0]
                    def fm(cols, m, lw=None):
                        pt, pb = bank()
                        l = (lw if lw is not None else W)
                        cc0 = cols if lw is not None else cm(cols)
                        S.mm(pt[0:m, :], [(l[:, kc, cc0:cc0 + m], H[:, kc, :]) for kc in range(8)], reads=[BW, BhT[par]], writes=[pb])
                        return pt, pb
                    for j in range(32 if part == "a" else 0):
                        pt, pb = fm(C_X + j * 128, 128)
                        k = stg_i[0] % 4
                        stg_i[0] += 1
                        S.op("act", lambda a, k=k: a.copy(stgb[k][:], pt[:]), reads=[pb], writes=[Bstgb[k]])
                        S.dma("st", uT_d[j * 128:(j + 1) * 128, g0:g0 + 512], stgb[k][:], reads=[Bstgb[k]])
                    if part == "a":
                        return
                    for j in range(5):
                        pt, pb = fm(C_QA + j * 128, 128)
                        S.op("act", lambda a, j=j: a.copy(qa[:, j, :], pt[:]), reads=[pb], writes=[Bqa])
                        S.op("pool", lambda v, j=j: v.tensor_tensor(sq[:, j, :], qa[:, j, :], qa[:, j, :], ALU.mult), reads=[Bqa], writes=[Bsq])
                    for (r, j0, n) in ((0, 0, 3), (1, 3, 2)):
                        pt, pb = bank()
                        for j in range(n):
                            S.op("pe", lambda pe, j=j: pe.matmul(pt[:], ones32, sq[:, j0 + j, :], start=(j == 0), stop=(j == n - 1)), reads=[Bsq, B_c], writes=[pb])
                        S.op("act", lambda a, r=r, n=n: a.activation(rs[:, r, :], pt[:], AF.Sqrt, bias=epst[:], scale=1.0 / (128 * n)), reads=[pb, B_c], writes=[Brs])
                        S.op("dve", lambda v, r=r: v.reciprocal(rs[:, r, :], rs[:, r, :]), reads=[Brs], writes=[Brs])
                        for j in range(n):
                            S.op("dve", lambda v, j=j, r=r: v.tensor_tensor(qn[:, j0 + j, :], qa[:, j0 + j, :], rs[:, r, :], ALU.mult), reads=[Bqa, Brs], writes=[Bqn])
                    S.dma("st", qnT_d[:, g0:g0 + 512].rearrange("(j p) t -> p j t", p=128), qn[:, 0:3, :], reads=[Bqn])
                    S.dma("st", cnT_d[:, g0:g0 + 512].rearrange("(j p) t -> p j t", p=128), qn[:, 3:5, :], reads=[Bqn])
                    S.dma("sp", cs[:, 0, :], cos_d[:, p0:p0 + 512], writes=[Bcs])
                    S.dma("sp", cs[:, 1, :], sin_d[:, p0:p0 + 512], writes=[Bcs])
                    pA, pbA = fm(0, 32, wkr)
                    pB, pbB = fm(32, 32, wkr)
                    S.op("dve", lambda v: v.tensor_tensor(krt[:, 0, :], pA[0:32, :], cs[:, 0, :], ALU.mult), reads=[pbA, Bcs], writes=[Bkr])
                    S.op("dve", lambda v: v.tensor_tensor(krt[:, 1, :], pB[0:32, :], cs[:, 1, :], ALU.mult), reads=[pbB, Bcs], writes=[Bkr])
                    S.op("dve", lambda v: v.tensor_tensor(krb[:], krt[:, 0, :], krt[:, 1, :], ALU.add), reads=[Bkr], writes=[Bkrb])
                    S.dma("st", kr_d[:, g0:g0 + 512], krb[:], reads=[Bkrb])
                    for j in range(16):
                        pt, pb = fm(C_G + j * 128, 128)
                        k = stg_i[0] % 4
                        stg_i[0] += 1
                        S.op("act", lambda a, k=k: a.activation(stgb[k][:], pt[:], AF.Sigmoid), reads=[pb], writes=[Bstgb[k]])
                        S.dma("st", gT_d[j * 128:(j + 1) * 128, g0:g0 + 512], stgb[k][:], reads=[Bstgb[k]])
                    for s4 in range(4):
                        Z = zst[s4 % 2]
                        for cg in range(4):
                            pt, pb = bank()
                            S.mm(pt[:], [(H[:, kc, s4 * 128:(s4 + 1) * 128], W[:, kc, cg * 512:(cg + 1) * 512]) for kc in range(8)], reads=[BW, BhT[par]], writes=[pb])
                            S.op("act", lambda a, s4=s4, cg=cg: a.activation(Z[:, cg * 512:(cg + 1) * 512], pt[:], AF.Silu), reads=[pb], writes=[Bz[s4 % 2]])
                        S.dma("st", zs_d[g0 + s4 * 128:g0 + (s4 + 1) * 128, :], Z[:], reads=[Bz[s4 % 2]])
                    pt, pb = bank()
                    for s4 in range(4):
                        S.mm(pt[:, s4 * 64:(s4 + 1) * 64], [(H[:, kc, s4 * 128:(s4 + 1) * 128], W[:, kc, cm(C_DT):cm(C_DT) + 64]) for kc in range(8)], reads=[BW, BhT[par]], writes=[pb])
                    d0, d1, d2, d3, d4 = (dts[:, k, :] for k in range(5))
                    S.op("dve", lambda v: v.tensor_tensor(d0, pt[:, 0:256], dtb_bc[:].rearrange("p a b -> p (a b)"), ALU.add), reads=[pb, BW], writes=[Bdts])
                    S.op("dve", lambda v: v.tensor_scalar(d1, d0, -1.0, None, ALU.mult), reads=[Bdts], writes=[Bdts])
                    S.op("dve", lambda v: v.tensor_tensor(d1, d0, d1, ALU.min), reads=[Bdts], writes=[Bdts])
                    S.op("act", lambda a: a.activation(d2, d1, AF.Exp), reads=[Bdts], writes=[Bdts])
                    S.op("act", lambda a: a.activation(d3, d2, AF.Ln, bias=onet[:], scale=1.0), reads=[Bdts, B_c], writes=[Bdts])
                    S.op("dve", lambda v: v.scalar_tensor_tensor(d4, d0, 0.0, d3, ALU.max, ALU.add), reads=[Bdts], writes=[Bdts])
                    S.dma("st", dt_d[g0:g0 + 512, :].rearrange("(s p) f -> p s f", p=128), d4.rearrange("p (s f) -> p s f", f=64), reads=[Bdts])
                stageA(0)
                stageB(0)
                for idx in range(len(tiles)):
                    if idx + 1 < len(tiles):
                        stageA(idx + 1)
                    body(idx)
                    if idx + 1 < len(tiles):
                        stageB(idx + 1)
                S.barrier()

        if "p1c" in phases:
            with ExitStack() as st:
                cw = sb(st, "pc_cw", [128, 32, 6], F32)
                u = [sb(st, "pc_u%d" % i, [128, 516], BF16) for i in range(4)]
                dw = sb(st, "pc_dw", [128, 32, 5, 128], BF16)
                ob = [sb(st, "pc_ob%d" % i, [128, 512], BF16) for i in range(4)]
                tk = [sb(st, "pc_tk%d" % i, [128, 4, 3072], BF16) for i in range(2)]
                Bcw, Bu, Bacc, Bob, Btk = Buf(), [Buf() for _ in range(4)], [Buf(), Buf()], [Buf() for _ in range(4)], [Buf(), Buf()]
                with nc.allow_non_contiguous_dma(reason="tiny"):
                    for k in range(5):
                        S.dma("sp", cw[:, :, k], conv_w[k, :].rearrange("(j p) -> p j", p=128), writes=[Bcw])
                    S.dma("sp", cw[:, :, 5], conv_b.rearrange("(j p) -> p j", p=128), writes=[Bcw])
                for j in range(32):
                    S.op("dve", lambda v, j=j: v.tensor_tensor(dw[:, j, :, :], identb.unsqueeze(1).broadcast_to([128, 5, 128]), cw[:, j, 0:5].unsqueeze(2).broadcast_to([128, 5, 128]), ALU.mult), reads=[Bcw, B_c], writes=[Bcw])
                it = 0
                for si in range(NS):
                    nt = seqs[si] // 512
                    for ti in range(nt):
                        g0 = tok0[si] + ti * 512
                        par = (g0 // 512) % 2
                        TK = tk[par]
                        for j in range(32):
                            U, BU = u[it % 4], Bu[it % 4]
                            O, BO = ob[it % 4], Bob[it % 4]
                            eng = "dve"
                            it += 1
                            lo = 0 if ti > 0 else 2
                            hi = 516 if ti < nt - 1 else 514
                            if lo:
                                S.op(eng, lambda v: v.memset(U[:, 0:2], 0.0), writes=[BU])
                            if hi < 516:
                                S.op(eng, lambda v: v.memset(U[:, 514:516], 0.0), writes=[BU])
                            S.dma("sp", U[:, lo:hi], uT_d[j * 128:(j + 1) * 128, g0 - 2 + lo:g0 - 2 + hi], writes=[BU])
                            pcv, pbcv = bank()
                            S.mm(pcv, [(dw[:, j, k, :], U[:, k:k + 512]) for k in range(5)], reads=[BU, Bcw], writes=[pbcv])
                            S.op("act", lambda a, j=j: a.activation(O[:], pcv, AF.Silu, bias=cw[:, j, 5:6], scale=1.0), reads=[pbcv, Bcw], writes=[BO])
                            if j >= 16:
                                S.dma("st", bcT_d[(j - 16) * 128:(j - 15) * 128, g0:g0 + 512], O[:], reads=[BO])
                            if j < 24:
                                pt, pb = bank()
                                ptb = pt[:].bitcast(BF16)
                                for s4 in range(4):
                                    S.op("pe", lambda pe, s4=s4: pe.transpose(ptb[:, s4 * 128:(s4 + 1) * 128], O[:, s4 * 128:(s4 + 1) * 128], identb), reads=[BO, B_c], writes=[pb])
                                S.op("act", lambda a, j=j: a.copy(TK[:, :, j * 128:(j + 1) * 128], ptb[:, 0:512].rearrange("p (s f) -> p s f", f=128)), reads=[pb], writes=[Btk[par]])
                        S.dma("st", xtok_d[g0:g0 + 512, :].rearrange("(s p) f -> p s f", p=128), TK[:], reads=[Btk[par]])
                S.barrier()

        ctx = dict(locals())
        if "p1d" in phases:
            emit_p1d(ctx)
        with ExitStack() as gstack:
            ctx["gstack"] = gstack
            gens = []
            if "p2" in phases:
                gens.append(gen_p2(ctx))
            if "p3" in phases:
                gens.append(gen_p3(ctx))
            run_interleaved(gens)
            S.barrier()
        if "p4a" in phases:
            emit_p4a(ctx)
        if "p4b" in phases:
            emit_p4b(ctx)
        S.barrier()
    return nc


def run_interleaved(gens):
    if not gens:
        return
    if len(gens) == 1:
        for _ in gens[0]:
            pass
        return
    prog = [0.0] * len(gens)
    alive = [True] * len(gens)
    while any(alive):
        k = min((i for i in range(len(gens)) if alive[i]), key=lambda i: prog[i])
        try:
            prog[k] = next(gens[k])
        except StopIteration:
            alive[k] = False


def emit_p1d(c):
    from contextlib import ExitStack
    S, nc, sb, bank, seqs, tok0, NS = c["S"], c["nc"], c["sb"], c["bank"], c["seqs"], c["tok0"], c["NS"]
    qnT_d, cnT_d, KT_d, QT_d, V_d, cos_d, sin_d = c["qnT_d"], c["cnT_d"], c["KT_d"], c["QT_d"], c["V_d"], c["cos_d"], c["sin_d"]
    w_q_b, w_kv_b, g_q, g_kv = c["w_q_b"], c["w_kv_b"], c["g_q"], c["g_kv"]
    with ExitStack() as st:
        wq = sb(st, "pd_wq", [128, 3, 1536], BF16)
        wqB = sb(st, "pd_wqB", [128, 3, 16, 96], BF16)
        wkv = sb(st, "pd_wkv", [128, 2, 2048], BF16)
        wk = sb(st, "pd_wk", [128, 2, 1024], BF16)
        wv = sb(st, "pd_wv", [128, 2, 1024], BF16)
        gq = sb(st, "pd_gq", [128, 5], F32)
        qn = [sb(st, "pd_qn", [128, 3, 512], BF16) for _ in range(2)]
        cn = [sb(st, "pd_cn", [128, 2, 512], BF16) for _ in range(2)]
        cst = [sb(st, "pd_cs", [128, 2, 512], F32) for _ in range(2)]
        kst = [sb(st, "pd_kst", [128, 512], BF16) for _ in range(3)]
        vst = [sb(st, "pd_vst", [128, 512], BF16) for _ in range(3)]
        qst = [sb(st, "pd_qst", [128, 512], BF16) for _ in range(3)]
        rt = [sb(st, "pd_rt", [128, 2, 512], F32) for _ in range(2)]
        Bw2 = Buf()
        Bqn, Bcn, Bcs = [Buf(), Buf()], [Buf(), Buf()], [Buf(), Buf()]
        Bk, Bv, Bq, Brt = [Buf() for _ in range(3)], [Buf() for _ in range(3)], [Buf() for _ in range(3)], [Buf(), Buf()]
        S.dma("pool", wq[:], w_q_b.rearrange("(kc p) f -> p kc f", p=128), writes=[Bw2])
        S.dma("pool", wkv[:], w_kv_b.rearrange("(kc p) f -> p kc f", p=128), writes=[Bw2])
        with nc.allow_non_contiguous_dma(reason="tiny"):
            S.dma("sp", gq[:, 0:3], g_q.rearrange("(j p) -> p j", p=128), writes=[Bw2])
            S.dma("sp", gq[:, 3:5], g_kv.rearrange("(j p) -> p j", p=128), writes=[Bw2])
        for kc in range(3):
            S.op("dve", lambda v: v.tensor_scalar(wq[:, kc, :], wq[:, kc, :], gq[:, kc:kc + 1], None, ALU.mult), reads=[Bw2], writes=[Bw2])
        for kc in range(2):
            S.op("dve", lambda v: v.tensor_scalar(wkv[:, kc, :], wkv[:, kc, :], gq[:, 3 + kc:4 + kc], None, ALU.mult), reads=[Bw2], writes=[Bw2])
            w4 = wkv[:, kc, :].rearrange("p (h t f) -> p h t f", t=2, f=64)
            S.op("dve", lambda v: v.tensor_copy(wk[:, kc, :].rearrange("p (h f) -> p h f", f=64), w4[:, :, 0, :]), reads=[Bw2], writes=[Bw2])
            S.op("dve", lambda v: v.tensor_copy(wv[:, kc, :].rearrange("p (h f) -> p h f", f=64), w4[:, :, 1, :]), reads=[Bw2], writes=[Bw2])
        S.op("dve", lambda v: v.memset(wqB[:], 0.0), writes=[Bw2])
        wq4 = wq[:].rearrange("p k (h f) -> p k h f", f=96)
        for kc in range(3):
            S.op("dve", lambda v: v.tensor_scalar(wqB[:, kc, :, 64:80], wq4[:, kc, :, 80:96], -1.0, None, ALU.mult), reads=[Bw2], writes=[Bw2])
            S.op("dve", lambda v: v.tensor_copy(wqB[:, kc, :, 80:96], wq4[:, kc, :, 64:80]), reads=[Bw2], writes=[Bw2])
        it = 0
        ki = vi = qi = 0
        for si in range(NS):
            Sq, t0 = seqs[si], tok0[si]
            nch = Sq // 128
            for ti in range(Sq // 512):
                g0 = t0 + ti * 512
                p0 = ti * 512
                k = it % 2
                it += 1
                QN, CN, CS = qn[k], cn[k], cst[k]
                S.dma("sp", QN[:], qnT_d[:, g0:g0 + 512].rearrange("(j p) t -> p j t", p=128), writes=[Bqn[k]])
                S.dma("sp", CN[:], cnT_d[:, g0:g0 + 512].rearrange("(j p) t -> p j t", p=128), writes=[Bcn[k]])
                S.dma("sp", CS[64:96, 0, :], cos_d[:, p0:p0 + 512], writes=[Bcs[k]])
                S.dma("sp", CS[64:96, 1, :], sin_d[:, p0:p0 + 512], writes=[Bcs[k]])
                for pr in range(8):
                    pt, pb = bank()
                    S.mm(pt, [(wk[:, kc, pr * 128:(pr + 1) * 128], CN[:, kc, :]) for kc in range(2)], reads=[Bw2, Bcn[k]], writes=[pb])
                    K_, BK_ = kst[ki % 3], Bk[ki % 3]
                    ki += 1
                    S.op("act", lambda a: a.copy(K_[:], pt), reads=[pb], writes=[BK_])
                    S.dma("st", KT_d[pr * 128:(pr + 1) * 128, g0:g0 + 512], K_[:], reads=[BK_])
                for s4 in range(4):
                    cidx = ti * 4 + s4
                    for hf in range(2):
                        pt, pb = bank()
                        S.mm(pt, [(CN[:, kc, s4 * 128:(s4 + 1) * 128], wv[:, kc, hf * 512:(hf + 1) * 512]) for kc in range(2)], reads=[Bw2, Bcn[k]], writes=[pb])
                        V_, BV_ = vst[vi % 3], Bv[vi % 3]
                        vi += 1
                        S.op("dve", lambda v: v.tensor_copy(V_[:], pt), reads=[pb], writes=[BV_])
                        dst = V_d[hf * 8:(hf + 1) * 8, t0 * 64:(t0 + Sq) * 64].rearrange("h (p c f) -> p h c f", p=128, f=64)[:, :, cidx, :]
                        S.dma("st", dst, V_[:].rearrange("p (h f) -> p h f", f=64), reads=[BV_])
                for h in range(16):
                    pA, pbA = bank()
                    S.mm(pA[0:96, :], [(wq[:, kc, h * 96:(h + 1) * 96], QN[:, kc, :]) for kc in range(3)], reads=[Bw2, Bqn[k]], writes=[pbA])
                    pB, pbB = bank()
                    S.mm(pB[0:96, :], [(wqB[:, kc, h, :], QN[:, kc, :]) for kc in range(3)], reads=[Bw2, Bqn[k]], writes=[pbB])
                    Q_, BQ_ = qst[qi % 3], Bq[qi % 3]
                    RT, BRT = rt[qi % 2], Brt[qi % 2]
                    qi += 1
                    S.op("act", lambda a: a.copy(Q_[0:64, :], pA[0:64, :]), reads=[pbA], writes=[BQ_])
                    S.op("dve", lambda v: v.tensor_tensor(RT[64:96, 0, :], pA[64:96, :], CS[64:96, 0, :], ALU.mult), reads=[pbA, Bcs[k]], writes=[BRT])
                    S.op("dve", lambda v: v.tensor_tensor(RT[64:96, 1, :], pB[64:96, :], CS[64:96, 1, :], ALU.mult), reads=[pbB, Bcs[k]], writes=[BRT])
                    S.op("pool", lambda v: v.tensor_tensor(Q_[64:96, :], RT[64:96, 0, :], RT[64:96, 1, :], ALU.add), reads=[BRT], writes=[BQ_])
                    S.dma("st", QT_d[h * 96:(h + 1) * 96, g0:g0 + 512], Q_[0:96, :], reads=[BQ_])
        S.barrier()


def gen_p2(c):
    from contextlib import ExitStack
    S, nc, sb, seqs, tok0, NS = c["S"], c["nc"], c["sb"], c["seqs"], c["tok0"], c["NS"]
    banks, bbuf, SMAX = c["banks"], c["bbuf"], c["SMAX"]
    psall = c["psall"]
    KT_d, QT_d, V_d, kr_d, OT_d = c["KT_d"], c["QT_d"], c["V_d"], c["kr_d"], c["OT_d"]
    st = c["gstack"]
    if True:
        KT = [sb(st, "p2_KT", [128, SMAX], BF16) for _ in range(2)]
        QT = [sb(st, "p2_QT", [128, SMAX], BF16) for _ in range(2)]
        VA = [sb(st, "p2_VA", [128, SMAX // 128, 128], BF16) for _ in range(2)]
        PT = [sb(st, "p2_PT", [128, 1024], BF16) for _ in range(3)]
        rd = sb(st, "p2_rd", [128, 512], F32)
        og = [sb(st, "p2_og", [128, 512], BF16) for _ in range(2)]
        BK, BQ, BV = [Buf(), Buf()], [Buf(), Buf()], [Buf(), Buf()]
        BPT, Brd, Bog = [Buf() for _ in range(3)], Buf(), [Buf(), Buf()]
        for i in range(2):
            S.op("pool", lambda v: v.memset(VA[i][:, :, 64:128], 1.0), writes=[BV[i]])
        heads = [(si, h) for si in range(NS) for h in range(16)]
        total = float(sum(seqs[si] // 512 * (seqs[si] // 256) * 5 for si, h in heads)) + 1.0
        done = 0

        def load(idx):
            si, h = heads[idx]
            hb = idx % 2
            Sq, t0 = seqs[si], tok0[si]
            S.dma("sp", KT[hb][0:64, 0:Sq], KT_d[h * 64:(h + 1) * 64, t0:t0 + Sq], writes=[BK[hb]])
            S.dma("sp", KT[hb][64:96, 0:Sq], kr_d[:, t0:t0 + Sq], writes=[BK[hb]])
            S.dma("sp", QT[hb][0:96, 0:Sq], QT_d[h * 96:(h + 1) * 96, t0:t0 + Sq], writes=[BQ[hb]])
            S.dma("sp", VA[hb][:, 0:Sq // 128, 0:64], V_d[h, t0 * 64:(t0 + Sq) * 64].rearrange("(p c f) -> p c f", p=128, f=64), writes=[BV[hb]])

        load(0)
        pi = 0
        oi = 0
        for idx, (si, h) in enumerate(heads):
            if idx + 1 < len(heads):
                load(idx + 1)
            hb = idx % 2
            K_, Q_, V_ = KT[hb], QT[hb], VA[hb]
            Sq, t0 = seqs[si], tok0[si]
            npair = Sq // 256
            for qt in range(Sq // 512):
                ql = slice(qt * 512, (qt + 1) * 512)
                pO, pbO = banks[4], bbuf[4]
                pend = None
                for cp in range(npair + 1):
                    if cp < npair:
                        b0 = (cp % 2) * 2
                        for e in range(2):
                            cc = cp * 2 + e
                            S.op("pe", lambda pe: pe.matmul(banks[b0 + e], K_[0:96, cc * 128:(cc + 1) * 128], Q_[0:96, ql], start=True, stop=True), reads=[BK[hb], BQ[hb]], writes=[bbuf[b0 + e]])
                            done += 1
                            yield done / total
                        P_, BP = PT[pi % 3], BPT[pi % 3]
                        pi += 1
                        S.op("act", lambda a: a.activation(P_[:], psall[:, b0 * 512:(b0 + 2) * 512], AF.Exp, scale=ATT_SCALE), reads=[bbuf[b0], bbuf[b0 + 1]], writes=[BP])
                        done += 1
                        yield done / total
                    if pend is not None:
                        pcp, PP, BPP = pend
                        for e in range(2):
                            cc = pcp * 2 + e
                            S.op("pe", lambda pe: pe.matmul(pO, V_[:, cc, :], PP[:, e * 512:(e + 1) * 512], start=(cc == 0), stop=(cc == 2 * npair - 1)), reads=[BV[hb], BPP], writes=[pbO])
                            done += 1
                            yield done / total
                    if cp < npair:
                        pend = (cp, P_, BP)
                O_, BO_ = og[oi % 2], Bog[oi % 2]
                oi += 1
                S.op("dve", lambda v: v.reciprocal(rd[64:128, :], pO[64:128, :]), reads=[pbO], writes=[Brd])
                S.op("dve", lambda v: v.tensor_tensor(O_[0:64, :], pO[0:64, :], rd[64:128, :], ALU.mult), reads=[pbO, Brd], writes=[BO_])
                S.dma("sp", OT_d[h * 64:(h + 1) * 64, t0 + qt * 512:t0 + (qt + 1) * 512], O_[0:64, :], reads=[BO_])
        yield 1.0


def gen_p3(c):
    from contextlib import ExitStack
    S, nc, sb, seqs, tok0, NS = c["S"], c["nc"], c["sb"], c["seqs"], c["tok0"], c["NS"]
    bank0 = c["bank"]
    bank = lambda: bank0("b")
    identb, Ufb, Ubb, Lfb, Lbb, onesb, B_c, epst = c["identb"], c["Ufb"], c["Ubb"], c["Lfb"], c["Lbb"], c["onesb"], c["B_c"], c["epst"]
    xtok_d, bcT_d, dt_d, zs_d, prevb_d, ygT_d, alog, dskip = c["xtok_d"], c["bcT_d"], c["dt_d"], c["zs_d"], c["prevb_d"], c["ygT_d"], c["alog"], c["dskip"]
    Q = "pool"
    st = c["gstack"]
    if True:
        a_bc = sb(st, "p3_a", [128, 64], F32)
        D_bc = sb(st, "p3_D", [128, 32], F32)
        Hs = sb(st, "p3_H", [128, 2048], F32)
        Hb16 = sb(st, "p3_Hb", [128, 2048], BF16)
        xts = [sb(st, "p3_xt", [128, 3072], BF16) for _ in range(2)]
        bcs = [sb(st, "p3_bc", [128, 16, 128], BF16) for _ in range(2)]
        dts_ = [sb(st, "p3_dt", [128, 64], F32) for _ in range(2)]
        zss = [sb(st, "p3_zs", [128, 2048], BF16) for _ in range(2)]
        pvs = [sb(st, "p3_pv", [128, 2048], BF16) for _ in range(2)]
        dA = sb(st, "p3_dA", [128, 64], F32)
        dAh = sb(st, "p3_dAh", [128, 64], BF16)
        dAhf = sb(st, "p3_dAhf", [128, 64], F32)
        dAl = sb(st, "p3_dAl", [128, 64], BF16)
        ct = sb(st, "p3_ct", [128, 128], F32)
        Et = sb(st, "p3_E", [128, 64], F32)
        wgt = sb(st, "p3_w", [128, 64], F32)
        ec = sb(st, "p3_ec", [128, 64], F32)
        dec = sb(st, "p3_dec", [128, 64], F32)
        xw = sb(st, "p3_xw", [128, 2048], BF16)
        xdt = [sb(st, "p3_xdt", [128, 2048], BF16) for _ in range(2)]
        Rs = [sb(st, "p3_R", [128, 2, 8, 128], BF16) for _ in range(2)]
        CBm = sb(st, "p3_CBm", [128, 2, 8, 128], BF16)
        Exs = [sb(st, "p3_Ex", [128, 512], BF16) for _ in range(2)]
        MTs = [sb(st, "p3_MT", [128, 2, 8, 128], BF16) for _ in range(2)]
        y = sb(st, "p3_y", [128, 2048], F32)
        t1s = [sb(st, "p3_t1", [128, 512], F32) for _ in range(2)]
        t2s = [sb(st, "p3_t2", [128, 512], F32) for _ in range(2)]
        sqt = sb(st, "p3_sq", [128, 2048], F32)
        gs = sb(st, "p3_gs", [128, 8], F32)
        ygn = sb(st, "p3_ygn", [128, 2048], BF16)
        ygs = [sb(st, "p3_ygs", [128, 16, 128], BF16) for _ in range(2)]
        Bk, BH, BHb, Bprevd = Buf(), Buf(), Buf(), Buf()
        Bxt, Bbc, Bdt, Bzs, Bpv = [Buf(), Buf()], [Buf(), Buf()], [Buf(), Buf()], [Buf(), Buf()], [Buf(), Buf()]
        BdA, Bct, BE, Bxw, Bxdt, BRs, BCB, BEx, BMTs = Buf(), Buf(), Buf(), Buf(), [Buf(), Buf()], [Buf(), Buf()], Buf(), [Buf(), Buf()], [Buf(), Buf()]
        By, Bt1, Bt2, Bsq, Bgs, Bygn, Bygs = Buf(), [Buf(), Buf()], [Buf(), Buf()], Buf(), Buf(), Buf(), [Buf(), Buf()]
        S.dma(Q, a_bc[:], alog.partition_broadcast(128), writes=[Bk])
        S.dma(Q, D_bc[:], dskip.partition_broadcast(128), writes=[Bk])
        S.op("act", lambda a: a.activation(a_bc[:], a_bc[:], AF.Exp), reads=[Bk], writes=[Bk])
        S.op("dve", lambda v: v.tensor_scalar(a_bc[:], a_bc[:], -1.0, None, ALU.mult), reads=[Bk], writes=[Bk])
        total = float(sum((sq // 128) * 292.4 for sq in seqs))
        done = [0.0]

        def tick():
            done[0] += 1.0
            return min(done[0] / total, 0.999)

        def bc3(ap2, n):
            return ap2.unsqueeze(2).broadcast_to([128, ap2.shape[1], n])

        v3 = lambda ap: ap.rearrange("p (h f) -> p h f", f=64)

        def prep(dtt, Bd):
            S.op("dve", lambda v: v.tensor_tensor(dA[:], dtt[:], a_bc[:], ALU.mult), reads=[Bd, Bk], writes=[BdA])
            yield tick()
            S.op("dve", lambda v: v.tensor_copy(dAh[:], dA[:]), reads=[BdA], writes=[BdA])
            yield tick()
            S.op("dve", lambda v: v.tensor_copy(dAhf[:], dAh[:]), reads=[BdA], writes=[BdA])
            yield tick()
            S.op("dve", lambda v: v.tensor_tensor(dAhf[:], dA[:], dAhf[:], ALU.subtract), reads=[BdA], writes=[BdA])
            yield tick()
            S.op("dve", lambda v: v.tensor_copy(dAl[:], dAhf[:]), reads=[BdA], writes=[BdA])
            yield tick()
            pc, pbc = bank()
            for (o0, o1, L) in ((0, 32, Ufb), (32, 64, Ubb)):
                S.op("pe", lambda pe: pe.matmul(pc[:, o0:o1], L, dAh[:, o0:o1], start=True, stop=False), reads=[BdA, B_c], writes=[pbc])
                yield tick()
                S.op("pe", lambda pe: pe.matmul(pc[:, o0:o1], L, dAl[:, o0:o1], start=False, stop=True), reads=[BdA, B_c], writes=[pbc])
                yield tick()
            S.op("pe", lambda pe: pe.matmul(pc[:, 64:128], onesb, dAh[:], start=True, stop=False), reads=[BdA, B_c], writes=[pbc])
            yield tick()
            S.op("pe", lambda pe: pe.matmul(pc[:, 64:128], onesb, dAl[:], start=False, stop=True), reads=[BdA, B_c], writes=[pbc])
            yield tick()
            S.op("dve", lambda v: v.tensor_copy(ct[:], pc[:, 0:128]), reads=[pbc], writes=[Bct])
            yield tick()
            S.op("dve", lambda v: v.tensor_tensor(Et[:], ct[:, 64:128], ct[:, 0:64], ALU.subtract), reads=[Bct], writes=[BE])
            yield tick()
            S.op("act", lambda a: a.activation(Et[:], Et[:], AF.Exp), reads=[BE], writes=[BE])
            yield tick()
            S.op("dve", lambda v: v.tensor_tensor(wgt[:], dtt[:], Et[:], ALU.mult), reads=[BE, Bd], writes=[BE])
            yield tick()
            S.op("act", lambda a: a.activation(dec[:], ct[:, 64:128], AF.Exp), reads=[Bct], writes=[BE])
            yield tick()
            S.op("act", lambda a: a.activation(ec[:], ct[:, 0:64], AF.Exp), reads=[Bct], writes=[BE])
            yield tick()

        def state_update(X, BX, d):
            S.op("pool", lambda v: v.tensor_tensor(v3(xw[:]), v3(X[:, 0:2048]), bc3(wgt[:, d * 32:(d + 1) * 32], 64), ALU.mult), reads=[BX, BE], writes=[Bxw])
            yield tick()
            S.op("dve", lambda v: v.tensor_tensor(v3(Hs[:]), v3(Hs[:]), bc3(dec[:, d * 32:(d + 1) * 32], 64), ALU.mult), reads=[BE, BHb], writes=[BH])
            yield tick()
            for gp in range(4):
                ps, pbs = bank()
                for gg in range(2):
                    g = gp * 2 + gg
                    S.op("pe", lambda pe: pe.matmul(ps[:, gg * 256:(gg + 1) * 256], X[:, 2048 + g * 128:2048 + (g + 1) * 128], xw[:, g * 256:(g + 1) * 256], start=True, stop=True), reads=[BX, Bxw], writes=[pbs])
                    yield tick()
                S.op("dve", lambda v: v.tensor_tensor(Hs[:, gp * 512:(gp + 1) * 512], Hs[:, gp * 512:(gp + 1) * 512], ps, ALU.add), reads=[pbs], writes=[BH])
                yield tick()

        it = 0
        for si in range(NS):
            Sq, t0 = seqs[si], tok0[si]
            nch = Sq // 128
            S.op("pool", lambda v: v.memset(Hs[:], 0.0), reads=[BHb], writes=[BH])
            yield tick()
            for cidx in range(nch - 1, -1, -1):
                g0 = t0 + cidx * 128
                k = it % 2
                it += 1
                S.dma(Q, xts[k][:], xtok_d[g0:g0 + 128, :], writes=[Bxt[k]])
                yield tick()
                S.dma(Q, dts_[k][:], dt_d[g0:g0 + 128, :], writes=[Bdt[k]])
                yield tick()
                S.op("act", lambda a: a.copy(Hb16[:], Hs[:]), reads=[BH], writes=[BHb])
                yield tick()
                S.dma(Q, prevb_d[cidx], Hb16[:], reads=[BHb], writes=[Bprevd])
                yield tick()
                if cidx > 0:
                    yield from prep(dts_[k], Bdt[k])
                    yield from state_update(xts[k], Bxt[k], 1)
            S.op("pool", lambda v: v.memset(Hs[:], 0.0), reads=[BHb], writes=[BH])
            yield tick()
            for cidx in range(nch):
                g0 = t0 + cidx * 128
                k = it % 2
                it += 1
                X, BX, BCt, BBC, dtt, Bd, Z, BZ, PV, BPV = xts[k], Bxt[k], bcs[k], Bbc[k], dts_[k], Bdt[k], zss[k], Bzs[k], pvs[k], Bpv[k]
                S.dma(Q, X[:], xtok_d[g0:g0 + 128, :], writes=[BX])
                yield tick()
                S.dma(Q, BCt[:], bcT_d[:, g0:g0 + 128].rearrange("(j p) t -> p j t", p=128), writes=[BBC])
                yield tick()
                S.dma(Q, dtt[:], dt_d[g0:g0 + 128, :], writes=[Bd])
                yield tick()
                S.dma(Q, Z[:], zs_d[g0:g0 + 128, :], writes=[BZ])
                yield tick()
                S.dma(Q, PV[:], prevb_d[cidx], reads=[Bprevd], writes=[BPV])
                yield tick()
                S.op("act", lambda a: a.copy(Hb16[:], Hs[:]), reads=[BH], writes=[BHb])
                yield tick()
                yield from prep(dtt, Bd)
                if cidx < nch - 1:
                    yield from state_update(X, BX, 0)
                for g4 in range(2):
                    pcb, pbcb = bank()
                    for gg in range(4):
                        g = g4 * 4 + gg
                        S.op("pe", lambda pe: pe.matmul(pcb[:, gg * 128:(gg + 1) * 128], BCt[:, g, :], BCt[:, 8 + g, :], start=True, stop=True), reads=[BBC], writes=[pbcb])
                        yield tick()
                    for d, Um in ((0, Ufb), (1, Ubb)):
                        S.op("dve", lambda v: v.tensor_tensor(CBm[:, d, g4 * 4:(g4 + 1) * 4, :], pcb.rearrange("p (g l) -> p g l", l=128), Um.unsqueeze(1).broadcast_to([128, 4, 128]), ALU.mult), reads=[pbcb, B_c], writes=[BCB])
                        yield tick()
                for d in range(2):
                    S.op("pool", lambda v: v.tensor_tensor(v3(xdt[d][:]), v3(X[:, 0:2048]), bc3(dtt[:, d * 32:(d + 1) * 32], 64), ALU.mult), reads=[BX, Bd], writes=[Bxdt[d]])
                    yield tick()
                ei = 0
                for gp in range(4):
                    MT, BMT = MTs[gp % 2], BMTs[gp % 2]
                    for d, Lm, Um in ((0, Lfb, Ufb), (1, Lbb, Ubb)):
                        R, BR = Rs[d], BRs[d]
                        h0 = d * 32 + gp * 8
                        for kk, src in ((0, dAh), (1, dAl)):
                            S.op("dve" if kk else "pool", lambda v: v.tensor_tensor(R[:, kk, :, :], Um.unsqueeze(1).broadcast_to([128, 8, 128]), bc3(src[:, h0:h0 + 8], 128), ALU.mult), reads=[BdA, B_c], writes=[BR])
                            yield tick()
                        for gg in range(2):
                            g = gp * 2 + gg
                            pseg, pbseg = bank()
                            S.op("pe", lambda pe: pe.matmul(pseg, Lm, R[:, 0, gg * 4:(gg + 1) * 4, :].rearrange("p h l -> p (h l)"), start=True, stop=False), reads=[BR, B_c], writes=[pbseg])
                            yield tick()
                            S.op("pe", lambda pe: pe.matmul(pseg, Lm, R[:, 1, gg * 4:(gg + 1) * 4, :].rearrange("p h l -> p (h l)"), start=False, stop=True), reads=[BR, B_c], writes=[pbseg])
                            yield tick()
                            Ex, BE_ = Exs[ei % 2], BEx[ei % 2]
                            ei += 1
                            S.op("act", lambda a: a.activation(Ex[:], pseg, AF.Exp), reads=[pbseg], writes=[BE_])
                            yield tick()
                            S.op("dve" if gg else "pool", lambda v: v.tensor_tensor(MT[:, d, gg * 4:(gg + 1) * 4, :], Ex[:].rearrange("p (h l) -> p h l", l=128), CBm[:, d, g:g + 1, :].broadcast_to([128, 4, 128]), ALU.mult), reads=[BE_, BCB], writes=[BMT])
                            yield tick()
                    py, pby = bank()
                    pof, pbof = bank()
                    pob, pbob = bank()
                    for gg in range(2):
                        g = gp * 2 + gg
                        for j in range(4):
                            h = 4 * g + j
                            for d in range(2):
                                S.op("pe", lambda pe: pe.matmul(py[:, gg * 256 + j * 64:gg * 256 + (j + 1) * 64], MT[:, d, gg * 4 + j, :], xdt[d][:, h * 64:(h + 1) * 64], start=(d == 0), stop=(d == 1)), reads=[BMT, Bxdt[d]], writes=[pby])
                                yield tick()
                        S.op("pe", lambda pe: pe.matmul(pof[:, gg * 256:(gg + 1) * 256], BCt[:, 8 + g, :], Hb16[:, g * 256:(g + 1) * 256], start=True, stop=True), reads=[BBC, BHb], writes=[pbof])
                        yield tick()
                        S.op("pe", lambda pe: pe.matmul(pob[:, gg * 256:(gg + 1) * 256], BCt[:, 8 + g, :], PV[:, g * 256:(g + 1) * 256], start=True, stop=True), reads=[BBC, BPV], writes=[pbob])
                        yield tick()
                    T1, BT1, T2, BT2 = t1s[gp % 2], Bt1[gp % 2], t2s[gp % 2], Bt2[gp % 2]
                    S.op("dve", lambda v: v.tensor_tensor(v3(T1[:]), v3(pof), bc3(ec[:, gp * 8:gp * 8 + 8], 64), ALU.mult), reads=[pbof, BE], writes=[BT1])
                    yield tick()
                    S.op("dve", lambda v: v.tensor_tensor(v3(T2[:]), v3(pob), bc3(ec[:, 32 + gp * 8:32 + gp * 8 + 8], 64), ALU.mult), reads=[pbob, BE], writes=[BT2])
                    yield tick()
                    S.op("pool", lambda v: v.tensor_tensor(T1[:], T1[:], T2[:], ALU.add), reads=[BT2], writes=[BT1])
                    yield tick()
                    S.op("pool", lambda v: v.tensor_tensor(v3(T2[:]), v3(X[:, gp * 512:(gp + 1) * 512]), bc3(D_bc[:, gp * 8:gp * 8 + 8], 64), ALU.mult), reads=[BX, Bk], writes=[BT2])
                    yield tick()
                    S.op("pool", lambda v: v.tensor_tensor(T1[:], T1[:], T2[:], ALU.add), reads=[BT2], writes=[BT1])
                    yield tick()
                    S.op("dve", lambda v: v.tensor_tensor(y[:, gp * 512:(gp + 1) * 512], py, T1[:], ALU.add), reads=[pby, BT1], writes=[By])
                    yield tick()
                S.op("dve", lambda v: v.tensor_tensor(y[:], y[:], Z[:], ALU.mult), reads=[BZ], writes=[By])
                yield tick()
                S.op("pool", lambda v: v.tensor_tensor(sqt[:], y[:], y[:], ALU.mult), reads=[By], writes=[Bsq])
                yield tick()
                S.op("dve", lambda v: v.tensor_reduce(gs[:], sqt[:].rearrange("p (g f) -> p g f", f=256), AX.X, ALU.add), reads=[Bsq], writes=[Bgs])
                yield tick()
                S.op("act", lambda a: a.activation(gs[:], gs[:], AF.Sqrt, bias=epst[:], scale=1.0 / 256), reads=[Bgs, B_c], writes=[Bgs])
                yield tick()
                S.op("dve", lambda v: v.reciprocal(gs[:], gs[:]), reads=[Bgs], writes=[Bgs])
                yield tick()
                S.op("dve", lambda v: v.tensor_tensor(ygn[:].rearrange("p (g f) -> p g f", f=256), y[:].rearrange("p (g f) -> p g f", f=256), bc3(gs[:], 256), ALU.mult), reads=[By, Bgs], writes=[Bygn])
                yield tick()
                YS, BYS = ygs[k], Bygs[k]
                for hf in range(2):
                    pt, pbt = bank()
                    ptb = pt.bitcast(BF16)
                    for jj in range(8):
                        j = hf * 8 + jj
                        S.op("pe", lambda pe: pe.transpose(ptb[:, jj * 128:(jj + 1) * 128], ygn[:, j * 128:(j + 1) * 128], identb), reads=[Bygn, B_c], writes=[pbt])
                        yield tick()
                    S.op("act", lambda a: a.copy(YS[:, hf * 8:(hf + 1) * 8, :], ptb[:, 0:1024].rearrange("p (j t) -> p j t", t=128)), reads=[pbt], writes=[BYS])
                    yield tick()
                S.dma(Q, ygT_d[:, g0:g0 + 128].rearrange("(j p) t -> p j t", p=128), YS[:], reads=[BYS])
                yield tick()
        yield 1.0


def emit_p4a(c):
    from contextlib import ExitStack
    S, nc, sb, bank, seqs, tok0, NS = c["S"], c["nc"], c["sb"], c["bank"], c["seqs"], c["tok0"], c["NS"]
    identb, B_c, epst = c["identb"], c["B_c"], c["epst"]
    ygT_d, OT_d, gT_d, x_d, x1_d, h2T_d, gate_d, adaT_d = c["ygT_d"], c["OT_d"], c["gT_d"], c["x_d"], c["x1_d"], c["h2T_d"], c["gate_d"], c["adaT_d"]
    w_ssd_out, w_mla_out, w_o, g_ssd = c["w_ssd_out"], c["w_mla_out"], c["w_o"], c["g_ssd"]
    with ExitStack() as st:
        ws = sb(st, "p4_ws", [128, 16, 1024], BF16)
        wm = sb(st, "p4_wm", [128, 8, 1024], BF16)
        wo = sb(st, "p4_wo", [128, 8, 1024], BF16)
        gsn = sb(st, "p4_gsn", [128, 16], F32)
        ygl = [sb(st, "p4_yg", [128, 16, 512], BF16) for _ in range(2)]
        otl = [sb(st, "p4_ot", [128, 8, 512], BF16) for _ in range(2)]
        Bygl, Botl = [Buf(), Buf()], [Buf(), Buf()]
        tcount = [0]
        gt = sb(st, "p4_gt", [128, 16, 512], BF16)
        xt = sb(st, "p4_x", [128, 4, 1024], F32)
        mixf = sb(st, "p4_mixf", [128, 512], F32)
        mixf2 = sb(st, "p4_mixf2", [128, 512], F32)
        mix = sb(st, "p4_mix", [128, 8, 512], BF16)
        g1 = sb(st, "p4_g1", [128, 1024], F32)
        ab = sb(st, "p4_ab", [128, 2, 8], F32)
        x1 = sb(st, "p4_x1", [128, 4, 1024], F32)
        junk = sb(st, "p4_junk", [128, 1024], BF16)
        ss = sb(st, "p4_ss", [128, 4], F32)
        xn = sb(st, "p4_xn", [128, 4, 1024], BF16)
        h2 = sb(st, "p4_h2", [128, 8, 512], BF16)
        Bw, Byg, Bot, Bgt, Bx, Bmf, Bmf2, Bmix, Bg1, Bab, Bx1, Bss, Bxn, Bh2 = (Buf() for _ in range(14))
        S.dma("pool", ws[:], w_ssd_out.rearrange("(kc p) f -> p kc f", p=128), writes=[Bw])
        S.dma("pool", wm[:], w_mla_out.rearrange("(kc p) f -> p kc f", p=128), writes=[Bw])
        S.dma("pool", wo[:], w_o.rearrange("(kc p) f -> p kc f", p=128), writes=[Bw])
        with nc.allow_non_contiguous_dma(reason="tiny"):
            S.dma("sp", gsn[:], g_ssd.rearrange("(j p) -> p j", p=128), writes=[Bw])
        for kc in range(16):
            S.op("dve", lambda v: v.tensor_scalar(ws[:, kc, :], ws[:, kc, :], gsn[:, kc:kc + 1], None, ALU.mult), reads=[Bw], writes=[Bw])
        for si in range(NS):
            S.dma("sp", g1[:], gate_d[si, 0, :].partition_broadcast(128), writes=[Bg1])
            with nc.allow_non_contiguous_dma(reason="tiny"):
                S.dma("sp", ab[:, 0, :], adaT_d[si, 2, :].rearrange("(kc p) -> p kc", p=128), writes=[Bab])
                S.dma("sp", ab[:, 1, :], adaT_d[si, 3, :].rearrange("(kc p) -> p kc", p=128), writes=[Bab])
            for ti in range(seqs[si] // 512):
                g0 = tok0[si] + ti * 512
                yg, ot, Byg, Bot = ygl[tcount[0] % 2], otl[tcount[0] % 2], Bygl[tcount[0] % 2], Botl[tcount[0] % 2]
                tcount[0] += 1
                S.dma("sp", yg[:], ygT_d[:, g0:g0 + 512].rearrange("(j p) t -> p j t", p=128), writes=[Byg])
                S.dma("sp", ot[:], OT_d[:, g0:g0 + 512].rearrange("(j p) t -> p j t", p=128), writes=[Bot])
                S.dma("sp", gt[:], gT_d[:, g0:g0 + 512].rearrange("(j p) t -> p j t", p=128), writes=[Bgt])
                S.dma("sp", xt[:], x_d[g0:g0 + 512, :].rearrange("(s p) f -> p s f", p=128), writes=[Bx])
                for oc in range(8):
                    pa, pba = bank()
                    S.mm(pa, [(ws[:, kc, oc * 128:(oc + 1) * 128], yg[:, kc, :]) for kc in range(16)], reads=[Bw, Byg], writes=[pba])
                    pm, pbm = bank()
                    S.mm(pm, [(wm[:, kc, oc * 128:(oc + 1) * 128], ot[:, kc, :]) for kc in range(8)], reads=[Bw, Bot], writes=[pbm])
                    S.op("dve", lambda v: v.tensor_tensor(mixf[:], pa[:], gt[:, oc, :], ALU.mult), reads=[pba, Bgt], writes=[Bmf])
                    S.op("dve", lambda v: v.tensor_tensor(mixf2[:], pm[:], gt[:, 8 + oc, :], ALU.mult), reads=[pbm, Bgt], writes=[Bmf2])
                    S.op("pool", lambda v: v.tensor_tensor(mix[:, oc, :], mixf[:], mixf2[:], ALU.add), reads=[Bmf, Bmf2], writes=[Bmix])
                for s4 in range(4):
                    for hf in range(2):
                        po, pbo = bank()
                        S.mm(po, [(mix[:, kc, s4 * 128:(s4 + 1) * 128], wo[:, kc, hf * 512:(hf + 1) * 512]) for kc in range(8)], reads=[Bw, Bmix], writes=[pbo])
                        S.op("dve", lambda v: v.tensor_tensor(x1[:, s4, hf * 512:(hf + 1) * 512], po[:], g1[:, hf * 512:(hf + 1) * 512], ALU.mult), reads=[pbo, Bg1], writes=[Bx1])
                    S.op("pool", lambda v: v.tensor_tensor(x1[:, s4, :], x1[:, s4, :], xt[:, s4, :], ALU.add), reads=[Bx], writes=[Bx1])
                    S.op("act", lambda a: a.activation(junk[:], x1[:, s4, :], AF.Square, accum_out=ss[:, s4:s4 + 1]), reads=[Bx1], writes=[Bss])
                S.dma("st", x1_d[g0:g0 + 512, :].rearrange("(s p) f -> p s f", p=128), x1[:], reads=[Bx1])
                S.op("act", lambda a: a.activation(ss[:], ss[:], AF.Sqrt, bias=epst[:], scale=1.0 / 1024), reads=[Bss, B_c], writes=[Bss])
                S.op("dve", lambda v: v.reciprocal(ss[:], ss[:]), reads=[Bss], writes=[Bss])
                for s4 in range(4):
                    S.op("pool" if s4 % 2 else "dve", lambda v: v.tensor_scalar(xn[:, s4, :], x1[:, s4, :], ss[:, s4:s4 + 1], None, ALU.mult), reads=[Bx1, Bss], writes=[Bxn])
                for kc in range(8):
                    pt, pbt = bank()
                    ptb = pt[:].bitcast(BF16)
                    for s4 in range(4):
                        S.op("pe", lambda pe: pe.transpose(ptb[:, s4 * 128:(s4 + 1) * 128], xn[:, s4, kc * 128:(kc + 1) * 128], identb), reads=[Bxn, B_c], writes=[pbt])
                    S.op("dve", lambda v: v.tensor_scalar(h2[:, kc, :], ptb[:, 0:512], ab[:, 0, kc:kc + 1], ab[:, 1, kc:kc + 1], ALU.mult, ALU.add), reads=[pbt, Bab], writes=[Bh2])
                S.dma("st", h2T_d[:, g0:g0 + 512].rearrange("(j p) t -> p j t", p=128), h2[:], reads=[Bh2])
        S.barrier()


def emit_p4b(c):
    from contextlib import ExitStack
    S, nc, sb, bank, seqs, tok0, NS = c["S"], c["nc"], c["sb"], c["bank"], c["seqs"], c["tok0"], c["NS"]
    B_c, epst = c["B_c"], c["epst"]
    x1_d, h2T_d, gate_d, y_d, w_mlp_in, w_mlp_out, g_final = c["x1_d"], c["h2T_d"], c["gate_d"], c["y_d"], c["w_mlp_in"], c["w_mlp_out"], c["g_final"]
    TT = 256
    with ExitStack() as st:
        w1 = sb(st, "p5_w1", [128, 8, 4096], BF16)
        w2 = sb(st, "p5_w2", [128, 32, 1024], BF16)
        gf = sb(st, "p5_gf", [128, 1024], F32)
        g2 = sb(st, "p5_g2", [128, 1024], F32)
        h2 = [sb(st, "p5_h2", [128, 8, TT], BF16) for _ in range(2)]
        x1 = [sb(st, "p5_x1", [128, 2, 1024], F32) for _ in range(2)]
        rl = [sb(st, "p5_rl", [128, TT], F32) for _ in range(2)]
        rT = sb(st, "p5_rT", [128, 32, TT], BF16)
        x2 = sb(st, "p5_x2", [128, 2, 1024], F32)
        junk = sb(st, "p5_junk", [128, 1024], BF16)
        ss = sb(st, "p5_ss", [128, 2], F32)
        yo = sb(st, "p5_yo", [128, 2, 1024], F32)
        Bw, Bg2, Bh2, Bx1, Brl, BrT, Bx2, Bss, Byo = Buf(), Buf(), [Buf(), Buf()], [Buf(), Buf()], [Buf(), Buf()], Buf(), Buf(), Buf(), Buf()
        S.dma("pool", w1[:], w_mlp_in.rearrange("(kc p) f -> p kc f", p=128), writes=[Bw])
        for q4 in range(4):
            S.dma("pool", w2[:, q4 * 8:(q4 + 1) * 8, :], w_mlp_out[q4 * 1024:(q4 + 1) * 1024, :].rearrange("(kc p) f -> p kc f", p=128), writes=[Bw])
        S.dma("sp", gf[:], g_final.partition_broadcast(128), writes=[Bw])
        it = 0
        for si in range(NS):
            S.dma("sp", g2[:], gate_d[si, 1, :].partition_broadcast(128), writes=[Bg2])
            for ti in range(seqs[si] // TT):
                g0 = tok0[si] + ti * TT
                k = it % 2
                it += 1
                H, X1 = h2[k], x1[k]
                S.dma("sp", H[:], h2T_d[:, g0:g0 + TT].rearrange("(j p) t -> p j t", p=128), writes=[Bh2[k]])
                S.dma("sp", X1[:], x1_d[g0:g0 + TT, :].rearrange("(s p) f -> p s f", p=128), writes=[Bx1[k]])
                for fc in range(32):
                    pf, pbf = bank()
                    S.mm(pf[:, 0:TT], [(w1[:, kc, fc * 128:(fc + 1) * 128], H[:, kc, :]) for kc in range(8)], reads=[Bw, Bh2[k]], writes=[pbf])
                    RL, BRL = rl[fc % 2], Brl[fc % 2]
                    S.op("act", lambda a: a.activation(RL[:], pf[:, 0:TT], AF.Relu), reads=[pbf], writes=[BRL])
                    S.op("pool" if fc % 2 else "dve", lambda v: v.tensor_tensor(rT[:, fc, :], RL[:], RL[:], ALU.mult), reads=[BRL], writes=[BrT])
                for s2 in range(2):
                    for hf in range(2):
                        po, pbo = bank()
                        S.mm(po, [(rT[:, kc, s2 * 128:(s2 + 1) * 128], w2[:, kc, hf * 512:(hf + 1) * 512]) for kc in range(32)], reads=[Bw, BrT], writes=[pbo])
                        S.op("dve", lambda v: v.tensor_tensor(x2[:, s2, hf * 512:(hf + 1) * 512], po[:], g2[:, hf * 512:(hf + 1) * 512], ALU.mult), reads=[pbo, Bg2], writes=[Bx2])
                    S.op("pool", lambda v: v.tensor_tensor(x2[:, s2, :], x2[:, s2, :], X1[:, s2, :], ALU.add), reads=[Bx1[k]], writes=[Bx2])
                    S.op("act", lambda a: a.activation(junk[:], x2[:, s2, :], AF.Square, accum_out=ss[:, s2:s2 + 1]), reads=[Bx2], writes=[Bss])
                S.op("act", lambda a: a.activation(ss[:], ss[:], AF.Sqrt, bias=epst[:], scale=1.0 / 1024), reads=[Bss, B_c], writes=[Bss])
                S.op("dve", lambda v: v.reciprocal(ss[:], ss[:]), reads=[Bss], writes=[Bss])
                for s2 in range(2):
                    S.op("dve", lambda v: v.scalar_tensor_tensor(yo[:, s2, :], x2[:, s2, :], ss[:, s2:s2 + 1], gf[:], ALU.mult, ALU.mult), reads=[Bx2, Bss, Bw], writes=[Byo])
                S.dma("st", y_d[g0:g0 + TT, :].rearrange("(s p) f -> p s f", p=128), yo[:], reads=[Byo])
        S.barrier()


def core_inputs(x_all, c_all, W, g_final, cos, sin):
    f = lambda a: np.ascontiguousarray(np.asarray(a, dtype=np.float32))
    return {
        "x": f(x_all), "c": f(c_all),
        "w_ada": f(W["w_ada"]), "b_ada": f(W["b_ada"]), "g_norm1": f(W["g_norm1"]), "w_in": f(W["w_in"]),
        "conv_w": f(W["conv_w"]), "conv_b": f(W["conv_b"]),
        "dt_bias": f(np.concatenate([W["dt_bias_fwd"], W["dt_bias_bwd"]])),
        "a_log": f(np.concatenate([W["a_log_fwd"], W["a_log_bwd"]])),
        "d_skip": f(W["d_skip"]), "g_ssd_norm": f(W["g_ssd_norm"]), "w_ssd_out": f(W["w_ssd_out"]),
        "g_q_norm": f(W["g_q_norm"]), "w_q_b": f(W["w_q_b"]), "g_kv_norm": f(W["g_kv_norm"]), "w_kv_b": f(W["w_kv_b"]),
        "w_mla_out": f(W["w_mla_out"]), "w_o": f(W["w_o"]), "g_norm2": f(W["g_norm2"]),
        "w_mlp_in": f(W["w_mlp_in"]), "w_mlp_out": f(W["w_mlp_out"]), "g_final": f(g_final),
        "consts": make_consts(), "cos_t": f(cos), "sin_t": f(sin),
    }


_WNAMES = ["w_ada", "b_ada", "g_norm1", "w_in", "conv_w", "conv_b", "dt_bias_fwd", "dt_bias_bwd", "a_log_fwd",
           "a_log_bwd", "d_skip", "g_ssd_norm", "w_ssd_out", "g_q_norm", "w_q_b", "g_kv_norm", "w_kv_b",
           "w_mla_out", "w_o", "g_norm2", "w_mlp_in", "w_mlp_out"]


def kernel(x_prompt, x_sample, c_prompt, c_sample, g_final, **kw):
    W = {k: np.asarray(kw[k])[0] for k in _WNAMES}
    x_prompt = np.asarray(x_prompt); x_sample = np.asarray(x_sample)
    c_prompt = np.asarray(c_prompt); c_sample = np.asarray(c_sample)
    n = 8
    seqs = (2048, 2048, 4096, 4096)
    cos, sin = rope_tables(4096)
    import os
    ph = os.environ.get("KPHASES")
    nc = build(seqs=seqs, phases=tuple(ph.split(","))) if ph else build(seqs=seqs)
    in_maps = []
    for i in range(n):
        xa = np.concatenate([x_prompt[2 * i].reshape(-1, D), x_prompt[2 * i + 1].reshape(-1, D),
                             x_sample[2 * i].reshape(-1, D), x_sample[2 * i + 1].reshape(-1, D)], axis=0)
        ca = np.stack([c_prompt[2 * i], c_prompt[2 * i + 1], c_sample[2 * i], c_sample[2 * i + 1]], axis=0)
        in_maps.append(core_inputs(xa, ca, W, g_final, cos, sin))
    res = run_bass_kernel_spmd(nc, in_maps, core_ids=list(range(n)))
    yp = np.zeros((16, 2048, D), np.float32)
    ys = np.zeros((16, 4096, D), np.float32)
    for i in range(n):
        y = np.asarray(res.results[i]["y"])
        yp[2 * i] = y[0:2048]
        yp[2 * i + 1] = y[2048:4096]
        ys[2 * i] = y[4096:8192]
        ys[2 * i + 1] = y[8192:12288]
    return (yp, ys)
```

```python
import numpy as np
import concourse.bass as bass
import concourse.mybir as mybir
from concourse.bass_utils import run_bass_kernel_spmd

F32 = mybir.dt.float32
BF16 = mybir.dt.bfloat16
AF = mybir.ActivationFunctionType
ALU = mybir.AluOpType
AX = mybir.AxisListType

D = 1024
DI = 2048
NH = 32
DIN = 8928
EPS = 1e-6
C_Z, C_X, C_DT, C_QA, C_KV, C_KR, C_G = 0, 2048, 6144, 6208, 6592, 6848, 6880
ATT_SCALE = 96 ** -0.5


class Buf:
    __slots__ = ("w", "r")

    def __init__(self):
        self.w = None
        self.r = {}


class Eng:
    def __init__(self, key, h, sem):
        self.key, self.h, self.sem = key, h, sem
        self.count = 0
        self.waited = {}
        self.slots = []
        self.dma_i = 0


class Slot:
    def __init__(self, key, sem):
        self.key, self.sem, self.count = key, sem, 0


class Sch:
    def __init__(self, nc, stack, nslots=12):
        self.nc = nc
        self.E = {}
        for key, h in (("pe", nc.tensor), ("act", nc.scalar), ("dve", nc.vector),
                       ("pool", nc.gpsimd), ("sp", nc.sync)):
            sem = stack.enter_context(nc.semaphore("sem_" + key))
            self.E[key] = Eng(key, h, sem)
        for q in ("sp", "pool", "act"):
            for i in range(nslots):
                sem = stack.enter_context(nc.semaphore("dq_%s_%d" % (q, i)))
                self.E[q].slots.append(Slot("dq_%s_%d" % (q, i), sem))

    def _wait(self, e, deps):
        for (key, sem, val) in deps:
            if key == e.key and key == "pe":
                continue
            if e.waited.get(key, 0) >= val:
                continue
            e.h.wait_ge(sem, val)
            e.waited[key] = val

    @staticmethod
    def _deps(reads, writes):
        deps = []
        for b in reads:
            if b.w is not None:
                deps.append(b.w)
        for b in writes:
            if b.w is not None:
                deps.append(b.w)
            for k, (s, v) in b.r.items():
                deps.append((k, s, v))
        return deps

    @staticmethod
    def _mark(tok, reads, writes):
        for b in reads:
            b.r[tok[0]] = (tok[1], tok[2])
        for b in writes:
            b.w = tok
            b.r = {}

    def op(self, eng, fn, reads=(), writes=()):
        e = self.E[eng]
        self._wait(e, self._deps(reads, writes))
        ins = fn(e.h)
        e.count += 1
        ins.then_inc(e.sem, 1)
        self._mark((e.key, e.sem, e.count), reads, writes)

    def mm(self, out, pairs, reads=(), writes=()):
        n = len(pairs)

        def fn(pe):
            ins = None
            for i, (l, r) in enumerate(pairs):
                ins = pe.matmul(out, l, r, start=(i == 0), stop=(i == n - 1))
            return ins
        self.op("pe", fn, reads=reads, writes=writes)

    def dma(self, q, out, in_, reads=(), writes=()):
        if q == "st":
            q = "pool"
        e = self.E[q]
        self._wait(e, self._deps(reads, writes))
        sl = e.slots[e.dma_i % len(e.slots)]
        e.dma_i += 1
        if sl.count > 0:
            self._wait(e, [(sl.key, sl.sem, 16 * sl.count)])
        e.h.dma_start(out=out, in_=in_).then_inc(sl.sem, 16)
        sl.count += 1
        self._mark((sl.key, sl.sem, 16 * sl.count), reads, writes)

    def barrier(self):
        toks = []
        for e in self.E.values():
            if e.count:
                toks.append((e.key, e.sem, e.count))
            for sl in e.slots:
                if sl.count:
                    toks.append((sl.key, sl.sem, 16 * sl.count))
        for e in self.E.values():
            self._wait(e, toks)


def make_consts():
    i = np.arange(128)
    c = np.zeros((128, 6, 128), np.float32)
    c[:, 0, :] = np.eye(128)
    c[:, 1, :] = (i[:, None] <= i[None, :])
    c[:, 2, :] = (i[:, None] >= i[None, :])
    c[:, 3, :] = (i[:, None] > i[None, :])
    c[:, 4, :] = (i[:, None] < i[None, :])
    c[:, 5, :] = 1.0
    return c.reshape(128, 768)


def rope_tables(smax):
    inv = (1.0 / (np.float32(10000.0) ** (np.arange(0, 32, 2, dtype=np.float32) / np.float32(32)))).astype(np.float32)
    ang = np.arange(smax, dtype=np.float32)[:, None] * inv[None, :]
    cos = np.cos(ang).astype(np.float32).T
    sin = np.sin(ang).astype(np.float32).T
    return (np.ascontiguousarray(np.concatenate([cos, cos], 0)),
            np.ascontiguousarray(np.concatenate([sin, sin], 0)))


def build(seqs=(2048, 2048, 4096, 4096), phases=("p0", "p1a", "p1b", "p1c", "p1d", "p2", "p3", "p4a", "p4b"), debug=False):
    nc = bass.Bass("TRN2", target_bir_lowering=False)
    NS = len(seqs)
    NT = sum(seqs)
    SMAX = max(seqs)
    tok0 = [sum(seqs[:i]) for i in range(NS)]
    skind = "ExternalOutput" if debug else "Internal"

    def din(name, shape, dt=F32):
        return nc.dram_tensor(name, list(shape), dt, kind="ExternalInput").ap()

    def dscr(name, shape, dt):
        return nc.dram_tensor(name, list(shape), dt, kind=skind).ap()

    x_d = din("x", [NT, D])
    c_d = din("c", [NS, D])
    w_ada = din("w_ada", [D, 6 * D])
    b_ada = din("b_ada", [6 * D])
    g_norm1 = din("g_norm1", [D])
    w_in = din("w_in", [D, DIN])
    conv_w = din("conv_w", [5, 4096])
    conv_b = din("conv_b", [4096])
    dtb = din("dt_bias", [64])
    alog = din("a_log", [64])
    dskip = din("d_skip", [32])
    g_ssd = din("g_ssd_norm", [DI])
    w_ssd_out = din("w_ssd_out", [DI, D])
    g_q = din("g_q_norm", [384])
    w_q_b = din("w_q_b", [384, 1536])
    g_kv = din("g_kv_norm", [256])
    w_kv_b = din("w_kv_b", [256, 2048])
    w_mla_out = din("w_mla_out", [D, D])
    w_o = din("w_o", [D, D])
    g_norm2 = din("g_norm2", [D])
    w_mlp_in = din("w_mlp_in", [D, 4 * D])
    w_mlp_out = din("w_mlp_out", [4 * D, D])
    g_final = din("g_final", [D])
    consts_d = din("consts", [128, 768])
    cos_d = din("cos_t", [32, SMAX])
    sin_d = din("sin_t", [32, SMAX])

    y_d = nc.dram_tensor("y", [NT, D], F32, kind="ExternalOutput").ap()

    adaT_d = dscr("adaT_s", [NS, 4, D], F32)
    gate_d = dscr("gate_s", [NS, 2, D], F32)
    uT_d = dscr("uT_s", [4096, NT], BF16)
    zs_d = dscr("zs_s", [NT, DI], BF16)
    dt_d = dscr("dt_s", [NT, 64], F32)
    qnT_d = dscr("qnT_s", [384, NT], BF16)
    cnT_d = dscr("cnT_s", [256, NT], BF16)
    kr_d = dscr("kr_s", [32, NT], BF16)
    gT_d = dscr("gT_s", [2048, NT], BF16)
    xtok_d = dscr("xtok_s", [NT, 3072], BF16)
    bcT_d = dscr("bcT_s", [2048, NT], BF16)
    KT_d = dscr("KT_s", [1024, NT], BF16)
    QT_d = dscr("QT_s", [1536, NT], BF16)
    V_d = dscr("V_s", [16, NT * 64], BF16)
    OT_d = dscr("OT_s", [D, NT], BF16)
    prevb_d = dscr("prevb_s", [SMAX // 128, 128, DI], BF16)
    ygT_d = dscr("ygT_s", [DI, NT], BF16)
    x1_d = dscr("x1_s", [NT, D], F32)
    h2T_d = dscr("h2T_s", [D, NT], BF16)

    from contextlib import ExitStack
    with ExitStack() as top:
        S = Sch(nc, top)
        _uid = [0]

        def sb(st, name, shape, dt):
            _uid[0] += 1
            return st.enter_context(nc.sbuf_tensor("%s_%d" % (name, _uid[0]), list(shape), dt))
        psall = top.enter_context(nc.psum_tensor("psall", [128, 4096], F32))
        banks = [psall[:, i * 512:(i + 1) * 512] for i in range(8)]
        bbuf = [Buf() for _ in range(8)]
        bi = {None: 0, "a": 0, "b": 0}
        pools = {None: list(range(8)), "a": [4], "b": [5, 6, 7]}

        def bank(pool=None):
            lst = pools[pool]
            i = lst[bi[pool] % len(lst)]
            bi[pool] += 1
            return banks[i], bbuf[i]

        cst32 = sb(top, "cst32", [128, 768], F32)
        cstb = sb(top, "cstb", [128, 768], BF16)
        epst = sb(top, "epst", [128, 1], F32)
        onet = sb(top, "onet", [128, 1], F32)
        B_c = Buf()
        S.dma("sp", cst32[:], consts_d, writes=[B_c])
        S.op("dve", lambda v: v.tensor_copy(cstb[:], cst32[:]), reads=[B_c], writes=[B_c])
        S.op("dve", lambda v: v.memset(epst[:], EPS), writes=[B_c])
        S.op("dve", lambda v: v.memset(onet[:], 1.0), writes=[B_c])
        identb = cstb[:, 0:128]
        Ufb, Ubb, Lfb, Lbb, onesb = (cstb[:, 128 * k:128 * (k + 1)] for k in range(1, 6))
        Uf32, Ub32 = cst32[:, 128:256], cst32[:, 256:384]
        ones32 = cst32[:, 640:768]

        def rstd_from_ss(eng_list, out, ss, n, rb, wb):
            S.op("act", lambda a: a.activation(out, ss, AF.Sqrt, bias=epst[0:out.shape[0], :], scale=1.0 / n), reads=rb + [B_c], writes=wb)
            S.op("dve", lambda v: v.reciprocal(out, out), reads=wb, writes=wb)

        if "p0" in phases:
            with ExitStack() as st:
                cT = sb(st, "p0_cT", [128, 8, NS], F32)
                cbc = sb(st, "p0_cbc", [128, 8, NS, 128], F32)
                wa = [sb(st, "p0_wa%d" % i, [128, 8, 1024], F32) for i in range(2)]
                bT = sb(st, "p0_bT", [128, 48], F32)
                brow = sb(st, "p0_brow", [128, 2, 1024], F32)
                gT1 = sb(st, "p0_g", [128, 2, 8], F32)
                res = sb(st, "p0_res", [128, 6, 8, NS], F32)
                vec = sb(st, "p0_vec", [128, NS, 4, 8], F32)
                grow = sb(st, "p0_grow", [128, 1024], F32)
                Bc, Bw, Bb, Br, Bv, Bg = Buf(), [Buf(), Buf()], Buf(), Buf(), Buf(), Buf()
                with nc.allow_non_contiguous_dma(reason="tiny transposed loads"):
                    for b in range(NS):
                        S.dma("sp", cT[:, :, b], c_d[b, :].rearrange("(kc p) -> p kc", p=128), writes=[Bc])
                    S.dma("sp", bT[:], b_ada.rearrange("(j p) -> p j", p=128), writes=[Bb])
                    S.dma("sp", gT1[:, 0, :], g_norm1.rearrange("(j p) -> p j", p=128), writes=[Bb])
                    S.dma("sp", gT1[:, 1, :], g_norm2.rearrange("(j p) -> p j", p=128), writes=[Bb])
                S.dma("sp", brow[:, 0, :], b_ada[2048:3072].partition_broadcast(128), writes=[Bb])
                S.dma("sp", brow[:, 1, :], b_ada[5120:6144].partition_broadcast(128), writes=[Bb])
                S.op("act", lambda a: a.activation(cT[:], cT[:], AF.Silu), reads=[Bc], writes=[Bc])
                S.op("dve", lambda v: v.tensor_copy(cbc[:], cT[:].unsqueeze(3).broadcast_to([128, 8, NS, 128])), reads=[Bc], writes=[Bc])
                for j in range(6):
                    w = wa[j % 2]
                    S.dma("sp", w[:], w_ada[:, j * 1024:(j + 1) * 1024].rearrange("(kc p) f -> p kc f", p=128), writes=[Bw[j % 2]])
                    pt, pb = bank()
                    for oc in range(8):
                        for kc in range(8):
                            S.op("pe", lambda pe, oc=oc, kc=kc: pe.matmul(pt[:, oc * NS:(oc + 1) * NS], w[:, kc, oc * 128:(oc + 1) * 128], cT[:, kc, :], start=(kc == 0), stop=(kc == 7)),
                                 reads=[Bw[j % 2], Bc], writes=[pb])
                    S.op("dve", lambda v, j=j: v.tensor_tensor(res[:, j, :, :], pt[:, 0:8 * NS].rearrange("p (o b) -> p o b", b=NS),
                                                           bT[:, j * 8:(j + 1) * 8].unsqueeze(2).broadcast_to([128, 8, NS]), ALU.add),
                         reads=[pb, Bb], writes=[Br])
                    if j in (2, 5):
                        gi = 0 if j == 2 else 1
                        for b in range(NS):
                            for hf in range(2):
                                pt2, pb2 = bank()
                                for kc in range(8):
                                    S.op("pe", lambda pe, kc=kc, b=b, hf=hf: pe.matmul(pt2[:], cbc[:, kc, b, :], w[:, kc, hf * 512:(hf + 1) * 512], start=(kc == 0), stop=(kc == 7)),
                                         reads=[Bw[j % 2], Bc], writes=[pb2])
                                S.op("dve", lambda v, hf=hf, gi=gi: v.tensor_tensor(grow[:, hf * 512:(hf + 1) * 512], pt2[:], brow[:, gi, hf * 512:(hf + 1) * 512], ALU.add),
                                     reads=[pb2, Bb], writes=[Bg])
                            S.dma("st", gate_d[b, gi, :], grow[0:1, :], reads=[Bg])
                for b in range(NS):
                    for (k, jsc, jsh, gi) in ((0, 1, 0, 0), (2, 4, 3, 1)):
                        S.op("dve", lambda v, b=b, k=k, jsc=jsc, gi=gi: v.scalar_tensor_tensor(vec[:, b, k, :], res[:, jsc, :, b], 1.0, gT1[:, gi, :], ALU.add, ALU.mult), reads=[Br, Bb], writes=[Bv])
                        S.op("dve", lambda v, b=b, k=k, jsh=jsh: v.tensor_copy(vec[:, b, k + 1, :], res[:, jsh, :, b]), reads=[Br], writes=[Bv])
                with nc.allow_non_contiguous_dma(reason="tiny transposed stores"):
                    for b in range(NS):
                        for k in range(4):
                            S.dma("st", adaT_d[b, k, :].rearrange("(kc p) -> p kc", p=128), vec[:, b, k, :], reads=[Bv])
                S.barrier()

        for part in ("a", "b"):
            if ("p1" + part) not in phases:
                continue
            WC = 4096 if part == "a" else 4832
            cm = (lambda c: c - 2048) if part == "a" else (lambda c: c if c < 2048 else c - 4096)
            with ExitStack() as st:
                W = sb(st, "p1_W", [128, 8, WC], BF16)
                wkr = sb(st, "p1_wkr", [128, 8, 64], BF16)
                dtb_bc = sb(st, "p1_dtb", [128, 4, 64], F32)
                ab = sb(st, "p1_ab", [128, 2, 8], F32)
                xt = [sb(st, "p1_x%d" % i, [128, 4, 1024], F32) for i in range(1)] * 2
                junk = sb(st, "p1_junk", [128, 1024], BF16)
                ssl = [sb(st, "p1_ss", [128, 4], F32) for _ in range(2)]
                xnl = [sb(st, "p1_xn", [128, 4, 1024], BF16) for _ in range(2)]
                Bssl, Bxnl = [Buf(), Buf()], [Buf(), Buf()]
                hT = [sb(st, "p1_hT%d" % i, [128, 8, 512], BF16) for i in range(1)] * 2
                stg = [sb(st, "p1_stg%d" % i, [128, 512], F32) for i in range(4)]
                stgb = [sb(st, "p1_stgb%d" % i, [128, 512], BF16) for i in range(4)]
                qa = sb(st, "p1_qa", [128, 5, 512], F32) if part == "b" else None
                sq = sb(st, "p1_sq", [128, 5, 512], F32) if part == "b" else None
                rs = sb(st, "p1_rs", [128, 2, 512], F32)
                qn = sb(st, "p1_qn", [128, 5, 512], BF16)
                cs = sb(st, "p1_cs", [32, 2, 512], F32)
                krt = sb(st, "p1_krt", [32, 2, 512], F32)
                krb = sb(st, "p1_krb", [32, 512], BF16)
                zst = [sb(st, "p1_zst%d" % i, [128, 2048], BF16) if part == "b" else None for i in range(2)]
                dts = sb(st, "p1_dts", [128, 5, 256], F32)
                BW, Bab, Bx, Bss, Bxn, BhT = Buf(), Buf(), [Buf()] * 2, Buf(), Buf(), [Buf()] * 2
                Bstg, Bstgb = [Buf() for _ in range(4)], [Buf() for _ in range(4)]
                Bqa, Bsq, Brs, Bqn, Bcs, Bkr, Bkrb, Bz, Bdts = Buf(), Buf(), Buf(), Buf(), Buf(), Buf(), Buf(), [Buf(), Buf()], Buf()
                for kc in range(8):
                    if part == "a":
                        S.dma("pool", W[:, kc, :], w_in[kc * 128:(kc + 1) * 128, 2048:6144], writes=[BW])
                    else:
                        S.dma("pool", W[:, kc, 0:2048], w_in[kc * 128:(kc + 1) * 128, 0:2048], writes=[BW])
                        S.dma("pool", W[:, kc, 2048:4832], w_in[kc * 128:(kc + 1) * 128, 6144:8928], writes=[BW])
                CKR = cm(C_KR) if part == "b" else 0
                S.op("dve", lambda v: v.tensor_copy(wkr[:, :, 0:32], W[:, :, CKR:CKR + 32]), reads=[BW], writes=[BW])
                S.op("dve", lambda v: v.tensor_scalar(wkr[:, :, 32:48], W[:, :, CKR + 16:CKR + 32], -1.0, None, ALU.mult), reads=[BW], writes=[BW])
                S.op("dve", lambda v: v.tensor_copy(wkr[:, :, 48:64], W[:, :, CKR:CKR + 16]), reads=[BW], writes=[BW])
                for s4 in range(4):
                    S.dma("sp", dtb_bc[:, s4, :], dtb.partition_broadcast(128), writes=[BW])
                stg_i = [0]
                tiles = [(si, ti) for si in range(NS) for ti in range(seqs[si] // 512)]

                def stageA(idx):
                    si, ti = tiles[idx]
                    g0 = tok0[si] + ti * 512
                    k2 = idx % 2
                    X = xt[0]
                    S.dma("sp", X[:], x_d[g0:g0 + 512, :].rearrange("(s p) f -> p s f", p=128), writes=[Bx[0]])
                    for s4 in range(4):
                        S.op("act", lambda a: a.activation(junk[:], X[:, s4, :], AF.Square, accum_out=ssl[k2][:, s4:s4 + 1]), reads=[Bx[0]], writes=[Bssl[k2]])
                    rstd_from_ss(None, ssl[k2][:], ssl[k2][:], 1024.0, [Bssl[k2]], [Bssl[k2]])
                    for s4 in range(4):
                        S.op("pool" if s4 % 2 else "dve", lambda v: v.tensor_scalar(xnl[k2][:, s4, :], X[:, s4, :], ssl[k2][:, s4:s4 + 1], None, ALU.mult), reads=[Bx[0], Bssl[k2]], writes=[Bxnl[k2]])

                def stageB(idx):
                    si, ti = tiles[idx]
                    k2 = idx % 2
                    H = hT[0]
                    if ti == 0:
                        with nc.allow_non_contiguous_dma(reason="tiny"):
                            S.dma("sp", ab[:, 0, :], adaT_d[si, 0, :].rearrange("(kc p) -> p kc", p=128), writes=[Bab])
                            S.dma("sp", ab[:, 1, :], adaT_d[si, 1, :].rearrange("(kc p) -> p kc", p=128), writes=[Bab])
                    for kc in range(8):
                        pt, pb = bank()
                        ptb = pt[:].bitcast(BF16)
                        for s4 in range(4):
                            S.op("pe", lambda pe: pe.transpose(ptb[:, s4 * 128:(s4 + 1) * 128], xnl[k2][:, s4, kc * 128:(kc + 1) * 128], identb), reads=[Bxnl[k2], B_c], writes=[pb])
                        S.op("dve", lambda v: v.tensor_scalar(H[:, kc, :], ptb[:, 0:512], ab[:, 0, kc:kc + 1], ab[:, 1, kc:kc + 1], ALU.mult, ALU.add), reads=[pb, Bab], writes=[BhT[0]])

                def body(idx):
                    si, ti = tiles[idx]
                    g0 = tok0[si] + ti * 512
                    p0 = ti * 512
                    par = 0
                    H = hT[0]
                    def fm(cols, m, lw=None):
                        pt, pb = bank()
                        l = (lw if lw is not None else W)
                        cc0 = cols if lw is not None else cm(cols)
                        S.mm(pt[0:m, :], [(l[:, kc, cc0:cc0 + m], H[:, kc, :]) for kc in range(8)], reads=[BW, BhT[par]], writes=[pb])
                        return pt, pb
                    for j in range(32 if part == "a" else 0):
                        pt, pb = fm(C_X + j * 128, 128)
                        k = stg_i[0] % 4
                        stg_i[0] += 1
                        S.op("act", lambda a, k=k: a.copy(stgb[k][:], pt[:]), reads=[pb], writes=[Bstgb[k]])
                        S.dma("st", uT_d[j * 128:(j + 1) * 128, g0:g0 + 512], stgb[k][:], reads=[Bstgb[k]])
                    if part == "a":
                        return
                    for j in range(5):
                        pt, pb = fm(C_QA + j * 128, 128)
                        S.op("act", lambda a, j=j: a.copy(qa[:, j, :], pt[:]), reads=[pb], writes=[Bqa])
                        S.op("pool", lambda v, j=j: v.tensor_tensor(sq[:, j, :], qa[:, j, :], qa[:, j, :], ALU.mult), reads=[Bqa], writes=[Bsq])
                    for (r, j0, n) in ((0, 0, 3), (1, 3, 2)):
                        pt, pb = bank()
                        for j in range(n):
                            S.op("pe", lambda pe, j=j: pe.matmul(pt[:], ones32, sq[:, j0 + j, :], start=(j == 0), stop=(j == n - 1)), reads=[Bsq, B_c], writes=[pb])
                        S.op("act", lambda a, r=r, n=n: a.activation(rs[:, r, :], pt[:], AF.Sqrt, bias=epst[:], scale=1.0 / (128 * n)), reads=[pb, B_c], writes=[Brs])
                        S.op("dve", lambda v, r=r: v.reciprocal(rs[:, r, :], rs[:, r, :]), reads=[Brs], writes=[Brs])
                        for j in range(n):
                            S.op("dve", lambda v, j=j, r=r: v.tensor_tensor(qn[:, j0 + j, :], qa[:, j0 + j, :], rs[:, r, :], ALU.mult), reads=[Bqa, Brs], writes=[Bqn])
                    S.dma("st", qnT_d[:, g0:g0 + 512].rearrange("(j p) t -> p j t", p=128), qn[:, 0:3, :], reads=[Bqn])
                    S.dma("st", cnT_d[:, g0:g0 + 512].rearrange("(j p) t -> p j t", p=128), qn[:, 3:5, :], reads=[Bqn])
                    S.dma("sp", cs[:, 0, :], cos_d[:, p0:p0 + 512], writes=[Bcs])
                    S.dma("sp", cs[:, 1, :], sin_d[:, p0:p0 + 512], writes=[Bcs])
                    pA, pbA = fm(0, 32, wkr)
                    pB, pbB = fm(32, 32, wkr)
                    S.op("dve", lambda v: v.tensor_tensor(krt[:, 0, :], pA[0:32, :], cs[:, 0, :], ALU.mult), reads=[pbA, Bcs], writes=[Bkr])
                    S.op("dve", lambda v: v.tensor_tensor(krt[:, 1, :], pB[0:32, :], cs[:, 1, :], ALU.mult), reads=[pbB, Bcs], writes=[Bkr])
                    S.op("dve", lambda v: v.tensor_tensor(krb[:], krt[:, 0, :], krt[:, 1, :], ALU.add), reads=[Bkr], writes=[Bkrb])
                    S.dma("st", kr_d[:, g0:g0 + 512], krb[:], reads=[Bkrb])
                    for j in range(16):
                        pt, pb = fm(C_G + j * 128, 128)
                        k = stg_i[0] % 4
                        stg_i[0] += 1
                        S.op("act", lambda a, k=k: a.activation(stgb[k][:], pt[:], AF.Sigmoid), reads=[pb], writes=[Bstgb[k]])
                        S.dma("st", gT_d[j * 128:(j + 1) * 128, g0:g0 + 512], stgb[k][:], reads=[Bstgb[k]])
                    for s4 in range(4):
                        Z = zst[s4 % 2]
                        for cg in range(4):
                            pt, pb = bank()
                            S.mm(pt[:], [(H[:, kc, s4 * 128:(s4 + 1) * 128], W[:, kc, cg * 512:(cg + 1) * 512]) for kc in range(8)], reads=[BW, BhT[par]], writes=[pb])
                            S.op("act", lambda a, s4=s4, cg=cg: a.activation(Z[:, cg * 512:(cg + 1) * 512], pt[:], AF.Silu), reads=[pb], writes=[Bz[s4 % 2]])
                        S.dma("st", zs_d[g0 + s4 * 128:g0 + (s4 + 1) * 128, :], Z[:], reads=[Bz[s4 % 2]])
                    pt, pb = bank()
                    for s4 in range(4):
                        S.mm(pt[:, s4 * 64:(s4 + 1) * 64], [(H[:, kc, s4 * 128:(s4 + 1) * 128], W[:, kc, cm(C_DT):cm(C_DT) + 64]) for kc in range(8)], reads=[BW, BhT[par]], writes=[pb])
                    d0, d1, d2, d3, d4 = (dts[:, k, :] for k in range(5))
                    S.op("dve", lambda v: v.tensor_tensor(d0, pt[:, 0:256], dtb_bc[:].rearrange("p a b -> p (a b)"), ALU.add), reads=[pb, BW], writes=[Bdts])
                    S.op("dve", lambda v: v.tensor_scalar(d1, d0, -1.0, None, ALU.mult), reads=[Bdts], writes=[Bdts])
                    S.op("dve", lambda v: v.tensor_tensor(d1, d0, d1, ALU.min), reads=[Bdts], writes=[Bdts])
                    S.op("act", lambda a: a.activation(d2, d1, AF.Exp), reads=[Bdts], writes=[Bdts])
                    S.op("act", lambda a: a.activation(d3, d2, AF.Ln, bias=onet[:], scale=1.0), reads=[Bdts, B_c], writes=[Bdts])
                    S.op("dve", lambda v: v.scalar_tensor_tensor(d4, d0, 0.0, d3, ALU.max, ALU.add), reads=[Bdts], writes=[Bdts])
                    S.dma("st", dt_d[g0:g0 + 512, :].rearrange("(s p) f -> p s f", p=128), d4.rearrange("p (s f) -> p s f", f=64), reads=[Bdts])
                stageA(0)
                stageB(0)
                for idx in range(len(tiles)):
                    if idx + 1 < len(tiles):
                        stageA(idx + 1)
                    body(idx)
                    if idx + 1 < len(tiles):
                        stageB(idx + 1)
                S.barrier()

        if "p1c" in phases:
            with ExitStack() as st:
                cw = sb(st, "pc_cw", [128, 32, 6], F32)
                u = [sb(st, "pc_u%d" % i, [128, 516], BF16) for i in range(4)]
                dw = sb(st, "pc_dw", [128, 32, 5, 128], BF16)
                ob = [sb(st, "pc_ob%d" % i, [128, 512], BF16) for i in range(4)]
                tk = [sb(st, "pc_tk%d" % i, [128, 4, 3072], BF16) for i in range(2)]
                Bcw, Bu, Bacc, Bob, Btk = Buf(), [Buf() for _ in range(4)], [Buf(), Buf()], [Buf() for _ in range(4)], [Buf(), Buf()]
                with nc.allow_non_contiguous_dma(reason="tiny"):
                    for k in range(5):
                        S.dma("sp", cw[:, :, k], conv_w[k, :].rearrange("(j p) -> p j", p=128), writes=[Bcw])
                    S.dma("sp", cw[:, :, 5], conv_b.rearrange("(j p) -> p j", p=128), writes=[Bcw])
                for j in range(32):
                    S.op("dve", lambda v, j=j: v.tensor_tensor(dw[:, j, :, :], identb.unsqueeze(1).broadcast_to([128, 5, 128]), cw[:, j, 0:5].unsqueeze(2).broadcast_to([128, 5, 128]), ALU.mult), reads=[Bcw, B_c], writes=[Bcw])
                it = 0
                for si in range(NS):
                    nt = seqs[si] // 512
                    for ti in range(nt):
                        g0 = tok0[si] + ti * 512
                        par = (g0 // 512) % 2
                        TK = tk[par]
                        for j in range(32):
                            U, BU = u[it % 4], Bu[it % 4]
                            O, BO = ob[it % 4], Bob[it % 4]
                            eng = "dve"
                            it += 1
                            lo = 0 if ti > 0 else 2
                            hi = 516 if ti < nt - 1 else 514
                            if lo:
                                S.op(eng, lambda v: v.memset(U[:, 0:2], 0.0), writes=[BU])
                            if hi < 516:
                                S.op(eng, lambda v: v.memset(U[:, 514:516], 0.0), writes=[BU])
                            S.dma("sp", U[:, lo:hi], uT_d[j * 128:(j + 1) * 128, g0 - 2 + lo:g0 - 2 + hi], writes=[BU])
                            pcv, pbcv = bank()
                            S.mm(pcv, [(dw[:, j, k, :], U[:, k:k + 512]) for k in range(5)], reads=[BU, Bcw], writes=[pbcv])
                            S.op("act", lambda a, j=j: a.activation(O[:], pcv, AF.Silu, bias=cw[:, j, 5:6], scale=1.0), reads=[pbcv, Bcw], writes=[BO])
                            if j >= 16:
                                S.dma("st", bcT_d[(j - 16) * 128:(j - 15) * 128, g0:g0 + 512], O[:], reads=[BO])
                            if j < 24:
                                pt, pb = bank()
                                ptb = pt[:].bitcast(BF16)
                                for s4 in range(4):
                                    S.op("pe", lambda pe, s4=s4: pe.transpose(ptb[:, s4 * 128:(s4 + 1) * 128], O[:, s4 * 128:(s4 + 1) * 128], identb), reads=[BO, B_c], writes=[pb])
                                S.op("act", lambda a, j=j: a.copy(TK[:, :, j * 128:(j + 1) * 128], ptb[:, 0:512].rearrange("p (s f) -> p s f", f=128)), reads=[pb], writes=[Btk[par]])
                        S.dma("st", xtok_d[g0:g0 + 512, :].rearrange("(s p) f -> p s f", p=128), TK[:], reads=[Btk[par]])
                S.barrier()

        ctx = dict(locals())
        if "p1d" in phases:
            emit_p1d(ctx)
        with ExitStack() as gstack:
            ctx["gstack"] = gstack
            gens = []
            if "p2" in phases:
                gens.append(gen_p2(ctx))
            if "p3" in phases:
                gens.extend(make_p3(ctx))
            run_interleaved(gens)
            S.barrier()
        if "p4a" in phases:
            emit_p4a(ctx)
        if "p4b" in phases:
            emit_p4b(ctx)
        S.barrier()
    return nc


def run_interleaved(gens):
    n = len(gens)
    prog = [0.0] * n
    alive = [True] * n
    blocked = [False] * n
    while any(alive):
        cand = [i for i in range(n) if alive[i] and not blocked[i]]
        if not cand:
            raise RuntimeError("interleave deadlock")
        k = min(cand, key=lambda i: prog[i])
        try:
            v = next(gens[k])
        except StopIteration:
            alive[k] = False
            blocked = [False] * n
            continue
        if v is None:
            blocked[k] = True
        else:
            prog[k] = v
            blocked = [False] * n


def emit_p1d(c):
    from contextlib import ExitStack
    S, nc, sb, bank, seqs, tok0, NS = c["S"], c["nc"], c["sb"], c["bank"], c["seqs"], c["tok0"], c["NS"]
    qnT_d, cnT_d, KT_d, QT_d, V_d, cos_d, sin_d = c["qnT_d"], c["cnT_d"], c["KT_d"], c["QT_d"], c["V_d"], c["cos_d"], c["sin_d"]
    w_q_b, w_kv_b, g_q, g_kv = c["w_q_b"], c["w_kv_b"], c["g_q"], c["g_kv"]
    with ExitStack() as st:
        wq = sb(st, "pd_wq", [128, 3, 1536], BF16)
        wqB = sb(st, "pd_wqB", [128, 3, 16, 96], BF16)
        wkv = sb(st, "pd_wkv", [128, 2, 2048], BF16)
        wk = sb(st, "pd_wk", [128, 2, 1024], BF16)
        wv = sb(st, "pd_wv", [128, 2, 1024], BF16)
        gq = sb(st, "pd_gq", [128, 5], F32)
        qn = [sb(st, "pd_qn", [128, 3, 512], BF16) for _ in range(2)]
        cn = [sb(st, "pd_cn", [128, 2, 512], BF16) for _ in range(2)]
        cst = [sb(st, "pd_cs", [128, 2, 512], F32) for _ in range(2)]
        kst = [sb(st, "pd_kst", [128, 512], BF16) for _ in range(3)]
        vst = [sb(st, "pd_vst", [128, 512], BF16) for _ in range(3)]
        qst = [sb(st, "pd_qst", [128, 512], BF16) for _ in range(3)]
        rt = [sb(st, "pd_rt", [128, 2, 512], F32) for _ in range(2)]
        Bw2 = Buf()
        Bqn, Bcn, Bcs = [Buf(), Buf()], [Buf(), Buf()], [Buf(), Buf()]
        Bk, Bv, Bq, Brt = [Buf() for _ in range(3)], [Buf() for _ in range(3)], [Buf() for _ in range(3)], [Buf(), Buf()]
        S.dma("pool", wq[:], w_q_b.rearrange("(kc p) f -> p kc f", p=128), writes=[Bw2])
        S.dma("pool", wkv[:], w_kv_b.rearrange("(kc p) f -> p kc f", p=128), writes=[Bw2])
        with nc.allow_non_contiguous_dma(reason="tiny"):
            S.dma("sp", gq[:, 0:3], g_q.rearrange("(j p) -> p j", p=128), writes=[Bw2])
            S.dma("sp", gq[:, 3:5], g_kv.rearrange("(j p) -> p j", p=128), writes=[Bw2])
        for kc in range(3):
            S.op("dve", lambda v: v.tensor_scalar(wq[:, kc, :], wq[:, kc, :], gq[:, kc:kc + 1], None, ALU.mult), reads=[Bw2], writes=[Bw2])
        for kc in range(2):
            S.op("dve", lambda v: v.tensor_scalar(wkv[:, kc, :], wkv[:, kc, :], gq[:, 3 + kc:4 + kc], None, ALU.mult), reads=[Bw2], writes=[Bw2])
            w4 = wkv[:, kc, :].rearrange("p (h t f) -> p h t f", t=2, f=64)
            S.op("dve", lambda v: v.tensor_copy(wk[:, kc, :].rearrange("p (h f) -> p h f", f=64), w4[:, :, 0, :]), reads=[Bw2], writes=[Bw2])
            S.op("dve", lambda v: v.tensor_copy(wv[:, kc, :].rearrange("p (h f) -> p h f", f=64), w4[:, :, 1, :]), reads=[Bw2], writes=[Bw2])
        S.op("dve", lambda v: v.memset(wqB[:], 0.0), writes=[Bw2])
        wq4 = wq[:].rearrange("p k (h f) -> p k h f", f=96)
        for kc in range(3):
            S.op("dve", lambda v: v.tensor_scalar(wqB[:, kc, :, 64:80], wq4[:, kc, :, 80:96], -1.0, None, ALU.mult), reads=[Bw2], writes=[Bw2])
            S.op("dve", lambda v: v.tensor_copy(wqB[:, kc, :, 80:96], wq4[:, kc, :, 64:80]), reads=[Bw2], writes=[Bw2])
        it = 0
        ki = vi = qi = 0
        for si in range(NS):
            Sq, t0 = seqs[si], tok0[si]
            nch = Sq // 128
            for ti in range(Sq // 512):
                g0 = t0 + ti * 512
                p0 = ti * 512
                k = it % 2
                it += 1
                QN, CN, CS = qn[k], cn[k], cst[k]
                S.dma("sp", QN[:], qnT_d[:, g0:g0 + 512].rearrange("(j p) t -> p j t", p=128), writes=[Bqn[k]])
                S.dma("sp", CN[:], cnT_d[:, g0:g0 + 512].rearrange("(j p) t -> p j t", p=128), writes=[Bcn[k]])
                S.dma("sp", CS[64:96, 0, :], cos_d[:, p0:p0 + 512], writes=[Bcs[k]])
                S.dma("sp", CS[64:96, 1, :], sin_d[:, p0:p0 + 512], writes=[Bcs[k]])
                for pr in range(8):
                    pt, pb = bank()
                    S.mm(pt, [(wk[:, kc, pr * 128:(pr + 1) * 128], CN[:, kc, :]) for kc in range(2)], reads=[Bw2, Bcn[k]], writes=[pb])
                    K_, BK_ = kst[ki % 3], Bk[ki % 3]
                    ki += 1
                    S.op("act", lambda a: a.copy(K_[:], pt), reads=[pb], writes=[BK_])
                    S.dma("st", KT_d[pr * 128:(pr + 1) * 128, g0:g0 + 512], K_[:], reads=[BK_])
                for s4 in range(4):
                    cidx = ti * 4 + s4
                    for hf in range(2):
                        pt, pb = bank()
                        S.mm(pt, [(CN[:, kc, s4 * 128:(s4 + 1) * 128], wv[:, kc, hf * 512:(hf + 1) * 512]) for kc in range(2)], reads=[Bw2, Bcn[k]], writes=[pb])
                        V_, BV_ = vst[vi % 3], Bv[vi % 3]
                        vi += 1
                        S.op("dve", lambda v: v.tensor_copy(V_[:], pt), reads=[pb], writes=[BV_])
                        dst = V_d[hf * 8:(hf + 1) * 8, t0 * 64:(t0 + Sq) * 64].rearrange("h (p c f) -> p h c f", p=128, f=64)[:, :, cidx, :]
                        S.dma("st", dst, V_[:].rearrange("p (h f) -> p h f", f=64), reads=[BV_])
                for h in range(16):
                    pA, pbA = bank()
                    S.mm(pA[0:96, :], [(wq[:, kc, h * 96:(h + 1) * 96], QN[:, kc, :]) for kc in range(3)], reads=[Bw2, Bqn[k]], writes=[pbA])
                    pB, pbB = bank()
                    S.mm(pB[0:96, :], [(wqB[:, kc, h, :], QN[:, kc, :]) for kc in range(3)], reads=[Bw2, Bqn[k]], writes=[pbB])
                    Q_, BQ_ = qst[qi % 3], Bq[qi % 3]
                    RT, BRT = rt[qi % 2], Brt[qi % 2]
                    qi += 1
                    S.op("act", lambda a: a.copy(Q_[0:64, :], pA[0:64, :]), reads=[pbA], writes=[BQ_])
                    S.op("dve", lambda v: v.tensor_tensor(RT[64:96, 0, :], pA[64:96, :], CS[64:96, 0, :], ALU.mult), reads=[pbA, Bcs[k]], writes=[BRT])
                    S.op("dve", lambda v: v.tensor_tensor(RT[64:96, 1, :], pB[64:96, :], CS[64:96, 1, :], ALU.mult), reads=[pbB, Bcs[k]], writes=[BRT])
                    S.op("pool", lambda v: v.tensor_tensor(Q_[64:96, :], RT[64:96, 0, :], RT[64:96, 1, :], ALU.add), reads=[BRT], writes=[BQ_])
                    S.dma("st", QT_d[h * 96:(h + 1) * 96, g0:g0 + 512], Q_[0:96, :], reads=[BQ_])
        S.barrier()


def gen_p2(c):
    from contextlib import ExitStack
    S, nc, sb, seqs, tok0, NS = c["S"], c["nc"], c["sb"], c["seqs"], c["tok0"], c["NS"]
    banks, bbuf, SMAX = c["banks"], c["bbuf"], c["SMAX"]
    psall = c["psall"]
    KT_d, QT_d, V_d, kr_d, OT_d = c["KT_d"], c["QT_d"], c["V_d"], c["kr_d"], c["OT_d"]
    st = c["gstack"]
    if True:
        KT = [sb(st, "p2_KT", [128, SMAX], BF16) for _ in range(2)]
        QT = [sb(st, "p2_QT", [128, SMAX], BF16) for _ in range(2)]
        VA = [sb(st, "p2_VA", [128, SMAX // 128, 128], BF16) for _ in range(2)]
        PT = [sb(st, "p2_PT", [128, 512], BF16) for _ in range(4)]
        rd = sb(st, "p2_rd", [128, 512], F32)
        og = [sb(st, "p2_og", [128, 512], BF16) for _ in range(2)]
        BK, BQ, BV = [Buf(), Buf()], [Buf(), Buf()], [Buf(), Buf()]
        BPT, Brd, Bog = [Buf() for _ in range(4)], Buf(), [Buf(), Buf()]
        for i in range(2):
            S.op("pool", lambda v: v.memset(VA[i][:, :, 64:128], 1.0), writes=[BV[i]])
        heads = [(si, h) for si in range(NS) for h in range(16)]
        total = float(sum(seqs[si] // 512 * (seqs[si] // 128) * 3 for si, h in heads)) + 1.0
        done = 0

        def load(idx):
            si, h = heads[idx]
            hb = idx % 2
            Sq, t0 = seqs[si], tok0[si]
            S.dma("sp", KT[hb][0:64, 0:Sq], KT_d[h * 64:(h + 1) * 64, t0:t0 + Sq], writes=[BK[hb]])
            S.dma("sp", KT[hb][64:96, 0:Sq], kr_d[:, t0:t0 + Sq], writes=[BK[hb]])
            S.dma("sp", QT[hb][0:96, 0:Sq], QT_d[h * 96:(h + 1) * 96, t0:t0 + Sq], writes=[BQ[hb]])
            S.dma("sp", VA[hb][:, 0:Sq // 128, 0:64], V_d[h, t0 * 64:(t0 + Sq) * 64].rearrange("(p c f) -> p c f", p=128, f=64), writes=[BV[hb]])

        load(0)
        pi = 0
        oi = 0
        for idx, (si, h) in enumerate(heads):
            if idx + 1 < len(heads):
                load(idx + 1)
            hb = idx % 2
            K_, Q_, V_ = KT[hb], QT[hb], VA[hb]
            Sq, t0 = seqs[si], tok0[si]
            npair = Sq // 256
            nkc = Sq // 128
            for qt in range(Sq // 512):
                ql = slice(qt * 512, (qt + 1) * 512)
                pO, pbO = banks[3], bbuf[3]
                pend = []
                for cc in range(nkc + 2):
                    if cc < nkc:
                        b0 = cc % 3
                        S.op("pe", lambda pe: pe.matmul(banks[b0], K_[0:96, cc * 128:(cc + 1) * 128], Q_[0:96, ql], start=True, stop=True), reads=[BK[hb], BQ[hb]], writes=[bbuf[b0]])
                        done += 1
                        yield done / total
                        P_, BP = PT[pi % 4], BPT[pi % 4]
                        pi += 1
                        S.op("act", lambda a: a.activation(P_[:], banks[b0], AF.Exp, scale=ATT_SCALE), reads=[bbuf[b0]], writes=[BP])
                        done += 1
                        yield done / total
                        pend.append((cc, P_, BP))
                    if cc >= 2:
                        pc, PP, BPP = pend.pop(0)
                        S.op("pe", lambda pe: pe.matmul(pO, V_[:, pc, :], PP[:], start=(pc == 0), stop=(pc == nkc - 1)), reads=[BV[hb], BPP], writes=[pbO])
                        done += 1
                        yield done / total
                O_, BO_ = og[oi % 2], Bog[oi % 2]
                oi += 1
                S.op("dve", lambda v: v.reciprocal(rd[64:128, :], pO[64:128, :]), reads=[pbO], writes=[Brd])
                S.op("dve", lambda v: v.tensor_tensor(O_[0:64, :], pO[0:64, :], rd[64:128, :], ALU.mult), reads=[pbO, Brd], writes=[BO_])
                S.dma("sp", OT_d[h * 64:(h + 1) * 64, t0 + qt * 512:t0 + (qt + 1) * 512], O_[0:64, :], reads=[BO_])
        yield 1.0


def make_p3(c):
    S, nc, sb, seqs, tok0, NS = c["S"], c["nc"], c["sb"], c["seqs"], c["tok0"], c["NS"]
    bank0 = c["bank"]
    bankA = lambda: bank0("a")
    bankB = lambda: bank0("b")
    identb, Ufb, Ubb, Lfb, Lbb, onesb, B_c, epst = c["identb"], c["Ufb"], c["Ubb"], c["Lfb"], c["Lbb"], c["onesb"], c["B_c"], c["epst"]
    xtok_d, bcT_d, dt_d, zs_d, prevb_d, ygT_d, alog, dskip = c["xtok_d"], c["bcT_d"], c["dt_d"], c["zs_d"], c["prevb_d"], c["ygT_d"], c["alog"], c["dskip"]
    Q = "pool"
    st = c["gstack"]
    two = lambda name, shape, dt: [sb(st, name, shape, dt) for _ in range(2)]
    a_bc = sb(st, "p3_a", [128, 64], F32)
    D_bc = sb(st, "p3_D", [128, 32], F32)
    Hs = sb(st, "p3_H", [128, 2048], F32)
    Hb16 = two("p3_Hb", [128, 2048], BF16)
    xts = two("p3_xt", [128, 3072], BF16)
    bcs = two("p3_bc", [128, 16, 128], BF16)
    dts_ = two("p3_dt", [128, 64], F32)
    zss = two("p3_zs", [128, 2048], BF16)
    pvs = two("p3_pv", [128, 2048], BF16)
    dA = sb(st, "p3_dA", [128, 64], F32)
    dAh = two("p3_dAh", [128, 64], BF16)
    dAhf = sb(st, "p3_dAhf", [128, 64], F32)
    dAl = two("p3_dAl", [128, 64], BF16)
    ct = sb(st, "p3_ct", [128, 128], F32)
    Et = sb(st, "p3_E", [128, 64], F32)
    wgt = sb(st, "p3_w", [128, 64], F32)
    ec = two("p3_ec", [128, 64], F32)
    dec = sb(st, "p3_dec", [128, 64], F32)
    xw = sb(st, "p3_xw", [128, 2048], BF16)
    xdt = [two("p3_xdt", [128, 2048], BF16) for _ in range(2)]
    Rs = two("p3_R", [128, 2, 8, 128], BF16)
    CBm = two("p3_CBm", [128, 2, 8, 128], BF16)
    Exs = [sb(st, "p3_Ex", [128, 512], BF16) for _ in range(3)]
    MTs = two("p3_MT", [128, 2, 8, 128], BF16)
    y = sb(st, "p3_y", [128, 2048], F32)
    T1 = sb(st, "p3_t1", [128, 512], F32)
    T2 = sb(st, "p3_t2", [128, 512], F32)
    T3 = sb(st, "p3_t3", [128, 512], F32)
    sqt = sb(st, "p3_sq", [128, 2048], BF16)
    gs = sb(st, "p3_gs", [128, 8], F32)
    ygn = sb(st, "p3_ygn", [128, 2048], BF16)
    ygs = sb(st, "p3_ygs", [128, 16, 128], BF16)
    B2 = lambda: [Buf(), Buf()]
    Bk, BH, Bprevd = Buf(), Buf(), Buf()
    BHb, Bxt, Bbc, Bdt, Bzs, Bpv = B2(), B2(), B2(), B2(), B2(), B2()
    BdA, BdAx, Bct, BE, Bec, Bxw, BCB = Buf(), B2(), Buf(), Buf(), B2(), Buf(), B2()
    Bxdt = [B2(), B2()]
    BRs, BEx, BMTs = B2(), [Buf() for _ in range(3)], B2()
    By, BT1, BT2, BT3, Bsq, Bgs, Bygn, Bygs = Buf(), Buf(), Buf(), Buf(), Buf(), Buf(), Buf(), Buf()
    S.dma(Q, a_bc[:], alog.partition_broadcast(128), writes=[Bk])
    S.dma(Q, D_bc[:], dskip.partition_broadcast(128), writes=[Bk])
    S.op("act", lambda a: a.activation(a_bc[:], a_bc[:], AF.Exp), reads=[Bk], writes=[Bk])
    S.op("dve", lambda v: v.tensor_scalar(a_bc[:], a_bc[:], -1.0, None, ALU.mult), reads=[Bk], writes=[Bk])
    nchs = [sq // 128 for sq in seqs]
    NCH = float(sum(nchs))
    gstart = [sum(nchs[:i]) for i in range(NS)]
    sh = {"front": 0, "back": 0}
    TA, TB = 70.0, 200.0

    def bc3(ap2, n):
        return ap2.unsqueeze(2).broadcast_to([128, ap2.shape[1], n])

    v3 = lambda ap: ap.rearrange("p (h f) -> p h f", f=64)

    def prep(k, dtt, Bd):
        S.op("dve", lambda v: v.tensor_tensor(dA[:], dtt[:], a_bc[:], ALU.mult), reads=[Bd, Bk], writes=[BdA])
        yield
        S.op("dve", lambda v: v.tensor_copy(dAh[k][:], dA[:]), reads=[BdA], writes=[BdAx[k]])
        yield
        S.op("dve", lambda v: v.tensor_copy(dAhf[:], dAh[k][:]), reads=[BdAx[k]], writes=[BdA])
        yield
        S.op("dve", lambda v: v.tensor_tensor(dAhf[:], dA[:], dAhf[:], ALU.subtract), reads=[BdA], writes=[BdA])
        yield
        S.op("dve", lambda v: v.tensor_copy(dAl[k][:], dAhf[:]), reads=[BdA], writes=[BdAx[k]])
        yield
        pc, pbc = bankA()
        for (o0, o1, L) in ((0, 32, Ufb), (32, 64, Ubb)):
            S.mm(pc[:, o0:o1], [(L, dAh[k][:, o0:o1]), (L, dAl[k][:, o0:o1])], reads=[BdAx[k], B_c], writes=[pbc])
            yield
        S.mm(pc[:, 64:128], [(onesb, dAh[k][:]), (onesb, dAl[k][:])], reads=[BdAx[k], B_c], writes=[pbc])
        yield
        S.op("dve", lambda v: v.tensor_copy(ct[:], pc[:, 0:128]), reads=[pbc], writes=[Bct])
        yield
        S.op("dve", lambda v: v.tensor_tensor(Et[:], ct[:, 64:128], ct[:, 0:64], ALU.subtract), reads=[Bct], writes=[BE])
        yield
        S.op("act", lambda a: a.activation(Et[:], Et[:], AF.Exp), reads=[BE], writes=[BE])
        yield
        S.op("dve", lambda v: v.tensor_tensor(wgt[:], dtt[:], Et[:], ALU.mult), reads=[BE, Bd], writes=[BE])
        yield
        S.op("act", lambda a: a.activation(dec[:], ct[:, 64:128], AF.Exp), reads=[Bct], writes=[BE])
        yield
        S.op("act", lambda a: a.activation(ec[k][:], ct[:, 0:64], AF.Exp), reads=[Bct], writes=[Bec[k]])
        yield

    def state_update(X, BX, d):
        S.op("dve", lambda v: v.tensor_tensor(v3(xw[:]), v3(X[:, 0:2048]), bc3(wgt[:, d * 32:(d + 1) * 32], 64), ALU.mult), reads=[BX, BE], writes=[Bxw])
        yield
        S.op("dve", lambda v: v.tensor_tensor(v3(Hs[:]), v3(Hs[:]), bc3(dec[:, d * 32:(d + 1) * 32], 64), ALU.mult), reads=[BE, BHb[0], BHb[1]], writes=[BH])
        yield
        for gp in range(4):
            ps, pbs = bankA()
            for gg in range(2):
                g = gp * 2 + gg
                S.op("pe", lambda pe: pe.matmul(ps[:, gg * 256:(gg + 1) * 256], X[:, 2048 + g * 128:2048 + (g + 1) * 128], xw[:, g * 256:(g + 1) * 256], start=True, stop=True), reads=[BX, Bxw], writes=[pbs])
                yield
            S.op("dve", lambda v: v.tensor_tensor(Hs[:, gp * 512:(gp + 1) * 512], Hs[:, gp * 512:(gp + 1) * 512], ps, ALU.add), reads=[pbs], writes=[BH])
            yield

    def genA():
        it = 0
        for si in range(NS):
            Sq, t0 = seqs[si], tok0[si]
            nch = nchs[si]
            base = gstart[si] / NCH
            while sh["back"] < gstart[si]:
                yield None
            S.op("pool", lambda v: v.memset(Hs[:], 0.0), reads=[BHb[0], BHb[1]], writes=[BH])
            for cidx in range(nch - 1, -1, -1):
                g0 = t0 + cidx * 128
                k = it % 2
                it += 1
                S.dma(Q, xts[k][:], xtok_d[g0:g0 + 128, :], writes=[Bxt[k]])
                S.dma(Q, dts_[k][:], dt_d[g0:g0 + 128, :], writes=[Bdt[k]])
                S.op("act", lambda a: a.copy(Hb16[k][:], Hs[:]), reads=[BH], writes=[BHb[k]])
                S.dma(Q, prevb_d[cidx], Hb16[k][:], reads=[BHb[k]], writes=[Bprevd])
                yield base
                if cidx > 0:
                    for _ in prep(k, dts_[k], Bdt[k]):
                        yield base
                    for _ in state_update(xts[k], Bxt[k], 1):
                        yield base
            S.op("pool", lambda v: v.memset(Hs[:], 0.0), reads=[BHb[0], BHb[1]], writes=[BH])
            for cidx in range(nch):
                gidx = gstart[si] + cidx
                k = gidx % 2
                n = [0]

                def pr():
                    n[0] += 1
                    return (gidx + min(n[0] / TA, 0.99)) / NCH
                while sh["back"] < gidx - 1:
                    yield None
                g0 = t0 + cidx * 128
                X, BX, BCt, BBC, dtt, Bd = xts[k], Bxt[k], bcs[k], Bbc[k], dts_[k], Bdt[k]
                S.dma(Q, X[:], xtok_d[g0:g0 + 128, :], writes=[BX])
                S.dma(Q, BCt[:], bcT_d[:, g0:g0 + 128].rearrange("(j p) t -> p j t", p=128), writes=[BBC])
                S.dma(Q, dtt[:], dt_d[g0:g0 + 128, :], writes=[Bd])
                S.dma(Q, zss[k][:], zs_d[g0:g0 + 128, :], writes=[Bzs[k]])
                S.dma(Q, pvs[k][:], prevb_d[cidx], reads=[Bprevd], writes=[Bpv[k]])
                S.op("act", lambda a: a.copy(Hb16[k][:], Hs[:]), reads=[BH], writes=[BHb[k]])
                yield pr()
                for _ in prep(k, dtt, Bd):
                    yield pr()
                if cidx < nch - 1:
                    for _ in state_update(X, BX, 0):
                        yield pr()
                for g4 in range(2):
                    pcb, pbcb = bankA()
                    for gg in range(4):
                        g = g4 * 4 + gg
                        S.op("pe", lambda pe: pe.matmul(pcb[:, gg * 128:(gg + 1) * 128], BCt[:, g, :], BCt[:, 8 + g, :], start=True, stop=True), reads=[BBC], writes=[pbcb])
                        yield pr()
                    for d, Um in ((0, Ufb), (1, Ubb)):
                        S.op("dve", lambda v: v.tensor_tensor(CBm[k][:, d, g4 * 4:(g4 + 1) * 4, :], pcb.rearrange("p (g l) -> p g l", l=128), Um.unsqueeze(1).broadcast_to([128, 4, 128]), ALU.mult), reads=[pbcb, B_c], writes=[BCB[k]])
                        yield pr()
                for d in range(2):
                    S.op("pool", lambda v: v.tensor_tensor(v3(xdt[k][d][:]), v3(X[:, 0:2048]), bc3(dtt[:, d * 32:(d + 1) * 32], 64), ALU.mult), reads=[BX, Bd], writes=[Bxdt[k][d]])
                    yield pr()
                sh["front"] = gidx + 1
        yield 1.0

    def genB():
        ei = [0]

        def stage2(k, gp):
            MT, BMT = MTs[gp % 2], BMTs[gp % 2]
            for d, Lm, Um in ((0, Lfb, Ufb), (1, Lbb, Ubb)):
                R, BR = Rs[d], BRs[d]
                h0 = d * 32 + gp * 8
                for kk, src in ((0, dAh[k]), (1, dAl[k])):
                    S.op("dve", lambda v: v.tensor_tensor(R[:, kk, :, :], Um.unsqueeze(1).broadcast_to([128, 8, 128]), bc3(src[:, h0:h0 + 8], 128), ALU.mult), reads=[BdAx[k], B_c], writes=[BR])
                    yield
                for gg in range(2):
                    g = gp * 2 + gg
                    pseg, pbseg = bankB()
                    S.mm(pseg, [(Lm, R[:, kk, gg * 4:(gg + 1) * 4, :].rearrange("p h l -> p (h l)")) for kk in range(2)], reads=[BR, B_c], writes=[pbseg])
                    yield
                    Ex, BE_ = Exs[ei[0] % 3], BEx[ei[0] % 3]
                    ei[0] += 1
                    S.op("act", lambda a: a.activation(Ex[:], pseg, AF.Exp), reads=[pbseg], writes=[BE_])
                    yield
                    S.op("dve", lambda v: v.tensor_tensor(MT[:, d, gg * 4:(gg + 1) * 4, :], Ex[:].rearrange("p (h l) -> p h l", l=128), CBm[k][:, d, g:g + 1, :].broadcast_to([128, 4, 128]), ALU.mult), reads=[BE_, BCB[k]], writes=[BMT])
                    yield

        def stage3(k, gp):
            MT, BMT = MTs[gp % 2], BMTs[gp % 2]
            X, BX, BCt, BBC = xts[k], Bxt[k], bcs[k], Bbc[k]
            S.op("pool", lambda v: v.tensor_tensor(v3(T3[:]), v3(X[:, gp * 512:(gp + 1) * 512]), bc3(D_bc[:, gp * 8:gp * 8 + 8], 64), ALU.mult), reads=[BX, Bk], writes=[BT3])
            yield
            py, pby = bankB()
            pof, pbof = bankB()
            pob, pbob = bankB()
            for gg in range(2):
                g = gp * 2 + gg
                for j in range(4):
                    h = 4 * g + j
                    S.mm(py[:, gg * 256 + j * 64:gg * 256 + (j + 1) * 64], [(MT[:, d, gg * 4 + j, :], xdt[k][d][:, h * 64:(h + 1) * 64]) for d in range(2)], reads=[BMT, Bxdt[k][0], Bxdt[k][1]], writes=[pby])
                    yield
                S.op("pe", lambda pe: pe.matmul(pof[:, gg * 256:(gg + 1) * 256], BCt[:, 8 + g, :], Hb16[k][:, g * 256:(g + 1) * 256], start=True, stop=True), reads=[BBC, BHb[k]], writes=[pbof])
                yield
                S.op("pe", lambda pe: pe.matmul(pob[:, gg * 256:(gg + 1) * 256], BCt[:, 8 + g, :], pvs[k][:, g * 256:(g + 1) * 256], start=True, stop=True), reads=[BBC, Bpv[k]], writes=[pbob])
                yield
            S.op("dve", lambda v: v.tensor_tensor(v3(T1[:]), v3(pof), bc3(ec[k][:, gp * 8:gp * 8 + 8], 64), ALU.mult), reads=[pbof, Bec[k]], writes=[BT1])
            yield
            S.op("dve", lambda v: v.tensor_tensor(v3(T2[:]), v3(pob), bc3(ec[k][:, 32 + gp * 8:32 + gp * 8 + 8], 64), ALU.mult), reads=[pbob, Bec[k]], writes=[BT2])
            yield
            S.op("pool", lambda v: v.tensor_tensor(T2[:], T2[:], T3[:], ALU.add), reads=[BT3], writes=[BT2])
            yield
            S.op("dve", lambda v: v.tensor_tensor(T1[:], T1[:], py, ALU.add), reads=[pby], writes=[BT1])
            yield
            S.op("dve", lambda v: v.tensor_tensor(y[:, gp * 512:(gp + 1) * 512], T1[:], T2[:], ALU.add), reads=[BT1, BT2], writes=[By])
            yield

        for si in range(NS):
            Sq, t0 = seqs[si], tok0[si]
            for cidx in range(nchs[si]):
                gidx = gstart[si] + cidx
                k = gidx % 2
                n = [0]

                def pr():
                    n[0] += 1
                    return (gidx + min(n[0] / TB, 0.99)) / NCH
                while sh["front"] <= gidx:
                    yield None
                g0 = t0 + cidx * 128
                for _ in stage2(k, 0):
                    yield pr()
                for gp in range(4):
                    if gp + 1 < 4:
                        for _ in stage2(k, gp + 1):
                            yield pr()
                    for _ in stage3(k, gp):
                        yield pr()
                Z, BZ = zss[k], Bzs[k]
                S.op("dve", lambda v: v.tensor_tensor(y[:], y[:], Z[:], ALU.mult), reads=[BZ], writes=[By])
                yield pr()
                S.op("dve", lambda v: v.tensor_tensor(sqt[:], y[:], y[:], ALU.mult), reads=[By], writes=[Bsq])
                yield pr()
                S.op("dve", lambda v: v.tensor_reduce(gs[:], sqt[:].rearrange("p (g f) -> p g f", f=256), AX.X, ALU.add), reads=[Bsq], writes=[Bgs])
                yield pr()
                S.op("act", lambda a: a.activation(gs[:], gs[:], AF.Sqrt, bias=epst[:], scale=1.0 / 256), reads=[Bgs, B_c], writes=[Bgs])
                yield pr()
                S.op("dve", lambda v: v.reciprocal(gs[:], gs[:]), reads=[Bgs], writes=[Bgs])
                yield pr()
                S.op("dve", lambda v: v.tensor_tensor(ygn[:].rearrange("p (g f) -> p g f", f=256), y[:].rearrange("p (g f) -> p g f", f=256), bc3(gs[:], 256), ALU.mult), reads=[By, Bgs], writes=[Bygn])
                yield pr()
                for hf in range(2):
                    pt, pbt = bankB()
                    ptb = pt.bitcast(BF16)
                    for jj in range(8):
                        j = hf * 8 + jj
                        S.op("pe", lambda pe: pe.transpose(ptb[:, jj * 128:(jj + 1) * 128], ygn[:, j * 128:(j + 1) * 128], identb), reads=[Bygn, B_c], writes=[pbt])
                        yield pr()
                    S.op("act", lambda a: a.copy(ygs[:, hf * 8:(hf + 1) * 8, :], ptb[:, 0:1024].rearrange("p (j t) -> p j t", t=128)), reads=[pbt], writes=[Bygs])
                    yield pr()
                S.dma(Q, ygT_d[:, g0:g0 + 128].rearrange("(j p) t -> p j t", p=128), ygs[:], reads=[Bygs])
                sh["back"] = gidx + 1
                yield pr()
        yield 1.0

    return genA(), genB()


def emit_p4a(c):
    from contextlib import ExitStack
    S, nc, sb, bank, seqs, tok0, NS = c["S"], c["nc"], c["sb"], c["bank"], c["seqs"], c["tok0"], c["NS"]
    identb, B_c, epst = c["identb"], c["B_c"], c["epst"]
    ygT_d, OT_d, gT_d, x_d, x1_d, h2T_d, gate_d, adaT_d = c["ygT_d"], c["OT_d"], c["gT_d"], c["x_d"], c["x1_d"], c["h2T_d"], c["gate_d"], c["adaT_d"]
    w_ssd_out, w_mla_out, w_o, g_ssd = c["w_ssd_out"], c["w_mla_out"], c["w_o"], c["g_ssd"]
    with ExitStack() as st:
        ws = sb(st, "p4_ws", [128, 16, 1024], BF16)
        wm = sb(st, "p4_wm", [128, 8, 1024], BF16)
        wo = sb(st, "p4_wo", [128, 8, 1024], BF16)
        gsn = sb(st, "p4_gsn", [128, 16], F32)
        ygl = [sb(st, "p4_yg", [128, 16, 512], BF16) for _ in range(2)]
        otl = [sb(st, "p4_ot", [128, 8, 512], BF16) for _ in range(2)]
        Bygl, Botl = [Buf(), Buf()], [Buf(), Buf()]
        tcount = [0]
        gt = sb(st, "p4_gt", [128, 16, 512], BF16)
        xt = sb(st, "p4_x", [128, 4, 1024], F32)
        mixf = sb(st, "p4_mixf", [128, 512], F32)
        mixf2 = sb(st, "p4_mixf2", [128, 512], F32)
        mix = sb(st, "p4_mix", [128, 8, 512], BF16)
        g1 = sb(st, "p4_g1", [128, 1024], F32)
        ab = sb(st, "p4_ab", [128, 2, 8], F32)
        x1 = sb(st, "p4_x1", [128, 4, 1024], F32)
        junk = sb(st, "p4_junk", [128, 1024], BF16)
        ss = sb(st, "p4_ss", [128, 4], F32)
        xn = sb(st, "p4_xn", [128, 4, 1024], BF16)
        h2 = sb(st, "p4_h2", [128, 8, 512], BF16)
        Bw, Byg, Bot, Bgt, Bx, Bmf, Bmf2, Bmix, Bg1, Bab, Bx1, Bss, Bxn, Bh2 = (Buf() for _ in range(14))
        S.dma("pool", ws[:], w_ssd_out.rearrange("(kc p) f -> p kc f", p=128), writes=[Bw])
        S.dma("pool", wm[:], w_mla_out.rearrange("(kc p) f -> p kc f", p=128), writes=[Bw])
        S.dma("pool", wo[:], w_o.rearrange("(kc p) f -> p kc f", p=128), writes=[Bw])
        with nc.allow_non_contiguous_dma(reason="tiny"):
            S.dma("sp", gsn[:], g_ssd.rearrange("(j p) -> p j", p=128), writes=[Bw])
        for kc in range(16):
            S.op("dve", lambda v: v.tensor_scalar(ws[:, kc, :], ws[:, kc, :], gsn[:, kc:kc + 1], None, ALU.mult), reads=[Bw], writes=[Bw])
        for si in range(NS):
            S.dma("sp", g1[:], gate_d[si, 0, :].partition_broadcast(128), writes=[Bg1])
            with nc.allow_non_contiguous_dma(reason="tiny"):
                S.dma("sp", ab[:, 0, :], adaT_d[si, 2, :].rearrange("(kc p) -> p kc", p=128), writes=[Bab])
                S.dma("sp", ab[:, 1, :], adaT_d[si, 3, :].rearrange("(kc p) -> p kc", p=128), writes=[Bab])
            for ti in range(seqs[si] // 512):
                g0 = tok0[si] + ti * 512
                yg, ot, Byg, Bot = ygl[tcount[0] % 2], otl[tcount[0] % 2], Bygl[tcount[0] % 2], Botl[tcount[0] % 2]
                tcount[0] += 1
                S.dma("sp", yg[:], ygT_d[:, g0:g0 + 512].rearrange("(j p) t -> p j t", p=128), writes=[Byg])
                S.dma("sp", ot[:], OT_d[:, g0:g0 + 512].rearrange("(j p) t -> p j t", p=128), writes=[Bot])
                S.dma("sp", gt[:], gT_d[:, g0:g0 + 512].rearrange("(j p) t -> p j t", p=128), writes=[Bgt])
                S.dma("sp", xt[:], x_d[g0:g0 + 512, :].rearrange("(s p) f -> p s f", p=128), writes=[Bx])
                for oc in range(8):
                    pa, pba = bank()
                    S.mm(pa, [(ws[:, kc, oc * 128:(oc + 1) * 128], yg[:, kc, :]) for kc in range(16)], reads=[Bw, Byg], writes=[pba])
                    pm, pbm = bank()
                    S.mm(pm, [(wm[:, kc, oc * 128:(oc + 1) * 128], ot[:, kc, :]) for kc in range(8)], reads=[Bw, Bot], writes=[pbm])
                    S.op("dve", lambda v: v.tensor_tensor(mixf[:], pa[:], gt[:, oc, :], ALU.mult), reads=[pba, Bgt], writes=[Bmf])
                    S.op("dve", lambda v: v.tensor_tensor(mixf2[:], pm[:], gt[:, 8 + oc, :], ALU.mult), reads=[pbm, Bgt], writes=[Bmf2])
                    S.op("pool", lambda v: v.tensor_tensor(mix[:, oc, :], mixf[:], mixf2[:], ALU.add), reads=[Bmf, Bmf2], writes=[Bmix])
                for s4 in range(4):
                    for hf in range(2):
                        po, pbo = bank()
                        S.mm(po, [(mix[:, kc, s4 * 128:(s4 + 1) * 128], wo[:, kc, hf * 512:(hf + 1) * 512]) for kc in range(8)], reads=[Bw, Bmix], writes=[pbo])
                        S.op("dve", lambda v: v.tensor_tensor(x1[:, s4, hf * 512:(hf + 1) * 512], po[:], g1[:, hf * 512:(hf + 1) * 512], ALU.mult), reads=[pbo, Bg1], writes=[Bx1])
                    S.op("pool", lambda v: v.tensor_tensor(x1[:, s4, :], x1[:, s4, :], xt[:, s4, :], ALU.add), reads=[Bx], writes=[Bx1])
                    S.op("act", lambda a: a.activation(junk[:], x1[:, s4, :], AF.Square, accum_out=ss[:, s4:s4 + 1]), reads=[Bx1], writes=[Bss])
                S.dma("st", x1_d[g0:g0 + 512, :].rearrange("(s p) f -> p s f", p=128), x1[:], reads=[Bx1])
                S.op("act", lambda a: a.activation(ss[:], ss[:], AF.Sqrt, bias=epst[:], scale=1.0 / 1024), reads=[Bss, B_c], writes=[Bss])
                S.op("dve", lambda v: v.reciprocal(ss[:], ss[:]), reads=[Bss], writes=[Bss])
                for s4 in range(4):
                    S.op("pool" if s4 % 2 else "dve", lambda v: v.tensor_scalar(xn[:, s4, :], x1[:, s4, :], ss[:, s4:s4 + 1], None, ALU.mult), reads=[Bx1, Bss], writes=[Bxn])
                for kc in range(8):
                    pt, pbt = bank()
                    ptb = pt[:].bitcast(BF16)
                    for s4 in range(4):
                        S.op("pe", lambda pe: pe.transpose(ptb[:, s4 * 128:(s4 + 1) * 128], xn[:, s4, kc * 128:(kc + 1) * 128], identb), reads=[Bxn, B_c], writes=[pbt])
                    S.op("dve", lambda v: v.tensor_scalar(h2[:, kc, :], ptb[:, 0:512], ab[:, 0, kc:kc + 1], ab[:, 1, kc:kc + 1], ALU.mult, ALU.add), reads=[pbt, Bab], writes=[Bh2])
                S.dma("st", h2T_d[:, g0:g0 + 512].rearrange("(j p) t -> p j t", p=128), h2[:], reads=[Bh2])
        S.barrier()


def emit_p4b(c):
    from contextlib import ExitStack
    S, nc, sb, bank, seqs, tok0, NS = c["S"], c["nc"], c["sb"], c["bank"], c["seqs"], c["tok0"], c["NS"]
    B_c, epst = c["B_c"], c["epst"]
    x1_d, h2T_d, gate_d, y_d, w_mlp_in, w_mlp_out, g_final = c["x1_d"], c["h2T_d"], c["gate_d"], c["y_d"], c["w_mlp_in"], c["w_mlp_out"], c["g_final"]
    TT = 256
    with ExitStack() as st:
        w1 = sb(st, "p5_w1", [128, 8, 4096], BF16)
        w2 = sb(st, "p5_w2", [128, 32, 1024], BF16)
        gf = sb(st, "p5_gf", [128, 1024], F32)
        g2 = sb(st, "p5_g2", [128, 1024], F32)
        h2 = [sb(st, "p5_h2", [128, 8, TT], BF16) for _ in range(2)]
        x1 = [sb(st, "p5_x1", [128, 2, 1024], F32) for _ in range(2)]
        rl = [sb(st, "p5_rl", [128, TT], F32) for _ in range(2)]
        rT = sb(st, "p5_rT", [128, 32, TT], BF16)
        x2 = sb(st, "p5_x2", [128, 2, 1024], F32)
        junk = sb(st, "p5_junk", [128, 1024], BF16)
        ss = sb(st, "p5_ss", [128, 2], F32)
        yo = sb(st, "p5_yo", [128, 2, 1024], F32)
        Bw, Bg2, Bh2, Bx1, Brl, BrT, Bx2, Bss, Byo = Buf(), Buf(), [Buf(), Buf()], [Buf(), Buf()], [Buf(), Buf()], Buf(), Buf(), Buf(), Buf()
        S.dma("pool", w1[:], w_mlp_in.rearrange("(kc p) f -> p kc f", p=128), writes=[Bw])
        for q4 in range(4):
            S.dma("pool", w2[:, q4 * 8:(q4 + 1) * 8, :], w_mlp_out[q4 * 1024:(q4 + 1) * 1024, :].rearrange("(kc p) f -> p kc f", p=128), writes=[Bw])
        S.dma("sp", gf[:], g_final.partition_broadcast(128), writes=[Bw])
        it = 0
        for si in range(NS):
            S.dma("sp", g2[:], gate_d[si, 1, :].partition_broadcast(128), writes=[Bg2])
            for ti in range(seqs[si] // TT):
                g0 = tok0[si] + ti * TT
                k = it % 2
                it += 1
                H, X1 = h2[k], x1[k]
                S.dma("sp", H[:], h2T_d[:, g0:g0 + TT].rearrange("(j p) t -> p j t", p=128), writes=[Bh2[k]])
                S.dma("sp", X1[:], x1_d[g0:g0 + TT, :].rearrange("(s p) f -> p s f", p=128), writes=[Bx1[k]])
                for fc in range(32):
                    pf, pbf = bank()
                    S.mm(pf[:, 0:TT], [(w1[:, kc, fc * 128:(fc + 1) * 128], H[:, kc, :]) for kc in range(8)], reads=[Bw, Bh2[k]], writes=[pbf])
                    RL, BRL = rl[fc % 2], Brl[fc % 2]
                    S.op("act", lambda a: a.activation(RL[:], pf[:, 0:TT], AF.Relu), reads=[pbf], writes=[BRL])
                    S.op("pool" if fc % 2 else "dve", lambda v: v.tensor_tensor(rT[:, fc, :], RL[:], RL[:], ALU.mult), reads=[BRL], writes=[BrT])
                for s2 in range(2):
                    for hf in range(2):
                        po, pbo = bank()
                        S.mm(po, [(rT[:, kc, s2 * 128:(s2 + 1) * 128], w2[:, kc, hf * 512:(hf + 1) * 512]) for kc in range(32)], reads=[Bw, BrT], writes=[pbo])
                        S.op("dve", lambda v: v.tensor_tensor(x2[:, s2, hf * 512:(hf + 1) * 512], po[:], g2[:, hf * 512:(hf + 1) * 512], ALU.mult), reads=[pbo, Bg2], writes=[Bx2])
                    S.op("pool", lambda v: v.tensor_tensor(x2[:, s2, :], x2[:, s2, :], X1[:, s2, :], ALU.add), reads=[Bx1[k]], writes=[Bx2])
                    S.op("act", lambda a: a.activation(junk[:], x2[:, s2, :], AF.Square, accum_out=ss[:, s2:s2 + 1]), reads=[Bx2], writes=[Bss])
                S.op("act", lambda a: a.activation(ss[:], ss[:], AF.Sqrt, bias=epst[:], scale=1.0 / 1024), reads=[Bss, B_c], writes=[Bss])
                S.op("dve", lambda v: v.reciprocal(ss[:], ss[:]), reads=[Bss], writes=[Bss])
                for s2 in range(2):
                    S.op("dve", lambda v: v.scalar_tensor_tensor(yo[:, s2, :], x2[:, s2, :], ss[:, s2:s2 + 1], gf[:], ALU.mult, ALU.mult), reads=[Bx2, Bss, Bw], writes=[Byo])
                S.dma("st", y_d[g0:g0 + TT, :].rearrange("(s p) f -> p s f", p=128), yo[:], reads=[Byo])
        S.barrier()


def core_inputs(x_all, c_all, W, g_final, cos, sin):
    f = lambda a: np.ascontiguousarray(np.asarray(a, dtype=np.float32))
    return {
        "x": f(x_all), "c": f(c_all),
        "w_ada": f(W["w_ada"]), "b_ada": f(W["b_ada"]), "g_norm1": f(W["g_norm1"]), "w_in": f(W["w_in"]),
        "conv_w": f(W["conv_w"]), "conv_b": f(W["conv_b"]),
        "dt_bias": f(np.concatenate([W["dt_bias_fwd"], W["dt_bias_bwd"]])),
        "a_log": f(np.concatenate([W["a_log_fwd"], W["a_log_bwd"]])),
        "d_skip": f(W["d_skip"]), "g_ssd_norm": f(W["g_ssd_norm"]), "w_ssd_out": f(W["w_ssd_out"]),
        "g_q_norm": f(W["g_q_norm"]), "w_q_b": f(W["w_q_b"]), "g_kv_norm": f(W["g_kv_norm"]), "w_kv_b": f(W["w_kv_b"]),
        "w_mla_out": f(W["w_mla_out"]), "w_o": f(W["w_o"]), "g_norm2": f(W["g_norm2"]),
        "w_mlp_in": f(W["w_mlp_in"]), "w_mlp_out": f(W["w_mlp_out"]), "g_final": f(g_final),
        "consts": make_consts(), "cos_t": f(cos), "sin_t": f(sin),
    }


_WNAMES = ["w_ada", "b_ada", "g_norm1", "w_in", "conv_w", "conv_b", "dt_bias_fwd", "dt_bias_bwd", "a_log_fwd",
           "a_log_bwd", "d_skip", "g_ssd_norm", "w_ssd_out", "g_q_norm", "w_q_b", "g_kv_norm", "w_kv_b",
           "w_mla_out", "w_o", "g_norm2", "w_mlp_in", "w_mlp_out"]


def kernel(x_prompt, x_sample, c_prompt, c_sample, g_final, **kw):
    W = {k: np.asarray(kw[k])[0] for k in _WNAMES}
    x_prompt = np.asarray(x_prompt); x_sample = np.asarray(x_sample)
    c_prompt = np.asarray(c_prompt); c_sample = np.asarray(c_sample)
    n = 8
    seqs = (2048, 2048, 4096, 4096)
    cos, sin = rope_tables(4096)
    import os
    ph = os.environ.get("KPHASES")
    nc = build(seqs=seqs, phases=tuple(ph.split(","))) if ph else build(seqs=seqs)
    in_maps = []
    for i in range(n):
        xa = np.concatenate([x_prompt[2 * i].reshape(-1, D), x_prompt[2 * i + 1].reshape(-1, D),
                             x_sample[2 * i].reshape(-1, D), x_sample[2 * i + 1].reshape(-1, D)], axis=0)
        ca = np.stack([c_prompt[2 * i], c_prompt[2 * i + 1], c_sample[2 * i], c_sample[2 * i + 1]], axis=0)
        in_maps.append(core_inputs(xa, ca, W, g_final, cos, sin))
    res = run_bass_kernel_spmd(nc, in_maps, core_ids=list(range(n)))
    yp = np.zeros((16, 2048, D), np.float32)
    ys = np.zeros((16, 4096, D), np.float32)
    for i in range(n):
        y = np.asarray(res.results[i]["y"])
        yp[2 * i] = y[0:2048]
        yp[2 * i + 1] = y[2048:4096]
        ys[2 * i] = y[4096:8192]
        ys[2 * i + 1] = y[8192:12288]
    return (yp, ys)
```

```python
import numpy as np
import concourse.bass as bass
import concourse.mybir as mybir
from concourse.bass_utils import run_bass_kernel_spmd

F32 = mybir.dt.float32
BF16 = mybir.dt.bfloat16
AF = mybir.ActivationFunctionType
ALU = mybir.AluOpType
AX = mybir.AxisListType

D = 1024
DI = 2048
NH = 32
DIN = 8928
EPS = 1e-6
C_Z, C_X, C_DT, C_QA, C_KV, C_KR, C_G = 0, 2048, 6144, 6208, 6592, 6848, 6880
ATT_SCALE = 96 ** -0.5


class Buf:
    __slots__ = ("w", "r")

    def __init__(self):
        self.w = None
        self.r = {}


class Eng:
    def __init__(self, key, h, sem):
        self.key, self.h, self.sem = key, h, sem
        self.count = 0
        self.waited = {}
        self.slots = []
        self.dma_i = 0


class Slot:
    def __init__(self, key, sem):
        self.key, self.sem, self.count = key, sem, 0


class Sch:
    def __init__(self, nc, stack, nslots=12):
        self.nc = nc
        self.E = {}
        for key, h in (("pe", nc.tensor), ("act", nc.scalar), ("dve", nc.vector),
                       ("pool", nc.gpsimd), ("sp", nc.sync)):
            sem = stack.enter_context(nc.semaphore("sem_" + key))
            self.E[key] = Eng(key, h, sem)
        for q in ("sp", "pool", "act"):
            for i in range(nslots):
                sem = stack.enter_context(nc.semaphore("dq_%s_%d" % (q, i)))
                self.E[q].slots.append(Slot("dq_%s_%d" % (q, i), sem))

    def _wait(self, e, deps):
        for (key, sem, val) in deps:
            if key == e.key and key == "pe":
                continue
            if e.waited.get(key, 0) >= val:
                continue
            e.h.wait_ge(sem, val)
            e.waited[key] = val

    @staticmethod
    def _deps(reads, writes):
        deps = []
        for b in reads:
            if b.w is not None:
                deps.append(b.w)
        for b in writes:
            if b.w is not None:
                deps.append(b.w)
            for k, (s, v) in b.r.items():
                deps.append((k, s, v))
        return deps

    @staticmethod
    def _mark(tok, reads, writes):
        for b in reads:
            b.r[tok[0]] = (tok[1], tok[2])
        for b in writes:
            b.w = tok
            b.r = {}

    def op(self, eng, fn, reads=(), writes=()):
        e = self.E[eng]
        self._wait(e, self._deps(reads, writes))
        ins = fn(e.h)
        e.count += 1
        ins.then_inc(e.sem, 1)
        self._mark((e.key, e.sem, e.count), reads, writes)

    def mm(self, out, pairs, reads=(), writes=()):
        n = len(pairs)

        def fn(pe):
            ins = None
            for i, (l, r) in enumerate(pairs):
                ins = pe.matmul(out, l, r, start=(i == 0), stop=(i == n - 1))
            return ins
        self.op("pe", fn, reads=reads, writes=writes)

    def dma(self, q, out, in_, reads=(), writes=()):
        if q == "st":
            q = "pool"
        e = self.E[q]
        self._wait(e, self._deps(reads, writes))
        sl = e.slots[e.dma_i % len(e.slots)]
        e.dma_i += 1
        if sl.count > 0:
            self._wait(e, [(sl.key, sl.sem, 16 * sl.count)])
        e.h.dma_start(out=out, in_=in_).then_inc(sl.sem, 16)
        sl.count += 1
        self._mark((sl.key, sl.sem, 16 * sl.count), reads, writes)

    def barrier(self):
        toks = []
        for e in self.E.values():
            if e.count:
                toks.append((e.key, e.sem, e.count))
            for sl in e.slots:
                if sl.count:
                    toks.append((sl.key, sl.sem, 16 * sl.count))
        for e in self.E.values():
            self._wait(e, toks)


def make_consts():
    i = np.arange(128)
    c = np.zeros((128, 6, 128), np.float32)
    c[:, 0, :] = np.eye(128)
    c[:, 1, :] = (i[:, None] <= i[None, :])
    c[:, 2, :] = (i[:, None] >= i[None, :])
    c[:, 3, :] = (i[:, None] > i[None, :])
    c[:, 4, :] = (i[:, None] < i[None, :])
    c[:, 5, :] = 1.0
    return c.reshape(128, 768)


def rope_tables(smax):
    inv = (1.0 / (np.float32(10000.0) ** (np.arange(0, 32, 2, dtype=np.float32) / np.float32(32)))).astype(np.float32)
    ang = np.arange(smax, dtype=np.float32)[:, None] * inv[None, :]
    cos = np.cos(ang).astype(np.float32).T
    sin = np.sin(ang).astype(np.float32).T
    return (np.ascontiguousarray(np.concatenate([cos, cos], 0)),
            np.ascontiguousarray(np.concatenate([sin, sin], 0)))


def build(seqs=(2048, 2048, 4096, 4096), phases=("p0", "p1a", "p1b", "p1c", "p1d", "p2", "p3", "p4a", "p4b"), debug=False):
    nc = bass.Bass("TRN2", target_bir_lowering=False)
    NS = len(seqs)
    NT = sum(seqs)
    SMAX = max(seqs)
    tok0 = [sum(seqs[:i]) for i in range(NS)]
    skind = "ExternalOutput" if debug else "Internal"

    def din(name, shape, dt=F32):
        return nc.dram_tensor(name, list(shape), dt, kind="ExternalInput").ap()

    def dscr(name, shape, dt):
        return nc.dram_tensor(name, list(shape), dt, kind=skind).ap()

    x_d = din("x", [NT, D])
    c_d = din("c", [NS, D])
    w_ada = din("w_ada", [D, 6 * D])
    b_ada = din("b_ada", [6 * D])
    g_norm1 = din("g_norm1", [D])
    w_in = din("w_in", [D, DIN])
    conv_w = din("conv_w", [5, 4096])
    conv_b = din("conv_b", [4096])
    dtb = din("dt_bias", [64])
    alog = din("a_log", [64])
    dskip = din("d_skip", [32])
    g_ssd = din("g_ssd_norm", [DI])
    w_ssd_out = din("w_ssd_out", [DI, D])
    g_q = din("g_q_norm", [384])
    w_q_b = din("w_q_b", [384, 1536])
    g_kv = din("g_kv_norm", [256])
    w_kv_b = din("w_kv_b", [256, 2048])
    w_mla_out = din("w_mla_out", [D, D])
    w_o = din("w_o", [D, D])
    g_norm2 = din("g_norm2", [D])
    w_mlp_in = din("w_mlp_in", [D, 4 * D])
    w_mlp_out = din("w_mlp_out", [4 * D, D])
    g_final = din("g_final", [D])
    consts_d = din("consts", [128, 768])
    cos_d = din("cos_t", [32, SMAX])
    sin_d = din("sin_t", [32, SMAX])

    y_d = nc.dram_tensor("y", [NT, D], F32, kind="ExternalOutput").ap()

    adaT_d = dscr("adaT_s", [NS, 4, D], F32)
    gate_d = dscr("gate_s", [NS, 2, D], F32)
    uT_d = dscr("uT_s", [4096, NT], BF16)
    zs_d = dscr("zs_s", [NT, DI], BF16)
    dt_d = dscr("dt_s", [NT, 64], F32)
    qnT_d = dscr("qnT_s", [384, NT], BF16)
    cnT_d = dscr("cnT_s", [256, NT], BF16)
    kr_d = dscr("kr_s", [32, NT], BF16)
    gT_d = dscr("gT_s", [2048, NT], BF16)
    xtok_d = dscr("xtok_s", [NT, 3072], BF16)
    bcT_d = dscr("bcT_s", [2048, NT], BF16)
    KT_d = dscr("KT_s", [1024, NT], BF16)
    QT_d = dscr("QT_s", [1536, NT], BF16)
    V_d = dscr("V_s", [16, NT * 64], BF16)
    OT_d = dscr("OT_s", [D, NT], BF16)
    prevb_d = dscr("prevb_s", [SMAX // 128, 128, DI], BF16)
    ygT_d = dscr("ygT_s", [DI, NT], BF16)
    x1_d = dscr("x1_s", [NT, D], F32)
    h2T_d = dscr("h2T_s", [D, NT], BF16)

    from contextlib import ExitStack
    with ExitStack() as top:
        S = Sch(nc, top)
        _uid = [0]

        def sb(st, name, shape, dt):
            _uid[0] += 1
            return st.enter_context(nc.sbuf_tensor("%s_%d" % (name, _uid[0]), list(shape), dt))
        psall = top.enter_context(nc.psum_tensor("psall", [128, 4096], F32))
        banks = [psall[:, i * 512:(i + 1) * 512] for i in range(8)]
        bbuf = [Buf() for _ in range(8)]
        bi = {None: 0, "a": 0, "b": 0}
        pools = {None: list(range(8)), "a": [4], "b": [5, 6, 7]}

        def bank(pool=None):
            lst = pools[pool]
            i = lst[bi[pool] % len(lst)]
            bi[pool] += 1
            return banks[i], bbuf[i]

        cst32 = sb(top, "cst32", [128, 768], F32)
        cstb = sb(top, "cstb", [128, 768], BF16)
        epst = sb(top, "epst", [128, 1], F32)
        onet = sb(top, "onet", [128, 1], F32)
        B_c = Buf()
        S.dma("sp", cst32[:], consts_d, writes=[B_c])
        S.op("dve", lambda v: v.tensor_copy(cstb[:], cst32[:]), reads=[B_c], writes=[B_c])
        S.op("dve", lambda v: v.memset(epst[:], EPS), writes=[B_c])
        S.op("dve", lambda v: v.memset(onet[:], 1.0), writes=[B_c])
        identb = cstb[:, 0:128]
        Ufb, Ubb, Lfb, Lbb, onesb = (cstb[:, 128 * k:128 * (k + 1)] for k in range(1, 6))
        Uf32, Ub32 = cst32[:, 128:256], cst32[:, 256:384]
        ones32 = cst32[:, 640:768]

        def rstd_from_ss(eng_list, out, ss, n, rb, wb):
            S.op("act", lambda a: a.activation(out, ss, AF.Sqrt, bias=epst[0:out.shape[0], :], scale=1.0 / n), reads=rb + [B_c], writes=wb)
            S.op("dve", lambda v: v.reciprocal(out, out), reads=wb, writes=wb)

        if "p0" in phases:
            with ExitStack() as st:
                cT = sb(st, "p0_cT", [128, 8, NS], F32)
                cbc = sb(st, "p0_cbc", [128, 8, NS, 128], F32)
                wa = [sb(st, "p0_wa%d" % i, [128, 8, 1024], F32) for i in range(2)]
                bT = sb(st, "p0_bT", [128, 48], F32)
                brow = sb(st, "p0_brow", [128, 2, 1024], F32)
                gT1 = sb(st, "p0_g", [128, 2, 8], F32)
                res = sb(st, "p0_res", [128, 6, 8, NS], F32)
                vec = sb(st, "p0_vec", [128, NS, 4, 8], F32)
                grow = sb(st, "p0_grow", [128, 1024], F32)
                Bc, Bw, Bb, Br, Bv, Bg = Buf(), [Buf(), Buf()], Buf(), Buf(), Buf(), Buf()
                with nc.allow_non_contiguous_dma(reason="tiny transposed loads"):
                    for b in range(NS):
                        S.dma("sp", cT[:, :, b], c_d[b, :].rearrange("(kc p) -> p kc", p=128), writes=[Bc])
                    S.dma("sp", bT[:], b_ada.rearrange("(j p) -> p j", p=128), writes=[Bb])
                    S.dma("sp", gT1[:, 0, :], g_norm1.rearrange("(j p) -> p j", p=128), writes=[Bb])
                    S.dma("sp", gT1[:, 1, :], g_norm2.rearrange("(j p) -> p j", p=128), writes=[Bb])
                S.dma("sp", brow[:, 0, :], b_ada[2048:3072].partition_broadcast(128), writes=[Bb])
                S.dma("sp", brow[:, 1, :], b_ada[5120:6144].partition_broadcast(128), writes=[Bb])
                S.op("act", lambda a: a.activation(cT[:], cT[:], AF.Silu), reads=[Bc], writes=[Bc])
                S.op("dve", lambda v: v.tensor_copy(cbc[:], cT[:].unsqueeze(3).broadcast_to([128, 8, NS, 128])), reads=[Bc], writes=[Bc])
                for j in range(6):
                    w = wa[j % 2]
                    S.dma("sp", w[:], w_ada[:, j * 1024:(j + 1) * 1024].rearrange("(kc p) f -> p kc f", p=128), writes=[Bw[j % 2]])
                    pt, pb = bank()
                    for oc in range(8):
                        for kc in range(8):
                            S.op("pe", lambda pe, oc=oc, kc=kc: pe.matmul(pt[:, oc * NS:(oc + 1) * NS], w[:, kc, oc * 128:(oc + 1) * 128], cT[:, kc, :], start=(kc == 0), stop=(kc == 7)),
                                 reads=[Bw[j % 2], Bc], writes=[pb])
                    S.op("dve", lambda v, j=j: v.tensor_tensor(res[:, j, :, :], pt[:, 0:8 * NS].rearrange("p (o b) -> p o b", b=NS),
                                                           bT[:, j * 8:(j + 1) * 8].unsqueeze(2).broadcast_to([128, 8, NS]), ALU.add),
                         reads=[pb, Bb], writes=[Br])
                    if j in (2, 5):
                        gi = 0 if j == 2 else 1
                        for b in range(NS):
                            for hf in range(2):
                                pt2, pb2 = bank()
                                for kc in range(8):
                                    S.op("pe", lambda pe, kc=kc, b=b, hf=hf: pe.matmul(pt2[:], cbc[:, kc, b, :], w[:, kc, hf * 512:(hf + 1) * 512], start=(kc == 0), stop=(kc == 7)),
                                         reads=[Bw[j % 2], Bc], writes=[pb2])
                                S.op("dve", lambda v, hf=hf, gi=gi: v.tensor_tensor(grow[:, hf * 512:(hf + 1) * 512], pt2[:], brow[:, gi, hf * 512:(hf + 1) * 512], ALU.add),
                                     reads=[pb2, Bb], writes=[Bg])
                            S.dma("st", gate_d[b, gi, :], grow[0:1, :], reads=[Bg])
                for b in range(NS):
                    for (k, jsc, jsh, gi) in ((0, 1, 0, 0), (2, 4, 3, 1)):
                        S.op("dve", lambda v, b=b, k=k, jsc=jsc, gi=gi: v.scalar_tensor_tensor(vec[:, b, k, :], res[:, jsc, :, b], 1.0, gT1[:, gi, :], ALU.add, ALU.mult), reads=[Br, Bb], writes=[Bv])
                        S.op("dve", lambda v, b=b, k=k, jsh=jsh: v.tensor_copy(vec[:, b, k + 1, :], res[:, jsh, :, b]), reads=[Br], writes=[Bv])
                with nc.allow_non_contiguous_dma(reason="tiny transposed stores"):
                    for b in range(NS):
                        for k in range(4):
                            S.dma("st", adaT_d[b, k, :].rearrange("(kc p) -> p kc", p=128), vec[:, b, k, :], reads=[Bv])
                S.barrier()

        for part in ("a", "b"):
            if ("p1" + part) not in phases:
                continue
            WC = 4096 if part == "a" else 4832
            cm = (lambda c: c - 2048) if part == "a" else (lambda c: c if c < 2048 else c - 4096)
            with ExitStack() as st:
                W = sb(st, "p1_W", [128, 8, WC], BF16)
                wkr = sb(st, "p1_wkr", [128, 8, 64], BF16)
                dtb_bc = sb(st, "p1_dtb", [128, 4, 64], F32)
                ab = sb(st, "p1_ab", [128, 2, 8], F32)
                xt = [sb(st, "p1_x%d" % i, [128, 4, 1024], F32) for i in range(1)] * 2
                junk = sb(st, "p1_junk", [128, 1024], BF16)
                ssl = [sb(st, "p1_ss", [128, 4], F32) for _ in range(2)]
                xnl = [sb(st, "p1_xn", [128, 4, 1024], BF16) for _ in range(2)]
                Bssl, Bxnl = [Buf(), Buf()], [Buf(), Buf()]
                hT = [sb(st, "p1_hT%d" % i, [128, 8, 512], BF16) for i in range(1)] * 2
                stg = [sb(st, "p1_stg%d" % i, [128, 512], F32) for i in range(4)]
                stgb = [sb(st, "p1_stgb%d" % i, [128, 512], BF16) for i in range(4)]
                qa = sb(st, "p1_qa", [128, 5, 512], F32) if part == "b" else None
                sq = sb(st, "p1_sq", [128, 5, 512], F32) if part == "b" else None
                rs = sb(st, "p1_rs", [128, 2, 512], F32)
                qn = sb(st, "p1_qn", [128, 5, 512], BF16)
                cs = sb(st, "p1_cs", [32, 2, 512], F32)
                krt = sb(st, "p1_krt", [32, 2, 512], F32)
                krb = sb(st, "p1_krb", [32, 512], BF16)
                zst = [sb(st, "p1_zst%d" % i, [128, 2048], BF16) if part == "b" else None for i in range(2)]
                dts = sb(st, "p1_dts", [128, 5, 256], F32)
                BW, Bab, Bx, Bss, Bxn, BhT = Buf(), Buf(), [Buf()] * 2, Buf(), Buf(), [Buf()] * 2
                Bstg, Bstgb = [Buf() for _ in range(4)], [Buf() for _ in range(4)]
                Bqa, Bsq, Brs, Bqn, Bcs, Bkr, Bkrb, Bz, Bdts = Buf(), Buf(), Buf(), Buf(), Buf(), Buf(), Buf(), [Buf(), Buf()], Buf()
                for kc in range(8):
                    if part == "a":
                        S.dma("pool", W[:, kc, :], w_in[kc * 128:(kc + 1) * 128, 2048:6144], writes=[BW])
                    else:
                        S.dma("pool", W[:, kc, 0:2048], w_in[kc * 128:(kc + 1) * 128, 0:2048], writes=[BW])
                        S.dma("pool", W[:, kc, 2048:4832], w_in[kc * 128:(kc + 1) * 128, 6144:8928], writes=[BW])
                CKR = cm(C_KR) if part == "b" else 0
                S.op("dve", lambda v: v.tensor_copy(wkr[:, :, 0:32], W[:, :, CKR:CKR + 32]), reads=[BW], writes=[BW])
                S.op("dve", lambda v: v.tensor_scalar(wkr[:, :, 32:48], W[:, :, CKR + 16:CKR + 32], -1.0, None, ALU.mult), reads=[BW], writes=[BW])
                S.op("dve", lambda v: v.tensor_copy(wkr[:, :, 48:64], W[:, :, CKR:CKR + 16]), reads=[BW], writes=[BW])
                for s4 in range(4):
                    S.dma("sp", dtb_bc[:, s4, :], dtb.partition_broadcast(128), writes=[BW])
                stg_i = [0]
                tiles = [(si, ti) for si in range(NS) for ti in range(seqs[si] // 512)]

                def stageA(idx):
                    si, ti = tiles[idx]
                    g0 = tok0[si] + ti * 512
                    k2 = idx % 2
                    X = xt[0]
                    S.dma("sp", X[:], x_d[g0:g0 + 512, :].rearrange("(s p) f -> p s f", p=128), writes=[Bx[0]])
                    for s4 in range(4):
                        S.op("act", lambda a: a.activation(junk[:], X[:, s4, :], AF.Square, accum_out=ssl[k2][:, s4:s4 + 1]), reads=[Bx[0]], writes=[Bssl[k2]])
                    rstd_from_ss(None, ssl[k2][:], ssl[k2][:], 1024.0, [Bssl[k2]], [Bssl[k2]])
                    for s4 in range(4):
                        S.op("pool" if s4 % 2 else "dve", lambda v: v.tensor_scalar(xnl[k2][:, s4, :], X[:, s4, :], ssl[k2][:, s4:s4 + 1], None, ALU.mult), reads=[Bx[0], Bssl[k2]], writes=[Bxnl[k2]])

                def stageB(idx):
                    si, ti = tiles[idx]
                    k2 = idx % 2
                    H = hT[0]
                    if ti == 0:
                        with nc.allow_non_contiguous_dma(reason="tiny"):
                            S.dma("sp", ab[:, 0, :], adaT_d[si, 0, :].rearrange("(kc p) -> p kc", p=128), writes=[Bab])
                            S.dma("sp", ab[:, 1, :], adaT_d[si, 1, :].rearrange("(kc p) -> p kc", p=128), writes=[Bab])
                    for kc in range(8):
                        pt, pb = bank()
                        ptb = pt[:].bitcast(BF16)
                        for s4 in range(4):
                            S.op("pe", lambda pe: pe.transpose(ptb[:, s4 * 128:(s4 + 1) * 128], xnl[k2][:, s4, kc * 128:(kc + 1) * 128], identb), reads=[Bxnl[k2], B_c], writes=[pb])
                        S.op("dve", lambda v: v.tensor_scalar(H[:, kc, :], ptb[:, 0:512], ab[:, 0, kc:kc + 1], ab[:, 1, kc:kc + 1], ALU.mult, ALU.add), reads=[pb, Bab], writes=[BhT[0]])

                def body(idx):
                    si, ti = tiles[idx]
                    g0 = tok0[si] + ti * 512
                    p0 = ti * 512
                    par = 0
                    H = hT[0]
                    def fm(cols, m, lw=None):
                        pt, pb = bank()
                        l = (lw if lw is not None else W)
                        cc0 = cols if lw is not None else cm(cols)
                        S.mm(pt[0:m, :], [(l[:, kc, cc0:cc0 + m], H[:, kc, :]) for kc in range(8)], reads=[BW, BhT[par]], writes=[pb])
                        return pt, pb
                    for j in range(32 if part == "a" else 0):
                        pt, pb = fm(C_X + j * 128, 128)
                        k = stg_i[0] % 4
                        stg_i[0] += 1
                        S.op("act", lambda a, k=k: a.copy(stgb[k][:], pt[:]), reads=[pb], writes=[Bstgb[k]])
                        S.dma("st", uT_d[j * 128:(j + 1) * 128, g0:g0 + 512], stgb[k][:], reads=[Bstgb[k]])
                    if part == "a":
                        return
                    for j in range(5):
                        pt, pb = fm(C_QA + j * 128, 128)
                        S.op("act", lambda a, j=j: a.copy(qa[:, j, :], pt[:]), reads=[pb], writes=[Bqa])
                        S.op("pool", lambda v, j=j: v.tensor_tensor(sq[:, j, :], qa[:, j, :], qa[:, j, :], ALU.mult), reads=[Bqa], writes=[Bsq])
                    for (r, j0, n) in ((0, 0, 3), (1, 3, 2)):
                        pt, pb = bank()
                        for j in range(n):
                            S.op("pe", lambda pe, j=j: pe.matmul(pt[:], ones32, sq[:, j0 + j, :], start=(j == 0), stop=(j == n - 1)), reads=[Bsq, B_c], writes=[pb])
                        S.op("act", lambda a, r=r, n=n: a.activation(rs[:, r, :], pt[:], AF.Sqrt, bias=epst[:], scale=1.0 / (128 * n)), reads=[pb, B_c], writes=[Brs])
                        S.op("dve", lambda v, r=r: v.reciprocal(rs[:, r, :], rs[:, r, :]), reads=[Brs], writes=[Brs])
                        for j in range(n):
                            S.op("dve", lambda v, j=j, r=r: v.tensor_tensor(qn[:, j0 + j, :], qa[:, j0 + j, :], rs[:, r, :], ALU.mult), reads=[Bqa, Brs], writes=[Bqn])
                    S.dma("st", qnT_d[:, g0:g0 + 512].rearrange("(j p) t -> p j t", p=128), qn[:, 0:3, :], reads=[Bqn])
                    S.dma("st", cnT_d[:, g0:g0 + 512].rearrange("(j p) t -> p j t", p=128), qn[:, 3:5, :], reads=[Bqn])
                    S.dma("sp", cs[:, 0, :], cos_d[:, p0:p0 + 512], writes=[Bcs])
                    S.dma("sp", cs[:, 1, :], sin_d[:, p0:p0 + 512], writes=[Bcs])
                    pA, pbA = fm(0, 32, wkr)
                    pB, pbB = fm(32, 32, wkr)
                    S.op("dve", lambda v: v.tensor_tensor(krt[:, 0, :], pA[0:32, :], cs[:, 0, :], ALU.mult), reads=[pbA, Bcs], writes=[Bkr])
                    S.op("dve", lambda v: v.tensor_tensor(krt[:, 1, :], pB[0:32, :], cs[:, 1, :], ALU.mult), reads=[pbB, Bcs], writes=[Bkr])
                    S.op("dve", lambda v: v.tensor_tensor(krb[:], krt[:, 0, :], krt[:, 1, :], ALU.add), reads=[Bkr], writes=[Bkrb])
                    S.dma("st", kr_d[:, g0:g0 + 512], krb[:], reads=[Bkrb])
                    for j in range(16):
                        pt, pb = fm(C_G + j * 128, 128)
                        k = stg_i[0] % 4
                        stg_i[0] += 1
                        S.op("act", lambda a, k=k: a.activation(stgb[k][:], pt[:], AF.Sigmoid), reads=[pb], writes=[Bstgb[k]])
                        S.dma("st", gT_d[j * 128:(j + 1) * 128, g0:g0 + 512], stgb[k][:], reads=[Bstgb[k]])
                    for s4 in range(4):
                        Z = zst[s4 % 2]
                        for cg in range(4):
                            pt, pb = bank()
                            S.mm(pt[:], [(H[:, kc, s4 * 128:(s4 + 1) * 128], W[:, kc, cg * 512:(cg + 1) * 512]) for kc in range(8)], reads=[BW, BhT[par]], writes=[pb])
                            S.op("act", lambda a, s4=s4, cg=cg: a.activation(Z[:, cg * 512:(cg + 1) * 512], pt[:], AF.Silu), reads=[pb], writes=[Bz[s4 % 2]])
                        S.dma("st", zs_d[g0 + s4 * 128:g0 + (s4 + 1) * 128, :], Z[:], reads=[Bz[s4 % 2]])
                    pt, pb = bank()
                    for s4 in range(4):
                        S.mm(pt[:, s4 * 64:(s4 + 1) * 64], [(H[:, kc, s4 * 128:(s4 + 1) * 128], W[:, kc, cm(C_DT):cm(C_DT) + 64]) for kc in range(8)], reads=[BW, BhT[par]], writes=[pb])
                    d0, d1, d2, d3, d4 = (dts[:, k, :] for k in range(5))
                    S.op("dve", lambda v: v.tensor_tensor(d0, pt[:, 0:256], dtb_bc[:].rearrange("p a b -> p (a b)"), ALU.add), reads=[pb, BW], writes=[Bdts])
                    S.op("dve", lambda v: v.tensor_scalar(d1, d0, -1.0, None, ALU.mult), reads=[Bdts], writes=[Bdts])
                    S.op("dve", lambda v: v.tensor_tensor(d1, d0, d1, ALU.min), reads=[Bdts], writes=[Bdts])
                    S.op("act", lambda a: a.activation(d2, d1, AF.Exp), reads=[Bdts], writes=[Bdts])
                    S.op("act", lambda a: a.activation(d3, d2, AF.Ln, bias=onet[:], scale=1.0), reads=[Bdts, B_c], writes=[Bdts])
                    S.op("dve", lambda v: v.scalar_tensor_tensor(d4, d0, 0.0, d3, ALU.max, ALU.add), reads=[Bdts], writes=[Bdts])
                    S.dma("st", dt_d[g0:g0 + 512, :].rearrange("(s p) f -> p s f", p=128), d4.rearrange("p (s f) -> p s f", f=64), reads=[Bdts])
                stageA(0)
                stageB(0)
                for idx in range(len(tiles)):
                    if idx + 1 < len(tiles):
                        stageA(idx + 1)
                    body(idx)
                    if idx + 1 < len(tiles):
                        stageB(idx + 1)
                S.barrier()

        if "p1c" in phases:
            with ExitStack() as st:
                cw = sb(st, "pc_cw", [128, 32, 6], F32)
                u = [sb(st, "pc_u%d" % i, [128, 516], BF16) for i in range(4)]
                dw = sb(st, "pc_dw", [128, 32, 5, 128], BF16)
                ob = [sb(st, "pc_ob%d" % i, [128, 512], BF16) for i in range(4)]
                tk = [sb(st, "pc_tk%d" % i, [128, 4, 3072], BF16) for i in range(2)]
                Bcw, Bu, Bacc, Bob, Btk = Buf(), [Buf() for _ in range(4)], [Buf(), Buf()], [Buf() for _ in range(4)], [Buf(), Buf()]
                with nc.allow_non_contiguous_dma(reason="tiny"):
                    for k in range(5):
                        S.dma("sp", cw[:, :, k], conv_w[k, :].rearrange("(j p) -> p j", p=128), writes=[Bcw])
                    S.dma("sp", cw[:, :, 5], conv_b.rearrange("(j p) -> p j", p=128), writes=[Bcw])
                for j in range(32):
                    S.op("dve", lambda v, j=j: v.tensor_tensor(dw[:, j, :, :], identb.unsqueeze(1).broadcast_to([128, 5, 128]), cw[:, j, 0:5].unsqueeze(2).broadcast_to([128, 5, 128]), ALU.mult), reads=[Bcw, B_c], writes=[Bcw])
                it = 0
                for si in range(NS):
                    nt = seqs[si] // 512
                    for ti in range(nt):
                        g0 = tok0[si] + ti * 512
                        par = (g0 // 512) % 2
                        TK = tk[par]
                        for j in range(32):
                            U, BU = u[it % 4], Bu[it % 4]
                            O, BO = ob[it % 4], Bob[it % 4]
                            eng = "dve"
                            it += 1
                            lo = 0 if ti > 0 else 2
                            hi = 516 if ti < nt - 1 else 514
                            if lo:
                                S.op(eng, lambda v: v.memset(U[:, 0:2], 0.0), writes=[BU])
                            if hi < 516:
                                S.op(eng, lambda v: v.memset(U[:, 514:516], 0.0), writes=[BU])
                            S.dma("sp", U[:, lo:hi], uT_d[j * 128:(j + 1) * 128, g0 - 2 + lo:g0 - 2 + hi], writes=[BU])
                            pcv, pbcv = bank()
                            S.mm(pcv, [(dw[:, j, k, :], U[:, k:k + 512]) for k in range(5)], reads=[BU, Bcw], writes=[pbcv])
                            S.op("act", lambda a, j=j: a.activation(O[:], pcv, AF.Silu, bias=cw[:, j, 5:6], scale=1.0), reads=[pbcv, Bcw], writes=[BO])
                            if j >= 16:
                                S.dma("st", bcT_d[(j - 16) * 128:(j - 15) * 128, g0:g0 + 512], O[:], reads=[BO])
                            if j < 24:
                                pt, pb = bank()
                                ptb = pt[:].bitcast(BF16)
                                for s4 in range(4):
                                    S.op("pe", lambda pe, s4=s4: pe.transpose(ptb[:, s4 * 128:(s4 + 1) * 128], O[:, s4 * 128:(s4 + 1) * 128], identb), reads=[BO, B_c], writes=[pb])
                                S.op("act", lambda a, j=j: a.copy(TK[:, :, j * 128:(j + 1) * 128], ptb[:, 0:512].rearrange("p (s f) -> p s f", f=128)), reads=[pb], writes=[Btk[par]])
                        S.dma("st", xtok_d[g0:g0 + 512, :].rearrange("(s p) f -> p s f", p=128), TK[:], reads=[Btk[par]])
                S.barrier()

        ctx = dict(locals())
        if "p1d" in phases:
            emit_p1d(ctx)
        with ExitStack() as gstack:
            ctx["gstack"] = gstack
            gens = []
            if "p2" in phases:
                gens.append(gen_p2(ctx))
            if "p3" in phases:
                gens.extend(make_p3(ctx))
            run_interleaved(gens)
            S.barrier()
        if "p4a" in phases:
            emit_p4a(ctx)
        if "p4b" in phases:
            emit_p4b(ctx)
        S.barrier()
    return nc


def run_interleaved(gens):
    n = len(gens)
    prog = [0.0] * n
    alive = [True] * n
    blocked = [False] * n
    while any(alive):
        cand = [i for i in range(n) if alive[i] and not blocked[i]]
        if not cand:
            raise RuntimeError("interleave deadlock")
        k = min(cand, key=lambda i: prog[i])
        try:
            v = next(gens[k])
        except StopIteration:
            alive[k] = False
            blocked = [False] * n
            continue
        if v is None:
            blocked[k] = True
        else:
            prog[k] = v
            blocked = [False] * n


def emit_p1d(c):
    from contextlib import ExitStack
    S, nc, sb, bank, seqs, tok0, NS = c["S"], c["nc"], c["sb"], c["bank"], c["seqs"], c["tok0"], c["NS"]
    qnT_d, cnT_d, KT_d, QT_d, V_d, cos_d, sin_d = c["qnT_d"], c["cnT_d"], c["KT_d"], c["QT_d"], c["V_d"], c["cos_d"], c["sin_d"]
    w_q_b, w_kv_b, g_q, g_kv = c["w_q_b"], c["w_kv_b"], c["g_q"], c["g_kv"]
    with ExitStack() as st:
        wq = sb(st, "pd_wq", [128, 3, 1536], BF16)
        wqB = sb(st, "pd_wqB", [128, 3, 16, 96], BF16)
        wkv = sb(st, "pd_wkv", [128, 2, 2048], BF16)
        wk = sb(st, "pd_wk", [128, 2, 1024], BF16)
        wv = sb(st, "pd_wv", [128, 2, 1024], BF16)
        gq = sb(st, "pd_gq", [128, 5], F32)
        qn = [sb(st, "pd_qn", [128, 3, 512], BF16) for _ in range(2)]
        cn = [sb(st, "pd_cn", [128, 2, 512], BF16) for _ in range(2)]
        cst = [sb(st, "pd_cs", [128, 2, 512], F32) for _ in range(2)]
        kst = [sb(st, "pd_kst", [128, 512], BF16) for _ in range(3)]
        vst = [sb(st, "pd_vst", [128, 512], BF16) for _ in range(3)]
        qst = [sb(st, "pd_qst", [128, 512], BF16) for _ in range(3)]
        rt = [sb(st, "pd_rt", [128, 2, 512], F32) for _ in range(2)]
        Bw2 = Buf()
        Bqn, Bcn, Bcs = [Buf(), Buf()], [Buf(), Buf()], [Buf(), Buf()]
        Bk, Bv, Bq, Brt = [Buf() for _ in range(3)], [Buf() for _ in range(3)], [Buf() for _ in range(3)], [Buf(), Buf()]
        S.dma("pool", wq[:], w_q_b.rearrange("(kc p) f -> p kc f", p=128), writes=[Bw2])
        S.dma("pool", wkv[:], w_kv_b.rearrange("(kc p) f -> p kc f", p=128), writes=[Bw2])
        with nc.allow_non_contiguous_dma(reason="tiny"):
            S.dma("sp", gq[:, 0:3], g_q.rearrange("(j p) -> p j", p=128), writes=[Bw2])
            S.dma("sp", gq[:, 3:5], g_kv.rearrange("(j p) -> p j", p=128), writes=[Bw2])
        for kc in range(3):
            S.op("dve", lambda v: v.tensor_scalar(wq[:, kc, :], wq[:, kc, :], gq[:, kc:kc + 1], None, ALU.mult), reads=[Bw2], writes=[Bw2])
        for kc in range(2):
            S.op("dve", lambda v: v.tensor_scalar(wkv[:, kc, :], wkv[:, kc, :], gq[:, 3 + kc:4 + kc], None, ALU.mult), reads=[Bw2], writes=[Bw2])
            w4 = wkv[:, kc, :].rearrange("p (h t f) -> p h t f", t=2, f=64)
            S.op("dve", lambda v: v.tensor_copy(wk[:, kc, :].rearrange("p (h f) -> p h f", f=64), w4[:, :, 0, :]), reads=[Bw2], writes=[Bw2])
            S.op("dve", lambda v: v.tensor_copy(wv[:, kc, :].rearrange("p (h f) -> p h f", f=64), w4[:, :, 1, :]), reads=[Bw2], writes=[Bw2])
        S.op("dve", lambda v: v.memset(wqB[:], 0.0), writes=[Bw2])
        wq4 = wq[:].rearrange("p k (h f) -> p k h f", f=96)
        for kc in range(3):
            S.op("dve", lambda v: v.tensor_scalar(wqB[:, kc, :, 64:80], wq4[:, kc, :, 80:96], -1.0, None, ALU.mult), reads=[Bw2], writes=[Bw2])
            S.op("dve", lambda v: v.tensor_copy(wqB[:, kc, :, 80:96], wq4[:, kc, :, 64:80]), reads=[Bw2], writes=[Bw2])
        it = 0
        ki = vi = qi = 0
        for si in range(NS):
            Sq, t0 = seqs[si], tok0[si]
            nch = Sq // 128
            for ti in range(Sq // 512):
                g0 = t0 + ti * 512
                p0 = ti * 512
                k = it % 2
                it += 1
                QN, CN, CS = qn[k], cn[k], cst[k]
                S.dma("sp", QN[:], qnT_d[:, g0:g0 + 512].rearrange("(j p) t -> p j t", p=128), writes=[Bqn[k]])
                S.dma("sp", CN[:], cnT_d[:, g0:g0 + 512].rearrange("(j p) t -> p j t", p=128), writes=[Bcn[k]])
                S.dma("sp", CS[64:96, 0, :], cos_d[:, p0:p0 + 512], writes=[Bcs[k]])
                S.dma("sp", CS[64:96, 1, :], sin_d[:, p0:p0 + 512], writes=[Bcs[k]])
                for pr in range(8):
                    pt, pb = bank()
                    S.mm(pt, [(wk[:, kc, pr * 128:(pr + 1) * 128], CN[:, kc, :]) for kc in range(2)], reads=[Bw2, Bcn[k]], writes=[pb])
                    K_, BK_ = kst[ki % 3], Bk[ki % 3]
                    ki += 1
                    S.op("act", lambda a: a.copy(K_[:], pt), reads=[pb], writes=[BK_])
                    S.dma("st", KT_d[pr * 128:(pr + 1) * 128, g0:g0 + 512], K_[:], reads=[BK_])
                for s4 in range(4):
                    cidx = ti * 4 + s4
                    for hf in range(2):
                        pt, pb = bank()
                        S.mm(pt, [(CN[:, kc, s4 * 128:(s4 + 1) * 128], wv[:, kc, hf * 512:(hf + 1) * 512]) for kc in range(2)], reads=[Bw2, Bcn[k]], writes=[pb])
                        V_, BV_ = vst[vi % 3], Bv[vi % 3]
                        vi += 1
                        S.op("dve", lambda v: v.tensor_copy(V_[:], pt), reads=[pb], writes=[BV_])
                        dst = V_d[hf * 8:(hf + 1) * 8, t0 * 64:(t0 + Sq) * 64].rearrange("h (p c f) -> p h c f", p=128, f=64)[:, :, cidx, :]
                        S.dma("st", dst, V_[:].rearrange("p (h f) -> p h f", f=64), reads=[BV_])
                for h in range(16):
                    pA, pbA = bank()
                    S.mm(pA[0:96, :], [(wq[:, kc, h * 96:(h + 1) * 96], QN[:, kc, :]) for kc in range(3)], reads=[Bw2, Bqn[k]], writes=[pbA])
                    pB, pbB = bank()
                    S.mm(pB[0:96, :], [(wqB[:, kc, h, :], QN[:, kc, :]) for kc in range(3)], reads=[Bw2, Bqn[k]], writes=[pbB])
                    Q_, BQ_ = qst[qi % 3], Bq[qi % 3]
                    RT, BRT = rt[qi % 2], Brt[qi % 2]
                    qi += 1
                    S.op("act", lambda a: a.copy(Q_[0:64, :], pA[0:64, :]), reads=[pbA], writes=[BQ_])
                    S.op("dve", lambda v: v.tensor_tensor(RT[64:96, 0, :], pA[64:96, :], CS[64:96, 0, :], ALU.mult), reads=[pbA, Bcs[k]], writes=[BRT])
                    S.op("dve", lambda v: v.tensor_tensor(RT[64:96, 1, :], pB[64:96, :], CS[64:96, 1, :], ALU.mult), reads=[pbB, Bcs[k]], writes=[BRT])
                    S.op("pool", lambda v: v.tensor_tensor(Q_[64:96, :], RT[64:96, 0, :], RT[64:96, 1, :], ALU.add), reads=[BRT], writes=[BQ_])
                    S.dma("st", QT_d[h * 96:(h + 1) * 96, g0:g0 + 512], Q_[0:96, :], reads=[BQ_])
        S.barrier()


def gen_p2(c):
    from contextlib import ExitStack
    S, nc, sb, seqs, tok0, NS = c["S"], c["nc"], c["sb"], c["seqs"], c["tok0"], c["NS"]
    banks, bbuf, SMAX = c["banks"], c["bbuf"], c["SMAX"]
    psall = c["psall"]
    KT_d, QT_d, V_d, kr_d, OT_d = c["KT_d"], c["QT_d"], c["V_d"], c["kr_d"], c["OT_d"]
    st = c["gstack"]
    if True:
        KT = [sb(st, "p2_KT", [128, SMAX], BF16) for _ in range(2)]
        QT = [sb(st, "p2_QT", [128, SMAX], BF16) for _ in range(2)]
        VA = [sb(st, "p2_VA", [128, SMAX // 128, 128], BF16) for _ in range(2)]
        PT = [sb(st, "p2_PT", [128, 512], BF16) for _ in range(4)]
        rd = sb(st, "p2_rd", [128, 512], F32)
        og = [sb(st, "p2_og", [128, 512], BF16) for _ in range(2)]
        BK, BQ, BV = [Buf(), Buf()], [Buf(), Buf()], [Buf(), Buf()]
        BPT, Brd, Bog = [Buf() for _ in range(4)], Buf(), [Buf(), Buf()]
        for i in range(2):
            S.op("pool", lambda v: v.memset(VA[i][:, :, 64:128], 1.0), writes=[BV[i]])
        heads = [(si, h) for si in range(NS) for h in range(16)]
        total = float(sum(seqs[si] // 512 * (seqs[si] // 128) * 3 for si, h in heads)) + 1.0
        done = 0

        def load(idx):
            si, h = heads[idx]
            hb = idx % 2
            Sq, t0 = seqs[si], tok0[si]
            S.dma("sp", KT[hb][0:64, 0:Sq], KT_d[h * 64:(h + 1) * 64, t0:t0 + Sq], writes=[BK[hb]])
            S.dma("sp", KT[hb][64:96, 0:Sq], kr_d[:, t0:t0 + Sq], writes=[BK[hb]])
            S.dma("sp", QT[hb][0:96, 0:Sq], QT_d[h * 96:(h + 1) * 96, t0:t0 + Sq], writes=[BQ[hb]])
            S.dma("sp", VA[hb][:, 0:Sq // 128, 0:64], V_d[h, t0 * 64:(t0 + Sq) * 64].rearrange("(p c f) -> p c f", p=128, f=64), writes=[BV[hb]])

        load(0)
        pi = 0
        oi = 0
        for idx, (si, h) in enumerate(heads):
            if idx + 1 < len(heads):
                load(idx + 1)
            hb = idx % 2
            K_, Q_, V_ = KT[hb], QT[hb], VA[hb]
            Sq, t0 = seqs[si], tok0[si]
            npair = Sq // 256
            nkc = Sq // 128
            for qt in range(Sq // 512):
                ql = slice(qt * 512, (qt + 1) * 512)
                pO, pbO = banks[3], bbuf[3]
                pend = []
                for cc in range(nkc + 2):
                    if cc < nkc:
                        b0 = cc % 3
                        S.op("pe", lambda pe: pe.matmul(banks[b0], K_[0:96, cc * 128:(cc + 1) * 128], Q_[0:96, ql], start=True, stop=True), reads=[BK[hb], BQ[hb]], writes=[bbuf[b0]])
                        done += 1
                        yield done / total
                        P_, BP = PT[pi % 4], BPT[pi % 4]
                        pi += 1
                        S.op("act", lambda a: a.activation(P_[:], banks[b0], AF.Exp, scale=ATT_SCALE), reads=[bbuf[b0]], writes=[BP])
                        done += 1
                        yield done / total
                        pend.append((cc, P_, BP))
                    if cc >= 2:
                        pc, PP, BPP = pend.pop(0)
                        S.op("pe", lambda pe: pe.matmul(pO, V_[:, pc, :], PP[:], start=(pc == 0), stop=(pc == nkc - 1)), reads=[BV[hb], BPP], writes=[pbO])
                        done += 1
                        yield done / total
                O_, BO_ = og[oi % 2], Bog[oi % 2]
                oi += 1
                S.op("dve", lambda v: v.reciprocal(rd[64:128, :], pO[64:128, :]), reads=[pbO], writes=[Brd])
                S.op("dve", lambda v: v.tensor_tensor(O_[0:64, :], pO[0:64, :], rd[64:128, :], ALU.mult), reads=[pbO, Brd], writes=[BO_])
                S.dma("sp", OT_d[h * 64:(h + 1) * 64, t0 + qt * 512:t0 + (qt + 1) * 512], O_[0:64, :], reads=[BO_])
        yield 1.0


def make_p3(c):
    S, nc, sb, seqs, tok0, NS = c["S"], c["nc"], c["sb"], c["seqs"], c["tok0"], c["NS"]
    bank0 = c["bank"]
    bankA = lambda: bank0("a")
    bankB = lambda: bank0("b")
    identb, Ufb, Ubb, Lfb, Lbb, onesb, B_c, epst = c["identb"], c["Ufb"], c["Ubb"], c["Lfb"], c["Lbb"], c["onesb"], c["B_c"], c["epst"]
    xtok_d, bcT_d, dt_d, zs_d, prevb_d, ygT_d, alog, dskip = c["xtok_d"], c["bcT_d"], c["dt_d"], c["zs_d"], c["prevb_d"], c["ygT_d"], c["alog"], c["dskip"]
    Q = "pool"
    st = c["gstack"]
    two = lambda name, shape, dt: [sb(st, name, shape, dt) for _ in range(2)]
    a_bc = sb(st, "p3_a", [128, 64], F32)
    D_bc = sb(st, "p3_D", [128, 32], F32)
    Hs = sb(st, "p3_H", [128, 2048], F32)
    Hb16 = two("p3_Hb", [128, 2048], BF16)
    xts = two("p3_xt", [128, 3072], BF16)
    bcs = two("p3_bc", [128, 16, 128], BF16)
    dts_ = two("p3_dt", [128, 64], F32)
    zss = two("p3_zs", [128, 2048], BF16)
    pvs = two("p3_pv", [128, 2048], BF16)
    dA = sb(st, "p3_dA", [128, 64], F32)
    dAh = two("p3_dAh", [128, 64], BF16)
    dAhf = sb(st, "p3_dAhf", [128, 64], F32)
    dAl = two("p3_dAl", [128, 64], BF16)
    ct = sb(st, "p3_ct", [128, 128], F32)
    Et = sb(st, "p3_E", [128, 64], F32)
    wgt = sb(st, "p3_w", [128, 64], F32)
    ec = two("p3_ec", [128, 64], F32)
    dec = sb(st, "p3_dec", [128, 64], F32)
    xw = sb(st, "p3_xw", [128, 2048], BF16)
    xdt = [two("p3_xdt", [128, 2048], BF16) for _ in range(2)]
    Rs = two("p3_R", [128, 2, 8, 128], BF16)
    CBm = two("p3_CBm", [128, 2, 8, 128], BF16)
    Exs = [sb(st, "p3_Ex", [128, 512], BF16) for _ in range(4)]
    MTs = two("p3_MT", [128, 2, 8, 128], BF16)
    y = sb(st, "p3_y", [128, 2048], F32)
    T1 = sb(st, "p3_t1", [128, 512], F32)
    T2 = sb(st, "p3_t2", [128, 512], F32)
    T3s = two("p3_t3", [128, 512], F32)
    yz = sb(st, "p3_yz", [128, 2048], F32)
    sqt = sb(st, "p3_sq", [128, 2048], BF16)
    gs = sb(st, "p3_gs", [128, 8], F32)
    ygn = sb(st, "p3_ygn", [128, 2048], BF16)
    ygs = sb(st, "p3_ygs", [128, 16, 128], BF16)
    B2 = lambda: [Buf(), Buf()]
    Bk, BH, Bprevd = Buf(), Buf(), Buf()
    BHb, Bxt, Bbc, Bdt, Bzs, Bpv = B2(), B2(), B2(), B2(), B2(), B2()
    BdA, BdAx, Bct, BE, Bec, Bxw, BCB = Buf(), B2(), Buf(), Buf(), B2(), Buf(), B2()
    Bxdt = [B2(), B2()]
    BRs, BEx, BMTs = B2(), [Buf() for _ in range(4)], B2()
    BT3s, Byz = B2(), Buf()
    By, BT1, BT2, BT3, Bsq, Bgs, Bygn, Bygs = Buf(), Buf(), Buf(), Buf(), Buf(), Buf(), Buf(), Buf()
    S.dma(Q, a_bc[:], alog.partition_broadcast(128), writes=[Bk])
    S.dma(Q, D_bc[:], dskip.partition_broadcast(128), writes=[Bk])
    S.op("act", lambda a: a.activation(a_bc[:], a_bc[:], AF.Exp), reads=[Bk], writes=[Bk])
    S.op("dve", lambda v: v.tensor_scalar(a_bc[:], a_bc[:], -1.0, None, ALU.mult), reads=[Bk], writes=[Bk])
    nchs = [sq // 128 for sq in seqs]
    NCH = float(sum(nchs))
    gstart = [sum(nchs[:i]) for i in range(NS)]
    sh = {"front": 0, "back": 0}
    TA, TB = 70.0, 200.0

    def bc3(ap2, n):
        return ap2.unsqueeze(2).broadcast_to([128, ap2.shape[1], n])

    v3 = lambda ap: ap.rearrange("p (h f) -> p h f", f=64)

    def prep(k, dtt, Bd):
        S.op("dve", lambda v: v.tensor_tensor(dA[:], dtt[:], a_bc[:], ALU.mult), reads=[Bd, Bk], writes=[BdA])
        yield
        S.op("dve", lambda v: v.tensor_copy(dAh[k][:], dA[:]), reads=[BdA], writes=[BdAx[k]])
        yield
        S.op("dve", lambda v: v.tensor_copy(dAhf[:], dAh[k][:]), reads=[BdAx[k]], writes=[BdA])
        yield
        S.op("dve", lambda v: v.tensor_tensor(dAhf[:], dA[:], dAhf[:], ALU.subtract), reads=[BdA], writes=[BdA])
        yield
        S.op("dve", lambda v: v.tensor_copy(dAl[k][:], dAhf[:]), reads=[BdA], writes=[BdAx[k]])
        yield
        pc, pbc = bankA()
        for (o0, o1, L) in ((0, 32, Ufb), (32, 64, Ubb)):
            S.mm(pc[:, o0:o1], [(L, dAh[k][:, o0:o1]), (L, dAl[k][:, o0:o1])], reads=[BdAx[k], B_c], writes=[pbc])
            yield
        S.mm(pc[:, 64:128], [(onesb, dAh[k][:]), (onesb, dAl[k][:])], reads=[BdAx[k], B_c], writes=[pbc])
        yield
        S.op("dve", lambda v: v.tensor_copy(ct[:], pc[:, 0:128]), reads=[pbc], writes=[Bct])
        yield
        S.op("dve", lambda v: v.tensor_tensor(Et[:], ct[:, 64:128], ct[:, 0:64], ALU.subtract), reads=[Bct], writes=[BE])
        yield
        S.op("act", lambda a: a.activation(Et[:], Et[:], AF.Exp), reads=[BE], writes=[BE])
        yield
        S.op("dve", lambda v: v.tensor_tensor(wgt[:], dtt[:], Et[:], ALU.mult), reads=[BE, Bd], writes=[BE])
        yield
        S.op("act", lambda a: a.activation(dec[:], ct[:, 64:128], AF.Exp), reads=[Bct], writes=[BE])
        yield
        S.op("act", lambda a: a.activation(ec[k][:], ct[:, 0:64], AF.Exp), reads=[Bct], writes=[Bec[k]])
        yield

    def state_update(X, BX, d):
        S.op("dve", lambda v: v.tensor_tensor(v3(xw[:]), v3(X[:, 0:2048]), bc3(wgt[:, d * 32:(d + 1) * 32], 64), ALU.mult), reads=[BX, BE], writes=[Bxw])
        yield
        S.op("dve", lambda v: v.tensor_tensor(v3(Hs[:]), v3(Hs[:]), bc3(dec[:, d * 32:(d + 1) * 32], 64), ALU.mult), reads=[BE, BHb[0], BHb[1]], writes=[BH])
        yield
        for gp in range(4):
            ps, pbs = bankA()
            for gg in range(2):
                g = gp * 2 + gg
                S.op("pe", lambda pe: pe.matmul(ps[:, gg * 256:(gg + 1) * 256], X[:, 2048 + g * 128:2048 + (g + 1) * 128], xw[:, g * 256:(g + 1) * 256], start=True, stop=True), reads=[BX, Bxw], writes=[pbs])
                yield
            S.op("dve", lambda v: v.tensor_tensor(Hs[:, gp * 512:(gp + 1) * 512], Hs[:, gp * 512:(gp + 1) * 512], ps, ALU.add), reads=[pbs], writes=[BH])
            yield

    def genA():
        it = 0
        for si in range(NS):
            Sq, t0 = seqs[si], tok0[si]
            nch = nchs[si]
            base = gstart[si] / NCH
            while sh["back"] < gstart[si]:
                yield None
            S.op("pool", lambda v: v.memset(Hs[:], 0.0), reads=[BHb[0], BHb[1]], writes=[BH])
            for cidx in range(nch - 1, -1, -1):
                g0 = t0 + cidx * 128
                k = it % 2
                it += 1
                S.dma(Q, xts[k][:], xtok_d[g0:g0 + 128, :], writes=[Bxt[k]])
                S.dma(Q, dts_[k][:], dt_d[g0:g0 + 128, :], writes=[Bdt[k]])
                S.op("act", lambda a: a.copy(Hb16[k][:], Hs[:]), reads=[BH], writes=[BHb[k]])
                S.dma(Q, prevb_d[cidx], Hb16[k][:], reads=[BHb[k]], writes=[Bprevd])
                yield base
                if cidx > 0:
                    for _ in prep(k, dts_[k], Bdt[k]):
                        yield base
                    for _ in state_update(xts[k], Bxt[k], 1):
                        yield base
            S.op("pool", lambda v: v.memset(Hs[:], 0.0), reads=[BHb[0], BHb[1]], writes=[BH])
            for cidx in range(nch):
                gidx = gstart[si] + cidx
                k = gidx % 2
                n = [0]

                def pr():
                    n[0] += 1
                    return (gidx + min(n[0] / TA, 0.99)) / NCH
                while sh["back"] < gidx - 1:
                    yield None
                g0 = t0 + cidx * 128
                X, BX, BCt, BBC, dtt, Bd = xts[k], Bxt[k], bcs[k], Bbc[k], dts_[k], Bdt[k]
                S.dma(Q, X[:], xtok_d[g0:g0 + 128, :], writes=[BX])
                S.dma(Q, BCt[:], bcT_d[:, g0:g0 + 128].rearrange("(j p) t -> p j t", p=128), writes=[BBC])
                S.dma(Q, dtt[:], dt_d[g0:g0 + 128, :], writes=[Bd])
                S.dma(Q, zss[k][:], zs_d[g0:g0 + 128, :], writes=[Bzs[k]])
                S.dma(Q, pvs[k][:], prevb_d[cidx], reads=[Bprevd], writes=[Bpv[k]])
                S.op("act", lambda a: a.copy(Hb16[k][:], Hs[:]), reads=[BH], writes=[BHb[k]])
                yield pr()
                for _ in prep(k, dtt, Bd):
                    yield pr()
                if cidx < nch - 1:
                    for _ in state_update(X, BX, 0):
                        yield pr()
                for g4 in range(2):
                    pcb, pbcb = bankA()
                    for gg in range(4):
                        g = g4 * 4 + gg
                        S.op("pe", lambda pe: pe.matmul(pcb[:, gg * 128:(gg + 1) * 128], BCt[:, g, :], BCt[:, 8 + g, :], start=True, stop=True), reads=[BBC], writes=[pbcb])
                        yield pr()
                    for d, Um in ((0, Ufb), (1, Ubb)):
                        S.op("dve", lambda v: v.tensor_tensor(CBm[k][:, d, g4 * 4:(g4 + 1) * 4, :], pcb.rearrange("p (g l) -> p g l", l=128), Um.unsqueeze(1).broadcast_to([128, 4, 128]), ALU.mult), reads=[pbcb, B_c], writes=[BCB[k]])
                        yield pr()
                for d in range(2):
                    S.op("pool", lambda v: v.tensor_tensor(v3(xdt[k][d][:]), v3(X[:, 0:2048]), bc3(dtt[:, d * 32:(d + 1) * 32], 64), ALU.mult), reads=[BX, Bd], writes=[Bxdt[k][d]])
                    yield pr()
                sh["front"] = gidx + 1
                yield (gidx + 0.995) / NCH
        yield 1.0

    def genB():
        ei = [0]

        def stage2(k, gp):
            MT, BMT = MTs[gp % 2], BMTs[gp % 2]
            X, BX = xts[k], Bxt[k]
            dl = ((0, Lfb, Ufb), (1, Lbb, Ubb))
            for d, Lm, Um in dl:
                h0 = d * 32 + gp * 8
                for kk, src in ((0, dAh[k]), (1, dAl[k])):
                    S.op("dve", lambda v: v.tensor_tensor(Rs[d][:, kk, :, :], Um.unsqueeze(1).broadcast_to([128, 8, 128]), bc3(src[:, h0:h0 + 8], 128), ALU.mult), reads=[BdAx[k], B_c], writes=[BRs[d]])
                    yield
            S.op("pool", lambda v: v.tensor_tensor(v3(T3s[gp % 2][:]), v3(X[:, gp * 512:(gp + 1) * 512]), bc3(D_bc[:, gp * 8:gp * 8 + 8], 64), ALU.mult), reads=[BX, Bk], writes=[BT3s[gp % 2]])
            yield
            segs = []
            for d, Lm, Um in dl:
                for gg in range(2):
                    pseg, pbseg = bankB()
                    S.mm(pseg, [(Lm, Rs[d][:, kk, gg * 4:(gg + 1) * 4, :].rearrange("p h l -> p (h l)")) for kk in range(2)], reads=[BRs[d], B_c], writes=[pbseg])
                    yield
                    Ex, BE_ = Exs[ei[0] % 4], BEx[ei[0] % 4]
                    ei[0] += 1
                    S.op("act", lambda a: a.activation(Ex[:], pseg, AF.Exp), reads=[pbseg], writes=[BE_])
                    yield
                    segs.append((d, gg, Ex, BE_))
            for (d, gg, Ex, BE_) in segs:
                g = gp * 2 + gg
                S.op("dve", lambda v: v.tensor_tensor(MT[:, d, gg * 4:(gg + 1) * 4, :], Ex[:].rearrange("p (h l) -> p h l", l=128), CBm[k][:, d, g:g + 1, :].broadcast_to([128, 4, 128]), ALU.mult), reads=[BE_, BCB[k]], writes=[BMT])
                yield

        def stage3(k, gp):
            MT, BMT = MTs[gp % 2], BMTs[gp % 2]
            BCt, BBC = bcs[k], Bbc[k]
            T3, BT3 = T3s[gp % 2], BT3s[gp % 2]
            py, pby = bankB()
            pof, pbof = bankB()
            pob, pbob = bankB()
            for gg in range(2):
                g = gp * 2 + gg
                S.op("pe", lambda pe: pe.matmul(pof[:, gg * 256:(gg + 1) * 256], BCt[:, 8 + g, :], Hb16[k][:, g * 256:(g + 1) * 256], start=True, stop=True), reads=[BBC, BHb[k]], writes=[pbof])
                yield
                S.op("pe", lambda pe: pe.matmul(pob[:, gg * 256:(gg + 1) * 256], BCt[:, 8 + g, :], pvs[k][:, g * 256:(g + 1) * 256], start=True, stop=True), reads=[BBC, Bpv[k]], writes=[pbob])
                yield
            for gg in range(2):
                g = gp * 2 + gg
                for j in range(4):
                    h = 4 * g + j
                    S.mm(py[:, gg * 256 + j * 64:gg * 256 + (j + 1) * 64], [(MT[:, d, gg * 4 + j, :], xdt[k][d][:, h * 64:(h + 1) * 64]) for d in range(2)], reads=[BMT, Bxdt[k][0], Bxdt[k][1]], writes=[pby])
                    yield
            S.op("dve", lambda v: v.tensor_tensor(v3(T1[:]), v3(pof), bc3(ec[k][:, gp * 8:gp * 8 + 8], 64), ALU.mult), reads=[pbof, Bec[k]], writes=[BT1])
            yield
            S.op("dve", lambda v: v.tensor_tensor(v3(T2[:]), v3(pob), bc3(ec[k][:, 32 + gp * 8:32 + gp * 8 + 8], 64), ALU.mult), reads=[pbob, Bec[k]], writes=[BT2])
            yield
            S.op("dve", lambda v: v.tensor_tensor(T2[:], T2[:], T3[:], ALU.add), reads=[BT3], writes=[BT2])
            yield
            S.op("dve", lambda v: v.tensor_tensor(T1[:], T1[:], py, ALU.add), reads=[pby], writes=[BT1])
            yield
            S.op("dve", lambda v: v.tensor_tensor(y[:, gp * 512:(gp + 1) * 512], T1[:], T2[:], ALU.add), reads=[BT1, BT2], writes=[By])
            yield

        def tail(k, g0):
            Z, BZ = zss[k], Bzs[k]
            S.op("dve", lambda v: v.tensor_tensor(yz[:], y[:], Z[:], ALU.mult), reads=[BZ, By], writes=[Byz])
            yield
            S.op("pool", lambda v: v.tensor_tensor(sqt[:], yz[:], yz[:], ALU.mult), reads=[Byz], writes=[Bsq])
            yield
            S.op("dve", lambda v: v.tensor_reduce(gs[:], sqt[:].rearrange("p (g f) -> p g f", f=256), AX.X, ALU.add), reads=[Bsq], writes=[Bgs])
            yield
            S.op("act", lambda a: a.activation(gs[:], gs[:], AF.Sqrt, bias=epst[:], scale=1.0 / 256), reads=[Bgs, B_c], writes=[Bgs])
            yield
            S.op("dve", lambda v: v.reciprocal(gs[:], gs[:]), reads=[Bgs], writes=[Bgs])
            yield
            S.op("dve", lambda v: v.tensor_tensor(ygn[:].rearrange("p (g f) -> p g f", f=256), yz[:].rearrange("p (g f) -> p g f", f=256), bc3(gs[:], 256), ALU.mult), reads=[Byz, Bgs], writes=[Bygn])
            yield
            for hf in range(2):
                pt, pbt = bankB()
                ptb = pt.bitcast(BF16)
                for jj in range(8):
                    j = hf * 8 + jj
                    S.op("pe", lambda pe: pe.transpose(ptb[:, jj * 128:(jj + 1) * 128], ygn[:, j * 128:(j + 1) * 128], identb), reads=[Bygn, B_c], writes=[pbt])
                    yield
                S.op("act", lambda a: a.copy(ygs[:, hf * 8:(hf + 1) * 8, :], ptb[:, 0:1024].rearrange("p (j t) -> p j t", t=128)), reads=[pbt], writes=[Bygs])
                yield
            S.dma(Q, ygT_d[:, g0:g0 + 128].rearrange("(j p) t -> p j t", p=128), ygs[:], reads=[Bygs])
            yield

        pending = None
        for si in range(NS):
            Sq, t0 = seqs[si], tok0[si]
            for cidx in range(nchs[si]):
                gidx = gstart[si] + cidx
                k = gidx % 2
                n = [0]

                def pr():
                    n[0] += 1
                    return (gidx + min(n[0] / TB, 0.99)) / NCH
                while sh["front"] <= gidx:
                    yield None
                g0 = t0 + cidx * 128
                for _ in stage2(k, 0):
                    yield pr()
                if pending is not None:
                    pk, pg0, pgidx = pending
                    for _ in tail(pk, pg0):
                        yield pr()
                    sh["back"] = pgidx + 1
                    yield pr()
                for gp in range(4):
                    if gp + 1 < 4:
                        for _ in stage2(k, gp + 1):
                            yield pr()
                    for _ in stage3(k, gp):
                        yield pr()
                pending = (k, g0, gidx)
                if cidx == nchs[si] - 1:
                    for _ in tail(k, g0):
                        yield pr()
                    pending = None
                    sh["back"] = gidx + 1
                    yield pr()
        yield 1.0

    return genA(), genB()


def emit_p4a(c):
    from contextlib import ExitStack
    S, nc, sb, bank, seqs, tok0, NS = c["S"], c["nc"], c["sb"], c["bank"], c["seqs"], c["tok0"], c["NS"]
    identb, B_c, epst = c["identb"], c["B_c"], c["epst"]
    ygT_d, OT_d, gT_d, x_d, x1_d, h2T_d, gate_d, adaT_d = c["ygT_d"], c["OT_d"], c["gT_d"], c["x_d"], c["x1_d"], c["h2T_d"], c["gate_d"], c["adaT_d"]
    w_ssd_out, w_mla_out, w_o, g_ssd = c["w_ssd_out"], c["w_mla_out"], c["w_o"], c["g_ssd"]
    with ExitStack() as st:
        ws = sb(st, "p4_ws", [128, 16, 1024], BF16)
        wm = sb(st, "p4_wm", [128, 8, 1024], BF16)
        wo = sb(st, "p4_wo", [128, 8, 1024], BF16)
        gsn = sb(st, "p4_gsn", [128, 16], F32)
        ygl = [sb(st, "p4_yg", [128, 16, 512], BF16) for _ in range(2)]
        otl = [sb(st, "p4_ot", [128, 8, 512], BF16) for _ in range(2)]
        Bygl, Botl = [Buf(), Buf()], [Buf(), Buf()]
        tcount = [0]
        gt = sb(st, "p4_gt", [128, 16, 512], BF16)
        xt = sb(st, "p4_x", [128, 4, 1024], F32)
        mixf = sb(st, "p4_mixf", [128, 512], F32)
        mixf2 = sb(st, "p4_mixf2", [128, 512], F32)
        mix = sb(st, "p4_mix", [128, 8, 512], BF16)
        g1 = sb(st, "p4_g1", [128, 1024], F32)
        ab = sb(st, "p4_ab", [128, 2, 8], F32)
        x1 = sb(st, "p4_x1", [128, 4, 1024], F32)
        junk = sb(st, "p4_junk", [128, 1024], BF16)
        ss = sb(st, "p4_ss", [128, 4], F32)
        xn = sb(st, "p4_xn", [128, 4, 1024], BF16)
        h2 = sb(st, "p4_h2", [128, 8, 512], BF16)
        Bw, Byg, Bot, Bgt, Bx, Bmf, Bmf2, Bmix, Bg1, Bab, Bx1, Bss, Bxn, Bh2 = (Buf() for _ in range(14))
        S.dma("pool", ws[:], w_ssd_out.rearrange("(kc p) f -> p kc f", p=128), writes=[Bw])
        S.dma("pool", wm[:], w_mla_out.rearrange("(kc p) f -> p kc f", p=128), writes=[Bw])
        S.dma("pool", wo[:], w_o.rearrange("(kc p) f -> p kc f", p=128), writes=[Bw])
        with nc.allow_non_contiguous_dma(reason="tiny"):
            S.dma("sp", gsn[:], g_ssd.rearrange("(j p) -> p j", p=128), writes=[Bw])
        for kc in range(16):
            S.op("dve", lambda v: v.tensor_scalar(ws[:, kc, :], ws[:, kc, :], gsn[:, kc:kc + 1], None, ALU.mult), reads=[Bw], writes=[Bw])
        for si in range(NS):
            S.dma("sp", g1[:], gate_d[si, 0, :].partition_broadcast(128), writes=[Bg1])
            with nc.allow_non_contiguous_dma(reason="tiny"):
                S.dma("sp", ab[:, 0, :], adaT_d[si, 2, :].rearrange("(kc p) -> p kc", p=128), writes=[Bab])
                S.dma("sp", ab[:, 1, :], adaT_d[si, 3, :].rearrange("(kc p) -> p kc", p=128), writes=[Bab])
            for ti in range(seqs[si] // 512):
                g0 = tok0[si] + ti * 512
                yg, ot, Byg, Bot = ygl[tcount[0] % 2], otl[tcount[0] % 2], Bygl[tcount[0] % 2], Botl[tcount[0] % 2]
                tcount[0] += 1
                S.dma("sp", yg[:], ygT_d[:, g0:g0 + 512].rearrange("(j p) t -> p j t", p=128), writes=[Byg])
                S.dma("sp", ot[:], OT_d[:, g0:g0 + 512].rearrange("(j p) t -> p j t", p=128), writes=[Bot])
                S.dma("sp", gt[:], gT_d[:, g0:g0 + 512].rearrange("(j p) t -> p j t", p=128), writes=[Bgt])
                S.dma("sp", xt[:], x_d[g0:g0 + 512, :].rearrange("(s p) f -> p s f", p=128), writes=[Bx])
                for oc in range(8):
                    pa, pba = bank()
                    S.mm(pa, [(ws[:, kc, oc * 128:(oc + 1) * 128], yg[:, kc, :]) for kc in range(16)], reads=[Bw, Byg], writes=[pba])
                    pm, pbm = bank()
                    S.mm(pm, [(wm[:, kc, oc * 128:(oc + 1) * 128], ot[:, kc, :]) for kc in range(8)], reads=[Bw, Bot], writes=[pbm])
                    S.op("dve", lambda v: v.tensor_tensor(mixf[:], pa[:], gt[:, oc, :], ALU.mult), reads=[pba, Bgt], writes=[Bmf])
                    S.op("dve", lambda v: v.tensor_tensor(mixf2[:], pm[:], gt[:, 8 + oc, :], ALU.mult), reads=[pbm, Bgt], writes=[Bmf2])
                    S.op("pool", lambda v: v.tensor_tensor(mix[:, oc, :], mixf[:], mixf2[:], ALU.add), reads=[Bmf, Bmf2], writes=[Bmix])
                for s4 in range(4):
                    for hf in range(2):
                        po, pbo = bank()
                        S.mm(po, [(mix[:, kc, s4 * 128:(s4 + 1) * 128], wo[:, kc, hf * 512:(hf + 1) * 512]) for kc in range(8)], reads=[Bw, Bmix], writes=[pbo])
                        S.op("dve", lambda v: v.tensor_tensor(x1[:, s4, hf * 512:(hf + 1) * 512], po[:], g1[:, hf * 512:(hf + 1) * 512], ALU.mult), reads=[pbo, Bg1], writes=[Bx1])
                    S.op("pool", lambda v: v.tensor_tensor(x1[:, s4, :], x1[:, s4, :], xt[:, s4, :], ALU.add), reads=[Bx], writes=[Bx1])
                    S.op("act", lambda a: a.activation(junk[:], x1[:, s4, :], AF.Square, accum_out=ss[:, s4:s4 + 1]), reads=[Bx1], writes=[Bss])
                S.dma("st", x1_d[g0:g0 + 512, :].rearrange("(s p) f -> p s f", p=128), x1[:], reads=[Bx1])
                S.op("act", lambda a: a.activation(ss[:], ss[:], AF.Sqrt, bias=epst[:], scale=1.0 / 1024), reads=[Bss, B_c], writes=[Bss])
                S.op("dve", lambda v: v.reciprocal(ss[:], ss[:]), reads=[Bss], writes=[Bss])
                for s4 in range(4):
                    S.op("pool" if s4 % 2 else "dve", lambda v: v.tensor_scalar(xn[:, s4, :], x1[:, s4, :], ss[:, s4:s4 + 1], None, ALU.mult), reads=[Bx1, Bss], writes=[Bxn])
                for kc in range(8):
                    pt, pbt = bank()
                    ptb = pt[:].bitcast(BF16)
                    for s4 in range(4):
                        S.op("pe", lambda pe: pe.transpose(ptb[:, s4 * 128:(s4 + 1) * 128], xn[:, s4, kc * 128:(kc + 1) * 128], identb), reads=[Bxn, B_c], writes=[pbt])
                    S.op("dve", lambda v: v.tensor_scalar(h2[:, kc, :], ptb[:, 0:512], ab[:, 0, kc:kc + 1], ab[:, 1, kc:kc + 1], ALU.mult, ALU.add), reads=[pbt, Bab], writes=[Bh2])
                S.dma("st", h2T_d[:, g0:g0 + 512].rearrange("(j p) t -> p j t", p=128), h2[:], reads=[Bh2])
        S.barrier()


def emit_p4b(c):
    from contextlib import ExitStack
    S, nc, sb, bank, seqs, tok0, NS = c["S"], c["nc"], c["sb"], c["bank"], c["seqs"], c["tok0"], c["NS"]
    B_c, epst = c["B_c"], c["epst"]
    x1_d, h2T_d, gate_d, y_d, w_mlp_in, w_mlp_out, g_final = c["x1_d"], c["h2T_d"], c["gate_d"], c["y_d"], c["w_mlp_in"], c["w_mlp_out"], c["g_final"]
    TT = 256
    with ExitStack() as st:
        w1 = sb(st, "p5_w1", [128, 8, 4096], BF16)
        w2 = sb(st, "p5_w2", [128, 32, 1024], BF16)
        gf = sb(st, "p5_gf", [128, 1024], F32)
        g2 = sb(st, "p5_g2", [128, 1024], F32)
        h2 = [sb(st, "p5_h2", [128, 8, TT], BF16) for _ in range(2)]
        x1 = [sb(st, "p5_x1", [128, 2, 1024], F32) for _ in range(2)]
        rl = [sb(st, "p5_rl", [128, TT], F32) for _ in range(2)]
        rT = sb(st, "p5_rT", [128, 32, TT], BF16)
        x2 = sb(st, "p5_x2", [128, 2, 1024], F32)
        junk = sb(st, "p5_junk", [128, 1024], BF16)
        ss = sb(st, "p5_ss", [128, 2], F32)
        yo = sb(st, "p5_yo", [128, 2, 1024], F32)
        Bw, Bg2, Bh2, Bx1, Brl, BrT, Bx2, Bss, Byo = Buf(), Buf(), [Buf(), Buf()], [Buf(), Buf()], [Buf(), Buf()], Buf(), Buf(), Buf(), Buf()
        S.dma("pool", w1[:], w_mlp_in.rearrange("(kc p) f -> p kc f", p=128), writes=[Bw])
        for q4 in range(4):
            S.dma("pool", w2[:, q4 * 8:(q4 + 1) * 8, :], w_mlp_out[q4 * 1024:(q4 + 1) * 1024, :].rearrange("(kc p) f -> p kc f", p=128), writes=[Bw])
        S.dma("sp", gf[:], g_final.partition_broadcast(128), writes=[Bw])
        it = 0
        for si in range(NS):
            S.dma("sp", g2[:], gate_d[si, 1, :].partition_broadcast(128), writes=[Bg2])
            for ti in range(seqs[si] // TT):
                g0 = tok0[si] + ti * TT
                k = it % 2
                it += 1
                H, X1 = h2[k], x1[k]
                S.dma("sp", H[:], h2T_d[:, g0:g0 + TT].rearrange("(j p) t -> p j t", p=128), writes=[Bh2[k]])
                S.dma("sp", X1[:], x1_d[g0:g0 + TT, :].rearrange("(s p) f -> p s f", p=128), writes=[Bx1[k]])
                for fc in range(32):
                    pf, pbf = bank()
                    S.mm(pf[:, 0:TT], [(w1[:, kc, fc * 128:(fc + 1) * 128], H[:, kc, :]) for kc in range(8)], reads=[Bw, Bh2[k]], writes=[pbf])
                    RL, BRL = rl[fc % 2], Brl[fc % 2]
                    S.op("act", lambda a: a.activation(RL[:], pf[:, 0:TT], AF.Relu), reads=[pbf], writes=[BRL])
                    S.op("pool" if fc % 2 else "dve", lambda v: v.tensor_tensor(rT[:, fc, :], RL[:], RL[:], ALU.mult), reads=[BRL], writes=[BrT])
                for s2 in range(2):
                    for hf in range(2):
                        po, pbo = bank()
                        S.mm(po, [(rT[:, kc, s2 * 128:(s2 + 1) * 128], w2[:, kc, hf * 512:(hf + 1) * 512]) for kc in range(32)], reads=[Bw, BrT], writes=[pbo])
                        S.op("dve", lambda v: v.tensor_tensor(x2[:, s2, hf * 512:(hf + 1) * 512], po[:], g2[:, hf * 512:(hf + 1) * 512], ALU.mult), reads=[pbo, Bg2], writes=[Bx2])
                    S.op("pool", lambda v: v.tensor_tensor(x2[:, s2, :], x2[:, s2, :], X1[:, s2, :], ALU.add), reads=[Bx1[k]], writes=[Bx2])
                    S.op("act", lambda a: a.activation(junk[:], x2[:, s2, :], AF.Square, accum_out=ss[:, s2:s2 + 1]), reads=[Bx2], writes=[Bss])
                S.op("act", lambda a: a.activation(ss[:], ss[:], AF.Sqrt, bias=epst[:], scale=1.0 / 1024), reads=[Bss, B_c], writes=[Bss])
                S.op("dve", lambda v: v.reciprocal(ss[:], ss[:]), reads=[Bss], writes=[Bss])
                for s2 in range(2):
                    S.op("dve", lambda v: v.scalar_tensor_tensor(yo[:, s2, :], x2[:, s2, :], ss[:, s2:s2 + 1], gf[:], ALU.mult, ALU.mult), reads=[Bx2, Bss, Bw], writes=[Byo])
                S.dma("st", y_d[g0:g0 + TT, :].rearrange("(s p) f -> p s f", p=128), yo[:], reads=[Byo])
        S.barrier()


def core_inputs(x_all, c_all, W, g_final, cos, sin):
    f = lambda a: np.ascontiguousarray(np.asarray(a, dtype=np.float32))
    return {
        "x": f(x_all), "c": f(c_all),
        "w_ada": f(W["w_ada"]), "b_ada": f(W["b_ada"]), "g_norm1": f(W["g_norm1"]), "w_in": f(W["w_in"]),
        "conv_w": f(W["conv_w"]), "conv_b": f(W["conv_b"]),
        "dt_bias": f(np.concatenate([W["dt_bias_fwd"], W["dt_bias_bwd"]])),
        "a_log": f(np.concatenate([W["a_log_fwd"], W["a_log_bwd"]])),
        "d_skip": f(W["d_skip"]), "g_ssd_norm": f(W["g_ssd_norm"]), "w_ssd_out": f(W["w_ssd_out"]),
        "g_q_norm": f(W["g_q_norm"]), "w_q_b": f(W["w_q_b"]), "g_kv_norm": f(W["g_kv_norm"]), "w_kv_b": f(W["w_kv_b"]),
        "w_mla_out": f(W["w_mla_out"]), "w_o": f(W["w_o"]), "g_norm2": f(W["g_norm2"]),
        "w_mlp_in": f(W["w_mlp_in"]), "w_mlp_out": f(W["w_mlp_out"]), "g_final": f(g_final),
        "consts": make_consts(), "cos_t": f(cos), "sin_t": f(sin),
    }


_WNAMES = ["w_ada", "b_ada", "g_norm1", "w_in", "conv_w", "conv_b", "dt_bias_fwd", "dt_bias_bwd", "a_log_fwd",
           "a_log_bwd", "d_skip", "g_ssd_norm", "w_ssd_out", "g_q_norm", "w_q_b", "g_kv_norm", "w_kv_b",
           "w_mla_out", "w_o", "g_norm2", "w_mlp_in", "w_mlp_out"]


def kernel(x_prompt, x_sample, c_prompt, c_sample, g_final, **kw):
    W = {k: np.asarray(kw[k])[0] for k in _WNAMES}
    x_prompt = np.asarray(x_prompt); x_sample = np.asarray(x_sample)
    c_prompt = np.asarray(c_prompt); c_sample = np.asarray(c_sample)
    n = 8
    seqs = (2048, 2048, 4096, 4096)
    cos, sin = rope_tables(4096)
    import os
    ph = os.environ.get("KPHASES")
    nc = build(seqs=seqs, phases=tuple(ph.split(","))) if ph else build(seqs=seqs)
    in_maps = []
    for i in range(n):
        xa = np.concatenate([x_prompt[2 * i].reshape(-1, D), x_prompt[2 * i + 1].reshape(-1, D),
                             x_sample[2 * i].reshape(-1, D), x_sample[2 * i + 1].reshape(-1, D)], axis=0)
        ca = np.stack([c_prompt[2 * i], c_prompt[2 * i + 1], c_sample[2 * i], c_sample[2 * i + 1]], axis=0)
        in_maps.append(core_inputs(xa, ca, W, g_final, cos, sin))
    res = run_bass_kernel_spmd(nc, in_maps, core_ids=list(range(n)))
    yp = np.zeros((16, 2048, D), np.float32)
    ys = np.zeros((16, 4096, D), np.float32)
    for i in range(n):
        y = np.asarray(res.results[i]["y"])
        yp[2 * i] = y[0:2048]
        yp[2 * i + 1] = y[2048:4096]
        ys[2 * i] = y[4096:8192]
        ys[2 * i + 1] = y[8192:12288]
    return (yp, ys)
```

```python
import numpy as np
import concourse.bass as bass
import concourse.mybir as mybir
from concourse.bass_utils import run_bass_kernel_spmd

F32 = mybir.dt.float32
BF16 = mybir.dt.bfloat16
AF = mybir.ActivationFunctionType
ALU = mybir.AluOpType
AX = mybir.AxisListType

D = 1024
DI = 2048
NH = 32
DIN = 8928
EPS = 1e-6
C_Z, C_X, C_DT, C_QA, C_KV, C_KR, C_G = 0, 2048, 6144, 6208, 6592, 6848, 6880
ATT_SCALE = 96 ** -0.5


class Buf:
    __slots__ = ("w", "r")

    def __init__(self):
        self.w = None
        self.r = {}


class Eng:
    def __init__(self, key, h, sem):
        self.key, self.h, self.sem = key, h, sem
        self.count = 0
        self.waited = {}
        self.slots = []
        self.dma_i = 0


class Slot:
    def __init__(self, key, sem):
        self.key, self.sem, self.count = key, sem, 0


class Sch:
    def __init__(self, nc, stack, nslots=12):
        self.nc = nc
        self.E = {}
        for key, h in (("pe", nc.tensor), ("act", nc.scalar), ("dve", nc.vector),
                       ("pool", nc.gpsimd), ("sp", nc.sync)):
            sem = stack.enter_context(nc.semaphore("sem_" + key))
            self.E[key] = Eng(key, h, sem)
        for q in ("sp", "pool", "act"):
            for i in range(nslots):
                sem = stack.enter_context(nc.semaphore("dq_%s_%d" % (q, i)))
                self.E[q].slots.append(Slot("dq_%s_%d" % (q, i), sem))

    def _wait(self, e, deps):
        for (key, sem, val) in deps:
            if key == e.key and key == "pe":
                continue
            if e.waited.get(key, 0) >= val:
                continue
            e.h.wait_ge(sem, val)
            e.waited[key] = val

    @staticmethod
    def _deps(reads, writes):
        deps = []
        for b in reads:
            if b.w is not None:
                deps.append(b.w)
        for b in writes:
            if b.w is not None:
                deps.append(b.w)
            for k, (s, v) in b.r.items():
                deps.append((k, s, v))
        return deps

    @staticmethod
    def _mark(tok, reads, writes):
        for b in reads:
            b.r[tok[0]] = (tok[1], tok[2])
        for b in writes:
            b.w = tok
            b.r = {}

    def op(self, eng, fn, reads=(), writes=()):
        e = self.E[eng]
        self._wait(e, self._deps(reads, writes))
        ins = fn(e.h)
        e.count += 1
        ins.then_inc(e.sem, 1)
        self._mark((e.key, e.sem, e.count), reads, writes)

    def mm(self, out, pairs, reads=(), writes=()):
        n = len(pairs)

        def fn(pe):
            ins = None
            for i, (l, r) in enumerate(pairs):
                ins = pe.matmul(out, l, r, start=(i == 0), stop=(i == n - 1))
            return ins
        self.op("pe", fn, reads=reads, writes=writes)

    def dma(self, q, out, in_, reads=(), writes=()):
        if q == "st":
            q = "pool"
        e = self.E[q]
        self._wait(e, self._deps(reads, writes))
        sl = e.slots[e.dma_i % len(e.slots)]
        e.dma_i += 1
        if sl.count > 0:
            self._wait(e, [(sl.key, sl.sem, 16 * sl.count)])
        e.h.dma_start(out=out, in_=in_).then_inc(sl.sem, 16)
        sl.count += 1
        self._mark((sl.key, sl.sem, 16 * sl.count), reads, writes)

    def barrier(self):
        toks = []
        for e in self.E.values():
            if e.count:
                toks.append((e.key, e.sem, e.count))
            for sl in e.slots:
                if sl.count:
                    toks.append((sl.key, sl.sem, 16 * sl.count))
        for e in self.E.values():
            self._wait(e, toks)


def make_consts():
    i = np.arange(128)
    c = np.zeros((128, 6, 128), np.float32)
    c[:, 0, :] = np.eye(128)
    c[:, 1, :] = (i[:, None] <= i[None, :])
    c[:, 2, :] = (i[:, None] >= i[None, :])
    c[:, 3, :] = (i[:, None] > i[None, :])
    c[:, 4, :] = (i[:, None] < i[None, :])
    c[:, 5, :] = 1.0
    return c.reshape(128, 768)


def rope_tables(smax):
    inv = (1.0 / (np.float32(10000.0) ** (np.arange(0, 32, 2, dtype=np.float32) / np.float32(32)))).astype(np.float32)
    ang = np.arange(smax, dtype=np.float32)[:, None] * inv[None, :]
    cos = np.cos(ang).astype(np.float32).T
    sin = np.sin(ang).astype(np.float32).T
    return (np.ascontiguousarray(np.concatenate([cos, cos], 0)),
            np.ascontiguousarray(np.concatenate([sin, sin], 0)))


def build(seqs=(2048, 2048, 4096, 4096), phases=("p0", "p1a", "p1b", "p1c", "p1d", "p2", "p3", "p4a", "p4b"), debug=False):
    nc = bass.Bass("TRN2", target_bir_lowering=False)
    NS = len(seqs)
    NT = sum(seqs)
    SMAX = max(seqs)
    tok0 = [sum(seqs[:i]) for i in range(NS)]
    skind = "ExternalOutput" if debug else "Internal"

    def din(name, shape, dt=F32):
        return nc.dram_tensor(name, list(shape), dt, kind="ExternalInput").ap()

    def dscr(name, shape, dt):
        return nc.dram_tensor(name, list(shape), dt, kind=skind).ap()

    x_d = din("x", [NT, D])
    c_d = din("c", [NS, D])
    w_ada = din("w_ada", [D, 6 * D])
    b_ada = din("b_ada", [6 * D])
    g_norm1 = din("g_norm1", [D])
    w_in = din("w_in", [D, DIN])
    conv_w = din("conv_w", [5, 4096])
    conv_b = din("conv_b", [4096])
    dtb = din("dt_bias", [64])
    alog = din("a_log", [64])
    dskip = din("d_skip", [32])
    g_ssd = din("g_ssd_norm", [DI])
    w_ssd_out = din("w_ssd_out", [DI, D])
    g_q = din("g_q_norm", [384])
    w_q_b = din("w_q_b", [384, 1536])
    g_kv = din("g_kv_norm", [256])
    w_kv_b = din("w_kv_b", [256, 2048])
    w_mla_out = din("w_mla_out", [D, D])
    w_o = din("w_o", [D, D])
    g_norm2 = din("g_norm2", [D])
    w_mlp_in = din("w_mlp_in", [D, 4 * D])
    w_mlp_out = din("w_mlp_out", [4 * D, D])
    g_final = din("g_final", [D])
    consts_d = din("consts", [128, 768])
    cos_d = din("cos_t", [32, SMAX])
    sin_d = din("sin_t", [32, SMAX])

    y_d = nc.dram_tensor("y", [NT, D], F32, kind="ExternalOutput").ap()

    adaT_d = dscr("adaT_s", [NS, 4, D], F32)
    gate_d = dscr("gate_s", [NS, 2, D], F32)
    uT_d = dscr("uT_s", [4096, NT], BF16)
    zs_d = dscr("zs_s", [NT, DI], BF16)
    dt_d = dscr("dt_s", [NT, 64], F32)
    qnT_d = dscr("qnT_s", [384, NT], BF16)
    cnT_d = dscr("cnT_s", [256, NT], BF16)
    kr_d = dscr("kr_s", [32, NT], BF16)
    gT_d = dscr("gT_s", [2048, NT], BF16)
    xtok_d = dscr("xtok_s", [NT, 3072], BF16)
    bcT_d = dscr("bcT_s", [2048, NT], BF16)
    KT_d = dscr("KT_s", [1024, NT], BF16)
    QT_d = dscr("QT_s", [1536, NT], BF16)
    V_d = dscr("V_s", [16, NT * 64], BF16)
    OT_d = dscr("OT_s", [D, NT], BF16)
    prevb_d = dscr("prevb_s", [SMAX // 128, 128, DI], BF16)
    ygT_d = dscr("ygT_s", [DI, NT], BF16)
    x1_d = dscr("x1_s", [NT, D], F32)
    h2T_d = dscr("h2T_s", [D, NT], BF16)

    from contextlib import ExitStack
    with ExitStack() as top:
        S = Sch(nc, top)
        _uid = [0]

        def sb(st, name, shape, dt):
            _uid[0] += 1
            return st.enter_context(nc.sbuf_tensor("%s_%d" % (name, _uid[0]), list(shape), dt))
        psall = top.enter_context(nc.psum_tensor("psall", [128, 4096], F32))
        banks = [psall[:, i * 512:(i + 1) * 512] for i in range(8)]
        bbuf = [Buf() for _ in range(8)]
        bi = {None: 0, "a": 0, "b": 0}
        pools = {None: list(range(8)), "a": [4], "b": [5, 6, 7]}

        def bank(pool=None):
            lst = pools[pool]
            i = lst[bi[pool] % len(lst)]
            bi[pool] += 1
            return banks[i], bbuf[i]

        cst32 = sb(top, "cst32", [128, 768], F32)
        cstb = sb(top, "cstb", [128, 768], BF16)
        epst = sb(top, "epst", [128, 1], F32)
        onet = sb(top, "onet", [128, 1], F32)
        B_c = Buf()
        S.dma("sp", cst32[:], consts_d, writes=[B_c])
        S.op("dve", lambda v: v.tensor_copy(cstb[:], cst32[:]), reads=[B_c], writes=[B_c])
        S.op("dve", lambda v: v.memset(epst[:], EPS), writes=[B_c])
        S.op("dve", lambda v: v.memset(onet[:], 1.0), writes=[B_c])
        identb = cstb[:, 0:128]
        Ufb, Ubb, Lfb, Lbb, onesb = (cstb[:, 128 * k:128 * (k + 1)] for k in range(1, 6))
        Uf32, Ub32 = cst32[:, 128:256], cst32[:, 256:384]
        ones32 = cst32[:, 640:768]

        def rstd_from_ss(eng_list, out, ss, n, rb, wb):
            S.op("act", lambda a: a.activation(out, ss, AF.Sqrt, bias=epst[0:out.shape[0], :], scale=1.0 / n), reads=rb + [B_c], writes=wb)
            S.op("dve", lambda v: v.reciprocal(out, out), reads=wb, writes=wb)

        if "p0" in phases:
            with ExitStack() as st:
                cT = sb(st, "p0_cT", [128, 8, NS], F32)
                cbc = sb(st, "p0_cbc", [128, 8, NS, 128], F32)
                wa = [sb(st, "p0_wa%d" % i, [128, 8, 1024], F32) for i in range(2)]
                bT = sb(st, "p0_bT", [128, 48], F32)
                brow = sb(st, "p0_brow", [128, 2, 1024], F32)
                gT1 = sb(st, "p0_g", [128, 2, 8], F32)
                res = sb(st, "p0_res", [128, 6, 8, NS], F32)
                vec = sb(st, "p0_vec", [128, NS, 4, 8], F32)
                grow = sb(st, "p0_grow", [128, 1024], F32)
                Bc, Bw, Bb, Br, Bv, Bg = Buf(), [Buf(), Buf()], Buf(), Buf(), Buf(), Buf()
                with nc.allow_non_contiguous_dma(reason="tiny transposed loads"):
                    for b in range(NS):
                        S.dma("sp", cT[:, :, b], c_d[b, :].rearrange("(kc p) -> p kc", p=128), writes=[Bc])
                    S.dma("sp", bT[:], b_ada.rearrange("(j p) -> p j", p=128), writes=[Bb])
                    S.dma("sp", gT1[:, 0, :], g_norm1.rearrange("(j p) -> p j", p=128), writes=[Bb])
                    S.dma("sp", gT1[:, 1, :], g_norm2.rearrange("(j p) -> p j", p=128), writes=[Bb])
                S.dma("sp", brow[:, 0, :], b_ada[2048:3072].partition_broadcast(128), writes=[Bb])
                S.dma("sp", brow[:, 1, :], b_ada[5120:6144].partition_broadcast(128), writes=[Bb])
                S.op("act", lambda a: a.activation(cT[:], cT[:], AF.Silu), reads=[Bc], writes=[Bc])
                S.op("dve", lambda v: v.tensor_copy(cbc[:], cT[:].unsqueeze(3).broadcast_to([128, 8, NS, 128])), reads=[Bc], writes=[Bc])
                for j in range(6):
                    w = wa[j % 2]
                    S.dma("sp", w[:], w_ada[:, j * 1024:(j + 1) * 1024].rearrange("(kc p) f -> p kc f", p=128), writes=[Bw[j % 2]])
                    pt, pb = bank()
                    for oc in range(8):
                        for kc in range(8):
                            S.op("pe", lambda pe, oc=oc, kc=kc: pe.matmul(pt[:, oc * NS:(oc + 1) * NS], w[:, kc, oc * 128:(oc + 1) * 128], cT[:, kc, :], start=(kc == 0), stop=(kc == 7)),
                                 reads=[Bw[j % 2], Bc], writes=[pb])
                    S.op("dve", lambda v, j=j: v.tensor_tensor(res[:, j, :, :], pt[:, 0:8 * NS].rearrange("p (o b) -> p o b", b=NS),
                                                           bT[:, j * 8:(j + 1) * 8].unsqueeze(2).broadcast_to([128, 8, NS]), ALU.add),
                         reads=[pb, Bb], writes=[Br])
                    if j in (2, 5):
                        gi = 0 if j == 2 else 1
                        for b in range(NS):
                            for hf in range(2):
                                pt2, pb2 = bank()
                                for kc in range(8):
                                    S.op("pe", lambda pe, kc=kc, b=b, hf=hf: pe.matmul(pt2[:], cbc[:, kc, b, :], w[:, kc, hf * 512:(hf + 1) * 512], start=(kc == 0), stop=(kc == 7)),
                                         reads=[Bw[j % 2], Bc], writes=[pb2])
                                S.op("dve", lambda v, hf=hf, gi=gi: v.tensor_tensor(grow[:, hf * 512:(hf + 1) * 512], pt2[:], brow[:, gi, hf * 512:(hf + 1) * 512], ALU.add),
                                     reads=[pb2, Bb], writes=[Bg])
                            S.dma("st", gate_d[b, gi, :], grow[0:1, :], reads=[Bg])
                for b in range(NS):
                    for (k, jsc, jsh, gi) in ((0, 1, 0, 0), (2, 4, 3, 1)):
                        S.op("dve", lambda v, b=b, k=k, jsc=jsc, gi=gi: v.scalar_tensor_tensor(vec[:, b, k, :], res[:, jsc, :, b], 1.0, gT1[:, gi, :], ALU.add, ALU.mult), reads=[Br, Bb], writes=[Bv])
                        S.op("dve", lambda v, b=b, k=k, jsh=jsh: v.tensor_copy(vec[:, b, k + 1, :], res[:, jsh, :, b]), reads=[Br], writes=[Bv])
                with nc.allow_non_contiguous_dma(reason="tiny transposed stores"):
                    for b in range(NS):
                        for k in range(4):
                            S.dma("st", adaT_d[b, k, :].rearrange("(kc p) -> p kc", p=128), vec[:, b, k, :], reads=[Bv])
                S.barrier()

        for part in ("a", "b"):
            if ("p1" + part) not in phases:
                continue
            WC = 4096 if part == "a" else 4832
            cm = (lambda c: c - 2048) if part == "a" else (lambda c: c if c < 2048 else c - 4096)
            with ExitStack() as st:
                W = sb(st, "p1_W", [128, 8, WC], BF16)
                wkr = sb(st, "p1_wkr", [128, 8, 64], BF16)
                dtb_bc = sb(st, "p1_dtb", [128, 4, 64], F32)
                ab = sb(st, "p1_ab", [128, 2, 8], F32)
                xt = [sb(st, "p1_x%d" % i, [128, 4, 1024], F32) for i in range(1)] * 2
                junk = sb(st, "p1_junk", [128, 1024], BF16)
                ssl = [sb(st, "p1_ss", [128, 4], F32) for _ in range(2)]
                xnl = [sb(st, "p1_xn", [128, 4, 1024], BF16) for _ in range(2)]
                Bssl, Bxnl = [Buf(), Buf()], [Buf(), Buf()]
                hT = [sb(st, "p1_hT%d" % i, [128, 8, 512], BF16) for i in range(1)] * 2
                stg = [sb(st, "p1_stg%d" % i, [128, 512], F32) for i in range(4)]
                stgb = [sb(st, "p1_stgb%d" % i, [128, 512], BF16) for i in range(4)]
                qa = sb(st, "p1_qa", [128, 5, 512], F32) if part == "b" else None
                sq = sb(st, "p1_sq", [128, 5, 512], F32) if part == "b" else None
                rs = sb(st, "p1_rs", [128, 2, 512], F32)
                qn = sb(st, "p1_qn", [128, 5, 512], BF16)
                cs = sb(st, "p1_cs", [32, 2, 512], F32)
                krt = sb(st, "p1_krt", [32, 2, 512], F32)
                krb = sb(st, "p1_krb", [32, 512], BF16)
                zst = [sb(st, "p1_zst%d" % i, [128, 2048], BF16) if part == "b" else None for i in range(2)]
                dts = sb(st, "p1_dts", [128, 5, 256], F32)
                BW, Bab, Bx, Bss, Bxn, BhT = Buf(), Buf(), [Buf()] * 2, Buf(), Buf(), [Buf()] * 2
                Bstg, Bstgb = [Buf() for _ in range(4)], [Buf() for _ in range(4)]
                Bqa, Bsq, Brs, Bqn, Bcs, Bkr, Bkrb, Bz, Bdts = Buf(), Buf(), Buf(), Buf(), Buf(), Buf(), Buf(), [Buf(), Buf()], Buf()
                for kc in range(8):
                    if part == "a":
                        S.dma("pool", W[:, kc, :], w_in[kc * 128:(kc + 1) * 128, 2048:6144], writes=[BW])
                    else:
                        S.dma("pool", W[:, kc, 0:2048], w_in[kc * 128:(kc + 1) * 128, 0:2048], writes=[BW])
                        S.dma("pool", W[:, kc, 2048:4832], w_in[kc * 128:(kc + 1) * 128, 6144:8928], writes=[BW])
                CKR = cm(C_KR) if part == "b" else 0
                S.op("dve", lambda v: v.tensor_copy(wkr[:, :, 0:32], W[:, :, CKR:CKR + 32]), reads=[BW], writes=[BW])
                S.op("dve", lambda v: v.tensor_scalar(wkr[:, :, 32:48], W[:, :, CKR + 16:CKR + 32], -1.0, None, ALU.mult), reads=[BW], writes=[BW])
                S.op("dve", lambda v: v.tensor_copy(wkr[:, :, 48:64], W[:, :, CKR:CKR + 16]), reads=[BW], writes=[BW])
                for s4 in range(4):
                    S.dma("sp", dtb_bc[:, s4, :], dtb.partition_broadcast(128), writes=[BW])
                stg_i = [0]
                tiles = [(si, ti) for si in range(NS) for ti in range(seqs[si] // 512)]

                def stageA(idx):
                    si, ti = tiles[idx]
                    g0 = tok0[si] + ti * 512
                    k2 = idx % 2
                    X = xt[0]
                    S.dma("sp", X[:], x_d[g0:g0 + 512, :].rearrange("(s p) f -> p s f", p=128), writes=[Bx[0]])
                    for s4 in range(4):
                        S.op("act", lambda a: a.activation(junk[:], X[:, s4, :], AF.Square, accum_out=ssl[k2][:, s4:s4 + 1]), reads=[Bx[0]], writes=[Bssl[k2]])
                    rstd_from_ss(None, ssl[k2][:], ssl[k2][:], 1024.0, [Bssl[k2]], [Bssl[k2]])
                    for s4 in range(4):
                        S.op("pool" if s4 % 2 else "dve", lambda v: v.tensor_scalar(xnl[k2][:, s4, :], X[:, s4, :], ssl[k2][:, s4:s4 + 1], None, ALU.mult), reads=[Bx[0], Bssl[k2]], writes=[Bxnl[k2]])

                def stageB(idx):
                    si, ti = tiles[idx]
                    k2 = idx % 2
                    H = hT[0]
                    if ti == 0:
                        with nc.allow_non_contiguous_dma(reason="tiny"):
                            S.dma("sp", ab[:, 0, :], adaT_d[si, 0, :].rearrange("(kc p) -> p kc", p=128), writes=[Bab])
                            S.dma("sp", ab[:, 1, :], adaT_d[si, 1, :].rearrange("(kc p) -> p kc", p=128), writes=[Bab])
                    for kc in range(8):
                        pt, pb = bank()
                        ptb = pt[:].bitcast(BF16)
                        for s4 in range(4):
                            S.op("pe", lambda pe: pe.transpose(ptb[:, s4 * 128:(s4 + 1) * 128], xnl[k2][:, s4, kc * 128:(kc + 1) * 128], identb), reads=[Bxnl[k2], B_c], writes=[pb])
                        S.op("dve", lambda v: v.tensor_scalar(H[:, kc, :], ptb[:, 0:512], ab[:, 0, kc:kc + 1], ab[:, 1, kc:kc + 1], ALU.mult, ALU.add), reads=[pb, Bab], writes=[BhT[0]])

                def body(idx):
                    si, ti = tiles[idx]
                    g0 = tok0[si] + ti * 512
                    p0 = ti * 512
                    par = 0
                    H = hT[0]
                    def fm(cols, m, lw=None):
                        pt, pb = bank()
                        l = (lw if lw is not None else W)
                        cc0 = cols if lw is not None else cm(cols)
                        S.mm(pt[0:m, :], [(l[:, kc, cc0:cc0 + m], H[:, kc, :]) for kc in range(8)], reads=[BW, BhT[par]], writes=[pb])
                        return pt, pb
                    for j in range(32 if part == "a" else 0):
                        pt, pb = fm(C_X + j * 128, 128)
                        k = stg_i[0] % 4
                        stg_i[0] += 1
                        S.op("act", lambda a, k=k: a.copy(stgb[k][:], pt[:]), reads=[pb], writes=[Bstgb[k]])
                        S.dma("st", uT_d[j * 128:(j + 1) * 128, g0:g0 + 512], stgb[k][:], reads=[Bstgb[k]])
                    if part == "a":
                        return
                    for j in range(5):
                        pt, pb = fm(C_QA + j * 128, 128)
                        S.op("act", lambda a, j=j: a.copy(qa[:, j, :], pt[:]), reads=[pb], writes=[Bqa])
                        S.op("dve", lambda v, j=j: v.tensor_tensor(sq[:, j, :], qa[:, j, :], qa[:, j, :], ALU.mult), reads=[Bqa], writes=[Bsq])
                    S.dma("sp", cs[:, 0, :], cos_d[:, p0:p0 + 512], writes=[Bcs])
                    S.dma("sp", cs[:, 1, :], sin_d[:, p0:p0 + 512], writes=[Bcs])
                    pA, pbA = fm(0, 32, wkr)
                    pB, pbB = fm(32, 32, wkr)
                    S.op("dve", lambda v: v.tensor_tensor(krt[:, 0, :], pA[0:32, :], cs[:, 0, :], ALU.mult), reads=[pbA, Bcs], writes=[Bkr])
                    S.op("dve", lambda v: v.tensor_tensor(krt[:, 1, :], pB[0:32, :], cs[:, 1, :], ALU.mult), reads=[pbB, Bcs], writes=[Bkr])
                    S.op("dve", lambda v: v.tensor_tensor(krb[:], krt[:, 0, :], krt[:, 1, :], ALU.add), reads=[Bkr], writes=[Bkrb])
                    S.dma("st", kr_d[:, g0:g0 + 512], krb[:], reads=[Bkrb])
                    for j in range(16):
                        pt, pb = fm(C_G + j * 128, 128)
                        k = stg_i[0] % 4
                        stg_i[0] += 1
                        S.op("act", lambda a, k=k: a.activation(stgb[k][:], pt[:], AF.Sigmoid), reads=[pb], writes=[Bstgb[k]])
                        S.dma("st", gT_d[j * 128:(j + 1) * 128, g0:g0 + 512], stgb[k][:], reads=[Bstgb[k]])
                    for (r, j0, n) in ((0, 0, 3), (1, 3, 2)):
                        pt, pb = bank()
                        for j in range(n):
                            S.op("pe", lambda pe, j=j: pe.matmul(pt[:], ones32, sq[:, j0 + j, :], start=(j == 0), stop=(j == n - 1)), reads=[Bsq, B_c], writes=[pb])
                        S.op("act", lambda a, r=r, n=n: a.activation(rs[:, r, :], pt[:], AF.Sqrt, bias=epst[:], scale=1.0 / (128 * n)), reads=[pb, B_c], writes=[Brs])
                        S.op("dve", lambda v, r=r: v.reciprocal(rs[:, r, :], rs[:, r, :]), reads=[Brs], writes=[Brs])
                        for j in range(n):
                            S.op("dve", lambda v, j=j, r=r: v.tensor_tensor(qn[:, j0 + j, :], qa[:, j0 + j, :], rs[:, r, :], ALU.mult), reads=[Bqa, Brs], writes=[Bqn])
                    S.dma("st", qnT_d[:, g0:g0 + 512].rearrange("(j p) t -> p j t", p=128), qn[:, 0:3, :], reads=[Bqn])
                    S.dma("st", cnT_d[:, g0:g0 + 512].rearrange("(j p) t -> p j t", p=128), qn[:, 3:5, :], reads=[Bqn])
                    for s4 in range(4):
                        Z = zst[s4 % 2]
                        for cg in range(4):
                            pt, pb = bank()
                            S.mm(pt[:], [(H[:, kc, s4 * 128:(s4 + 1) * 128], W[:, kc, cg * 512:(cg + 1) * 512]) for kc in range(8)], reads=[BW, BhT[par]], writes=[pb])
                            S.op("act", lambda a, s4=s4, cg=cg: a.activation(Z[:, cg * 512:(cg + 1) * 512], pt[:], AF.Silu), reads=[pb], writes=[Bz[s4 % 2]])
                        S.dma("st", zs_d[g0 + s4 * 128:g0 + (s4 + 1) * 128, :], Z[:], reads=[Bz[s4 % 2]])
                    pt, pb = bank()
                    for s4 in range(4):
                        S.mm(pt[:, s4 * 64:(s4 + 1) * 64], [(H[:, kc, s4 * 128:(s4 + 1) * 128], W[:, kc, cm(C_DT):cm(C_DT) + 64]) for kc in range(8)], reads=[BW, BhT[par]], writes=[pb])
                    d0, d1, d2, d3, d4 = (dts[:, k, :] for k in range(5))
                    S.op("dve", lambda v: v.tensor_tensor(d0, pt[:, 0:256], dtb_bc[:].rearrange("p a b -> p (a b)"), ALU.add), reads=[pb, BW], writes=[Bdts])
                    S.op("dve", lambda v: v.tensor_scalar(d1, d0, -1.0, None, ALU.mult), reads=[Bdts], writes=[Bdts])
                    S.op("dve", lambda v: v.tensor_tensor(d1, d0, d1, ALU.min), reads=[Bdts], writes=[Bdts])
                    S.op("act", lambda a: a.activation(d2, d1, AF.Exp), reads=[Bdts], writes=[Bdts])
                    S.op("act", lambda a: a.activation(d3, d2, AF.Ln, bias=onet[:], scale=1.0), reads=[Bdts, B_c], writes=[Bdts])
                    S.op("dve", lambda v: v.scalar_tensor_tensor(d4, d0, 0.0, d3, ALU.max, ALU.add), reads=[Bdts], writes=[Bdts])
                    S.dma("st", dt_d[g0:g0 + 512, :].rearrange("(s p) f -> p s f", p=128), d4.rearrange("p (s f) -> p s f", f=64), reads=[Bdts])
                stageA(0)
                stageB(0)
                for idx in range(len(tiles)):
                    if idx + 1 < len(tiles):
                        stageA(idx + 1)
                    body(idx)
                    if idx + 1 < len(tiles):
                        stageB(idx + 1)
                S.barrier()

        if "p1c" in phases:
            with ExitStack() as st:
                cw = sb(st, "pc_cw", [128, 32, 6], F32)
                u = [sb(st, "pc_u%d" % i, [128, 516], BF16) for i in range(4)]
                dw = sb(st, "pc_dw", [128, 32, 5, 128], BF16)
                ob = [sb(st, "pc_ob%d" % i, [128, 512], BF16) for i in range(4)]
                tk = [sb(st, "pc_tk%d" % i, [128, 4, 3072], BF16) for i in range(2)]
                Bcw, Bu, Bacc, Bob, Btk = Buf(), [Buf() for _ in range(4)], [Buf(), Buf()], [Buf() for _ in range(4)], [Buf(), Buf()]
                with nc.allow_non_contiguous_dma(reason="tiny"):
                    for k in range(5):
                        S.dma("sp", cw[:, :, k], conv_w[k, :].rearrange("(j p) -> p j", p=128), writes=[Bcw])
                    S.dma("sp", cw[:, :, 5], conv_b.rearrange("(j p) -> p j", p=128), writes=[Bcw])
                for j in range(32):
                    S.op("dve", lambda v, j=j: v.tensor_tensor(dw[:, j, :, :], identb.unsqueeze(1).broadcast_to([128, 5, 128]), cw[:, j, 0:5].unsqueeze(2).broadcast_to([128, 5, 128]), ALU.mult), reads=[Bcw, B_c], writes=[Bcw])
                it = 0
                for si in range(NS):
                    nt = seqs[si] // 512
                    for ti in range(nt):
                        g0 = tok0[si] + ti * 512
                        par = (g0 // 512) % 2
                        TK = tk[par]
                        for j in range(32):
                            U, BU = u[it % 4], Bu[it % 4]
                            O, BO = ob[it % 4], Bob[it % 4]
                            eng = "dve"
                            it += 1
                            lo = 0 if ti > 0 else 2
                            hi = 516 if ti < nt - 1 else 514
                            if lo:
                                S.op(eng, lambda v: v.memset(U[:, 0:2], 0.0), writes=[BU])
                            if hi < 516:
                                S.op(eng, lambda v: v.memset(U[:, 514:516], 0.0), writes=[BU])
                            S.dma("sp", U[:, lo:hi], uT_d[j * 128:(j + 1) * 128, g0 - 2 + lo:g0 - 2 + hi], writes=[BU])
                            pcv, pbcv = bank()
                            S.mm(pcv, [(dw[:, j, k, :], U[:, k:k + 512]) for k in range(5)], reads=[BU, Bcw], writes=[pbcv])
                            S.op("act", lambda a, j=j: a.activation(O[:], pcv, AF.Silu, bias=cw[:, j, 5:6], scale=1.0), reads=[pbcv, Bcw], writes=[BO])
                            if j >= 16:
                                S.dma("st", bcT_d[(j - 16) * 128:(j - 15) * 128, g0:g0 + 512], O[:], reads=[BO])
                            if j < 24:
                                pt, pb = bank()
                                ptb = pt[:].bitcast(BF16)
                                for s4 in range(4):
                                    S.op("pe", lambda pe, s4=s4: pe.transpose(ptb[:, s4 * 128:(s4 + 1) * 128], O[:, s4 * 128:(s4 + 1) * 128], identb), reads=[BO, B_c], writes=[pb])
                                S.op("act", lambda a, j=j: a.copy(TK[:, :, j * 128:(j + 1) * 128], ptb[:, 0:512].rearrange("p (s f) -> p s f", f=128)), reads=[pb], writes=[Btk[par]])
                        S.dma("st", xtok_d[g0:g0 + 512, :].rearrange("(s p) f -> p s f", p=128), TK[:], reads=[Btk[par]])
                S.barrier()

        ctx = dict(locals())
        if "p1d" in phases:
            emit_p1d(ctx)
        with ExitStack() as gstack:
            ctx["gstack"] = gstack
            gens = []
            if "p2" in phases:
                gens.append(gen_p2(ctx))
            if "p3" in phases:
                gens.extend(make_p3(ctx))
            run_interleaved(gens)
            S.barrier()
        if "p4a" in phases:
            emit_p4a(ctx)
        if "p4b" in phases:
            emit_p4b(ctx)
        S.barrier()
    return nc


def run_interleaved(gens):
    n = len(gens)
    prog = [0.0] * n
    alive = [True] * n
    blocked = [False] * n
    while any(alive):
        cand = [i for i in range(n) if alive[i] and not blocked[i]]
        if not cand:
            raise RuntimeError("interleave deadlock")
        k = min(cand, key=lambda i: prog[i])
        try:
            v = next(gens[k])
        except StopIteration:
            alive[k] = False
            blocked = [False] * n
            continue
        if v is None:
            blocked[k] = True
        else:
            prog[k] = v
            blocked = [False] * n


def emit_p1d(c):
    from contextlib import ExitStack
    S, nc, sb, bank, seqs, tok0, NS = c["S"], c["nc"], c["sb"], c["bank"], c["seqs"], c["tok0"], c["NS"]
    qnT_d, cnT_d, KT_d, QT_d, V_d, cos_d, sin_d = c["qnT_d"], c["cnT_d"], c["KT_d"], c["QT_d"], c["V_d"], c["cos_d"], c["sin_d"]
    w_q_b, w_kv_b, g_q, g_kv = c["w_q_b"], c["w_kv_b"], c["g_q"], c["g_kv"]
    with ExitStack() as st:
        wq = sb(st, "pd_wq", [128, 3, 1536], BF16)
        wqB = sb(st, "pd_wqB", [128, 3, 16, 96], BF16)
        wkv = sb(st, "pd_wkv", [128, 2, 2048], BF16)
        wk = sb(st, "pd_wk", [128, 2, 1024], BF16)
        wv = sb(st, "pd_wv", [128, 2, 1024], BF16)
        gq = sb(st, "pd_gq", [128, 5], F32)
        qn = [sb(st, "pd_qn", [128, 3, 512], BF16) for _ in range(2)]
        cn = [sb(st, "pd_cn", [128, 2, 512], BF16) for _ in range(2)]
        cst = [sb(st, "pd_cs", [128, 2, 512], F32) for _ in range(2)]
        kst = [sb(st, "pd_kst", [128, 512], BF16) for _ in range(3)]
        vst = [sb(st, "pd_vst", [128, 512], BF16) for _ in range(3)]
        qst = [sb(st, "pd_qst", [128, 512], BF16) for _ in range(3)]
        rt = [sb(st, "pd_rt", [128, 2, 512], F32) for _ in range(2)]
        Bw2 = Buf()
        Bqn, Bcn, Bcs = [Buf(), Buf()], [Buf(), Buf()], [Buf(), Buf()]
        Bk, Bv, Bq, Brt = [Buf() for _ in range(3)], [Buf() for _ in range(3)], [Buf() for _ in range(3)], [Buf(), Buf()]
        S.dma("pool", wq[:], w_q_b.rearrange("(kc p) f -> p kc f", p=128), writes=[Bw2])
        S.dma("pool", wkv[:], w_kv_b.rearrange("(kc p) f -> p kc f", p=128), writes=[Bw2])
        with nc.allow_non_contiguous_dma(reason="tiny"):
            S.dma("sp", gq[:, 0:3], g_q.rearrange("(j p) -> p j", p=128), writes=[Bw2])
            S.dma("sp", gq[:, 3:5], g_kv.rearrange("(j p) -> p j", p=128), writes=[Bw2])
        for kc in range(3):
            S.op("dve", lambda v: v.tensor_scalar(wq[:, kc, :], wq[:, kc, :], gq[:, kc:kc + 1], None, ALU.mult), reads=[Bw2], writes=[Bw2])
        for kc in range(2):
            S.op("dve", lambda v: v.tensor_scalar(wkv[:, kc, :], wkv[:, kc, :], gq[:, 3 + kc:4 + kc], None, ALU.mult), reads=[Bw2], writes=[Bw2])
            w4 = wkv[:, kc, :].rearrange("p (h t f) -> p h t f", t=2, f=64)
            S.op("dve", lambda v: v.tensor_copy(wk[:, kc, :].rearrange("p (h f) -> p h f", f=64), w4[:, :, 0, :]), reads=[Bw2], writes=[Bw2])
            S.op("dve", lambda v: v.tensor_copy(wv[:, kc, :].rearrange("p (h f) -> p h f", f=64), w4[:, :, 1, :]), reads=[Bw2], writes=[Bw2])
        S.op("dve", lambda v: v.memset(wqB[:], 0.0), writes=[Bw2])
        wq4 = wq[:].rearrange("p k (h f) -> p k h f", f=96)
        for kc in range(3):
            S.op("dve", lambda v: v.tensor_scalar(wqB[:, kc, :, 64:80], wq4[:, kc, :, 80:96], -1.0, None, ALU.mult), reads=[Bw2], writes=[Bw2])
            S.op("dve", lambda v: v.tensor_copy(wqB[:, kc, :, 80:96], wq4[:, kc, :, 64:80]), reads=[Bw2], writes=[Bw2])
        it = 0
        ki = vi = qi = 0
        for si in range(NS):
            Sq, t0 = seqs[si], tok0[si]
            nch = Sq // 128
            for ti in range(Sq // 512):
                g0 = t0 + ti * 512
                p0 = ti * 512
                k = it % 2
                it += 1
                QN, CN, CS = qn[k], cn[k], cst[k]
                S.dma("sp", QN[:], qnT_d[:, g0:g0 + 512].rearrange("(j p) t -> p j t", p=128), writes=[Bqn[k]])
                S.dma("sp", CN[:], cnT_d[:, g0:g0 + 512].rearrange("(j p) t -> p j t", p=128), writes=[Bcn[k]])
                S.dma("sp", CS[64:96, 0, :], cos_d[:, p0:p0 + 512], writes=[Bcs[k]])
                S.dma("sp", CS[64:96, 1, :], sin_d[:, p0:p0 + 512], writes=[Bcs[k]])
                for pr in range(8):
                    pt, pb = bank()
                    S.mm(pt, [(wk[:, kc, pr * 128:(pr + 1) * 128], CN[:, kc, :]) for kc in range(2)], reads=[Bw2, Bcn[k]], writes=[pb])
                    K_, BK_ = kst[ki % 3], Bk[ki % 3]
                    ki += 1
                    S.op("act", lambda a: a.copy(K_[:], pt), reads=[pb], writes=[BK_])
                    S.dma("st", KT_d[pr * 128:(pr + 1) * 128, g0:g0 + 512], K_[:], reads=[BK_])
                for s4 in range(4):
                    cidx = ti * 4 + s4
                    for hf in range(2):
                        pt, pb = bank()
                        S.mm(pt, [(CN[:, kc, s4 * 128:(s4 + 1) * 128], wv[:, kc, hf * 512:(hf + 1) * 512]) for kc in range(2)], reads=[Bw2, Bcn[k]], writes=[pb])
                        V_, BV_ = vst[vi % 3], Bv[vi % 3]
                        vi += 1
                        S.op("dve", lambda v: v.tensor_copy(V_[:], pt), reads=[pb], writes=[BV_])
                        dst = V_d[hf * 8:(hf + 1) * 8, t0 * 64:(t0 + Sq) * 64].rearrange("h (p c f) -> p h c f", p=128, f=64)[:, :, cidx, :]
                        S.dma("st", dst, V_[:].rearrange("p (h f) -> p h f", f=64), reads=[BV_])
                for h in range(16):
                    pA, pbA = bank()
                    S.mm(pA[0:96, :], [(wq[:, kc, h * 96:(h + 1) * 96], QN[:, kc, :]) for kc in range(3)], reads=[Bw2, Bqn[k]], writes=[pbA])
                    pB, pbB = bank()
                    S.mm(pB[0:96, :], [(wqB[:, kc, h, :], QN[:, kc, :]) for kc in range(3)], reads=[Bw2, Bqn[k]], writes=[pbB])
                    Q_, BQ_ = qst[qi % 3], Bq[qi % 3]
                    RT, BRT = rt[qi % 2], Brt[qi % 2]
                    qi += 1
                    S.op("act", lambda a: a.copy(Q_[0:64, :], pA[0:64, :]), reads=[pbA], writes=[BQ_])
                    S.op("dve", lambda v: v.tensor_tensor(RT[64:96, 0, :], pA[64:96, :], CS[64:96, 0, :], ALU.mult), reads=[pbA, Bcs[k]], writes=[BRT])
                    S.op("dve", lambda v: v.tensor_tensor(RT[64:96, 1, :], pB[64:96, :], CS[64:96, 1, :], ALU.mult), reads=[pbB, Bcs[k]], writes=[BRT])
                    S.op("pool", lambda v: v.tensor_tensor(Q_[64:96, :], RT[64:96, 0, :], RT[64:96, 1, :], ALU.add), reads=[BRT], writes=[BQ_])
                    S.dma("st", QT_d[h * 96:(h + 1) * 96, g0:g0 + 512], Q_[0:96, :], reads=[BQ_])
        S.barrier()


def gen_p2(c):
    from contextlib import ExitStack
    S, nc, sb, seqs, tok0, NS = c["S"], c["nc"], c["sb"], c["seqs"], c["tok0"], c["NS"]
    banks, bbuf, SMAX = c["banks"], c["bbuf"], c["SMAX"]
    psall = c["psall"]
    KT_d, QT_d, V_d, kr_d, OT_d = c["KT_d"], c["QT_d"], c["V_d"], c["kr_d"], c["OT_d"]
    st = c["gstack"]
    if True:
        KT = [sb(st, "p2_KT", [128, SMAX], BF16) for _ in range(2)]
        QT = [sb(st, "p2_QT", [128, SMAX], BF16) for _ in range(2)]
        VA = [sb(st, "p2_VA", [128, SMAX // 128, 128], BF16) for _ in range(2)]
        PT = [sb(st, "p2_PT", [128, 512], BF16) for _ in range(4)]
        rd = sb(st, "p2_rd", [128, 512], F32)
        og = [sb(st, "p2_og", [128, 512], BF16) for _ in range(2)]
        BK, BQ, BV = [Buf(), Buf()], [Buf(), Buf()], [Buf(), Buf()]
        BPT, Brd, Bog = [Buf() for _ in range(4)], Buf(), [Buf(), Buf()]
        for i in range(2):
            S.op("pool", lambda v: v.memset(VA[i][:, :, 64:128], 1.0), writes=[BV[i]])
        heads = [(si, h) for si in range(NS) for h in range(16)]
        total = float(sum(seqs[si] // 512 * (seqs[si] // 128) * 3 for si, h in heads)) + 1.0
        done = 0

        def load(idx):
            si, h = heads[idx]
            hb = idx % 2
            Sq, t0 = seqs[si], tok0[si]
            S.dma("sp", KT[hb][0:64, 0:Sq], KT_d[h * 64:(h + 1) * 64, t0:t0 + Sq], writes=[BK[hb]])
            S.dma("sp", KT[hb][64:96, 0:Sq], kr_d[:, t0:t0 + Sq], writes=[BK[hb]])
            S.dma("sp", QT[hb][0:96, 0:Sq], QT_d[h * 96:(h + 1) * 96, t0:t0 + Sq], writes=[BQ[hb]])
            S.dma("sp", VA[hb][:, 0:Sq // 128, 0:64], V_d[h, t0 * 64:(t0 + Sq) * 64].rearrange("(p c f) -> p c f", p=128, f=64), writes=[BV[hb]])

        load(0)
        pi = 0
        oi = 0
        for idx, (si, h) in enumerate(heads):
            if idx + 1 < len(heads):
                load(idx + 1)
            hb = idx % 2
            K_, Q_, V_ = KT[hb], QT[hb], VA[hb]
            Sq, t0 = seqs[si], tok0[si]
            npair = Sq // 256
            nkc = Sq // 128
            for qt in range(Sq // 512):
                ql = slice(qt * 512, (qt + 1) * 512)
                pO, pbO = banks[3], bbuf[3]
                pend = []
                for cc in range(nkc + 2):
                    if cc < nkc:
                        b0 = cc % 3
                        S.op("pe", lambda pe: pe.matmul(banks[b0], K_[0:96, cc * 128:(cc + 1) * 128], Q_[0:96, ql], start=True, stop=True), reads=[BK[hb], BQ[hb]], writes=[bbuf[b0]])
                        done += 1
                        yield done / total
                        P_, BP = PT[pi % 4], BPT[pi % 4]
                        pi += 1
                        S.op("act", lambda a: a.activation(P_[:], banks[b0], AF.Exp, scale=ATT_SCALE), reads=[bbuf[b0]], writes=[BP])
                        done += 1
                        yield done / total
                        pend.append((cc, P_, BP))
                    if cc >= 2:
                        pc, PP, BPP = pend.pop(0)
                        S.op("pe", lambda pe: pe.matmul(pO, V_[:, pc, :], PP[:], start=(pc == 0), stop=(pc == nkc - 1)), reads=[BV[hb], BPP], writes=[pbO])
                        done += 1
                        yield done / total
                O_, BO_ = og[oi % 2], Bog[oi % 2]
                oi += 1
                S.op("dve", lambda v: v.reciprocal(rd[64:128, :], pO[64:128, :]), reads=[pbO], writes=[Brd])
                S.op("dve", lambda v: v.tensor_tensor(O_[0:64, :], pO[0:64, :], rd[64:128, :], ALU.mult), reads=[pbO, Brd], writes=[BO_])
                S.dma("sp", OT_d[h * 64:(h + 1) * 64, t0 + qt * 512:t0 + (qt + 1) * 512], O_[0:64, :], reads=[BO_])
        yield 1.0


def make_p3(c):
    S, nc, sb, seqs, tok0, NS = c["S"], c["nc"], c["sb"], c["seqs"], c["tok0"], c["NS"]
    bank0 = c["bank"]
    bankA = lambda: bank0("a")
    bankB = lambda: bank0("b")
    identb, Ufb, Ubb, Lfb, Lbb, onesb, B_c, epst = c["identb"], c["Ufb"], c["Ubb"], c["Lfb"], c["Lbb"], c["onesb"], c["B_c"], c["epst"]
    xtok_d, bcT_d, dt_d, zs_d, prevb_d, ygT_d, alog, dskip = c["xtok_d"], c["bcT_d"], c["dt_d"], c["zs_d"], c["prevb_d"], c["ygT_d"], c["alog"], c["dskip"]
    Q = "pool"
    st = c["gstack"]
    two = lambda name, shape, dt: [sb(st, name, shape, dt) for _ in range(2)]
    a_bc = sb(st, "p3_a", [128, 64], F32)
    D_bc = sb(st, "p3_D", [128, 32], F32)
    Hs = sb(st, "p3_H", [128, 2048], F32)
    Hb16 = two("p3_Hb", [128, 2048], BF16)
    xts = two("p3_xt", [128, 3072], BF16)
    bcs = two("p3_bc", [128, 16, 128], BF16)
    dts_ = two("p3_dt", [128, 64], F32)
    zss = two("p3_zs", [128, 2048], BF16)
    pvs = two("p3_pv", [128, 2048], BF16)
    dA = sb(st, "p3_dA", [128, 64], F32)
    dAh = two("p3_dAh", [128, 64], BF16)
    dAhf = sb(st, "p3_dAhf", [128, 64], F32)
    dAl = two("p3_dAl", [128, 64], BF16)
    ct = sb(st, "p3_ct", [128, 128], F32)
    Et = sb(st, "p3_E", [128, 64], F32)
    wgt = sb(st, "p3_w", [128, 64], F32)
    ec = two("p3_ec", [128, 64], F32)
    dec = sb(st, "p3_dec", [128, 64], F32)
    xw = sb(st, "p3_xw", [128, 2048], BF16)
    xdt = [two("p3_xdt", [128, 2048], BF16) for _ in range(2)]
    Rs = two("p3_R", [128, 2, 8, 128], BF16)
    CBm = two("p3_CBm", [128, 2, 8, 128], BF16)
    Exs = [sb(st, "p3_Ex", [128, 512], BF16) for _ in range(4)]
    MTs = two("p3_MT", [128, 2, 8, 128], BF16)
    y = sb(st, "p3_y", [128, 2048], F32)
    T1 = sb(st, "p3_t1", [128, 512], F32)
    T2 = sb(st, "p3_t2", [128, 512], F32)
    T3s = two("p3_t3", [128, 512], F32)
    yz = sb(st, "p3_yz", [128, 2048], F32)
    sqt = sb(st, "p3_sq", [128, 2048], BF16)
    gs = sb(st, "p3_gs", [128, 8], F32)
    ygn = sb(st, "p3_ygn", [128, 2048], BF16)
    ygs = sb(st, "p3_ygs", [128, 16, 128], BF16)
    B2 = lambda: [Buf(), Buf()]
    Bk, BH, Bprevd = Buf(), Buf(), Buf()
    BHb, Bxt, Bbc, Bdt, Bzs, Bpv = B2(), B2(), B2(), B2(), B2(), B2()
    BdA, BdAx, Bct, BE, Bec, Bxw, BCB = Buf(), B2(), Buf(), Buf(), B2(), Buf(), B2()
    Bxdt = [B2(), B2()]
    BRs, BEx, BMTs = B2(), [Buf() for _ in range(4)], B2()
    BT3s, Byz = B2(), Buf()
    By, BT1, BT2, BT3, Bsq, Bgs, Bygn, Bygs = Buf(), Buf(), Buf(), Buf(), Buf(), Buf(), Buf(), Buf()
    S.dma(Q, a_bc[:], alog.partition_broadcast(128), writes=[Bk])
    S.dma(Q, D_bc[:], dskip.partition_broadcast(128), writes=[Bk])
    S.op("act", lambda a: a.activation(a_bc[:], a_bc[:], AF.Exp), reads=[Bk], writes=[Bk])
    S.op("dve", lambda v: v.tensor_scalar(a_bc[:], a_bc[:], -1.0, None, ALU.mult), reads=[Bk], writes=[Bk])
    nchs = [sq // 128 for sq in seqs]
    NCH = float(sum(nchs))
    gstart = [sum(nchs[:i]) for i in range(NS)]
    sh = {"front": 0, "back": 0}
    TA, TB = 70.0, 200.0

    def bc3(ap2, n):
        return ap2.unsqueeze(2).broadcast_to([128, ap2.shape[1], n])

    v3 = lambda ap: ap.rearrange("p (h f) -> p h f", f=64)

    def prep(k, dtt, Bd):
        S.op("dve", lambda v: v.tensor_tensor(dA[:], dtt[:], a_bc[:], ALU.mult), reads=[Bd, Bk], writes=[BdA])
        yield
        S.op("dve", lambda v: v.tensor_copy(dAh[k][:], dA[:]), reads=[BdA], writes=[BdAx[k]])
        yield
        S.op("dve", lambda v: v.tensor_copy(dAhf[:], dAh[k][:]), reads=[BdAx[k]], writes=[BdA])
        yield
        S.op("dve", lambda v: v.tensor_tensor(dAhf[:], dA[:], dAhf[:], ALU.subtract), reads=[BdA], writes=[BdA])
        yield
        S.op("dve", lambda v: v.tensor_copy(dAl[k][:], dAhf[:]), reads=[BdA], writes=[BdAx[k]])
        yield
        pc, pbc = bankA()
        for (o0, o1, L) in ((0, 32, Ufb), (32, 64, Ubb)):
            S.mm(pc[:, o0:o1], [(L, dAh[k][:, o0:o1]), (L, dAl[k][:, o0:o1])], reads=[BdAx[k], B_c], writes=[pbc])
            yield
        S.mm(pc[:, 64:128], [(onesb, dAh[k][:]), (onesb, dAl[k][:])], reads=[BdAx[k], B_c], writes=[pbc])
        yield
        S.op("dve", lambda v: v.tensor_copy(ct[:], pc[:, 0:128]), reads=[pbc], writes=[Bct])
        yield
        S.op("dve", lambda v: v.tensor_tensor(Et[:], ct[:, 64:128], ct[:, 0:64], ALU.subtract), reads=[Bct], writes=[BE])
        yield
        S.op("act", lambda a: a.activation(Et[:], Et[:], AF.Exp), reads=[BE], writes=[BE])
        yield
        S.op("dve", lambda v: v.tensor_tensor(wgt[:], dtt[:], Et[:], ALU.mult), reads=[BE, Bd], writes=[BE])
        yield
        S.op("act", lambda a: a.activation(dec[:], ct[:, 64:128], AF.Exp), reads=[Bct], writes=[BE])
        yield
        S.op("act", lambda a: a.activation(ec[k][:], ct[:, 0:64], AF.Exp), reads=[Bct], writes=[Bec[k]])
        yield

    def state_update(X, BX, d):
        S.op("dve", lambda v: v.tensor_tensor(v3(xw[:]), v3(X[:, 0:2048]), bc3(wgt[:, d * 32:(d + 1) * 32], 64), ALU.mult), reads=[BX, BE], writes=[Bxw])
        yield
        S.op("dve", lambda v: v.tensor_tensor(v3(Hs[:]), v3(Hs[:]), bc3(dec[:, d * 32:(d + 1) * 32], 64), ALU.mult), reads=[BE, BHb[0], BHb[1]], writes=[BH])
        yield
        for gp in range(4):
            ps, pbs = bankA()
            for gg in range(2):
                g = gp * 2 + gg
                S.op("pe", lambda pe: pe.matmul(ps[:, gg * 256:(gg + 1) * 256], X[:, 2048 + g * 128:2048 + (g + 1) * 128], xw[:, g * 256:(g + 1) * 256], start=True, stop=True), reads=[BX, Bxw], writes=[pbs])
                yield
            S.op("dve", lambda v: v.tensor_tensor(Hs[:, gp * 512:(gp + 1) * 512], Hs[:, gp * 512:(gp + 1) * 512], ps, ALU.add), reads=[pbs], writes=[BH])
            yield

    def genA():
        it = 0
        for si in range(NS):
            Sq, t0 = seqs[si], tok0[si]
            nch = nchs[si]
            base = gstart[si] / NCH
            while sh["back"] < gstart[si]:
                yield None
            S.op("pool", lambda v: v.memset(Hs[:], 0.0), reads=[BHb[0], BHb[1]], writes=[BH])
            for cidx in range(nch - 1, -1, -1):
                g0 = t0 + cidx * 128
                k = it % 2
                it += 1
                S.dma(Q, xts[k][:], xtok_d[g0:g0 + 128, :], writes=[Bxt[k]])
                S.dma(Q, dts_[k][:], dt_d[g0:g0 + 128, :], writes=[Bdt[k]])
                S.op("act", lambda a: a.copy(Hb16[k][:], Hs[:]), reads=[BH], writes=[BHb[k]])
                S.dma(Q, prevb_d[cidx], Hb16[k][:], reads=[BHb[k]], writes=[Bprevd])
                yield base
                if cidx > 0:
                    for _ in prep(k, dts_[k], Bdt[k]):
                        yield base
                    for _ in state_update(xts[k], Bxt[k], 1):
                        yield base
            S.op("pool", lambda v: v.memset(Hs[:], 0.0), reads=[BHb[0], BHb[1]], writes=[BH])
            for cidx in range(nch):
                gidx = gstart[si] + cidx
                k = gidx % 2
                n = [0]

                def pr():
                    n[0] += 1
                    return (gidx + min(n[0] / TA, 0.99)) / NCH
                while sh["back"] < gidx - 1:
                    yield None
                g0 = t0 + cidx * 128
                X, BX, BCt, BBC, dtt, Bd = xts[k], Bxt[k], bcs[k], Bbc[k], dts_[k], Bdt[k]
                S.dma(Q, X[:], xtok_d[g0:g0 + 128, :], writes=[BX])
                S.dma(Q, BCt[:], bcT_d[:, g0:g0 + 128].rearrange("(j p) t -> p j t", p=128), writes=[BBC])
                S.dma(Q, dtt[:], dt_d[g0:g0 + 128, :], writes=[Bd])
                S.dma(Q, zss[k][:], zs_d[g0:g0 + 128, :], writes=[Bzs[k]])
                S.dma(Q, pvs[k][:], prevb_d[cidx], reads=[Bprevd], writes=[Bpv[k]])
                S.op("act", lambda a: a.copy(Hb16[k][:], Hs[:]), reads=[BH], writes=[BHb[k]])
                yield pr()
                for _ in prep(k, dtt, Bd):
                    yield pr()
                if cidx < nch - 1:
                    for _ in state_update(X, BX, 0):
                        yield pr()
                for g4 in range(2):
                    pcb, pbcb = bankA()
                    for gg in range(4):
                        g = g4 * 4 + gg
                        S.op("pe", lambda pe: pe.matmul(pcb[:, gg * 128:(gg + 1) * 128], BCt[:, g, :], BCt[:, 8 + g, :], start=True, stop=True), reads=[BBC], writes=[pbcb])
                        yield pr()
                    for d, Um in ((0, Ufb), (1, Ubb)):
                        S.op("dve", lambda v: v.tensor_tensor(CBm[k][:, d, g4 * 4:(g4 + 1) * 4, :], pcb.rearrange("p (g l) -> p g l", l=128), Um.unsqueeze(1).broadcast_to([128, 4, 128]), ALU.mult), reads=[pbcb, B_c], writes=[BCB[k]])
                        yield pr()
                for d in range(2):
                    S.op("pool", lambda v: v.tensor_tensor(v3(xdt[k][d][:]), v3(X[:, 0:2048]), bc3(dtt[:, d * 32:(d + 1) * 32], 64), ALU.mult), reads=[BX, Bd], writes=[Bxdt[k][d]])
                    yield pr()
                sh["front"] = gidx + 1
                yield (gidx + 0.995) / NCH
        yield 1.0

    def genB():
        ei = [0]

        def stage2(k, gp):
            MT, BMT = MTs[gp % 2], BMTs[gp % 2]
            X, BX = xts[k], Bxt[k]
            dl = ((0, Lfb, Ufb), (1, Lbb, Ubb))
            for d, Lm, Um in dl:
                h0 = d * 32 + gp * 8
                for kk, src in ((0, dAh[k]), (1, dAl[k])):
                    S.op("dve", lambda v: v.tensor_tensor(Rs[d][:, kk, :, :], Um.unsqueeze(1).broadcast_to([128, 8, 128]), bc3(src[:, h0:h0 + 8], 128), ALU.mult), reads=[BdAx[k], B_c], writes=[BRs[d]])
                    yield
            S.op("pool", lambda v: v.tensor_tensor(v3(T3s[gp % 2][:]), v3(X[:, gp * 512:(gp + 1) * 512]), bc3(D_bc[:, gp * 8:gp * 8 + 8], 64), ALU.mult), reads=[BX, Bk], writes=[BT3s[gp % 2]])
            yield
            segs = []
            for d, Lm, Um in dl:
                for gg in range(2):
                    pseg, pbseg = bankB()
                    S.mm(pseg, [(Lm, Rs[d][:, kk, gg * 4:(gg + 1) * 4, :].rearrange("p h l -> p (h l)")) for kk in range(2)], reads=[BRs[d], B_c], writes=[pbseg])
                    yield
                    Ex, BE_ = Exs[ei[0] % 4], BEx[ei[0] % 4]
                    ei[0] += 1
                    S.op("act", lambda a: a.activation(Ex[:], pseg, AF.Exp), reads=[pbseg], writes=[BE_])
                    yield
                    segs.append((d, gg, Ex, BE_))
            for (d, gg, Ex, BE_) in segs:
                g = gp * 2 + gg
                S.op("dve", lambda v: v.tensor_tensor(MT[:, d, gg * 4:(gg + 1) * 4, :], Ex[:].rearrange("p (h l) -> p h l", l=128), CBm[k][:, d, g:g + 1, :].broadcast_to([128, 4, 128]), ALU.mult), reads=[BE_, BCB[k]], writes=[BMT])
                yield

        def stage3(k, gp):
            MT, BMT = MTs[gp % 2], BMTs[gp % 2]
            BCt, BBC = bcs[k], Bbc[k]
            T3, BT3 = T3s[gp % 2], BT3s[gp % 2]
            py, pby = bankB()
            pof, pbof = bankB()
            pob, pbob = bankB()
            for gg in range(2):
                g = gp * 2 + gg
                S.op("pe", lambda pe: pe.matmul(pof[:, gg * 256:(gg + 1) * 256], BCt[:, 8 + g, :], Hb16[k][:, g * 256:(g + 1) * 256], start=True, stop=True), reads=[BBC, BHb[k]], writes=[pbof])
                yield
                S.op("pe", lambda pe: pe.matmul(pob[:, gg * 256:(gg + 1) * 256], BCt[:, 8 + g, :], pvs[k][:, g * 256:(g + 1) * 256], start=True, stop=True), reads=[BBC, Bpv[k]], writes=[pbob])
                yield
            for gg in range(2):
                g = gp * 2 + gg
                for j in range(4):
                    h = 4 * g + j
                    S.mm(py[:, gg * 256 + j * 64:gg * 256 + (j + 1) * 64], [(MT[:, d, gg * 4 + j, :], xdt[k][d][:, h * 64:(h + 1) * 64]) for d in range(2)], reads=[BMT, Bxdt[k][0], Bxdt[k][1]], writes=[pby])
                    yield
            S.op("dve", lambda v: v.tensor_tensor(v3(T1[:]), v3(pof), bc3(ec[k][:, gp * 8:gp * 8 + 8], 64), ALU.mult), reads=[pbof, Bec[k]], writes=[BT1])
            yield
            S.op("dve", lambda v: v.tensor_tensor(v3(T2[:]), v3(pob), bc3(ec[k][:, 32 + gp * 8:32 + gp * 8 + 8], 64), ALU.mult), reads=[pbob, Bec[k]], writes=[BT2])
            yield
            S.op("dve", lambda v: v.tensor_tensor(T2[:], T2[:], T3[:], ALU.add), reads=[BT3], writes=[BT2])
            yield
            S.op("dve", lambda v: v.tensor_tensor(T1[:], T1[:], py, ALU.add), reads=[pby], writes=[BT1])
            yield
            S.op("dve", lambda v: v.tensor_tensor(y[:, gp * 512:(gp + 1) * 512], T1[:], T2[:], ALU.add), reads=[BT1, BT2], writes=[By])
            yield

        def tail(k, g0):
            Z, BZ = zss[k], Bzs[k]
            S.op("dve", lambda v: v.tensor_tensor(yz[:], y[:], Z[:], ALU.mult), reads=[BZ, By], writes=[Byz])
            yield
            S.op("pool", lambda v: v.tensor_tensor(sqt[:], yz[:], yz[:], ALU.mult), reads=[Byz], writes=[Bsq])
            yield
            S.op("dve", lambda v: v.tensor_reduce(gs[:], sqt[:].rearrange("p (g f) -> p g f", f=256), AX.X, ALU.add), reads=[Bsq], writes=[Bgs])
            yield
            S.op("act", lambda a: a.activation(gs[:], gs[:], AF.Sqrt, bias=epst[:], scale=1.0 / 256), reads=[Bgs, B_c], writes=[Bgs])
            yield
            S.op("dve", lambda v: v.reciprocal(gs[:], gs[:]), reads=[Bgs], writes=[Bgs])
            yield
            S.op("dve", lambda v: v.tensor_tensor(ygn[:].rearrange("p (g f) -> p g f", f=256), yz[:].rearrange("p (g f) -> p g f", f=256), bc3(gs[:], 256), ALU.mult), reads=[Byz, Bgs], writes=[Bygn])
            yield
            for hf in range(2):
                pt, pbt = bankB()
                ptb = pt.bitcast(BF16)
                for jj in range(8):
                    j = hf * 8 + jj
                    S.op("pe", lambda pe: pe.transpose(ptb[:, jj * 128:(jj + 1) * 128], ygn[:, j * 128:(j + 1) * 128], identb), reads=[Bygn, B_c], writes=[pbt])
                    yield
                S.op("act", lambda a: a.copy(ygs[:, hf * 8:(hf + 1) * 8, :], ptb[:, 0:1024].rearrange("p (j t) -> p j t", t=128)), reads=[pbt], writes=[Bygs])
                yield
            S.dma(Q, ygT_d[:, g0:g0 + 128].rearrange("(j p) t -> p j t", p=128), ygs[:], reads=[Bygs])
            yield

        pending = None
        for si in range(NS):
            Sq, t0 = seqs[si], tok0[si]
            for cidx in range(nchs[si]):
                gidx = gstart[si] + cidx
                k = gidx % 2
                n = [0]

                def pr():
                    n[0] += 1
                    return (gidx + min(n[0] / TB, 0.99)) / NCH
                while sh["front"] <= gidx:
                    yield None
                g0 = t0 + cidx * 128
                for _ in stage2(k, 0):
                    yield pr()
                if pending is not None:
                    pk, pg0, pgidx = pending
                    for _ in tail(pk, pg0):
                        yield pr()
                    sh["back"] = pgidx + 1
                    yield pr()
                for gp in range(4):
                    if gp + 1 < 4:
                        for _ in stage2(k, gp + 1):
                            yield pr()
                    for _ in stage3(k, gp):
                        yield pr()
                pending = (k, g0, gidx)
                if cidx == nchs[si] - 1:
                    for _ in tail(k, g0):
                        yield pr()
                    pending = None
                    sh["back"] = gidx + 1
                    yield pr()
        yield 1.0

    return genA(), genB()


def emit_p4a(c):
    from contextlib import ExitStack
    S, nc, sb, bank, seqs, tok0, NS = c["S"], c["nc"], c["sb"], c["bank"], c["seqs"], c["tok0"], c["NS"]
    identb, B_c, epst = c["identb"], c["B_c"], c["epst"]
    ygT_d, OT_d, gT_d, x_d, x1_d, h2T_d, gate_d, adaT_d = c["ygT_d"], c["OT_d"], c["gT_d"], c["x_d"], c["x1_d"], c["h2T_d"], c["gate_d"], c["adaT_d"]
    w_ssd_out, w_mla_out, w_o, g_ssd = c["w_ssd_out"], c["w_mla_out"], c["w_o"], c["g_ssd"]
    with ExitStack() as st:
        ws = sb(st, "p4_ws", [128, 16, 1024], BF16)
        wm = sb(st, "p4_wm", [128, 8, 1024], BF16)
        wo = sb(st, "p4_wo", [128, 8, 1024], BF16)
        gsn = sb(st, "p4_gsn", [128, 16], F32)
        ygl = [sb(st, "p4_yg", [128, 16, 512], BF16) for _ in range(2)]
        otl = [sb(st, "p4_ot", [128, 8, 512], BF16) for _ in range(2)]
        Bygl, Botl = [Buf(), Buf()], [Buf(), Buf()]
        tcount = [0]
        gt = sb(st, "p4_gt", [128, 16, 512], BF16)
        xt = sb(st, "p4_x", [128, 4, 1024], F32)
        mixf = sb(st, "p4_mixf", [128, 512], F32)
        mixf2 = sb(st, "p4_mixf2", [128, 512], F32)
        mix = sb(st, "p4_mix", [128, 8, 512], BF16)
        g1 = sb(st, "p4_g1", [128, 1024], F32)
        ab = sb(st, "p4_ab", [128, 2, 8], F32)
        x1 = sb(st, "p4_x1", [128, 4, 1024], F32)
        junk = sb(st, "p4_junk", [128, 1024], BF16)
        ss = sb(st, "p4_ss", [128, 4], F32)
        xn = sb(st, "p4_xn", [128, 4, 1024], BF16)
        h2 = sb(st, "p4_h2", [128, 8, 512], BF16)
        Bw, Byg, Bot, Bgt, Bx, Bmf, Bmf2, Bmix, Bg1, Bab, Bx1, Bss, Bxn, Bh2 = (Buf() for _ in range(14))
        S.dma("pool", ws[:], w_ssd_out.rearrange("(kc p) f -> p kc f", p=128), writes=[Bw])
        S.dma("pool", wm[:], w_mla_out.rearrange("(kc p) f -> p kc f", p=128), writes=[Bw])
        S.dma("pool", wo[:], w_o.rearrange("(kc p) f -> p kc f", p=128), writes=[Bw])
        with nc.allow_non_contiguous_dma(reason="tiny"):
            S.dma("sp", gsn[:], g_ssd.rearrange("(j p) -> p j", p=128), writes=[Bw])
        for kc in range(16):
            S.op("dve", lambda v: v.tensor_scalar(ws[:, kc, :], ws[:, kc, :], gsn[:, kc:kc + 1], None, ALU.mult), reads=[Bw], writes=[Bw])
        for si in range(NS):
            S.dma("sp", g1[:], gate_d[si, 0, :].partition_broadcast(128), writes=[Bg1])
            with nc.allow_non_contiguous_dma(reason="tiny"):
                S.dma("sp", ab[:, 0, :], adaT_d[si, 2, :].rearrange("(kc p) -> p kc", p=128), writes=[Bab])
                S.dma("sp", ab[:, 1, :], adaT_d[si, 3, :].rearrange("(kc p) -> p kc", p=128), writes=[Bab])
            for ti in range(seqs[si] // 512):
                g0 = tok0[si] + ti * 512
                yg, ot, Byg, Bot = ygl[tcount[0] % 2], otl[tcount[0] % 2], Bygl[tcount[0] % 2], Botl[tcount[0] % 2]
                tcount[0] += 1
                S.dma("sp", yg[:], ygT_d[:, g0:g0 + 512].rearrange("(j p) t -> p j t", p=128), writes=[Byg])
                S.dma("sp", ot[:], OT_d[:, g0:g0 + 512].rearrange("(j p) t -> p j t", p=128), writes=[Bot])
                S.dma("sp", gt[:], gT_d[:, g0:g0 + 512].rearrange("(j p) t -> p j t", p=128), writes=[Bgt])
                S.dma("sp", xt[:], x_d[g0:g0 + 512, :].rearrange("(s p) f -> p s f", p=128), writes=[Bx])
                for oc in range(8):
                    pa, pba = bank()
                    S.mm(pa, [(ws[:, kc, oc * 128:(oc + 1) * 128], yg[:, kc, :]) for kc in range(16)], reads=[Bw, Byg], writes=[pba])
                    pm, pbm = bank()
                    S.mm(pm, [(wm[:, kc, oc * 128:(oc + 1) * 128], ot[:, kc, :]) for kc in range(8)], reads=[Bw, Bot], writes=[pbm])
                    S.op("dve", lambda v: v.tensor_tensor(mixf[:], pa[:], gt[:, oc, :], ALU.mult), reads=[pba, Bgt], writes=[Bmf])
                    S.op("dve", lambda v: v.tensor_tensor(mixf2[:], pm[:], gt[:, 8 + oc, :], ALU.mult), reads=[pbm, Bgt], writes=[Bmf2])
                    S.op("dve", lambda v: v.tensor_tensor(mix[:, oc, :], mixf[:], mixf2[:], ALU.add), reads=[Bmf, Bmf2], writes=[Bmix])
                for s4 in range(4):
                    for hf in range(2):
                        po, pbo = bank()
                        S.mm(po, [(mix[:, kc, s4 * 128:(s4 + 1) * 128], wo[:, kc, hf * 512:(hf + 1) * 512]) for kc in range(8)], reads=[Bw, Bmix], writes=[pbo])
                        S.op("dve", lambda v: v.tensor_tensor(x1[:, s4, hf * 512:(hf + 1) * 512], po[:], g1[:, hf * 512:(hf + 1) * 512], ALU.mult), reads=[pbo, Bg1], writes=[Bx1])
                    S.op("dve", lambda v: v.tensor_tensor(x1[:, s4, :], x1[:, s4, :], xt[:, s4, :], ALU.add), reads=[Bx], writes=[Bx1])
                    S.op("act", lambda a: a.activation(junk[:], x1[:, s4, :], AF.Square, accum_out=ss[:, s4:s4 + 1]), reads=[Bx1], writes=[Bss])
                S.dma("st", x1_d[g0:g0 + 512, :].rearrange("(s p) f -> p s f", p=128), x1[:], reads=[Bx1])
                S.op("act", lambda a: a.activation(ss[:], ss[:], AF.Sqrt, bias=epst[:], scale=1.0 / 1024), reads=[Bss, B_c], writes=[Bss])
                S.op("dve", lambda v: v.reciprocal(ss[:], ss[:]), reads=[Bss], writes=[Bss])
                for s4 in range(4):
                    S.op("dve", lambda v: v.tensor_scalar(xn[:, s4, :], x1[:, s4, :], ss[:, s4:s4 + 1], None, ALU.mult), reads=[Bx1, Bss], writes=[Bxn])
                for kc in range(8):
                    pt, pbt = bank()
                    ptb = pt[:].bitcast(BF16)
                    for s4 in range(4):
                        S.op("pe", lambda pe: pe.transpose(ptb[:, s4 * 128:(s4 + 1) * 128], xn[:, s4, kc * 128:(kc + 1) * 128], identb), reads=[Bxn, B_c], writes=[pbt])
                    S.op("dve", lambda v: v.tensor_scalar(h2[:, kc, :], ptb[:, 0:512], ab[:, 0, kc:kc + 1], ab[:, 1, kc:kc + 1], ALU.mult, ALU.add), reads=[pbt, Bab], writes=[Bh2])
                S.dma("st", h2T_d[:, g0:g0 + 512].rearrange("(j p) t -> p j t", p=128), h2[:], reads=[Bh2])
        S.barrier()


def emit_p4b(c):
    from contextlib import ExitStack
    S, nc, sb, bank, seqs, tok0, NS = c["S"], c["nc"], c["sb"], c["bank"], c["seqs"], c["tok0"], c["NS"]
    B_c, epst = c["B_c"], c["epst"]
    x1_d, h2T_d, gate_d, y_d, w_mlp_in, w_mlp_out, g_final = c["x1_d"], c["h2T_d"], c["gate_d"], c["y_d"], c["w_mlp_in"], c["w_mlp_out"], c["g_final"]
    TT = 256
    with ExitStack() as st:
        w1 = sb(st, "p5_w1", [128, 8, 4096], BF16)
        w2 = sb(st, "p5_w2", [128, 32, 1024], BF16)
        gf = sb(st, "p5_gf", [128, 1024], F32)
        g2 = sb(st, "p5_g2", [128, 1024], F32)
        h2 = [sb(st, "p5_h2", [128, 8, TT], BF16) for _ in range(2)]
        x1 = [sb(st, "p5_x1", [128, 2, 1024], F32) for _ in range(2)]
        rl = [sb(st, "p5_rl", [128, TT], F32) for _ in range(2)]
        rT = sb(st, "p5_rT", [128, 32, TT], BF16)
        x2 = sb(st, "p5_x2", [128, 2, 1024], F32)
        junk = sb(st, "p5_junk", [128, 1024], BF16)
        ss = sb(st, "p5_ss", [128, 2], F32)
        yo = sb(st, "p5_yo", [128, 2, 1024], F32)
        Bw, Bg2, Bh2, Bx1, Brl, BrT, Bx2, Bss, Byo = Buf(), Buf(), [Buf(), Buf()], [Buf(), Buf()], [Buf(), Buf()], Buf(), Buf(), Buf(), Buf()
        S.dma("pool", w1[:], w_mlp_in.rearrange("(kc p) f -> p kc f", p=128), writes=[Bw])
        for q4 in range(4):
            S.dma("pool", w2[:, q4 * 8:(q4 + 1) * 8, :], w_mlp_out[q4 * 1024:(q4 + 1) * 1024, :].rearrange("(kc p) f -> p kc f", p=128), writes=[Bw])
        S.dma("sp", gf[:], g_final.partition_broadcast(128), writes=[Bw])
        it = 0
        for si in range(NS):
            S.dma("sp", g2[:], gate_d[si, 1, :].partition_broadcast(128), writes=[Bg2])
            for ti in range(seqs[si] // TT):
                g0 = tok0[si] + ti * TT
                k = it % 2
                it += 1
                H, X1 = h2[k], x1[k]
                S.dma("sp", H[:], h2T_d[:, g0:g0 + TT].rearrange("(j p) t -> p j t", p=128), writes=[Bh2[k]])
                S.dma("sp", X1[:], x1_d[g0:g0 + TT, :].rearrange("(s p) f -> p s f", p=128), writes=[Bx1[k]])
                for fc in range(32):
                    pf, pbf = bank()
                    S.mm(pf[:, 0:TT], [(w1[:, kc, fc * 128:(fc + 1) * 128], H[:, kc, :]) for kc in range(8)], reads=[Bw, Bh2[k]], writes=[pbf])
                    RL, BRL = rl[fc % 2], Brl[fc % 2]
                    S.op("act", lambda a: a.activation(RL[:], pf[:, 0:TT], AF.Relu), reads=[pbf], writes=[BRL])
                    S.op("pool" if fc % 2 else "dve", lambda v: v.tensor_tensor(rT[:, fc, :], RL[:], RL[:], ALU.mult), reads=[BRL], writes=[BrT])
                for s2 in range(2):
                    for hf in range(2):
                        po, pbo = bank()
                        S.mm(po, [(rT[:, kc, s2 * 128:(s2 + 1) * 128], w2[:, kc, hf * 512:(hf + 1) * 512]) for kc in range(32)], reads=[Bw, BrT], writes=[pbo])
                        S.op("dve", lambda v: v.tensor_tensor(x2[:, s2, hf * 512:(hf + 1) * 512], po[:], g2[:, hf * 512:(hf + 1) * 512], ALU.mult), reads=[pbo, Bg2], writes=[Bx2])
                    S.op("pool", lambda v: v.tensor_tensor(x2[:, s2, :], x2[:, s2, :], X1[:, s2, :], ALU.add), reads=[Bx1[k]], writes=[Bx2])
                    S.op("act", lambda a: a.activation(junk[:], x2[:, s2, :], AF.Square, accum_out=ss[:, s2:s2 + 1]), reads=[Bx2], writes=[Bss])
                S.op("act", lambda a: a.activation(ss[:], ss[:], AF.Sqrt, bias=epst[:], scale=1.0 / 1024), reads=[Bss, B_c], writes=[Bss])
                S.op("dve", lambda v: v.reciprocal(ss[:], ss[:]), reads=[Bss], writes=[Bss])
                for s2 in range(2):
                    S.op("dve", lambda v: v.scalar_tensor_tensor(yo[:, s2, :], x2[:, s2, :], ss[:, s2:s2 + 1], gf[:], ALU.mult, ALU.mult), reads=[Bx2, Bss, Bw], writes=[Byo])
                S.dma("st", y_d[g0:g0 + TT, :].rearrange("(s p) f -> p s f", p=128), yo[:], reads=[Byo])
        S.barrier()


def core_inputs(x_all, c_all, W, g_final, cos, sin):
    f = lambda a: np.ascontiguousarray(np.asarray(a, dtype=np.float32))
    return {
        "x": f(x_all), "c": f(c_all),
        "w_ada": f(W["w_ada"]), "b_ada": f(W["b_ada"]), "g_norm1": f(W["g_norm1"]), "w_in": f(W["w_in"]),
        "conv_w": f(W["conv_w"]), "conv_b": f(W["conv_b"]),
        "dt_bias": f(np.concatenate([W["dt_bias_fwd"], W["dt_bias_bwd"]])),
        "a_log": f(np.concatenate([W["a_log_fwd"], W["a_log_bwd"]])),
        "d_skip": f(W["d_skip"]), "g_ssd_norm": f(W["g_ssd_norm"]), "w_ssd_out": f(W["w_ssd_out"]),
        "g_q_norm": f(W["g_q_norm"]), "w_q_b": f(W["w_q_b"]), "g_kv_norm": f(W["g_kv_norm"]), "w_kv_b": f(W["w_kv_b"]),
        "w_mla_out": f(W["w_mla_out"]), "w_o": f(W["w_o"]), "g_norm2": f(W["g_norm2"]),
        "w_mlp_in": f(W["w_mlp_in"]), "w_mlp_out": f(W["w_mlp_out"]), "g_final": f(g_final),
        "consts": make_consts(), "cos_t": f(cos), "sin_t": f(sin),
    }


_WNAMES = ["w_ada", "b_ada", "g_norm1", "w_in", "conv_w", "conv_b", "dt_bias_fwd", "dt_bias_bwd", "a_log_fwd",
           "a_log_bwd", "d_skip", "g_ssd_norm", "w_ssd_out", "g_q_norm", "w_q_b", "g_kv_norm", "w_kv_b",
           "w_mla_out", "w_o", "g_norm2", "w_mlp_in", "w_mlp_out"]


def kernel(x_prompt, x_sample, c_prompt, c_sample, g_final, **kw):
    W = {k: np.asarray(kw[k])[0] for k in _WNAMES}
    x_prompt = np.asarray(x_prompt); x_sample = np.asarray(x_sample)
    c_prompt = np.asarray(c_prompt); c_sample = np.asarray(c_sample)
    n = 8
    seqs = (2048, 2048, 4096, 4096)
    cos, sin = rope_tables(4096)
    import os
    ph = os.environ.get("KPHASES")
    nc = build(seqs=seqs, phases=tuple(ph.split(","))) if ph else build(seqs=seqs)
    in_maps = []
    for i in range(n):
        xa = np.concatenate([x_prompt[2 * i].reshape(-1, D), x_prompt[2 * i + 1].reshape(-1, D),
                             x_sample[2 * i].reshape(-1, D), x_sample[2 * i + 1].reshape(-1, D)], axis=0)
        ca = np.stack([c_prompt[2 * i], c_prompt[2 * i + 1], c_sample[2 * i], c_sample[2 * i + 1]], axis=0)
        in_maps.append(core_inputs(xa, ca, W, g_final, cos, sin))
    res = run_bass_kernel_spmd(nc, in_maps, core_ids=list(range(n)))
    yp = np.zeros((16, 2048, D), np.float32)
    ys = np.zeros((16, 4096, D), np.float32)
    for i in range(n):
        y = np.asarray(res.results[i]["y"])
        yp[2 * i] = y[0:2048]
        yp[2 * i + 1] = y[2048:4096]
        ys[2 * i] = y[4096:8192]
        ys[2 * i + 1] = y[8192:12288]
    return (yp, ys)
```

```python
import numpy as np
import concourse.bass as bass
import concourse.mybir as mybir
from concourse.bass_utils import run_bass_kernel_spmd

F32 = mybir.dt.float32
BF16 = mybir.dt.bfloat16
AF = mybir.ActivationFunctionType
ALU = mybir.AluOpType
AX = mybir.AxisListType

D = 1024
DI = 2048
NH = 32
DIN = 8928
EPS = 1e-6
C_Z, C_X, C_DT, C_QA, C_KV, C_KR, C_G = 0, 2048, 6144, 6208, 6592, 6848, 6880
ATT_SCALE = 96 ** -0.5


class Buf:
    __slots__ = ("w", "r")

    def __init__(self):
        self.w = None
        self.r = {}


class Eng:
    def __init__(self, key, h, sem):
        self.key, self.h, self.sem = key, h, sem
        self.count = 0
        self.waited = {}
        self.slots = []
        self.dma_i = 0


class Slot:
    def __init__(self, key, sem):
        self.key, self.sem, self.count = key, sem, 0


class Sch:
    def __init__(self, nc, stack, nslots=12):
        self.nc = nc
        self.E = {}
        for key, h in (("pe", nc.tensor), ("act", nc.scalar), ("dve", nc.vector),
                       ("pool", nc.gpsimd), ("sp", nc.sync)):
            sem = stack.enter_context(nc.semaphore("sem_" + key))
            self.E[key] = Eng(key, h, sem)
        for q in ("sp", "pool", "act"):
            for i in range(nslots):
                sem = stack.enter_context(nc.semaphore("dq_%s_%d" % (q, i)))
                self.E[q].slots.append(Slot("dq_%s_%d" % (q, i), sem))

    def _wait(self, e, deps):
        for (key, sem, val) in deps:
            if key == e.key and key == "pe":
                continue
            if e.waited.get(key, 0) >= val:
                continue
            e.h.wait_ge(sem, val)
            e.waited[key] = val

    @staticmethod
    def _deps(reads, writes):
        deps = []
        for b in reads:
            if b.w is not None:
                deps.append(b.w)
        for b in writes:
            if b.w is not None:
                deps.append(b.w)
            for k, (s, v) in b.r.items():
                deps.append((k, s, v))
        return deps

    @staticmethod
    def _mark(tok, reads, writes):
        for b in reads:
            b.r[tok[0]] = (tok[1], tok[2])
        for b in writes:
            b.w = tok
            b.r = {}

    def op(self, eng, fn, reads=(), writes=()):
        e = self.E[eng]
        self._wait(e, self._deps(reads, writes))
        ins = fn(e.h)
        e.count += 1
        ins.then_inc(e.sem, 1)
        self._mark((e.key, e.sem, e.count), reads, writes)

    def mm(self, out, pairs, reads=(), writes=()):
        n = len(pairs)

        def fn(pe):
            ins = None
            for i, (l, r) in enumerate(pairs):
                ins = pe.matmul(out, l, r, start=(i == 0), stop=(i == n - 1))
            return ins
        self.op("pe", fn, reads=reads, writes=writes)

    def dma(self, q, out, in_, reads=(), writes=()):
        if q == "st":
            q = "pool"
        e = self.E[q]
        self._wait(e, self._deps(reads, writes))
        sl = e.slots[e.dma_i % len(e.slots)]
        e.dma_i += 1
        if sl.count > 0:
            self._wait(e, [(sl.key, sl.sem, 16 * sl.count)])
        e.h.dma_start(out=out, in_=in_).then_inc(sl.sem, 16)
        sl.count += 1
        self._mark((sl.key, sl.sem, 16 * sl.count), reads, writes)

    def barrier(self):
        toks = []
        for e in self.E.values():
            if e.count:
                toks.append((e.key, e.sem, e.count))
            for sl in e.slots:
                if sl.count:
                    toks.append((sl.key, sl.sem, 16 * sl.count))
        for e in self.E.values():
            self._wait(e, toks)


def make_consts():
    i = np.arange(128)
    c = np.zeros((128, 6, 128), np.float32)
    c[:, 0, :] = np.eye(128)
    c[:, 1, :] = (i[:, None] <= i[None, :])
    c[:, 2, :] = (i[:, None] >= i[None, :])
    c[:, 3, :] = (i[:, None] > i[None, :])
    c[:, 4, :] = (i[:, None] < i[None, :])
    c[:, 5, :] = 1.0
    return c.reshape(128, 768)


def rope_tables(smax):
    inv = (1.0 / (np.float32(10000.0) ** (np.arange(0, 32, 2, dtype=np.float32) / np.float32(32)))).astype(np.float32)
    ang = np.arange(smax, dtype=np.float32)[:, None] * inv[None, :]
    cos = np.cos(ang).astype(np.float32).T
    sin = np.sin(ang).astype(np.float32).T
    return (np.ascontiguousarray(np.concatenate([cos, cos], 0)),
            np.ascontiguousarray(np.concatenate([sin, sin], 0)))


def build(seqs=(2048, 2048, 4096, 4096), phases=("p0", "p1a", "p1b", "p1c", "p1d", "p2", "p3", "p4a", "p4b"), debug=False):
    nc = bass.Bass("TRN2", target_bir_lowering=False)
    NS = len(seqs)
    NT = sum(seqs)
    SMAX = max(seqs)
    tok0 = [sum(seqs[:i]) for i in range(NS)]
    skind = "ExternalOutput" if debug else "Internal"

    def din(name, shape, dt=F32):
        return nc.dram_tensor(name, list(shape), dt, kind="ExternalInput").ap()

    def dscr(name, shape, dt):
        return nc.dram_tensor(name, list(shape), dt, kind=skind).ap()

    x_d = din("x", [NT, D])
    c_d = din("c", [NS, D])
    w_ada = din("w_ada", [D, 6 * D])
    b_ada = din("b_ada", [6 * D])
    g_norm1 = din("g_norm1", [D])
    w_in = din("w_in", [D, DIN])
    conv_w = din("conv_w", [5, 4096])
    conv_b = din("conv_b", [4096])
    dtb = din("dt_bias", [64])
    alog = din("a_log", [64])
    dskip = din("d_skip", [32])
    g_ssd = din("g_ssd_norm", [DI])
    w_ssd_out = din("w_ssd_out", [DI, D])
    g_q = din("g_q_norm", [384])
    w_q_b = din("w_q_b", [384, 1536])
    g_kv = din("g_kv_norm", [256])
    w_kv_b = din("w_kv_b", [256, 2048])
    w_mla_out = din("w_mla_out", [D, D])
    w_o = din("w_o", [D, D])
    g_norm2 = din("g_norm2", [D])
    w_mlp_in = din("w_mlp_in", [D, 4 * D])
    w_mlp_out = din("w_mlp_out", [4 * D, D])
    g_final = din("g_final", [D])
    consts_d = din("consts", [128, 768])
    cos_d = din("cos_t", [32, SMAX])
    sin_d = din("sin_t", [32, SMAX])

    y_d = nc.dram_tensor("y", [NT, D], F32, kind="ExternalOutput").ap()

    adaT_d = dscr("adaT_s", [NS, 4, D], F32)
    gate_d = dscr("gate_s", [NS, 2, D], F32)
    uT_d = dscr("uT_s", [4096, NT], BF16)
    zs_d = dscr("zs_s", [NT, DI], BF16)
    dt_d = dscr("dt_s", [NT, 64], F32)
    qnT_d = dscr("qnT_s", [384, NT], BF16)
    cnT_d = dscr("cnT_s", [256, NT], BF16)
    kr_d = dscr("kr_s", [32, NT], BF16)
    gT_d = dscr("gT_s", [2048, NT], BF16)
    xtok_d = dscr("xtok_s", [NT, 3072], BF16)
    bcT_d = dscr("bcT_s", [2048, NT], BF16)
    KT_d = dscr("KT_s", [1024, NT], BF16)
    QT_d = dscr("QT_s", [1536, NT], BF16)
    V_d = dscr("V_s", [16, NT * 64], BF16)
    OT_d = dscr("OT_s", [D, NT], BF16)
    prevb_d = dscr("prevb_s", [SMAX // 128, 128, DI], BF16)
    ygT_d = dscr("ygT_s", [DI, NT], BF16)
    x1_d = dscr("x1_s", [NT, D], F32)
    h2T_d = dscr("h2T_s", [D, NT], BF16)

    from contextlib import ExitStack
    with ExitStack() as top:
        S = Sch(nc, top)
        _uid = [0]

        def sb(st, name, shape, dt):
            _uid[0] += 1
            return st.enter_context(nc.sbuf_tensor("%s_%d" % (name, _uid[0]), list(shape), dt))
        psall = top.enter_context(nc.psum_tensor("psall", [128, 4096], F32))
        banks = [psall[:, i * 512:(i + 1) * 512] for i in range(8)]
        bbuf = [Buf() for _ in range(8)]
        bi = {None: 0, "a": 0, "b": 0}
        pools = {None: list(range(8)), "a": [4], "b": [5, 6, 7]}

        def bank(pool=None):
            lst = pools[pool]
            i = lst[bi[pool] % len(lst)]
            bi[pool] += 1
            return banks[i], bbuf[i]

        cst32 = sb(top, "cst32", [128, 768], F32)
        cstb = sb(top, "cstb", [128, 768], BF16)
        epst = sb(top, "epst", [128, 1], F32)
        onet = sb(top, "onet", [128, 1], F32)
        B_c = Buf()
        S.dma("sp", cst32[:], consts_d, writes=[B_c])
        S.op("dve", lambda v: v.tensor_copy(cstb[:], cst32[:]), reads=[B_c], writes=[B_c])
        S.op("dve", lambda v: v.memset(epst[:], EPS), writes=[B_c])
        S.op("dve", lambda v: v.memset(onet[:], 1.0), writes=[B_c])
        identb = cstb[:, 0:128]
        Ufb, Ubb, Lfb, Lbb, onesb = (cstb[:, 128 * k:128 * (k + 1)] for k in range(1, 6))
        Uf32, Ub32 = cst32[:, 128:256], cst32[:, 256:384]
        ones32 = cst32[:, 640:768]

        def rstd_from_ss(eng_list, out, ss, n, rb, wb):
            S.op("act", lambda a: a.activation(out, ss, AF.Sqrt, bias=epst[0:out.shape[0], :], scale=1.0 / n), reads=rb + [B_c], writes=wb)
            S.op("dve", lambda v: v.reciprocal(out, out), reads=wb, writes=wb)

        if "p0" in phases:
            with ExitStack() as st:
                cT = sb(st, "p0_cT", [128, 8, NS], F32)
                cbc = sb(st, "p0_cbc", [128, 8, NS, 128], F32)
                wa = [sb(st, "p0_wa%d" % i, [128, 8, 1024], F32) for i in range(2)]
                bT = sb(st, "p0_bT", [128, 48], F32)
                brow = sb(st, "p0_brow", [128, 2, 1024], F32)
                gT1 = sb(st, "p0_g", [128, 2, 8], F32)
                res = sb(st, "p0_res", [128, 6, 8, NS], F32)
                vec = sb(st, "p0_vec", [128, NS, 4, 8], F32)
                grow = sb(st, "p0_grow", [128, 1024], F32)
                Bc, Bw, Bb, Br, Bv, Bg = Buf(), [Buf(), Buf()], Buf(), Buf(), Buf(), Buf()
                with nc.allow_non_contiguous_dma(reason="tiny transposed loads"):
                    for b in range(NS):
                        S.dma("sp", cT[:, :, b], c_d[b, :].rearrange("(kc p) -> p kc", p=128), writes=[Bc])
                    S.dma("sp", bT[:], b_ada.rearrange("(j p) -> p j", p=128), writes=[Bb])
                    S.dma("sp", gT1[:, 0, :], g_norm1.rearrange("(j p) -> p j", p=128), writes=[Bb])
                    S.dma("sp", gT1[:, 1, :], g_norm2.rearrange("(j p) -> p j", p=128), writes=[Bb])
                S.dma("sp", brow[:, 0, :], b_ada[2048:3072].partition_broadcast(128), writes=[Bb])
                S.dma("sp", brow[:, 1, :], b_ada[5120:6144].partition_broadcast(128), writes=[Bb])
                S.op("act", lambda a: a.activation(cT[:], cT[:], AF.Silu), reads=[Bc], writes=[Bc])
                S.op("dve", lambda v: v.tensor_copy(cbc[:], cT[:].unsqueeze(3).broadcast_to([128, 8, NS, 128])), reads=[Bc], writes=[Bc])
                for j in range(6):
                    w = wa[j % 2]
                    S.dma("sp", w[:], w_ada[:, j * 1024:(j + 1) * 1024].rearrange("(kc p) f -> p kc f", p=128), writes=[Bw[j % 2]])
                    pt, pb = bank()
                    for oc in range(8):
                        for kc in range(8):
                            S.op("pe", lambda pe, oc=oc, kc=kc: pe.matmul(pt[:, oc * NS:(oc + 1) * NS], w[:, kc, oc * 128:(oc + 1) * 128], cT[:, kc, :], start=(kc == 0), stop=(kc == 7)),
                                 reads=[Bw[j % 2], Bc], writes=[pb])
                    S.op("dve", lambda v, j=j: v.tensor_tensor(res[:, j, :, :], pt[:, 0:8 * NS].rearrange("p (o b) -> p o b", b=NS),
                                                           bT[:, j * 8:(j + 1) * 8].unsqueeze(2).broadcast_to([128, 8, NS]), ALU.add),
                         reads=[pb, Bb], writes=[Br])
                    if j in (2, 5):
                        gi = 0 if j == 2 else 1
                        for b in range(NS):
                            for hf in range(2):
                                pt2, pb2 = bank()
                                for kc in range(8):
                                    S.op("pe", lambda pe, kc=kc, b=b, hf=hf: pe.matmul(pt2[:], cbc[:, kc, b, :], w[:, kc, hf * 512:(hf + 1) * 512], start=(kc == 0), stop=(kc == 7)),
                                         reads=[Bw[j % 2], Bc], writes=[pb2])
                                S.op("dve", lambda v, hf=hf, gi=gi: v.tensor_tensor(grow[:, hf * 512:(hf + 1) * 512], pt2[:], brow[:, gi, hf * 512:(hf + 1) * 512], ALU.add),
                                     reads=[pb2, Bb], writes=[Bg])
                            S.dma("st", gate_d[b, gi, :], grow[0:1, :], reads=[Bg])
                for b in range(NS):
                    for (k, jsc, jsh, gi) in ((0, 1, 0, 0), (2, 4, 3, 1)):
                        S.op("dve", lambda v, b=b, k=k, jsc=jsc, gi=gi: v.scalar_tensor_tensor(vec[:, b, k, :], res[:, jsc, :, b], 1.0, gT1[:, gi, :], ALU.add, ALU.mult), reads=[Br, Bb], writes=[Bv])
                        S.op("dve", lambda v, b=b, k=k, jsh=jsh: v.tensor_copy(vec[:, b, k + 1, :], res[:, jsh, :, b]), reads=[Br], writes=[Bv])
                with nc.allow_non_contiguous_dma(reason="tiny transposed stores"):
                    for b in range(NS):
                        for k in range(4):
                            S.dma("st", adaT_d[b, k, :].rearrange("(kc p) -> p kc", p=128), vec[:, b, k, :], reads=[Bv])
                S.barrier()

        for part in ("a", "b"):
            if ("p1" + part) not in phases:
                continue
            WC = 4096 if part == "a" else 4832
            cm = (lambda c: c - 2048) if part == "a" else (lambda c: c if c < 2048 else c - 4096)
            with ExitStack() as st:
                W = sb(st, "p1_W", [128, 8, WC], BF16)
                wkr = sb(st, "p1_wkr", [128, 8, 64], BF16)
                dtb_bc = sb(st, "p1_dtb", [128, 4, 64], F32)
                ab = sb(st, "p1_ab", [128, 2, 8], F32)
                xt = [sb(st, "p1_x%d" % i, [128, 4, 1024], F32) for i in range(1)] * 2
                junk = sb(st, "p1_junk", [128, 1024], BF16)
                ssl = [sb(st, "p1_ss", [128, 4], F32) for _ in range(2)]
                xnl = [sb(st, "p1_xn", [128, 4, 1024], BF16) for _ in range(2)]
                Bssl, Bxnl = [Buf(), Buf()], [Buf(), Buf()]
                hT = [sb(st, "p1_hT%d" % i, [128, 8, 512], BF16) for i in range(1)] * 2
                stg = [sb(st, "p1_stg%d" % i, [128, 512], F32) for i in range(4)]
                stgb = [sb(st, "p1_stgb%d" % i, [128, 512], BF16) for i in range(4)]
                qa = sb(st, "p1_qa", [128, 5, 512], F32) if part == "b" else None
                sq = sb(st, "p1_sq", [128, 5, 512], F32) if part == "b" else None
                rs = sb(st, "p1_rs", [128, 2, 512], F32)
                qn = sb(st, "p1_qn", [128, 5, 512], BF16)
                cs = sb(st, "p1_cs", [32, 2, 512], F32)
                krt = sb(st, "p1_krt", [32, 2, 512], F32)
                krb = sb(st, "p1_krb", [32, 512], BF16)
                zst = [sb(st, "p1_zst%d" % i, [128, 2048], BF16) if part == "b" else None for i in range(2)]
                dts = sb(st, "p1_dts", [128, 5, 256], F32)
                BW, Bab, Bx, Bss, Bxn, BhT = Buf(), Buf(), [Buf()] * 2, Buf(), Buf(), [Buf()] * 2
                Bstg, Bstgb = [Buf() for _ in range(4)], [Buf() for _ in range(4)]
                Bqa, Bsq, Brs, Bqn, Bcs, Bkr, Bkrb, Bz, Bdts = Buf(), Buf(), Buf(), Buf(), Buf(), Buf(), Buf(), [Buf(), Buf()], Buf()
                for kc in range(8):
                    if part == "a":
                        S.dma("pool", W[:, kc, :], w_in[kc * 128:(kc + 1) * 128, 2048:6144], writes=[BW])
                    else:
                        S.dma("pool", W[:, kc, 0:2048], w_in[kc * 128:(kc + 1) * 128, 0:2048], writes=[BW])
                        S.dma("pool", W[:, kc, 2048:4832], w_in[kc * 128:(kc + 1) * 128, 6144:8928], writes=[BW])
                CKR = cm(C_KR) if part == "b" else 0
                S.op("dve", lambda v: v.tensor_copy(wkr[:, :, 0:32], W[:, :, CKR:CKR + 32]), reads=[BW], writes=[BW])
                S.op("dve", lambda v: v.tensor_scalar(wkr[:, :, 32:48], W[:, :, CKR + 16:CKR + 32], -1.0, None, ALU.mult), reads=[BW], writes=[BW])
                S.op("dve", lambda v: v.tensor_copy(wkr[:, :, 48:64], W[:, :, CKR:CKR + 16]), reads=[BW], writes=[BW])
                for s4 in range(4):
                    S.dma("sp", dtb_bc[:, s4, :], dtb.partition_broadcast(128), writes=[BW])
                stg_i = [0]
                tiles = [(si, ti) for si in range(NS) for ti in range(seqs[si] // 512)]

                def stageA(idx):
                    si, ti = tiles[idx]
                    g0 = tok0[si] + ti * 512
                    k2 = idx % 2
                    X = xt[0]
                    S.dma("sp", X[:], x_d[g0:g0 + 512, :].rearrange("(s p) f -> p s f", p=128), writes=[Bx[0]])
                    for s4 in range(4):
                        S.op("act", lambda a: a.activation(junk[:], X[:, s4, :], AF.Square, accum_out=ssl[k2][:, s4:s4 + 1]), reads=[Bx[0]], writes=[Bssl[k2]])
                    rstd_from_ss(None, ssl[k2][:], ssl[k2][:], 1024.0, [Bssl[k2]], [Bssl[k2]])
                    for s4 in range(4):
                        S.op("pool" if s4 % 2 else "dve", lambda v: v.tensor_scalar(xnl[k2][:, s4, :], X[:, s4, :], ssl[k2][:, s4:s4 + 1], None, ALU.mult), reads=[Bx[0], Bssl[k2]], writes=[Bxnl[k2]])

                def stageB(idx):
                    si, ti = tiles[idx]
                    k2 = idx % 2
                    H = hT[0]
                    if ti == 0:
                        with nc.allow_non_contiguous_dma(reason="tiny"):
                            S.dma("sp", ab[:, 0, :], adaT_d[si, 0, :].rearrange("(kc p) -> p kc", p=128), writes=[Bab])
                            S.dma("sp", ab[:, 1, :], adaT_d[si, 1, :].rearrange("(kc p) -> p kc", p=128), writes=[Bab])
                    for kc in range(8):
                        pt, pb = bank()
                        ptb = pt[:].bitcast(BF16)
                        for s4 in range(4):
                            S.op("pe", lambda pe: pe.transpose(ptb[:, s4 * 128:(s4 + 1) * 128], xnl[k2][:, s4, kc * 128:(kc + 1) * 128], identb), reads=[Bxnl[k2], B_c], writes=[pb])
                        S.op("dve", lambda v: v.tensor_scalar(H[:, kc, :], ptb[:, 0:512], ab[:, 0, kc:kc + 1], ab[:, 1, kc:kc + 1], ALU.mult, ALU.add), reads=[pb, Bab], writes=[BhT[0]])

                def body(idx):
                    si, ti = tiles[idx]
                    g0 = tok0[si] + ti * 512
                    p0 = ti * 512
                    par = 0
                    H = hT[0]
                    def fm(cols, m, lw=None):
                        pt, pb = bank()
                        l = (lw if lw is not None else W)
                        cc0 = cols if lw is not None else cm(cols)
                        S.mm(pt[0:m, :], [(l[:, kc, cc0:cc0 + m], H[:, kc, :]) for kc in range(8)], reads=[BW, BhT[par]], writes=[pb])
                        return pt, pb
                    for j in range(32 if part == "a" else 0):
                        pt, pb = fm(C_X + j * 128, 128)
                        k = stg_i[0] % 4
                        stg_i[0] += 1
                        S.op("act", lambda a, k=k: a.copy(stgb[k][:], pt[:]), reads=[pb], writes=[Bstgb[k]])
                        S.dma("st", uT_d[j * 128:(j + 1) * 128, g0:g0 + 512], stgb[k][:], reads=[Bstgb[k]])
                    if part == "a":
                        return
                    for j in range(5):
                        pt, pb = fm(C_QA + j * 128, 128)
                        S.op("act", lambda a, j=j: a.copy(qa[:, j, :], pt[:]), reads=[pb], writes=[Bqa])
                        S.op("dve", lambda v, j=j: v.tensor_tensor(sq[:, j, :], qa[:, j, :], qa[:, j, :], ALU.mult), reads=[Bqa], writes=[Bsq])
                    S.dma("sp", cs[:, 0, :], cos_d[:, p0:p0 + 512], writes=[Bcs])
                    S.dma("sp", cs[:, 1, :], sin_d[:, p0:p0 + 512], writes=[Bcs])
                    pA, pbA = fm(0, 32, wkr)
                    pB, pbB = fm(32, 32, wkr)
                    S.op("dve", lambda v: v.tensor_tensor(krt[:, 0, :], pA[0:32, :], cs[:, 0, :], ALU.mult), reads=[pbA, Bcs], writes=[Bkr])
                    S.op("dve", lambda v: v.tensor_tensor(krt[:, 1, :], pB[0:32, :], cs[:, 1, :], ALU.mult), reads=[pbB, Bcs], writes=[Bkr])
                    S.op("dve", lambda v: v.tensor_tensor(krb[:], krt[:, 0, :], krt[:, 1, :], ALU.add), reads=[Bkr], writes=[Bkrb])
                    S.dma("st", kr_d[:, g0:g0 + 512], krb[:], reads=[Bkrb])
                    for j in range(16):
                        pt, pb = fm(C_G + j * 128, 128)
                        k = stg_i[0] % 4
                        stg_i[0] += 1
                        S.op("act", lambda a, k=k: a.activation(stgb[k][:], pt[:], AF.Sigmoid), reads=[pb], writes=[Bstgb[k]])
                        S.dma("st", gT_d[j * 128:(j + 1) * 128, g0:g0 + 512], stgb[k][:], reads=[Bstgb[k]])
                    for (r, j0, n) in ((0, 0, 3), (1, 3, 2)):
                        pt, pb = bank()
                        for j in range(n):
                            S.op("pe", lambda pe, j=j: pe.matmul(pt[:], ones32, sq[:, j0 + j, :], start=(j == 0), stop=(j == n - 1)), reads=[Bsq, B_c], writes=[pb])
                        S.op("act", lambda a, r=r, n=n: a.activation(rs[:, r, :], pt[:], AF.Sqrt, bias=epst[:], scale=1.0 / (128 * n)), reads=[pb, B_c], writes=[Brs])
                        S.op("dve", lambda v, r=r: v.reciprocal(rs[:, r, :], rs[:, r, :]), reads=[Brs], writes=[Brs])
                        for j in range(n):
                            S.op("dve", lambda v, j=j, r=r: v.tensor_tensor(qn[:, j0 + j, :], qa[:, j0 + j, :], rs[:, r, :], ALU.mult), reads=[Bqa, Brs], writes=[Bqn])
                    S.dma("st", qnT_d[:, g0:g0 + 512].rearrange("(j p) t -> p j t", p=128), qn[:, 0:3, :], reads=[Bqn])
                    S.dma("st", cnT_d[:, g0:g0 + 512].rearrange("(j p) t -> p j t", p=128), qn[:, 3:5, :], reads=[Bqn])
                    for s4 in range(4):
                        Z = zst[s4 % 2]
                        for cg in range(4):
                            pt, pb = bank()
                            S.mm(pt[:], [(H[:, kc, s4 * 128:(s4 + 1) * 128], W[:, kc, cg * 512:(cg + 1) * 512]) for kc in range(8)], reads=[BW, BhT[par]], writes=[pb])
                            S.op("act", lambda a, s4=s4, cg=cg: a.activation(Z[:, cg * 512:(cg + 1) * 512], pt[:], AF.Silu), reads=[pb], writes=[Bz[s4 % 2]])
                        S.dma("st", zs_d[g0 + s4 * 128:g0 + (s4 + 1) * 128, :], Z[:], reads=[Bz[s4 % 2]])
                    pt, pb = bank()
                    for s4 in range(4):
                        S.mm(pt[:, s4 * 64:(s4 + 1) * 64], [(H[:, kc, s4 * 128:(s4 + 1) * 128], W[:, kc, cm(C_DT):cm(C_DT) + 64]) for kc in range(8)], reads=[BW, BhT[par]], writes=[pb])
                    d0, d1, d2, d3, d4 = (dts[:, k, :] for k in range(5))
                    S.op("dve", lambda v: v.tensor_tensor(d0, pt[:, 0:256], dtb_bc[:].rearrange("p a b -> p (a b)"), ALU.add), reads=[pb, BW], writes=[Bdts])
                    S.op("dve", lambda v: v.tensor_scalar(d1, d0, -1.0, None, ALU.mult), reads=[Bdts], writes=[Bdts])
                    S.op("dve", lambda v: v.tensor_tensor(d1, d0, d1, ALU.min), reads=[Bdts], writes=[Bdts])
                    S.op("act", lambda a: a.activation(d2, d1, AF.Exp), reads=[Bdts], writes=[Bdts])
                    S.op("act", lambda a: a.activation(d3, d2, AF.Ln, bias=onet[:], scale=1.0), reads=[Bdts, B_c], writes=[Bdts])
                    S.op("dve", lambda v: v.scalar_tensor_tensor(d4, d0, 0.0, d3, ALU.max, ALU.add), reads=[Bdts], writes=[Bdts])
                    S.dma("st", dt_d[g0:g0 + 512, :].rearrange("(s p) f -> p s f", p=128), d4.rearrange("p (s f) -> p s f", f=64), reads=[Bdts])
                stageA(0)
                stageB(0)
                for idx in range(len(tiles)):
                    if idx + 1 < len(tiles):
                        stageA(idx + 1)
                    body(idx)
                    if idx + 1 < len(tiles):
                        stageB(idx + 1)
                S.barrier()

        if "p1c" in phases:
            with ExitStack() as st:
                cw = sb(st, "pc_cw", [128, 32, 6], F32)
                u = [sb(st, "pc_u%d" % i, [128, 516], BF16) for i in range(4)]
                dw = sb(st, "pc_dw", [128, 32, 5, 128], BF16)
                ob = [sb(st, "pc_ob%d" % i, [128, 512], BF16) for i in range(4)]
                tk = [sb(st, "pc_tk%d" % i, [128, 4, 3072], BF16) for i in range(2)]
                Bcw, Bu, Bacc, Bob, Btk = Buf(), [Buf() for _ in range(4)], [Buf(), Buf()], [Buf() for _ in range(4)], [Buf(), Buf()]
                with nc.allow_non_contiguous_dma(reason="tiny"):
                    for k in range(5):
                        S.dma("sp", cw[:, :, k], conv_w[k, :].rearrange("(j p) -> p j", p=128), writes=[Bcw])
                    S.dma("sp", cw[:, :, 5], conv_b.rearrange("(j p) -> p j", p=128), writes=[Bcw])
                for j in range(32):
                    S.op("dve", lambda v, j=j: v.tensor_tensor(dw[:, j, :, :], identb.unsqueeze(1).broadcast_to([128, 5, 128]), cw[:, j, 0:5].unsqueeze(2).broadcast_to([128, 5, 128]), ALU.mult), reads=[Bcw, B_c], writes=[Bcw])
                it = 0
                for si in range(NS):
                    nt = seqs[si] // 512
                    for ti in range(nt):
                        g0 = tok0[si] + ti * 512
                        par = (g0 // 512) % 2
                        TK = tk[par]
                        for j in range(32):
                            U, BU = u[it % 4], Bu[it % 4]
                            O, BO = ob[it % 4], Bob[it % 4]
                            eng = "dve"
                            it += 1
                            lo = 0 if ti > 0 else 2
                            hi = 516 if ti < nt - 1 else 514
                            if lo:
                                S.op(eng, lambda v: v.memset(U[:, 0:2], 0.0), writes=[BU])
                            if hi < 516:
                                S.op(eng, lambda v: v.memset(U[:, 514:516], 0.0), writes=[BU])
                            S.dma("sp", U[:, lo:hi], uT_d[j * 128:(j + 1) * 128, g0 - 2 + lo:g0 - 2 + hi], writes=[BU])
                            pcv, pbcv = bank()
                            S.mm(pcv, [(dw[:, j, k, :], U[:, k:k + 512]) for k in range(5)], reads=[BU, Bcw], writes=[pbcv])
                            S.op("act", lambda a, j=j: a.activation(O[:], pcv, AF.Silu, bias=cw[:, j, 5:6], scale=1.0), reads=[pbcv, Bcw], writes=[BO])
                            if j >= 16:
                                S.dma("st", bcT_d[(j - 16) * 128:(j - 15) * 128, g0:g0 + 512], O[:], reads=[BO])
                            if j < 24:
                                pt, pb = bank()
                                ptb = pt[:].bitcast(BF16)
                                for s4 in range(4):
                                    S.op("pe", lambda pe, s4=s4: pe.transpose(ptb[:, s4 * 128:(s4 + 1) * 128], O[:, s4 * 128:(s4 + 1) * 128], identb), reads=[BO, B_c], writes=[pb])
                                S.op("act", lambda a, j=j: a.copy(TK[:, :, j * 128:(j + 1) * 128], ptb[:, 0:512].rearrange("p (s f) -> p s f", f=128)), reads=[pb], writes=[Btk[par]])
                        S.dma("st", xtok_d[g0:g0 + 512, :].rearrange("(s p) f -> p s f", p=128), TK[:], reads=[Btk[par]])
                S.barrier()

        ctx = dict(locals())
        if "p1d" in phases:
            emit_p1d(ctx)
        with ExitStack() as gstack:
            ctx["gstack"] = gstack
            gens = []
            if "p2" in phases:
                gens.append(gen_p2(ctx))
            if "p3" in phases:
                gens.extend(make_p3(ctx))
            run_interleaved(gens)
            S.barrier()
        if "p4a" in phases:
            emit_p4a(ctx)
        if "p4b" in phases:
            emit_p4b(ctx)
        S.barrier()
    return nc


def run_interleaved(gens):
    n = len(gens)
    prog = [0.0] * n
    alive = [True] * n
    blocked = [False] * n
    while any(alive):
        cand = [i for i in range(n) if alive[i] and not blocked[i]]
        if not cand:
            raise RuntimeError("interleave deadlock")
        k = min(cand, key=lambda i: prog[i])
        try:
            v = next(gens[k])
        except StopIteration:
            alive[k] = False
            blocked = [False] * n
            continue
        if v is None:
            blocked[k] = True
        else:
            prog[k] = v
            blocked = [False] * n


def emit_p1d(c):
    from contextlib import ExitStack
    S, nc, sb, bank, seqs, tok0, NS = c["S"], c["nc"], c["sb"], c["bank"], c["seqs"], c["tok0"], c["NS"]
    qnT_d, cnT_d, KT_d, QT_d, V_d, cos_d, sin_d = c["qnT_d"], c["cnT_d"], c["KT_d"], c["QT_d"], c["V_d"], c["cos_d"], c["sin_d"]
    w_q_b, w_kv_b, g_q, g_kv = c["w_q_b"], c["w_kv_b"], c["g_q"], c["g_kv"]
    with ExitStack() as st:
        wq = sb(st, "pd_wq", [128, 3, 1536], BF16)
        wqB = sb(st, "pd_wqB", [128, 3, 16, 96], BF16)
        wkv = sb(st, "pd_wkv", [128, 2, 2048], BF16)
        wk = sb(st, "pd_wk", [128, 2, 1024], BF16)
        wv = sb(st, "pd_wv", [128, 2, 1024], BF16)
        gq = sb(st, "pd_gq", [128, 5], F32)
        qn = [sb(st, "pd_qn", [128, 3, 512], BF16) for _ in range(2)]
        cn = [sb(st, "pd_cn", [128, 2, 512], BF16) for _ in range(2)]
        cst = [sb(st, "pd_cs", [128, 2, 512], F32) for _ in range(2)]
        kst = [sb(st, "pd_kst", [128, 512], BF16) for _ in range(3)]
        vst = [sb(st, "pd_vst", [128, 512], BF16) for _ in range(3)]
        qst = [sb(st, "pd_qst", [128, 512], BF16) for _ in range(3)]
        rt = [sb(st, "pd_rt", [128, 2, 512], F32) for _ in range(2)]
        Bw2 = Buf()
        Bqn, Bcn, Bcs = [Buf(), Buf()], [Buf(), Buf()], [Buf(), Buf()]
        Bk, Bv, Bq, Brt = [Buf() for _ in range(3)], [Buf() for _ in range(3)], [Buf() for _ in range(3)], [Buf(), Buf()]
        S.dma("pool", wq[:], w_q_b.rearrange("(kc p) f -> p kc f", p=128), writes=[Bw2])
        S.dma("pool", wkv[:], w_kv_b.rearrange("(kc p) f -> p kc f", p=128), writes=[Bw2])
        with nc.allow_non_contiguous_dma(reason="tiny"):
            S.dma("sp", gq[:, 0:3], g_q.rearrange("(j p) -> p j", p=128), writes=[Bw2])
            S.dma("sp", gq[:, 3:5], g_kv.rearrange("(j p) -> p j", p=128), writes=[Bw2])
        for kc in range(3):
            S.op("dve", lambda v: v.tensor_scalar(wq[:, kc, :], wq[:, kc, :], gq[:, kc:kc + 1], None, ALU.mult), reads=[Bw2], writes=[Bw2])
        for kc in range(2):
            S.op("dve", lambda v: v.tensor_scalar(wkv[:, kc, :], wkv[:, kc, :], gq[:, 3 + kc:4 + kc], None, ALU.mult), reads=[Bw2], writes=[Bw2])
            w4 = wkv[:, kc, :].rearrange("p (h t f) -> p h t f", t=2, f=64)
            S.op("dve", lambda v: v.tensor_copy(wk[:, kc, :].rearrange("p (h f) -> p h f", f=64), w4[:, :, 0, :]), reads=[Bw2], writes=[Bw2])
            S.op("dve", lambda v: v.tensor_copy(wv[:, kc, :].rearrange("p (h f) -> p h f", f=64), w4[:, :, 1, :]), reads=[Bw2], writes=[Bw2])
        S.op("dve", lambda v: v.memset(wqB[:], 0.0), writes=[Bw2])
        wq4 = wq[:].rearrange("p k (h f) -> p k h f", f=96)
        for kc in range(3):
            S.op("dve", lambda v: v.tensor_scalar(wqB[:, kc, :, 64:80], wq4[:, kc, :, 80:96], -1.0, None, ALU.mult), reads=[Bw2], writes=[Bw2])
            S.op("dve", lambda v: v.tensor_copy(wqB[:, kc, :, 80:96], wq4[:, kc, :, 64:80]), reads=[Bw2], writes=[Bw2])
        it = 0
        ki = vi = qi = 0
        for si in range(NS):
            Sq, t0 = seqs[si], tok0[si]
            nch = Sq // 128
            for ti in range(Sq // 512):
                g0 = t0 + ti * 512
                p0 = ti * 512
                k = it % 2
                it += 1
                QN, CN, CS = qn[k], cn[k], cst[k]
                S.dma("sp", QN[:], qnT_d[:, g0:g0 + 512].rearrange("(j p) t -> p j t", p=128), writes=[Bqn[k]])
                S.dma("sp", CN[:], cnT_d[:, g0:g0 + 512].rearrange("(j p) t -> p j t", p=128), writes=[Bcn[k]])
                S.dma("sp", CS[64:96, 0, :], cos_d[:, p0:p0 + 512], writes=[Bcs[k]])
                S.dma("sp", CS[64:96, 1, :], sin_d[:, p0:p0 + 512], writes=[Bcs[k]])
                for pr in range(8):
                    pt, pb = bank()
                    S.mm(pt, [(wk[:, kc, pr * 128:(pr + 1) * 128], CN[:, kc, :]) for kc in range(2)], reads=[Bw2, Bcn[k]], writes=[pb])
                    K_, BK_ = kst[ki % 3], Bk[ki % 3]
                    ki += 1
                    S.op("act", lambda a: a.copy(K_[:], pt), reads=[pb], writes=[BK_])
                    S.dma("st", KT_d[pr * 128:(pr + 1) * 128, g0:g0 + 512], K_[:], reads=[BK_])
                for s4 in range(4):
                    cidx = ti * 4 + s4
                    for hf in range(2):
                        pt, pb = bank()
                        S.mm(pt, [(CN[:, kc, s4 * 128:(s4 + 1) * 128], wv[:, kc, hf * 512:(hf + 1) * 512]) for kc in range(2)], reads=[Bw2, Bcn[k]], writes=[pb])
                        V_, BV_ = vst[vi % 3], Bv[vi % 3]
                        vi += 1
                        S.op("dve", lambda v: v.tensor_copy(V_[:], pt), reads=[pb], writes=[BV_])
                        dst = V_d[hf * 8:(hf + 1) * 8, t0 * 64:(t0 + Sq) * 64].rearrange("h (p c f) -> p h c f", p=128, f=64)[:, :, cidx, :]
                        S.dma("st", dst, V_[:].rearrange("p (h f) -> p h f", f=64), reads=[BV_])
                for h in range(16):
                    pA, pbA = bank()
                    S.mm(pA[0:96, :], [(wq[:, kc, h * 96:(h + 1) * 96], QN[:, kc, :]) for kc in range(3)], reads=[Bw2, Bqn[k]], writes=[pbA])
                    pB, pbB = bank()
                    S.mm(pB[0:96, :], [(wqB[:, kc, h, :], QN[:, kc, :]) for kc in range(3)], reads=[Bw2, Bqn[k]], writes=[pbB])
                    Q_, BQ_ = qst[qi % 3], Bq[qi % 3]
                    RT, BRT = rt[qi % 2], Brt[qi % 2]
                    qi += 1
                    S.op("act", lambda a: a.copy(Q_[0:64, :], pA[0:64, :]), reads=[pbA], writes=[BQ_])
                    S.op("dve", lambda v: v.tensor_tensor(RT[64:96, 0, :], pA[64:96, :], CS[64:96, 0, :], ALU.mult), reads=[pbA, Bcs[k]], writes=[BRT])
                    S.op("dve", lambda v: v.tensor_tensor(RT[64:96, 1, :], pB[64:96, :], CS[64:96, 1, :], ALU.mult), reads=[pbB, Bcs[k]], writes=[BRT])
                    S.op("dve", lambda v: v.tensor_tensor(Q_[64:96, :], RT[64:96, 0, :], RT[64:96, 1, :], ALU.add), reads=[BRT], writes=[BQ_])
                    S.dma("st", QT_d[h * 96:(h + 1) * 96, g0:g0 + 512], Q_[0:96, :], reads=[BQ_])
        S.barrier()


def gen_p2(c):
    from contextlib import ExitStack
    S, nc, sb, seqs, tok0, NS = c["S"], c["nc"], c["sb"], c["seqs"], c["tok0"], c["NS"]
    banks, bbuf, SMAX = c["banks"], c["bbuf"], c["SMAX"]
    psall = c["psall"]
    KT_d, QT_d, V_d, kr_d, OT_d = c["KT_d"], c["QT_d"], c["V_d"], c["kr_d"], c["OT_d"]
    st = c["gstack"]
    if True:
        KT = [sb(st, "p2_KT", [128, SMAX], BF16) for _ in range(2)]
        QT = [sb(st, "p2_QT", [128, SMAX], BF16) for _ in range(2)]
        VA = [sb(st, "p2_VA", [128, SMAX // 128, 128], BF16) for _ in range(2)]
        PT = [sb(st, "p2_PT", [128, 512], BF16) for _ in range(4)]
        rd = sb(st, "p2_rd", [128, 512], F32)
        og = [sb(st, "p2_og", [128, 512], BF16) for _ in range(2)]
        BK, BQ, BV = [Buf(), Buf()], [Buf(), Buf()], [Buf(), Buf()]
        BPT, Brd, Bog = [Buf() for _ in range(4)], Buf(), [Buf(), Buf()]
        for i in range(2):
            S.op("pool", lambda v: v.memset(VA[i][:, :, 64:128], 1.0), writes=[BV[i]])
        heads = [(si, h) for si in range(NS) for h in range(16)]
        total = float(sum(seqs[si] // 512 * (seqs[si] // 128) * 3 for si, h in heads)) + 1.0
        done = 0

        def load(idx):
            si, h = heads[idx]
            hb = idx % 2
            Sq, t0 = seqs[si], tok0[si]
            S.dma("sp", KT[hb][0:64, 0:Sq], KT_d[h * 64:(h + 1) * 64, t0:t0 + Sq], writes=[BK[hb]])
            S.dma("sp", KT[hb][64:96, 0:Sq], kr_d[:, t0:t0 + Sq], writes=[BK[hb]])
            S.dma("sp", QT[hb][0:96, 0:Sq], QT_d[h * 96:(h + 1) * 96, t0:t0 + Sq], writes=[BQ[hb]])
            S.dma("sp", VA[hb][:, 0:Sq // 128, 0:64], V_d[h, t0 * 64:(t0 + Sq) * 64].rearrange("(p c f) -> p c f", p=128, f=64), writes=[BV[hb]])

        load(0)
        pi = 0
        oi = 0
        for idx, (si, h) in enumerate(heads):
            if idx + 1 < len(heads):
                load(idx + 1)
            hb = idx % 2
            K_, Q_, V_ = KT[hb], QT[hb], VA[hb]
            Sq, t0 = seqs[si], tok0[si]
            npair = Sq // 256
            nkc = Sq // 128
            for qt in range(Sq // 512):
                ql = slice(qt * 512, (qt + 1) * 512)
                pO, pbO = banks[3], bbuf[3]
                pend = []
                for cc in range(nkc + 2):
                    if cc < nkc:
                        b0 = cc % 3
                        S.op("pe", lambda pe: pe.matmul(banks[b0], K_[0:96, cc * 128:(cc + 1) * 128], Q_[0:96, ql], start=True, stop=True), reads=[BK[hb], BQ[hb]], writes=[bbuf[b0]])
                        done += 1
                        yield done / total
                        P_, BP = PT[pi % 4], BPT[pi % 4]
                        pi += 1
                        S.op("act", lambda a: a.activation(P_[:], banks[b0], AF.Exp, scale=ATT_SCALE), reads=[bbuf[b0]], writes=[BP])
                        done += 1
                        yield done / total
                        pend.append((cc, P_, BP))
                    if cc >= 2:
                        pc, PP, BPP = pend.pop(0)
                        S.op("pe", lambda pe: pe.matmul(pO, V_[:, pc, :], PP[:], start=(pc == 0), stop=(pc == nkc - 1)), reads=[BV[hb], BPP], writes=[pbO])
                        done += 1
                        yield done / total
                O_, BO_ = og[oi % 2], Bog[oi % 2]
                oi += 1
                S.op("act", lambda a: a.activation(rd[64:128, :], pO[64:128, :], AF.Ln), reads=[pbO], writes=[Brd])
                S.op("act", lambda a: a.activation(rd[64:128, :], rd[64:128, :], AF.Exp, scale=-1.0), reads=[Brd], writes=[Brd])
                S.op("dve", lambda v: v.tensor_tensor(O_[0:64, :], pO[0:64, :], rd[64:128, :], ALU.mult), reads=[pbO, Brd], writes=[BO_])
                S.dma("sp", OT_d[h * 64:(h + 1) * 64, t0 + qt * 512:t0 + (qt + 1) * 512], O_[0:64, :], reads=[BO_])
        yield 1.0


def make_p3(c):
    S, nc, sb, seqs, tok0, NS = c["S"], c["nc"], c["sb"], c["seqs"], c["tok0"], c["NS"]
    bank0 = c["bank"]
    bankA = lambda: bank0("a")
    bankB = lambda: bank0("b")
    identb, Ufb, Ubb, Lfb, Lbb, onesb, B_c, epst = c["identb"], c["Ufb"], c["Ubb"], c["Lfb"], c["Lbb"], c["onesb"], c["B_c"], c["epst"]
    xtok_d, bcT_d, dt_d, zs_d, prevb_d, ygT_d, alog, dskip = c["xtok_d"], c["bcT_d"], c["dt_d"], c["zs_d"], c["prevb_d"], c["ygT_d"], c["alog"], c["dskip"]
    Q = "pool"
    st = c["gstack"]
    two = lambda name, shape, dt: [sb(st, name, shape, dt) for _ in range(2)]
    a_bc = sb(st, "p3_a", [128, 64], F32)
    D_bc = sb(st, "p3_D", [128, 32], F32)
    Hs = sb(st, "p3_H", [128, 2048], F32)
    Hb16 = two("p3_Hb", [128, 2048], BF16)
    xts = two("p3_xt", [128, 3072], BF16)
    bcs = two("p3_bc", [128, 16, 128], BF16)
    dts_ = two("p3_dt", [128, 64], F32)
    zss = two("p3_zs", [128, 2048], BF16)
    pvs = two("p3_pv", [128, 2048], BF16)
    dA = sb(st, "p3_dA", [128, 64], F32)
    dAh = two("p3_dAh", [128, 64], BF16)
    dAhf = sb(st, "p3_dAhf", [128, 64], F32)
    dAl = two("p3_dAl", [128, 64], BF16)
    ct = sb(st, "p3_ct", [128, 128], F32)
    Et = sb(st, "p3_E", [128, 64], F32)
    wgt = sb(st, "p3_w", [128, 64], F32)
    ec = two("p3_ec", [128, 64], F32)
    dec = sb(st, "p3_dec", [128, 64], F32)
    xw = sb(st, "p3_xw", [128, 2048], BF16)
    xdt = [two("p3_xdt", [128, 2048], BF16) for _ in range(2)]
    Rs = two("p3_R", [128, 2, 8, 128], BF16)
    CBm = two("p3_CBm", [128, 2, 8, 128], BF16)
    Exs = [sb(st, "p3_Ex", [128, 512], BF16) for _ in range(4)]
    MTs = two("p3_MT", [128, 2, 8, 128], BF16)
    y = sb(st, "p3_y", [128, 2048], F32)
    T1 = sb(st, "p3_t1", [128, 512], F32)
    T2 = sb(st, "p3_t2", [128, 512], F32)
    T3s = two("p3_t3", [128, 512], F32)
    yz = sb(st, "p3_yz", [128, 2048], F32)
    sqt = sb(st, "p3_sq", [128, 2048], BF16)
    gs = sb(st, "p3_gs", [128, 8], F32)
    ygn = sb(st, "p3_ygn", [128, 2048], BF16)
    ygs = sb(st, "p3_ygs", [128, 16, 128], BF16)
    B2 = lambda: [Buf(), Buf()]
    Bk, BH, Bprevd = Buf(), Buf(), Buf()
    BHb, Bxt, Bbc, Bdt, Bzs, Bpv = B2(), B2(), B2(), B2(), B2(), B2()
    BdA, BdAx, Bct, BE, Bec, Bxw, BCB = Buf(), B2(), Buf(), Buf(), B2(), Buf(), B2()
    Bxdt = [B2(), B2()]
    BRs, BEx, BMTs = B2(), [Buf() for _ in range(4)], B2()
    BT3s, Byz = B2(), Buf()
    By, BT1, BT2, BT3, Bsq, Bgs, Bygn, Bygs = Buf(), Buf(), Buf(), Buf(), Buf(), Buf(), Buf(), Buf()
    S.dma(Q, a_bc[:], alog.partition_broadcast(128), writes=[Bk])
    S.dma(Q, D_bc[:], dskip.partition_broadcast(128), writes=[Bk])
    S.op("act", lambda a: a.activation(a_bc[:], a_bc[:], AF.Exp), reads=[Bk], writes=[Bk])
    S.op("dve", lambda v: v.tensor_scalar(a_bc[:], a_bc[:], -1.0, None, ALU.mult), reads=[Bk], writes=[Bk])
    nchs = [sq // 128 for sq in seqs]
    NCH = float(sum(nchs))
    gstart = [sum(nchs[:i]) for i in range(NS)]
    sh = {"front": 0, "back": 0}
    TA, TB = 70.0, 200.0

    def bc3(ap2, n):
        return ap2.unsqueeze(2).broadcast_to([128, ap2.shape[1], n])

    v3 = lambda ap: ap.rearrange("p (h f) -> p h f", f=64)

    def prep(k, dtt, Bd):
        S.op("dve", lambda v: v.tensor_tensor(dA[:], dtt[:], a_bc[:], ALU.mult), reads=[Bd, Bk], writes=[BdA])
        yield
        S.op("dve", lambda v: v.tensor_copy(dAh[k][:], dA[:]), reads=[BdA], writes=[BdAx[k]])
        yield
        S.op("dve", lambda v: v.tensor_copy(dAhf[:], dAh[k][:]), reads=[BdAx[k]], writes=[BdA])
        yield
        S.op("dve", lambda v: v.tensor_tensor(dAhf[:], dA[:], dAhf[:], ALU.subtract), reads=[BdA], writes=[BdA])
        yield
        S.op("dve", lambda v: v.tensor_copy(dAl[k][:], dAhf[:]), reads=[BdA], writes=[BdAx[k]])
        yield
        pc, pbc = bankA()
        for (o0, o1, L) in ((0, 32, Ufb), (32, 64, Ubb)):
            S.mm(pc[:, o0:o1], [(L, dAh[k][:, o0:o1]), (L, dAl[k][:, o0:o1])], reads=[BdAx[k], B_c], writes=[pbc])
            yield
        S.mm(pc[:, 64:128], [(onesb, dAh[k][:]), (onesb, dAl[k][:])], reads=[BdAx[k], B_c], writes=[pbc])
        yield
        S.op("dve", lambda v: v.tensor_copy(ct[:], pc[:, 0:128]), reads=[pbc], writes=[Bct])
        yield
        S.op("dve", lambda v: v.tensor_tensor(Et[:], ct[:, 64:128], ct[:, 0:64], ALU.subtract), reads=[Bct], writes=[BE])
        yield
        S.op("act", lambda a: a.activation(Et[:], Et[:], AF.Exp), reads=[BE], writes=[BE])
        yield
        S.op("dve", lambda v: v.tensor_tensor(wgt[:], dtt[:], Et[:], ALU.mult), reads=[BE, Bd], writes=[BE])
        yield
        S.op("act", lambda a: a.activation(dec[:], ct[:, 64:128], AF.Exp), reads=[Bct], writes=[BE])
        yield
        S.op("act", lambda a: a.activation(ec[k][:], ct[:, 0:64], AF.Exp), reads=[Bct], writes=[Bec[k]])
        yield

    def state_update(X, BX, d):
        S.op("dve", lambda v: v.tensor_tensor(v3(xw[:]), v3(X[:, 0:2048]), bc3(wgt[:, d * 32:(d + 1) * 32], 64), ALU.mult), reads=[BX, BE], writes=[Bxw])
        yield
        S.op("dve", lambda v: v.tensor_tensor(v3(Hs[:]), v3(Hs[:]), bc3(dec[:, d * 32:(d + 1) * 32], 64), ALU.mult), reads=[BE, BHb[0], BHb[1]], writes=[BH])
        yield
        for gp in range(4):
            ps, pbs = bankA()
            for gg in range(2):
                g = gp * 2 + gg
                S.op("pe", lambda pe: pe.matmul(ps[:, gg * 256:(gg + 1) * 256], X[:, 2048 + g * 128:2048 + (g + 1) * 128], xw[:, g * 256:(g + 1) * 256], start=True, stop=True), reads=[BX, Bxw], writes=[pbs])
                yield
            S.op("dve", lambda v: v.tensor_tensor(Hs[:, gp * 512:(gp + 1) * 512], Hs[:, gp * 512:(gp + 1) * 512], ps, ALU.add), reads=[pbs], writes=[BH])
            yield

    def genA():
        it = 0
        for si in range(NS):
            Sq, t0 = seqs[si], tok0[si]
            nch = nchs[si]
            base = gstart[si] / NCH
            while sh["back"] < gstart[si]:
                yield None
            S.op("pool", lambda v: v.memset(Hs[:], 0.0), reads=[BHb[0], BHb[1]], writes=[BH])
            for cidx in range(nch - 1, -1, -1):
                g0 = t0 + cidx * 128
                k = it % 2
                it += 1
                S.dma(Q, xts[k][:], xtok_d[g0:g0 + 128, :], writes=[Bxt[k]])
                S.dma(Q, dts_[k][:], dt_d[g0:g0 + 128, :], writes=[Bdt[k]])
                S.op("act", lambda a: a.copy(Hb16[k][:], Hs[:]), reads=[BH], writes=[BHb[k]])
                S.dma(Q, prevb_d[cidx], Hb16[k][:], reads=[BHb[k]], writes=[Bprevd])
                yield base
                if cidx > 0:
                    for _ in prep(k, dts_[k], Bdt[k]):
                        yield base
                    for _ in state_update(xts[k], Bxt[k], 1):
                        yield base
            S.op("pool", lambda v: v.memset(Hs[:], 0.0), reads=[BHb[0], BHb[1]], writes=[BH])
            for cidx in range(nch):
                gidx = gstart[si] + cidx
                k = gidx % 2
                n = [0]

                def pr():
                    n[0] += 1
                    return (gidx + min(n[0] / TA, 0.99)) / NCH
                while sh["back"] < gidx - 1:
                    yield None
                g0 = t0 + cidx * 128
                X, BX, BCt, BBC, dtt, Bd = xts[k], Bxt[k], bcs[k], Bbc[k], dts_[k], Bdt[k]
                S.dma(Q, X[:], xtok_d[g0:g0 + 128, :], writes=[BX])
                S.dma(Q, BCt[:], bcT_d[:, g0:g0 + 128].rearrange("(j p) t -> p j t", p=128), writes=[BBC])
                S.dma(Q, dtt[:], dt_d[g0:g0 + 128, :], writes=[Bd])
                S.dma(Q, zss[k][:], zs_d[g0:g0 + 128, :], writes=[Bzs[k]])
                S.dma(Q, pvs[k][:], prevb_d[cidx], reads=[Bprevd], writes=[Bpv[k]])
                S.op("act", lambda a: a.copy(Hb16[k][:], Hs[:]), reads=[BH], writes=[BHb[k]])
                yield pr()
                for _ in prep(k, dtt, Bd):
                    yield pr()
                if cidx < nch - 1:
                    for _ in state_update(X, BX, 0):
                        yield pr()
                for g4 in range(2):
                    pcb, pbcb = bankA()
                    for gg in range(4):
                        g = g4 * 4 + gg
                        S.op("pe", lambda pe: pe.matmul(pcb[:, gg * 128:(gg + 1) * 128], BCt[:, g, :], BCt[:, 8 + g, :], start=True, stop=True), reads=[BBC], writes=[pbcb])
                        yield pr()
                    for d, Um in ((0, Ufb), (1, Ubb)):
                        S.op("dve", lambda v: v.tensor_tensor(CBm[k][:, d, g4 * 4:(g4 + 1) * 4, :], pcb.rearrange("p (g l) -> p g l", l=128), Um.unsqueeze(1).broadcast_to([128, 4, 128]), ALU.mult), reads=[pbcb, B_c], writes=[BCB[k]])
                        yield pr()
                for d in range(2):
                    S.op("pool", lambda v: v.tensor_tensor(v3(xdt[k][d][:]), v3(X[:, 0:2048]), bc3(dtt[:, d * 32:(d + 1) * 32], 64), ALU.mult), reads=[BX, Bd], writes=[Bxdt[k][d]])
                    yield pr()
                sh["front"] = gidx + 1
                yield (gidx + 0.995) / NCH
        yield 1.0

    def genB():
        ei = [0]

        def stage2(k, gp):
            MT, BMT = MTs[gp % 2], BMTs[gp % 2]
            X, BX = xts[k], Bxt[k]
            dl = ((0, Lfb, Ufb), (1, Lbb, Ubb))
            for d, Lm, Um in dl:
                h0 = d * 32 + gp * 8
                for kk, src in ((0, dAh[k]), (1, dAl[k])):
                    S.op("dve", lambda v: v.tensor_tensor(Rs[d][:, kk, :, :], Um.unsqueeze(1).broadcast_to([128, 8, 128]), bc3(src[:, h0:h0 + 8], 128), ALU.mult), reads=[BdAx[k], B_c], writes=[BRs[d]])
                    yield
            S.op("pool", lambda v: v.tensor_tensor(v3(T3s[gp % 2][:]), v3(X[:, gp * 512:(gp + 1) * 512]), bc3(D_bc[:, gp * 8:gp * 8 + 8], 64), ALU.mult), reads=[BX, Bk], writes=[BT3s[gp % 2]])
            yield
            segs = []
            for d, Lm, Um in dl:
                for gg in range(2):
                    pseg, pbseg = bankB()
                    S.mm(pseg, [(Lm, Rs[d][:, kk, gg * 4:(gg + 1) * 4, :].rearrange("p h l -> p (h l)")) for kk in range(2)], reads=[BRs[d], B_c], writes=[pbseg])
                    yield
                    Ex, BE_ = Exs[ei[0] % 4], BEx[ei[0] % 4]
                    ei[0] += 1
                    S.op("act", lambda a: a.activation(Ex[:], pseg, AF.Exp), reads=[pbseg], writes=[BE_])
                    yield
                    segs.append((d, gg, Ex, BE_))
            for (d, gg, Ex, BE_) in segs:
                g = gp * 2 + gg
                S.op("dve", lambda v: v.tensor_tensor(MT[:, d, gg * 4:(gg + 1) * 4, :], Ex[:].rearrange("p (h l) -> p h l", l=128), CBm[k][:, d, g:g + 1, :].broadcast_to([128, 4, 128]), ALU.mult), reads=[BE_, BCB[k]], writes=[BMT])
                yield

        def stage3(k, gp):
            MT, BMT = MTs[gp % 2], BMTs[gp % 2]
            BCt, BBC = bcs[k], Bbc[k]
            T3, BT3 = T3s[gp % 2], BT3s[gp % 2]
            py, pby = bankB()
            pof, pbof = bankB()
            pob, pbob = bankB()
            for gg in range(2):
                g = gp * 2 + gg
                S.op("pe", lambda pe: pe.matmul(pof[:, gg * 256:(gg + 1) * 256], BCt[:, 8 + g, :], Hb16[k][:, g * 256:(g + 1) * 256], start=True, stop=True), reads=[BBC, BHb[k]], writes=[pbof])
                yield
                S.op("pe", lambda pe: pe.matmul(pob[:, gg * 256:(gg + 1) * 256], BCt[:, 8 + g, :], pvs[k][:, g * 256:(g + 1) * 256], start=True, stop=True), reads=[BBC, Bpv[k]], writes=[pbob])
                yield
            for gg in range(2):
                g = gp * 2 + gg
                for j in range(4):
                    h = 4 * g + j
                    S.mm(py[:, gg * 256 + j * 64:gg * 256 + (j + 1) * 64], [(MT[:, d, gg * 4 + j, :], xdt[k][d][:, h * 64:(h + 1) * 64]) for d in range(2)], reads=[BMT, Bxdt[k][0], Bxdt[k][1]], writes=[pby])
                    yield
            S.op("dve", lambda v: v.tensor_tensor(v3(T1[:]), v3(pof), bc3(ec[k][:, gp * 8:gp * 8 + 8], 64), ALU.mult), reads=[pbof, Bec[k]], writes=[BT1])
            yield
            S.op("dve", lambda v: v.tensor_tensor(v3(T2[:]), v3(pob), bc3(ec[k][:, 32 + gp * 8:32 + gp * 8 + 8], 64), ALU.mult), reads=[pbob, Bec[k]], writes=[BT2])
            yield
            S.op("dve", lambda v: v.tensor_tensor(T2[:], T2[:], T3[:], ALU.add), reads=[BT3], writes=[BT2])
            yield
            S.op("dve", lambda v: v.tensor_tensor(T1[:], T1[:], py, ALU.add), reads=[pby], writes=[BT1])
            yield
            S.op("dve", lambda v: v.tensor_tensor(y[:, gp * 512:(gp + 1) * 512], T1[:], T2[:], ALU.add), reads=[BT1, BT2], writes=[By])
            yield

        def tail(k, g0):
            Z, BZ = zss[k], Bzs[k]
            S.op("dve", lambda v: v.tensor_tensor(yz[:], y[:], Z[:], ALU.mult), reads=[BZ, By], writes=[Byz])
            yield
            S.op("pool", lambda v: v.tensor_tensor(sqt[:], yz[:], yz[:], ALU.mult), reads=[Byz], writes=[Bsq])
            yield
            S.op("dve", lambda v: v.tensor_reduce(gs[:], sqt[:].rearrange("p (g f) -> p g f", f=256), AX.X, ALU.add), reads=[Bsq], writes=[Bgs])
            yield
            S.op("act", lambda a: a.activation(gs[:], gs[:], AF.Sqrt, bias=epst[:], scale=1.0 / 256), reads=[Bgs, B_c], writes=[Bgs])
            yield
            S.op("dve", lambda v: v.reciprocal(gs[:], gs[:]), reads=[Bgs], writes=[Bgs])
            yield
            S.op("dve", lambda v: v.tensor_tensor(ygn[:].rearrange("p (g f) -> p g f", f=256), yz[:].rearrange("p (g f) -> p g f", f=256), bc3(gs[:], 256), ALU.mult), reads=[Byz, Bgs], writes=[Bygn])
            yield
            for hf in range(2):
                pt, pbt = bankB()
                ptb = pt.bitcast(BF16)
                for jj in range(8):
                    j = hf * 8 + jj
                    S.op("pe", lambda pe: pe.transpose(ptb[:, jj * 128:(jj + 1) * 128], ygn[:, j * 128:(j + 1) * 128], identb), reads=[Bygn, B_c], writes=[pbt])
                    yield
                S.op("act", lambda a: a.copy(ygs[:, hf * 8:(hf + 1) * 8, :], ptb[:, 0:1024].rearrange("p (j t) -> p j t", t=128)), reads=[pbt], writes=[Bygs])
                yield
            S.dma(Q, ygT_d[:, g0:g0 + 128].rearrange("(j p) t -> p j t", p=128), ygs[:], reads=[Bygs])
            yield

        pending = None
        for si in range(NS):
            Sq, t0 = seqs[si], tok0[si]
            for cidx in range(nchs[si]):
                gidx = gstart[si] + cidx
                k = gidx % 2
                n = [0]

                def pr():
                    n[0] += 1
                    return (gidx + min(n[0] / TB, 0.99)) / NCH
                while sh["front"] <= gidx:
                    yield None
                g0 = t0 + cidx * 128
                for _ in stage2(k, 0):
                    yield pr()
                if pending is not None:
                    pk, pg0, pgidx = pending
                    for _ in tail(pk, pg0):
                        yield pr()
                    sh["back"] = pgidx + 1
                    yield pr()
                for gp in range(4):
                    if gp + 1 < 4:
                        for _ in stage2(k, gp + 1):
                            yield pr()
                    for _ in stage3(k, gp):
                        yield pr()
                pending = (k, g0, gidx)
                if cidx == nchs[si] - 1:
                    for _ in tail(k, g0):
                        yield pr()
                    pending = None
                    sh["back"] = gidx + 1
                    yield pr()
        yield 1.0

    return genA(), genB()


def emit_p4a(c):
    from contextlib import ExitStack
    S, nc, sb, bank, seqs, tok0, NS = c["S"], c["nc"], c["sb"], c["bank"], c["seqs"], c["tok0"], c["NS"]
    identb, B_c, epst = c["identb"], c["B_c"], c["epst"]
    ygT_d, OT_d, gT_d, x_d, x1_d, h2T_d, gate_d, adaT_d = c["ygT_d"], c["OT_d"], c["gT_d"], c["x_d"], c["x1_d"], c["h2T_d"], c["gate_d"], c["adaT_d"]
    w_ssd_out, w_mla_out, w_o, g_ssd = c["w_ssd_out"], c["w_mla_out"], c["w_o"], c["g_ssd"]
    with ExitStack() as st:
        ws = sb(st, "p4_ws", [128, 16, 1024], BF16)
        wm = sb(st, "p4_wm", [128, 8, 1024], BF16)
        wo = sb(st, "p4_wo", [128, 8, 1024], BF16)
        gsn = sb(st, "p4_gsn", [128, 16], F32)
        ygl = [sb(st, "p4_yg", [128, 16, 512], BF16) for _ in range(2)]
        otl = [sb(st, "p4_ot", [128, 8, 512], BF16) for _ in range(2)]
        Bygl, Botl = [Buf(), Buf()], [Buf(), Buf()]
        tcount = [0]
        gt = sb(st, "p4_gt", [128, 16, 512], BF16)
        xt = sb(st, "p4_x", [128, 4, 1024], F32)
        mixf = sb(st, "p4_mixf", [128, 512], F32)
        mixf2 = sb(st, "p4_mixf2", [128, 512], F32)
        mix = sb(st, "p4_mix", [128, 8, 512], BF16)
        g1 = sb(st, "p4_g1", [128, 1024], F32)
        ab = sb(st, "p4_ab", [128, 2, 8], F32)
        x1 = sb(st, "p4_x1", [128, 4, 1024], F32)
        junk = sb(st, "p4_junk", [128, 1024], BF16)
        ss = sb(st, "p4_ss", [128, 4], F32)
        xn = sb(st, "p4_xn", [128, 4, 1024], BF16)
        h2 = sb(st, "p4_h2", [128, 8, 512], BF16)
        Bw, Byg, Bot, Bgt, Bx, Bmf, Bmf2, Bmix, Bg1, Bab, Bx1, Bss, Bxn, Bh2 = (Buf() for _ in range(14))
        S.dma("pool", ws[:], w_ssd_out.rearrange("(kc p) f -> p kc f", p=128), writes=[Bw])
        S.dma("pool", wm[:], w_mla_out.rearrange("(kc p) f -> p kc f", p=128), writes=[Bw])
        S.dma("pool", wo[:], w_o.rearrange("(kc p) f -> p kc f", p=128), writes=[Bw])
        with nc.allow_non_contiguous_dma(reason="tiny"):
            S.dma("sp", gsn[:], g_ssd.rearrange("(j p) -> p j", p=128), writes=[Bw])
        for kc in range(16):
            S.op("dve", lambda v: v.tensor_scalar(ws[:, kc, :], ws[:, kc, :], gsn[:, kc:kc + 1], None, ALU.mult), reads=[Bw], writes=[Bw])
        for si in range(NS):
            S.dma("sp", g1[:], gate_d[si, 0, :].partition_broadcast(128), writes=[Bg1])
            with nc.allow_non_contiguous_dma(reason="tiny"):
                S.dma("sp", ab[:, 0, :], adaT_d[si, 2, :].rearrange("(kc p) -> p kc", p=128), writes=[Bab])
                S.dma("sp", ab[:, 1, :], adaT_d[si, 3, :].rearrange("(kc p) -> p kc", p=128), writes=[Bab])
            for ti in range(seqs[si] // 512):
                g0 = tok0[si] + ti * 512
                yg, ot, Byg, Bot = ygl[tcount[0] % 2], otl[tcount[0] % 2], Bygl[tcount[0] % 2], Botl[tcount[0] % 2]
                tcount[0] += 1
                S.dma("sp", yg[:], ygT_d[:, g0:g0 + 512].rearrange("(j p) t -> p j t", p=128), writes=[Byg])
                S.dma("sp", ot[:], OT_d[:, g0:g0 + 512].rearrange("(j p) t -> p j t", p=128), writes=[Bot])
                S.dma("sp", gt[:], gT_d[:, g0:g0 + 512].rearrange("(j p) t -> p j t", p=128), writes=[Bgt])
                S.dma("sp", xt[:], x_d[g0:g0 + 512, :].rearrange("(s p) f -> p s f", p=128), writes=[Bx])
                for oc in range(8):
                    pa, pba = bank()
                    S.mm(pa, [(ws[:, kc, oc * 128:(oc + 1) * 128], yg[:, kc, :]) for kc in range(16)], reads=[Bw, Byg], writes=[pba])
                    pm, pbm = bank()
                    S.mm(pm, [(wm[:, kc, oc * 128:(oc + 1) * 128], ot[:, kc, :]) for kc in range(8)], reads=[Bw, Bot], writes=[pbm])
                    S.op("dve", lambda v: v.tensor_tensor(mixf[:], pa[:], gt[:, oc, :], ALU.mult), reads=[pba, Bgt], writes=[Bmf])
                    S.op("dve", lambda v: v.tensor_tensor(mixf2[:], pm[:], gt[:, 8 + oc, :], ALU.mult), reads=[pbm, Bgt], writes=[Bmf2])
                    S.op("dve", lambda v: v.tensor_tensor(mix[:, oc, :], mixf[:], mixf2[:], ALU.add), reads=[Bmf, Bmf2], writes=[Bmix])
                for s4 in range(4):
                    for hf in range(2):
                        po, pbo = bank()
                        S.mm(po, [(mix[:, kc, s4 * 128:(s4 + 1) * 128], wo[:, kc, hf * 512:(hf + 1) * 512]) for kc in range(8)], reads=[Bw, Bmix], writes=[pbo])
                        S.op("dve", lambda v: v.tensor_tensor(x1[:, s4, hf * 512:(hf + 1) * 512], po[:], g1[:, hf * 512:(hf + 1) * 512], ALU.mult), reads=[pbo, Bg1], writes=[Bx1])
                    S.op("dve", lambda v: v.tensor_tensor(x1[:, s4, :], x1[:, s4, :], xt[:, s4, :], ALU.add), reads=[Bx], writes=[Bx1])
                    S.op("act", lambda a: a.activation(junk[:], x1[:, s4, :], AF.Square, accum_out=ss[:, s4:s4 + 1]), reads=[Bx1], writes=[Bss])
                S.dma("st", x1_d[g0:g0 + 512, :].rearrange("(s p) f -> p s f", p=128), x1[:], reads=[Bx1])
                S.op("act", lambda a: a.activation(ss[:], ss[:], AF.Sqrt, bias=epst[:], scale=1.0 / 1024), reads=[Bss, B_c], writes=[Bss])
                S.op("dve", lambda v: v.reciprocal(ss[:], ss[:]), reads=[Bss], writes=[Bss])
                for s4 in range(4):
                    S.op("dve", lambda v: v.tensor_scalar(xn[:, s4, :], x1[:, s4, :], ss[:, s4:s4 + 1], None, ALU.mult), reads=[Bx1, Bss], writes=[Bxn])
                for kc in range(8):
                    pt, pbt = bank()
                    ptb = pt[:].bitcast(BF16)
                    for s4 in range(4):
                        S.op("pe", lambda pe: pe.transpose(ptb[:, s4 * 128:(s4 + 1) * 128], xn[:, s4, kc * 128:(kc + 1) * 128], identb), reads=[Bxn, B_c], writes=[pbt])
                    S.op("dve", lambda v: v.tensor_scalar(h2[:, kc, :], ptb[:, 0:512], ab[:, 0, kc:kc + 1], ab[:, 1, kc:kc + 1], ALU.mult, ALU.add), reads=[pbt, Bab], writes=[Bh2])
                S.dma("st", h2T_d[:, g0:g0 + 512].rearrange("(j p) t -> p j t", p=128), h2[:], reads=[Bh2])
        S.barrier()


def emit_p4b(c):
    from contextlib import ExitStack
    S, nc, sb, bank, seqs, tok0, NS = c["S"], c["nc"], c["sb"], c["bank"], c["seqs"], c["tok0"], c["NS"]
    B_c, epst = c["B_c"], c["epst"]
    x1_d, h2T_d, gate_d, y_d, w_mlp_in, w_mlp_out, g_final = c["x1_d"], c["h2T_d"], c["gate_d"], c["y_d"], c["w_mlp_in"], c["w_mlp_out"], c["g_final"]
    TT = 256
    with ExitStack() as st:
        w1 = sb(st, "p5_w1", [128, 8, 4096], BF16)
        w2 = sb(st, "p5_w2", [128, 32, 1024], BF16)
        gf = sb(st, "p5_gf", [128, 1024], F32)
        g2 = sb(st, "p5_g2", [128, 1024], F32)
        h2 = [sb(st, "p5_h2", [128, 8, TT], BF16) for _ in range(2)]
        x1 = [sb(st, "p5_x1", [128, 2, 1024], F32) for _ in range(2)]
        rl = [sb(st, "p5_rl", [128, TT], F32) for _ in range(2)]
        rT = sb(st, "p5_rT", [128, 32, TT], BF16)
        x2 = sb(st, "p5_x2", [128, 2, 1024], F32)
        junk = sb(st, "p5_junk", [128, 1024], BF16)
        ss = sb(st, "p5_ss", [128, 2], F32)
        yo = sb(st, "p5_yo", [128, 2, 1024], F32)
        Bw, Bg2, Bh2, Bx1, Brl, BrT, Bx2, Bss, Byo = Buf(), Buf(), [Buf(), Buf()], [Buf(), Buf()], [Buf(), Buf()], Buf(), Buf(), Buf(), Buf()
        S.dma("pool", w1[:], w_mlp_in.rearrange("(kc p) f -> p kc f", p=128), writes=[Bw])
        for q4 in range(4):
            S.dma("pool", w2[:, q4 * 8:(q4 + 1) * 8, :], w_mlp_out[q4 * 1024:(q4 + 1) * 1024, :].rearrange("(kc p) f -> p kc f", p=128), writes=[Bw])
        S.dma("sp", gf[:], g_final.partition_broadcast(128), writes=[Bw])
        it = 0
        for si in range(NS):
            S.dma("sp", g2[:], gate_d[si, 1, :].partition_broadcast(128), writes=[Bg2])
            for ti in range(seqs[si] // TT):
                g0 = tok0[si] + ti * TT
                k = it % 2
                it += 1
                H, X1 = h2[k], x1[k]
                S.dma("sp", H[:], h2T_d[:, g0:g0 + TT].rearrange("(j p) t -> p j t", p=128), writes=[Bh2[k]])
                S.dma("sp", X1[:], x1_d[g0:g0 + TT, :].rearrange("(s p) f -> p s f", p=128), writes=[Bx1[k]])
                for fc in range(32):
                    pf, pbf = bank()
                    S.mm(pf[:, 0:TT], [(w1[:, kc, fc * 128:(fc + 1) * 128], H[:, kc, :]) for kc in range(8)], reads=[Bw, Bh2[k]], writes=[pbf])
                    RL, BRL = rl[fc % 2], Brl[fc % 2]
                    S.op("act", lambda a: a.activation(RL[:], pf[:, 0:TT], AF.Relu), reads=[pbf], writes=[BRL])
                    S.op("pool" if fc % 2 else "dve", lambda v: v.tensor_tensor(rT[:, fc, :], RL[:], RL[:], ALU.mult), reads=[BRL], writes=[BrT])
                for s2 in range(2):
                    for hf in range(2):
                        po, pbo = bank()
                        S.mm(po, [(rT[:, kc, s2 * 128:(s2 + 1) * 128], w2[:, kc, hf * 512:(hf + 1) * 512]) for kc in range(32)], reads=[Bw, BrT], writes=[pbo])
                        S.op("dve", lambda v: v.tensor_tensor(x2[:, s2, hf * 512:(hf + 1) * 512], po[:], g2[:, hf * 512:(hf + 1) * 512], ALU.mult), reads=[pbo, Bg2], writes=[Bx2])
                    S.op("pool", lambda v: v.tensor_tensor(x2[:, s2, :], x2[:, s2, :], X1[:, s2, :], ALU.add), reads=[Bx1[k]], writes=[Bx2])
                    S.op("act", lambda a: a.activation(junk[:], x2[:, s2, :], AF.Square, accum_out=ss[:, s2:s2 + 1]), reads=[Bx2], writes=[Bss])
                S.op("act", lambda a: a.activation(ss[:], ss[:], AF.Sqrt, bias=epst[:], scale=1.0 / 1024), reads=[Bss, B_c], writes=[Bss])
                S.op("dve", lambda v: v.reciprocal(ss[:], ss[:]), reads=[Bss], writes=[Bss])
                for s2 in range(2):
                    S.op("dve", lambda v: v.scalar_tensor_tensor(yo[:, s2, :], x2[:, s2, :], ss[:, s2:s2 + 1], gf[:], ALU.mult, ALU.mult), reads=[Bx2, Bss, Bw], writes=[Byo])
                S.dma("st", y_d[g0:g0 + TT, :].rearrange("(s p) f -> p s f", p=128), yo[:], reads=[Byo])
        S.barrier()


def core_inputs(x_all, c_all, W, g_final, cos, sin):
    f = lambda a: np.ascontiguousarray(np.asarray(a, dtype=np.float32))
    return {
        "x": f(x_all), "c": f(c_all),
        "w_ada": f(W["w_ada"]), "b_ada": f(W["b_ada"]), "g_norm1": f(W["g_norm1"]), "w_in": f(W["w_in"]),
        "conv_w": f(W["conv_w"]), "conv_b": f(W["conv_b"]),
        "dt_bias": f(np.concatenate([W["dt_bias_fwd"], W["dt_bias_bwd"]])),
        "a_log": f(np.concatenate([W["a_log_fwd"], W["a_log_bwd"]])),
        "d_skip": f(W["d_skip"]), "g_ssd_norm": f(W["g_ssd_norm"]), "w_ssd_out": f(W["w_ssd_out"]),
        "g_q_norm": f(W["g_q_norm"]), "w_q_b": f(W["w_q_b"]), "g_kv_norm": f(W["g_kv_norm"]), "w_kv_b": f(W["w_kv_b"]),
        "w_mla_out": f(W["w_mla_out"]), "w_o": f(W["w_o"]), "g_norm2": f(W["g_norm2"]),
        "w_mlp_in": f(W["w_mlp_in"]), "w_mlp_out": f(W["w_mlp_out"]), "g_final": f(g_final),
        "consts": make_consts(), "cos_t": f(cos), "sin_t": f(sin),
    }


_WNAMES = ["w_ada", "b_ada", "g_norm1", "w_in", "conv_w", "conv_b", "dt_bias_fwd", "dt_bias_bwd", "a_log_fwd",
           "a_log_bwd", "d_skip", "g_ssd_norm", "w_ssd_out", "g_q_norm", "w_q_b", "g_kv_norm", "w_kv_b",
           "w_mla_out", "w_o", "g_norm2", "w_mlp_in", "w_mlp_out"]


def kernel(x_prompt, x_sample, c_prompt, c_sample, g_final, **kw):
    W = {k: np.asarray(kw[k])[0] for k in _WNAMES}
    x_prompt = np.asarray(x_prompt); x_sample = np.asarray(x_sample)
    c_prompt = np.asarray(c_prompt); c_sample = np.asarray(c_sample)
    n = 8
    seqs = (2048, 2048, 4096, 4096)
    cos, sin = rope_tables(4096)
    import os
    ph = os.environ.get("KPHASES")
    nc = build(seqs=seqs, phases=tuple(ph.split(","))) if ph else build(seqs=seqs)
    in_maps = []
    for i in range(n):
        xa = np.concatenate([x_prompt[2 * i].reshape(-1, D), x_prompt[2 * i + 1].reshape(-1, D),
                             x_sample[2 * i].reshape(-1, D), x_sample[2 * i + 1].reshape(-1, D)], axis=0)
        ca = np.stack([c_prompt[2 * i], c_prompt[2 * i + 1], c_sample[2 * i], c_sample[2 * i + 1]], axis=0)
        in_maps.append(core_inputs(xa, ca, W, g_final, cos, sin))
    res = run_bass_kernel_spmd(nc, in_maps, core_ids=list(range(n)))
    yp = np.zeros((16, 2048, D), np.float32)
    ys = np.zeros((16, 4096, D), np.float32)
    for i in range(n):
        y = np.asarray(res.results[i]["y"])
        yp[2 * i] = y[0:2048]
        yp[2 * i + 1] = y[2048:4096]
        ys[2 * i] = y[4096:8192]
        ys[2 * i + 1] = y[8192:12288]
    return (yp, ys)
```
